# Optimizing a Trainium2 kernel written in Bass

```python
import math
import jax
import jax.numpy as jnp
from jax import lax
import numpy as np

D_MODEL = 1024
BATCH = 16
SEQ = 4096
DEPTH = 2

MEM_LEN = 256
HEAD_DIM = 64
ROPE_THETA = 500000.0
ROPE_FRACTION = 4
NORM_EPS = 1e-5

DIL_PATTERNS = ((128, 1), (512, 4), (2048, 16))
DIL_GROUPS = len(DIL_PATTERNS)
DIL_HEADS = 4
DIL_BLOCK = 128
DIL_WIDTH = DIL_HEADS * HEAD_DIM
DIFF_HEADS = 4
DIFF_WIDTH = DIFF_HEADS * 2 * HEAD_DIM
Q_BLOCK = 128
HGRN_HEADS = 4
HGRN_DK = 64
HGRN_DV = 64
HGRN_CHUNK = 64
HGRN_KW = HGRN_HEADS * HGRN_DK
HGRN_VW = HGRN_HEADS * HGRN_DV
RWKV_HEADS = 4
RWKV_HEAD_DIM = 64
RWKV_WIDTH = RWKV_HEADS * RWKV_HEAD_DIM
RWKV_DECAY_LORA = 64
RWKV_A_LORA = 64
RWKV_MV_LORA = 32
RWKV_GATE_LORA = 128
RWKV_GN_EPS = 1e-5 * RWKV_HEAD_DIM
N_BRANCH = 4
SEG_A = DIL_GROUPS * 3 * DIL_WIDTH
SEG_B = 3 * DIFF_WIDTH
SEG_C = 2 * HGRN_KW + 2 * HGRN_VW
SEG_D = 3 * RWKV_WIDTH + RWKV_DECAY_LORA + RWKV_A_LORA + RWKV_GATE_LORA
SEG_G = N_BRANCH * D_MODEL
SEGMENTS = (SEG_A, SEG_B, SEG_C, SEG_D, SEG_G)
N_IN = SEG_A + SEG_B + SEG_C + SEG_D + SEG_G
MEM_HEADS = 4
MEM_HEAD_DIM = D_MODEL // MEM_HEADS
D_FF = 2816
CONV_WIDTH = 3

kernel_name = 'hybrid_gated_parallel_mixer_trunk'
F32 = jnp.float32


def split_last(t, sizes):
    return jnp.split(t, np.cumsum(sizes)[:-1].tolist(), axis=-1)


def to_heads(t, n_heads):
    return t.reshape(t.shape[:-1] + (n_heads, t.shape[-1] // n_heads))


def rms_norm(x, g, eps=NORM_EPS):
    xf = x.astype(F32)
    y = xf * lax.rsqrt(jnp.mean(xf * xf, axis=-1, keepdims=True) + eps)
    return (y * g.astype(F32)).astype(x.dtype)


def partial_rotary(t, positions):
    rot = t.shape[-1] // ROPE_FRACTION
    half = rot // 2
    inv_freq = ROPE_THETA ** (-jnp.arange(half, dtype=F32) / half)
    ang = positions.astype(F32)[:, :, None, None] * inv_freq
    cos, sin = jnp.cos(ang), jnp.sin(ang)
    tf = t.astype(F32)
    t1, t2 = tf[..., :half], tf[..., half:rot]
    out = jnp.concatenate([t1 * cos - t2 * sin, t2 * cos + t1 * sin, tf[..., rot:]], axis=-1)
    return out.astype(t.dtype)


def dilated_window_attention(q, k, v, window, dilation):
    B, S, H, dh = q.shape
    span = window // dilation
    blk = DIL_BLOCK
    unit = dilation * blk
    s_pad = -(-S // unit) * unit
    sub_len = s_pad // dilation
    nb = sub_len // blk

    def to_blocks(t):
        t = jnp.pad(t, ((0, 0), (0, s_pad - S), (0, 0), (0, 0)))
        t = t.reshape(B, sub_len, dilation, H, dh).transpose(0, 2, 3, 1, 4)
        return t.reshape(B, dilation, H, nb, blk, dh)

    def with_prev(t):
        prev = jnp.pad(t[:, :, :, :-1], ((0, 0), (0, 0), (0, 0), (1, 0), (0, 0), (0, 0)))
        return jnp.concatenate([prev, t], axis=4)

    qb = to_blocks(q)
    kb = with_prev(to_blocks(k))
    vb = with_prev(to_blocks(v))
    s = jnp.einsum('brhnqd,brhnkd->brhnqk', qb, kb).astype(F32) * dh ** -0.5
    qi = jnp.arange(blk)[None, :, None]
    kj = jnp.arange(2 * blk)[None, None, :]
    bi = jnp.arange(nb)[:, None, None]
    dist = qi + blk - kj
    valid = (dist >= 0) & (dist <= span) & (bi * blk + kj >= blk)
    s = jnp.where(valid, s, -jnp.inf)
    m = jnp.max(s, axis=-1, keepdims=True)
    p = jnp.exp(s - m)
    den = jnp.sum(p, axis=-1, keepdims=True)
    o = jnp.einsum('brhnqk,brhnkd->brhnqd', (p / den).astype(v.dtype), vb)
    lse = (m + jnp.log(den))[..., 0]
    o = o.reshape(B, dilation, H, sub_len, dh).transpose(0, 3, 1, 2, 4).reshape(B, s_pad, H, dh)
    lse = lse.reshape(B, dilation, H, sub_len).transpose(0, 3, 1, 2).reshape(B, s_pad, H)
    return o[:, :S], lse[:, :S]


def dilated_branch(seg, positions):
    B, S, _ = seg.shape
    qkv = seg.reshape(B, S, DIL_GROUPS, 3, DIL_HEADS, HEAD_DIM)
    outs, lses = [], []
    for g, (window, dilation) in enumerate(DIL_PATTERNS):
        q = partial_rotary(qkv[:, :, g, 0], positions)
        k = partial_rotary(qkv[:, :, g, 1], positions)
        o, lse = dilated_window_attention(q, k, qkv[:, :, g, 2], window, dilation)
        outs.append(o)
        lses.append(lse)
    wts = jax.nn.softmax(jnp.stack(lses, 0), axis=0).astype(seg.dtype)
    o = jnp.einsum('gbsh,gbshd->bshd', wts, jnp.stack(outs, 0))
    return o.reshape(B, S, DIL_WIDTH)


def differential_attention(q1, q2, k1, k2, v, lam):
    B, S, H, d = q1.shape
    nb = S // Q_BLOCK
    scale = d ** -0.5
    kpos = jnp.arange(S)

    def blocks(t):
        return jnp.moveaxis(t.reshape(B, nb, Q_BLOCK, H, d), 1, 0)

    def probs(qx, kx, mask):
        s = jnp.einsum('bqhd,bkhd->bhqk', qx, kx).astype(F32) * scale
        return jax.nn.softmax(jnp.where(mask, s, -jnp.inf), axis=-1)

    def one_block(args):
        i, qa, qb = args
        mask = kpos[None, :] <= (i * Q_BLOCK + jnp.arange(Q_BLOCK))[:, None]
        p = probs(qa, k1, mask) - lam * probs(qb, k2, mask)
        return jnp.einsum('bhqk,bkhe->bqhe', p.astype(v.dtype), v)

    o = lax.map(one_block, (jnp.arange(nb), blocks(q1), blocks(q2)))
    return jnp.moveaxis(o, 0, 1).reshape(B, S, H, v.shape[-1])


def diff_branch(seg, positions, lam_vecs, norm_g, lam_init):
    B, S, _ = seg.shape
    q, k, v = split_last(seg, (DIFF_WIDTH, DIFF_WIDTH, DIFF_WIDTH))
    q = q.reshape(B, S, DIFF_HEADS, 2, HEAD_DIM)
    k = k.reshape(B, S, DIFF_HEADS, 2, HEAD_DIM)
    q1 = partial_rotary(q[:, :, :, 0], positions)
    q2 = partial_rotary(q[:, :, :, 1], positions)
    k1 = partial_rotary(k[:, :, :, 0], positions)
    k2 = partial_rotary(k[:, :, :, 1], positions)
    v = v.reshape(B, S, DIFF_HEADS, 2 * HEAD_DIM)
    lv = lam_vecs.astype(F32)
    lam = jnp.exp(jnp.sum(lv[0] * lv[1])) - jnp.exp(jnp.sum(lv[2] * lv[3])) + lam_init
    o = differential_attention(q1, q2, k1, k2, v, lam)
    o = rms_norm(o, norm_g) * (1.0 - lam_init)
    return o.reshape(B, S, DIFF_WIDTH)


def hgrn2_chunked(q, log_f, k, i):
    B, S, H, dk = q.shape
    dv = i.shape[-1]
    C = HGRN_CHUNK
    n = S // C

    def chunks(t):
        return t.astype(F32).reshape(B, n, C, H, t.shape[-1]).transpose(1, 0, 3, 2, 4)

    causal = jnp.tril(jnp.ones((C, C), dtype=bool))[:, :, None]

    def step(state, xs):
        qc, lfc, kc, ic = xs
        b = jnp.cumsum(lfc, axis=2)
        o_inter = jnp.einsum('bhck,bhkv->bhcv', qc * jnp.exp(b), state)
        diff = b[:, :, :, None, :] - b[:, :, None, :, :]
        decay = jnp.exp(jnp.where(causal, diff, -jnp.inf))
        att = jnp.einsum('bhtk,bhtsk,bhsk->bhts', qc, decay, kc)
        o = o_inter + jnp.einsum('bhts,bhsv->bhtv', att, ic)
        b_last = b[:, :, -1:, :]
        state = (jnp.exp(b_last[:, :, 0, :])[..., None] * state
                 + jnp.einsum('bhsk,bhsv->bhkv', kc * jnp.exp(b_last - b), ic))
        return state, o

    s0 = jnp.zeros((B, H, dk, dv), F32)
    _, o = lax.scan(step, s0, (chunks(q), chunks(log_f), chunks(k), chunks(i)))
    return o.transpose(1, 0, 3, 2, 4).reshape(B, S, H, dv)


def hgrn_branch(seg, lb, norm_g):
    B, S, _ = seg.shape
    q, f, i, g = split_last(seg, (HGRN_KW, HGRN_KW, HGRN_VW, HGRN_VW))
    lb = lb.astype(F32)
    f = f.astype(F32)
    log_f = jnp.logaddexp(jnp.log(lb), jnp.log1p(-lb) + jax.nn.log_sigmoid(f))
    k = (1.0 - lb) * jax.nn.sigmoid(-f)
    o = hgrn2_chunked(to_heads(jax.nn.silu(q), HGRN_HEADS), to_heads(log_f, HGRN_HEADS),
                      to_heads(k, HGRN_HEADS), to_heads(i, HGRN_HEADS))
    o = rms_norm(o, norm_g).reshape(B, S, HGRN_VW) * jax.nn.silu(g.astype(F32))
    return o.astype(seg.dtype)


def rwkv7_scan(r, w, k, v, kk, a):
    B, S, H, d = r.shape

    def step(state, xs):
        rt, wt, kt, vt, kkt, at = xs
        sa = jnp.einsum('bhvk,bhk->bhv', state, -kkt)
        state = (state * wt[:, :, None, :] + sa[..., None] * (kkt * at)[:, :, None, :]
                 + vt[..., None] * kt[:, :, None, :])
        return state, jnp.einsum('bhvk,bhk->bhv', state, rt)

    xs = tuple(jnp.moveaxis(t.astype(F32), 1, 0) for t in (r, w, k, v, kk, a))
    _, y = lax.scan(step, jnp.zeros((B, H, d, d), F32), xs)
    return jnp.moveaxis(y, 0, 1)


def rwkv_branch(seg, mu, w0, w2, a0, a2, g2, k_k, k_a, r_k, lnx_g, lnx_b, v_first, vmix):
    B, S, _ = seg.shape
    prev = jnp.pad(seg, ((0, 0), (1, 0), (0, 0)))[:, :-1]
    seg = seg + (prev - seg) * mu
    r, k, v, w_low, a_low, g_low = split_last(
        seg, (RWKV_WIDTH, RWKV_WIDTH, RWKV_WIDTH, RWKV_DECAY_LORA, RWKV_A_LORA, RWKV_GATE_LORA))
    w = -jax.nn.softplus(-(w0 + jnp.tanh(w_low) @ w2).astype(F32)) - 0.5
    decay = jnp.exp(-jnp.exp(w))
    a = jax.nn.sigmoid(a0 + a_low @ a2)
    g = jax.nn.sigmoid(g_low) @ g2
    kk = to_heads(k * k_k, RWKV_HEADS).astype(F32)
    kk = kk / jnp.maximum(jnp.sqrt(jnp.sum(kk * kk, axis=-1, keepdims=True)), 1e-12)
    k = k * (1.0 + (a - 1.0) * k_a)
    if vmix is None:
        v_first = v
    else:
        v0, v1, v2 = vmix
        v = v + (v_first - v) * jax.nn.sigmoid(v0 + (v @ v1) @ v2)
    rh, kh, vh, ah = (to_heads(t, RWKV_HEADS) for t in (r, k, v, a))
    y = rwkv7_scan(rh, to_heads(decay, RWKV_HEADS), kh, vh, kk, ah)
    mean = jnp.mean(y, axis=-1, keepdims=True)
    var = jnp.mean(jnp.square(y - mean), axis=-1, keepdims=True)
    y = ((y - mean) * lax.rsqrt(var + RWKV_GN_EPS)).reshape(B, S, RWKV_WIDTH) * lnx_g + lnx_b
    bonus = (jnp.sum(rh * kh * r_k, axis=-1, keepdims=True) * vh).reshape(B, S, RWKV_WIDTH)
    y = (y + bonus) * g
    return y.astype(seg.dtype), v_first


def setup_inputs(seed: int = 0) -> dict:
    key = jax.random.key(seed)
    keys = iter(jax.random.split(key, 64))
    L, D, W = DEPTH, D_MODEL, RWKV_WIDTH

    def normal(shape, scale):
        return scale * jax.random.normal(next(keys), shape, F32)

    def uniform(shape, lo, hi):
        return jax.random.uniform(next(keys), shape, F32, lo, hi)

    def gain(shape):
        return 1.0 + normal(shape, 0.02)

    def dense(shape):
        return normal(shape, shape[-2] ** -0.5)

    x = normal((BATCH, SEQ, D), 1.0)
    mem = normal((BATCH, MEM_LEN, D), 1.0)
    start = jax.random.randint(next(keys), (BATCH, 1), 0, 1024, jnp.int32)
    positions = start + jnp.arange(SEQ, dtype=jnp.int32)[None, :]
    conv_w = normal((L, CONV_WIDTH, 2 * D_FF), 0.2).at[:, CONV_WIDTH - 1].add(1.0)
    return {
        'x': x,
        'mem': mem,
        'positions': positions,
        'mix_norm_g': gain((L, D)),
        'w_in': dense((L, D, N_IN)),
        'diff_lam': normal((L, 4, HEAD_DIM), 0.1),
        'diff_norm_g': gain((L, 2 * HEAD_DIM)),
        'hgrn_lb_logits': normal((L, HGRN_KW), 1.0),
        'hgrn_norm_g': gain((L, HGRN_DV)),
        'rwkv_mu': uniform((L, SEG_D), 0.0, 1.0),
        'rwkv_w0': uniform((L, W), -5.0, 0.0),
        'rwkv_w2': normal((L, RWKV_DECAY_LORA, W), 0.1 * RWKV_DECAY_LORA ** -0.5),
        'rwkv_a0': normal((L, W), 0.1),
        'rwkv_a2': normal((L, RWKV_A_LORA, W), 0.1 * RWKV_A_LORA ** -0.5),
        'rwkv_g2': dense((L, RWKV_GATE_LORA, W)),
        'rwkv_k_k': 0.85 + normal((L, W), 0.05),
        'rwkv_k_a': 1.0 + normal((L, W), 0.05),
        'rwkv_r_k': normal((L, RWKV_HEADS, RWKV_HEAD_DIM), 0.1),
        'rwkv_lnx_g': gain((L, W)),
        'rwkv_lnx_b': normal((L, W), 0.02),
        'rwkv_v0': 1.0 + normal((L - 1, W), 0.1),
        'rwkv_v1': normal((L - 1, W, RWKV_MV_LORA), 0.1 * W ** -0.5),
        'rwkv_v2': normal((L - 1, RWKV_MV_LORA, W), 0.1 * RWKV_MV_LORA ** -0.5),
        'p_a': dense((L, DIL_WIDTH, D)),
        'p_b': dense((L, DIFF_WIDTH, D)),
        'p_c': dense((L, HGRN_VW, D)),
        'p_d': dense((L, RWKV_WIDTH, D)),
        'w_mix_out': dense((L, D, D)),
        'mem_q_norm_g': gain((L, D)),
        'mem_kv_norm_g': gain((L, D)),
        'w_mem_q': dense((L, D, D)),
        'w_mem_kv': dense((L, D, 2 * D)),
        'w_mem_o': dense((L, D, D)),
        'ffn_norm_g': gain((L, D)),
        'w_ffn_in': dense((L, D, 2 * D_FF)),
        'ffn_conv_w': conv_w,
        'ffn_conv_b': normal((L, 2 * D_FF), 0.02),
        'w_ffn_out': dense((L, D_FF, D)),
        'final_norm_g': gain((D,)),
    }


def reference(x, mem, positions, mix_norm_g, w_in, diff_lam, diff_norm_g, hgrn_lb_logits, hgrn_norm_g,
              rwkv_mu, rwkv_w0, rwkv_w2, rwkv_a0, rwkv_a2, rwkv_g2, rwkv_k_k, rwkv_k_a, rwkv_r_k,
              rwkv_lnx_g, rwkv_lnx_b, rwkv_v0, rwkv_v1, rwkv_v2, p_a, p_b, p_c, p_d, w_mix_out,
              mem_q_norm_g, mem_kv_norm_g, w_mem_q, w_mem_kv, w_mem_o,
              ffn_norm_g, w_ffn_in, ffn_conv_w, ffn_conv_b, w_ffn_out, final_norm_g):
    B, S, D = x.shape
    M = mem.shape[1]
    lb_all = jnp.cumsum(jax.nn.softmax(hgrn_lb_logits.astype(F32), axis=0), axis=0)
    lb_all = lb_all - lb_all[0:1]
    v_first = None
    for l in range(DEPTH):
        lam_init = 0.8 - 0.6 * math.exp(-0.3 * l)
        h = rms_norm(x, mix_norm_g[l])
        seg_a, seg_b, seg_c, seg_d, seg_g = split_last(h @ w_in[l], SEGMENTS)
        y_a = dilated_branch(seg_a, positions)
        y_b = diff_branch(seg_b, positions, diff_lam[l], diff_norm_g[l], lam_init)
        y_c = hgrn_branch(seg_c, lb_all[l], hgrn_norm_g[l])
        vmix = None if l == 0 else (rwkv_v0[l - 1], rwkv_v1[l - 1], rwkv_v2[l - 1])
        y_d, v_first = rwkv_branch(seg_d, rwkv_mu[l], rwkv_w0[l], rwkv_w2[l], rwkv_a0[l], rwkv_a2[l],
                                   rwkv_g2[l], rwkv_k_k[l], rwkv_k_a[l], rwkv_r_k[l], rwkv_lnx_g[l],
                                   rwkv_lnx_b[l], v_first, vmix)
        gates = jax.nn.sigmoid(seg_g.reshape(B, S, N_BRANCH, D))
        merged = (gates[:, :, 0] * (y_a @ p_a[l]) + gates[:, :, 1] * (y_b @ p_b[l])
                  + gates[:, :, 2] * (y_c @ p_c[l]) + gates[:, :, 3] * (y_d @ p_d[l]))
        x = x + merged @ w_mix_out[l]
        hq = to_heads(rms_norm(x, mem_q_norm_g[l]) @ w_mem_q[l], MEM_HEADS)
        kv = (rms_norm(mem, mem_kv_norm_g[l]) @ w_mem_kv[l]).reshape(B, M, 2, MEM_HEADS, MEM_HEAD_DIM)
        s = jnp.einsum('bshd,bmhd->bhsm', hq, kv[:, :, 0]).astype(F32) * MEM_HEAD_DIM ** -0.5
        p = jax.nn.softmax(s, axis=-1).astype(x.dtype)
        o = jnp.einsum('bhsm,bmhd->bshd', p, kv[:, :, 1]).reshape(B, S, D)
        x = x + o @ w_mem_o[l]
        u = rms_norm(x, ffn_norm_g[l]) @ w_ffn_in[l]
        u_pad = jnp.pad(u, ((0, 0), (CONV_WIDTH - 1, 0), (0, 0)))
        conv = ffn_conv_b[l]
        for j in range(CONV_WIDTH):
            conv = conv + u_pad[:, j:j + S] * ffn_conv_w[l, j]
        gate, val = jnp.split(conv, 2, axis=-1)
        x = x + (jax.nn.silu(gate) * val) @ w_ffn_out[l]
    return rms_norm(x, final_norm_g)
```

```python
import contextlib
import math
import numpy as np
import concourse.bass as bass
import concourse.mybir as mybir
from concourse.bass_utils import run_bass_kernel_spmd

F32 = mybir.dt.float32
BF16 = mybir.dt.bfloat16
I32 = mybir.dt.int32
AF = mybir.ActivationFunctionType
ALU = mybir.AluOpType
AX = mybir.AxisListType

D = 1024
S_LEN = 4096
NT = S_LEN // 128
N_IN = 9984
OFF_A, OFF_B, OFF_C, OFF_D, OFF_G = 0, 2304, 3840, 4864, 5888
DFF = 2816
MEM = 256
EPS = 1e-5
HOFF = 8


class Tr:
    __slots__ = ("w", "r", "x")

    def __init__(self, excl=False):
        self.w = None
        self.r = {}
        self.x = excl


class Sched:
    COMPUTE = ("pe", "act", "dve", "pool")
    NDMASEM = 12

    def __init__(self, nc):
        self.nc = nc
        self.engobj = {"pe": nc.tensor, "act": nc.scalar, "dve": nc.vector, "pool": nc.gpsimd, "sp": nc.sync}
        self.prog = {k: [] for k in self.engobj}
        self.sems = {}
        self.cnt = {}
        self.seen = {k: {} for k in self.engobj}
        self._semctx = []
        for k in self.COMPUTE:
            self._mksem(k)
        self.dmasems = {}
        self.dmarr = {}
        for q in ("sp", "act", "pool"):
            self.dmasems[q] = [self._mksem(f"d_{q}_{i}") for i in range(self.NDMASEM)]
            self.dmarr[q] = 0
        self.ninst = 0

    def _mksem(self, key):
        ctx = self.nc.semaphore(key)
        h = ctx.__enter__()
        self._semctx.append(ctx)
        self.sems[key] = h
        self.cnt[key] = 0
        return key

    def _deps(self, stream, own_key, reads, writes):
        need = {}

        def add(kv):
            if kv is None:
                return
            k, v = kv
            if k == own_key and k == "pe":
                return
            if need.get(k, 0) < v:
                need[k] = v
        for t in reads:
            add(t.w)
            if t.x:
                for k, v in t.r.items():
                    if k != own_key:
                        add((k, v))
        for t in writes:
            add(t.w)
            for k, v in t.r.items():
                add((k, v))
        out = []
        seen = self.seen[stream]
        for k, v in need.items():
            if seen.get(k, 0) < v:
                seen[k] = v
                out.append((k, v))
        return out

    def _commit(self, key, val, reads, writes):
        for t in reads:
            if t.r.get(key, 0) < val:
                t.r[key] = val
        for t in writes:
            t.w = (key, val)
            t.r = {}

    def op(self, eng, fn, reads=(), writes=()):
        waits = self._deps(eng, eng, reads, writes)
        self.cnt[eng] += 1
        val = self.cnt[eng]
        self._commit(eng, val, reads, writes)
        sem = self.sems[eng]
        sems = self.sems

        def thunk(e, waits=waits, fn=fn, sem=sem):
            for k, v in waits:
                e.wait_ge(sems[k], v)
            fn(e).then_inc(sem, 1)
        self.prog[eng].append(thunk)
        self.ninst += 1

    def dma(self, q, out, in_, reads=(), writes=(), **kw):
        i = self.dmarr[q]
        self.dmarr[q] = (i + 1) % self.NDMASEM
        key = self.dmasems[q][i]
        waits = self._deps(q, key, reads, writes)
        prev = self.cnt[key]
        if prev > 0 and self.seen[q].get(key, 0) < prev:
            self.seen[q][key] = prev
            waits.append((key, prev))
        self.cnt[key] += 16
        val = self.cnt[key]
        self._commit(key, val, reads, writes)
        sem = self.sems[key]
        sems = self.sems

        def thunk(e, waits=waits, sem=sem, out=out, in_=in_, kw=kw):
            for k, v in waits:
                e.wait_ge(sems[k], v)
            e.dma_start(out=out, in_=in_, **kw).then_inc(sem, 16)
        self.prog[q].append(thunk)
        self.ninst += 1

    def barrier(self):
        snap = {k: v for k, v in self.cnt.items() if v > 0}
        sems = self.sems
        for stream in self.prog:
            seen = self.seen[stream]
            waits = []
            for k, v in snap.items():
                if k == stream:
                    continue
                if seen.get(k, 0) < v:
                    seen[k] = v
                    waits.append((k, v))
            if waits:
                def thunk(e, waits=waits):
                    for k, v in waits:
                        e.wait_ge(sems[k], v)
                self.prog[stream].append(thunk)

    def finish(self):
        nc = self.nc
        finals = [(k, v) for k, v in self.cnt.items() if v > 0]
        sems = self.sems
        prog = self.prog
        with nc.Block() as block:
            @block.tensor
            def _(e):
                for t in prog["pe"]:
                    t(e)

            @block.scalar
            def _(e):
                for t in prog["act"]:
                    t(e)

            @block.vector
            def _(e):
                for t in prog["dve"]:
                    t(e)

            @block.gpsimd
            def _(e):
                for t in prog["pool"]:
                    t(e)

            @block.sync
            def _(e):
                for t in prog["sp"]:
                    t(e)
                for k, v in finals:
                    e.wait_ge(sems[k], v)
        for ctx in reversed(self._semctx):
            ctx.__exit__(None, None, None)


WNAMES = ["w_in", "p_a", "p_b", "p_c", "p_d", "w_mix_out", "w_mem_q", "w_mem_kv", "w_mem_o", "w_ffn_in",
          "w_ffn_out", "rwkv_w2", "rwkv_a2", "rwkv_g2", "rwkv_v1", "rwkv_v2"]
VNAMES = ["mix_norm_g", "diff_lam", "diff_norm_g", "hgrn_lb_logits", "hgrn_norm_g", "rwkv_mu", "rwkv_w0", "rwkv_a0",
          "rwkv_k_k", "rwkv_k_a", "rwkv_r_k", "rwkv_lnx_g", "rwkv_lnx_b", "rwkv_v0", "mem_q_norm_g", "mem_kv_norm_g",
          "ffn_norm_g", "ffn_conv_w", "ffn_conv_b", "final_norm_g"]
SHAPES = {
    "mix_norm_g": (2, 1024), "w_in": (2, 1024, 9984), "diff_lam": (2, 4, 64), "diff_norm_g": (2, 128),
    "hgrn_lb_logits": (2, 256), "hgrn_norm_g": (2, 64), "rwkv_mu": (2, 1024), "rwkv_w0": (2, 256),
    "rwkv_w2": (2, 64, 256), "rwkv_a0": (2, 256), "rwkv_a2": (2, 64, 256), "rwkv_g2": (2, 128, 256),
    "rwkv_k_k": (2, 256), "rwkv_k_a": (2, 256), "rwkv_r_k": (2, 4, 64), "rwkv_lnx_g": (2, 256),
    "rwkv_lnx_b": (2, 256), "rwkv_v0": (1, 256), "rwkv_v1": (1, 256, 32), "rwkv_v2": (1, 32, 256),
    "p_a": (2, 256, 1024), "p_b": (2, 512, 1024), "p_c": (2, 256, 1024), "p_d": (2, 256, 1024),
    "w_mix_out": (2, 1024, 1024), "mem_q_norm_g": (2, 1024), "mem_kv_norm_g": (2, 1024),
    "w_mem_q": (2, 1024, 1024), "w_mem_kv": (2, 1024, 2048), "w_mem_o": (2, 1024, 1024),
    "ffn_norm_g": (2, 1024), "w_ffn_in": (2, 1024, 5632), "ffn_conv_w": (2, 3, 5632), "ffn_conv_b": (2, 5632),
    "w_ffn_out": (2, 2816, 1024), "final_norm_g": (1024,),
}


def host_consts():
    c = {}
    c["c_ident"] = np.eye(128, dtype=np.float32)
    s = np.arange(128)[:, None]
    t = np.arange(512)[None, :]
    c["c_maskD"] = (s <= t).astype(np.float32)
    dm = np.zeros((128, 256), np.float32)
    dm[:, :128] = (s <= np.arange(128)[None, :])
    dm[:, 128:] = (s >= np.arange(128)[None, :])
    c["c_maskA"] = np.concatenate([dm, dm], axis=1)
    tt = np.arange(128)[None, :]
    si = np.concatenate([(s < tt), (s <= tt)], axis=1).astype(np.float32)
    c["c_maskSI"] = si
    c["c_maskSI2"] = np.concatenate([si, si], axis=1)
    c["c_triR"] = (s > tt).astype(np.float32)
    c["c_maskLow"] = (tt < s).astype(np.float32)
    c["c_tri"] = (s <= tt).astype(np.float32)
    c["c_tri2"] = np.concatenate([c["c_tri"], c["c_tri"]], axis=1)
    c["c_triM"] = ((s <= tt).astype(np.float32) - (s <= 63).astype(np.float32))
    c["c_o63"] = np.broadcast_to((s <= 63), (128, 128)).astype(np.float32).copy()
    c["c_ones"] = np.ones((128, 128), np.float32)
    sel = np.zeros((128, 2), np.float32)
    sel[:64, 0] = 1.0
    sel[:, 1] = 1.0
    c["c_sel"] = sel
    pm = np.zeros((128, 128), np.float32)
    for hb in (0, 64):
        for i in range(8):
            pm[hb + i + 8, hb + i] = -1.0
            pm[hb + i, hb + i + 8] = 1.0
    c["c_pm"] = pm
    invf = np.zeros((128, 1), np.float32)
    f = (500000.0 ** (-np.arange(8, dtype=np.float32) / 8)).astype(np.float32)
    for hb in (0, 64):
        invf[hb:hb + 8, 0] = f
        invf[hb + 8:hb + 16, 0] = f
    c["c_invf"] = invf
    return c


class K:
    def __init__(self, cfg):
        self.cfg = cfg
        nc = bass.Bass("TRN2", target_bir_lowering=False)
        self.nc = nc
        self.S = Sched(nc)
        self.es = contextlib.ExitStack()
        self.uid = 0

    def sb(self, es, shape, dt, name=None):
        self.uid += 1
        return es.enter_context(self.nc.sbuf_tensor(f"{name or 't'}_{self.uid}", list(shape), dt))

    def dram(self, name, shape, dt, kind="Internal"):
        return self.nc.dram_tensor(name, list(shape), dt, kind=kind).ap()

    def mm(self, out, lhsT, rhs, start, stop, r, w):
        self.S.op("pe", lambda e: e.matmul(out, lhsT=lhsT, rhs=rhs, start=start, stop=stop), reads=r, writes=w)

    def tp(self, out, in_, r, w):
        idt = self.ident_b
        self.S.op("pe", lambda e: e.transpose(out, in_, idt), reads=list(r) + [self.t_const], writes=w)

    def act(self, out, in_, func, r, w, **kw):
        self.S.op("act", lambda e: e.activation(out=out, in_=in_, func=func, **kw), reads=r, writes=w)

    def tt(self, eng, out, in0, in1, op, r, w):
        self.S.op(eng, lambda e: e.tensor_tensor(out=out, in0=in0, in1=in1, op=op), reads=r, writes=w)

    def ts(self, eng, out, in0, s1, s2, op0, op1, r, w):
        self.S.op(eng, lambda e: e.tensor_scalar(out=out, in0=in0, scalar1=s1, scalar2=s2, op0=op0, op1=op1),
                  reads=r, writes=w)

    def stt(self, out, in0, scalar, in1, op0, op1, r, w):
        self.S.op("dve", lambda e: e.scalar_tensor_tensor(out=out, in0=in0, scalar=scalar, in1=in1, op0=op0, op1=op1),
                  reads=r, writes=w)

    def cp(self, eng, out, in_, r, w):
        if eng == "act":
            self.S.op("act", lambda e: e.copy(out=out, in_=in_), reads=r, writes=w)
        else:
            self.S.op(eng, lambda e: e.tensor_copy(out=out, in_=in_), reads=r, writes=w)

    def recip(self, out, in_, r, w):
        self.S.op("dve", lambda e: e.reciprocal(out=out, in_=in_), reads=r, writes=w)

    def memset(self, ap, val, w):
        self.S.op("dve", lambda e: e.memset(ap, val), reads=(), writes=w)

    def ps(self):
        i = self.ps_i
        self.ps_i = (i + 1) % len(self.ps_f)
        return self.ps_f[i], self.ps_ft[i]

    def psb(self):
        i = self.psb_i
        self.psb_i = (i + 1) % len(self.ps_b)
        return self.ps_b[i], self.ps_bt[i]


def build(cfg):
    k = K(cfg)
    nc, S = k.nc, k.S
    NS = cfg.get("nseq", 2)
    NL = cfg.get("layers", 2)
    dbg = cfg.get("debug")
    x_in = k.dram("x", [NS, S_LEN, D], F32, "ExternalInput")
    mem_in = k.dram("mem", [NS, MEM, D], F32, "ExternalInput")
    pos_in = k.dram("positions", [NS, S_LEN], I32, "ExternalInput")
    wf = {n: k.dram(n, SHAPES[n], F32, "ExternalInput") for n in WNAMES}
    vf = {n: k.dram(n, SHAPES[n], F32, "ExternalInput") for n in VNAMES}
    consts = host_consts()
    cf = {n: k.dram(n, v.shape, F32, "ExternalInput") for n, v in consts.items()}
    out = k.dram("out", [NS, S_LEN, D], F32, "ExternalOutput")
    xbuf = k.dram("xbuf", [NS, S_LEN, D], F32)
    yT = k.dram("yT", [1280, S_LEN], BF16)
    vfirst = k.dram("vfirst", [NS, S_LEN, 256], F32)
    wb = {n: k.dram(n + "_bf", SHAPES[n], BF16) for n in WNAMES}
    yin = None
    if cfg.get("yin"):
        yin = k.dram("yin", [1280, S_LEN], F32, "ExternalInput")
    dbg_out = None
    if dbg:
        dbg_out = k.dram("dbg", dbg["shape"], F32, "ExternalOutput")

    t_x = [[Tr() for _ in range(NT)] for _ in range(NS)]
    t_yT = [[Tr() for _ in range(8)] for _ in range(10)]
    t_vf = [[Tr() for _ in range(NT)] for _ in range(NS)]
    t_w = {}

    with contextlib.ExitStack() as g:
        k.ps_f, k.ps_ft, k.ps_b, k.ps_bt = [], [], [], []
        for i in range(6):
            k.ps_f.append(g.enter_context(nc.psum_tensor(f"psf{i}", [128, 512], F32)))
            k.ps_ft.append(Tr(True))
        for i in range(2):
            k.ps_b.append(g.enter_context(nc.psum_tensor(f"psb{i}", [128, 1024], BF16)))
            k.ps_bt.append(Tr(True))
        k.ps_i = 0
        k.psb_i = 0
        k.t_const = Tr()
        ident_b = k.sb(g, [128, 128], BF16, "ident")
        k.ident_b = ident_b[:]
        S.dma("pool", ident_b[:], cf["c_ident"], writes=[k.t_const])
        cs = {}
        for n, dt in [("c_maskD", BF16), ("c_maskA", BF16), ("c_maskSI", F32), ("c_maskSI2", F32), ("c_triR", F32), ("c_maskLow", F32), ("c_tri", F32), ("c_tri2", F32),
                      ("c_triM", F32), ("c_o63", F32), ("c_ones", F32), ("c_sel", F32), ("c_pm", BF16),
                      ("c_invf", F32)]:
            t = k.sb(g, consts[n].shape, dt, n)
            S.dma("pool" if dt == BF16 else "sp", t[:], cf[n], writes=[k.t_const])
            cs[n] = t
        ones_b = k.sb(g, [128, 128], BF16, "ones_b")
        S.dma("pool", ones_b[:], cf["c_ones"], writes=[k.t_const])
        cs["ones_b"] = ones_b
        k.cs = cs
        for n in WNAMES:
            tot = int(np.prod(SHAPES[n]))
            rows = tot // 2048
            src = wf[n].flatten().rearrange("(r c) -> r c", c=2048) if len(SHAPES[n]) > 1 else None
            nd = len(SHAPES[n])
            letters = "abc"[:nd]
            flat_s = wf[n].rearrange(f"{' '.join(letters)} -> ({' '.join(letters)})").rearrange("(r c) -> r c", c=2048)
            flat_d = wb[n].rearrange(f"{' '.join(letters)} -> ({' '.join(letters)})").rearrange("(r c) -> r c", c=2048)
            t_w[n] = []
            for r0 in range(0, rows, 512):
                r1 = min(rows, r0 + 512)
                tr_ = Tr()
                S.dma("pool", flat_d[r0:r1, :], flat_s[r0:r1, :], writes=[tr_])
                t_w[n].append(tr_)
        k.wb, k.t_w, k.vf = wb, t_w, vf

        ctx = dict(k=k, x_in=x_in, mem_in=mem_in, pos_in=pos_in, out=out, xbuf=xbuf, yT=yT, vfirst=vfirst,
                   t_x=t_x, t_yT=t_yT, t_vf=t_vf, yin=yin, dbg=dbg, dbg_out=dbg_out, cfg=cfg)
        phases = cfg.get("phases", "ABCDGMF")
        if yin is None and not any(c in phases for c in "ABCD"):
            with contextlib.ExitStack() as zx:
                zt = k.sb(zx, [128, 512], BF16, "zt")
                t_z = Tr()
                k.memset(zt[:], 0.0, [t_z])
                for rc in range(10):
                    for gq in range(8):
                        S.dma("sp", yT[rc * 128:(rc + 1) * 128, gq * 512:(gq + 1) * 512], zt[:], reads=[t_z],
                              writes=[t_yT[rc][gq]])
                S.barrier()
        for s in range(NS):
            for l in range(NL):
                xsrc = x_in if l == 0 else xbuf
                with contextlib.ExitStack() as mx:
                    hT = k.sb(mx, [128, 8, HOFF + S_LEN], BF16, "hT")
                    t_hT = [Tr() for _ in range(NT)]
                    t_h0 = Tr()
                    k.memset(hT[:, :, 0:HOFF], 0.0, [t_h0])
                    phase_norm_T(ctx, s, l, xsrc, hT, t_hT)
                    if yin is not None:
                        phase_yin(ctx)
                    else:
                        with contextlib.ExitStack() as rx:
                            tabs = phase_rope(ctx, rx, s) if ("A" in phases or "B" in phases) else None
                            if "A" in phases:
                                phase_A(ctx, s, l, hT, t_hT, tabs)
                            if "B" in phases:
                                phase_B(ctx, s, l, hT, t_hT, tabs)
                        if "C" in phases:
                            phase_C(ctx, s, l, hT, t_hT)
                        if "D" in phases:
                            phase_D(ctx, s, l, hT, t_hT, t_h0)
                    if dbg and dbg.get("what") == "yT" and dbg.get("l", 0) == l and s == 0:
                        for rc in dbg.get("rcs", range(10)):
                            for gq in range(8):
                                S.dma("pool", dbg_out[rc * 128:(rc + 1) * 128, gq * 512:(gq + 1) * 512],
                                      yT[rc * 128:(rc + 1) * 128, gq * 512:(gq + 1) * 512], reads=[t_yT[rc][gq]])
                    if "G" in phases:
                        phase_G(ctx, s, l, xsrc, hT, t_hT)
                if "M" in phases:
                    phase_M(ctx, s, l)
                if "F" in phases:
                    phase_F(ctx, s, l)
            if cfg.get("final", True):
                phase_final(ctx, s)
        S.finish()
    return nc


def load_bc(k, es, src_row_ap, n, name="bc"):
    t = k.sb(es, [128, n], F32, name)
    tr_ = Tr()
    k.S.dma("sp", t[:], src_row_ap.partition_broadcast(128), writes=[tr_])
    return t, tr_


def rms_to_bf(k, xt, t_xt, gbc, t_g, obf, t_o, ss, t_ss):
    k.act(obf[:], xt[:], AF.Square, [t_xt], [t_o, t_ss], accum_out=ss[:, 0:1])
    k.ts("dve", ss[:, 1:2], ss[:, 0:1], 1.0 / D, EPS, ALU.mult, ALU.add, [t_ss], [t_ss])
    k.act(ss[:, 2:3], ss[:, 1:2], AF.Sqrt, [t_ss], [t_ss])
    k.recip(ss[:, 3:4], ss[:, 2:3], [t_ss], [t_ss])
    k.stt(obf[:], xt[:], ss[:, 3:4], gbc[:], ALU.mult, ALU.mult, [t_xt, t_ss, t_g], [t_o])


def phase_norm_T(ctx, s, l, xsrc, hT, t_hT):
    k = ctx["k"]
    S = k.S
    with contextlib.ExitStack() as es:
        gbc, t_g = load_bc(k, es, k.vf["mix_norm_g"][l], D, "gmix")
        xts = [(k.sb(es, [128, D], F32, "xt"), Tr()) for _ in range(2)]
        obs = [(k.sb(es, [128, D], BF16, "ob"), Tr()) for _ in range(2)]
        sss = [(k.sb(es, [128, 4], F32, "ss"), Tr()) for _ in range(2)]
        for tt_ in range(NT):
            xt, t_xt = xts[tt_ % 2]
            ob, t_o = obs[tt_ % 2]
            ss, t_ss = sss[tt_ % 2]
            rd = [ctx["t_x"][s][tt_]] if l > 0 else []
            S.dma("sp", xt[:], xsrc[s, tt_ * 128:(tt_ + 1) * 128, :], reads=rd, writes=[t_xt])
            rms_to_bf(k, xt, t_xt, gbc, t_g, ob, t_o, ss, t_ss)
            pb, t_pb = k.psb()
            for j in range(8):
                k.tp(pb[:, j * 128:(j + 1) * 128], ob[:, j * 128:(j + 1) * 128], [t_o], [t_pb])
            k.cp("act" if tt_ % 2 else "dve", hT[:, :, HOFF + tt_ * 128:HOFF + (tt_ + 1) * 128],
                 pb[:].rearrange("p (j t) -> p j t", j=8), [t_pb], [t_hT[tt_]])
        S.barrier()


def phase_yin(ctx):
    k = ctx["k"]
    for rc in range(10):
        for gq in range(8):
            k.S.dma("pool", ctx["yT"][rc * 128:(rc + 1) * 128, gq * 512:(gq + 1) * 512],
                    ctx["yin"][rc * 128:(rc + 1) * 128, gq * 512:(gq + 1) * 512], writes=[ctx["t_yT"][rc][gq]])


def load_w(k, es, name, l, r0, nr, c0, ncols, tag="w"):
    kc = max(1, nr // 128)
    p = min(128, nr)
    t = k.sb(es, [p, kc, ncols], BF16, tag)
    tr_ = Tr()
    src = k.wb[name][l, r0:r0 + nr, c0:c0 + ncols].rearrange("(kc p) n -> p kc n", p=p)
    k.S.dma("sp", t[:], src, reads=k.t_w[name], writes=[tr_])
    return t, tr_


def load_w_into(k, t, tr_, name, l, r0, nr, c0, ncols):
    p = min(128, nr)
    src = k.wb[name][l, r0:r0 + nr, c0:c0 + ncols].rearrange("(kc p) n -> p kc n", p=p)
    k.S.dma("sp", t[:], src, reads=k.t_w[name], writes=[tr_])


def phase_G(ctx, s, l, xsrc, hT, t_hT):
    k = ctx["k"]
    S = k.S
    yT, t_yT = ctx["yT"], ctx["t_yT"]
    with contextlib.ExitStack() as es:
        pcat = k.sb(es, [128, 10, D], BF16, "pcat")
        t_p = Tr()
        for nm, c0, n in (("p_a", 0, 2), ("p_b", 2, 4), ("p_c", 6, 2), ("p_d", 8, 2)):
            S.dma("sp", pcat[:, c0:c0 + n, :], k.wb[nm][l].rearrange("(kc p) n -> p kc n", p=128),
                  reads=k.t_w[nm], writes=[t_p])
        wmo, t_wmo = load_w(k, es, "w_mix_out", l, 0, D, 0, D, "wmo")
        wgs = [(k.sb(es, [128, 8, 4, 128], BF16, "wg"), Tr()) for _ in range(2)]
        yts = [(k.sb(es, [128, 10, 512], BF16, "yt"), Tr()) for _ in range(2)]
        mT = k.sb(es, [128, 8, 512], BF16, "mT")
        t_mT = Tr()
        accs = [(k.sb(es, [128, 512], F32, "acc"), Tr()) for _ in range(2)]
        sigs = [(k.sb(es, [128, 512], F32, "sig"), Tr()) for _ in range(3)]
        xts = [(k.sb(es, [128, D], F32, "xg"), Tr()) for _ in range(2)]
        branches = ((0, 2), (2, 4), (6, 2), (8, 2))
        it = 0
        for gq in range(8):
            yt, t_yt = yts[gq % 2]
            for rc in range(10):
                S.dma("sp", yt[:, rc, :], yT[rc * 128:(rc + 1) * 128, gq * 512:(gq + 1) * 512],
                      reads=[t_yT[rc][gq]], writes=[t_yt])
            hsl = slice(HOFF + gq * 512, HOFF + (gq + 1) * 512)
            rh = t_hT[gq * 4:(gq + 1) * 4]
            for oc in range(8):
                wg, t_wg = wgs[it % 2]
                it += 1
                for bi in range(4):
                    c0 = OFF_G + bi * D + oc * 128
                    S.dma("sp", wg[:, :, bi, :], k.wb["w_in"][l, :, c0:c0 + 128].rearrange("(kc p) n -> p kc n", p=128),
                          reads=k.t_w["w_in"], writes=[t_wg])
                acc, t_acc = accs[oc % 2]
                for bi, (c0, n) in enumerate(branches):
                    pg, t_pg = k.ps()
                    for kc in range(8):
                        k.mm(pg[:], wg[:, kc, bi, :], hT[:, kc, hsl], kc == 0, kc == 7, [t_wg] + rh, [t_pg])
                    sg, t_sg = sigs[(oc * 4 + bi) % 3]
                    k.act(sg[:], pg[:], AF.Sigmoid, [t_pg], [t_sg])
                    py, t_py = k.ps()
                    for j in range(n):
                        k.mm(py[:], pcat[:, c0 + j, oc * 128:(oc + 1) * 128], yt[:, c0 + j, :], j == 0, j == n - 1,
                             [t_p, t_yt], [t_py])
                    if bi == 0:
                        k.tt("dve", acc[:], py[:], sg[:], ALU.mult, [t_py, t_sg], [t_acc])
                    else:
                        k.tt("dve", sg[:], py[:], sg[:], ALU.mult, [t_py, t_sg], [t_sg])
                        if bi < 3:
                            k.tt("pool", acc[:], acc[:], sg[:], ALU.add, [t_acc, t_sg], [t_acc])
                        else:
                            k.tt("pool", mT[:, oc, :], acc[:], sg[:], ALU.add, [t_acc, t_sg], [t_mT])
            for q in range(4):
                tt_ = gq * 4 + q
                xt, t_xt = xts[q % 2]
                rd = [ctx["t_x"][s][tt_]] if l > 0 else []
                S.dma("sp", xt[:], xsrc[s, tt_ * 128:(tt_ + 1) * 128, :], reads=rd, writes=[t_xt])
                for cg in range(2):
                    po, t_po = k.ps()
                    for oc in range(8):
                        k.mm(po[:], mT[:, oc, q * 128:(q + 1) * 128], wmo[:, oc, cg * 512:(cg + 1) * 512], oc == 0,
                             oc == 7, [t_mT, t_wmo], [t_po])
                    k.tt("dve", xt[:, cg * 512:(cg + 1) * 512], xt[:, cg * 512:(cg + 1) * 512], po[:], ALU.add,
                         [t_xt, t_po], [t_xt])
                S.dma("sp", ctx["xbuf"][s, tt_ * 128:(tt_ + 1) * 128, :], xt[:], reads=[t_xt],
                      writes=[ctx["t_x"][s][tt_]])
        S.barrier()


def phase_M(ctx, s, l):
    k = ctx["k"]
    S = k.S
    t_x = ctx["t_x"][s]
    xbuf = ctx["xbuf"]
    sc = 256 ** -0.5
    with contextlib.ExitStack() as es:
        kmT = k.sb(es, [128, 8, MEM], BF16, "kmT")
        vm = k.sb(es, [128, 2, D], BF16, "vm")
        t_km, t_vm = Tr(), Tr()
        with contextlib.ExitStack() as e2:
            gkv, t_gkv = load_bc(k, e2, k.vf["mem_kv_norm_g"][l], D, "gkv")
            mnT = k.sb(e2, [128, 8, MEM], BF16, "mnT")
            t_mn = Tr()
            xt = k.sb(e2, [128, D], F32, "mx")
            ob = k.sb(e2, [128, D], BF16, "mob")
            ss = k.sb(e2, [128, 4], F32, "mss")
            t_xt, t_o, t_ss = Tr(), Tr(), Tr()
            for mt in range(2):
                S.dma("sp", xt[:], ctx["mem_in"][s, mt * 128:(mt + 1) * 128, :], writes=[t_xt])
                rms_to_bf(k, xt, t_xt, gkv, t_gkv, ob, t_o, ss, t_ss)
                pb, t_pb = k.psb()
                for j in range(8):
                    k.tp(pb[:, j * 128:(j + 1) * 128], ob[:, j * 128:(j + 1) * 128], [t_o], [t_pb])
                k.cp("dve", mnT[:, :, mt * 128:(mt + 1) * 128], pb[:].rearrange("p (j t) -> p j t", j=8), [t_pb], [t_mn])
            for half in range(4):
                wkv, t_wkv = load_w(k, e2, "w_mem_kv", l, 0, D, half * 512, 512, "wkv")
                if half < 2:
                    for fc in range(4):
                        p_, t_p = k.ps()
                        for kc in range(8):
                            k.mm(p_[:, 0:MEM], wkv[:, kc, fc * 128:(fc + 1) * 128], mnT[:, kc, :], kc == 0, kc == 7,
                                 [t_wkv, t_mn], [t_p])
                        k.cp("act", kmT[:, half * 4 + fc, :], p_[:, 0:MEM], [t_p], [t_km])
                else:
                    for mt in range(2):
                        p_, t_p = k.ps()
                        for kc in range(8):
                            k.mm(p_[:], mnT[:, kc, mt * 128:(mt + 1) * 128], wkv[:, kc, :], kc == 0, kc == 7,
                                 [t_wkv, t_mn], [t_p])
                        k.cp("act", vm[:, mt, (half - 2) * 512:(half - 1) * 512], p_[:], [t_p], [t_vm])
            S.barrier()
        gq_, t_gq = load_bc(k, es, k.vf["mem_q_norm_g"][l], D, "gq")
        wq, t_wq = load_w(k, es, "w_mem_q", l, 0, D, 0, D, "wq")
        wo, t_wo = load_w(k, es, "w_mem_o", l, 0, D, 0, D, "wo")
        xts = [(k.sb(es, [128, D], F32, "xq"), Tr()) for _ in range(4)]
        ob = k.sb(es, [128, D], BF16, "qob")
        ss = k.sb(es, [128, 4], F32, "qss")
        t_o, t_ss = Tr(), Tr()
        xnT = k.sb(es, [128, 8, 512], BF16, "xnT")
        t_xn = Tr()
        qT = k.sb(es, [128, 8, 512], BF16, "qT")
        t_qT = Tr()
        ETs = [(k.sb(es, [128, 2, 512], BF16, "ET"), Tr()) for _ in range(2)]
        oT = k.sb(es, [128, 8, 512], BF16, "oT")
        t_oT = Tr()
        rden = k.sb(es, [128, 512], F32, "rden")
        t_rd = Tr()
        for gq in range(8):
            for q in range(4):
                tt_ = gq * 4 + q
                xt, t_xt = xts[q]
                S.dma("sp", xt[:], xbuf[s, tt_ * 128:(tt_ + 1) * 128, :], reads=[t_x[tt_]], writes=[t_xt])
                rms_to_bf(k, xt, t_xt, gq_, t_gq, ob, t_o, ss, t_ss)
                pb, t_pb = k.psb()
                for j in range(8):
                    k.tp(pb[:, j * 128:(j + 1) * 128], ob[:, j * 128:(j + 1) * 128], [t_o], [t_pb])
                k.cp("dve", xnT[:, :, q * 128:(q + 1) * 128], pb[:].rearrange("p (j t) -> p j t", j=8), [t_pb], [t_xn])
            for fc in range(8):
                p_, t_p = k.ps()
                for kc in range(8):
                    k.mm(p_[:], wq[:, kc, fc * 128:(fc + 1) * 128], xnT[:, kc, :], kc == 0, kc == 7, [t_wq, t_xn], [t_p])
                k.cp("act", qT[:, fc, :], p_[:], [t_p], [t_qT])
            for h in range(4):
                ET, t_ET = ETs[h % 2]
                for mt in range(2):
                    p_, t_p = k.ps()
                    for j in range(2):
                        k.mm(p_[:], kmT[:, h * 2 + j, mt * 128:(mt + 1) * 128], qT[:, h * 2 + j, :], j == 0, j == 1,
                             [t_km, t_qT], [t_p])
                    k.act(ET[:, mt, :], p_[:], AF.Exp, [t_p], [t_ET], scale=sc)
                pd, t_pd = k.ps()
                for mt in range(2):
                    k.mm(pd[:], k.cs["ones_b"][:], ET[:, mt, :], mt == 0, mt == 1, [t_ET, k.t_const], [t_pd])
                k.recip(rden[:], pd[:], [t_pd], [t_rd])
                for j in range(2):
                    p_, t_p = k.ps()
                    for mt in range(2):
                        k.mm(p_[:], vm[:, mt, (h * 2 + j) * 128:(h * 2 + j + 1) * 128], ET[:, mt, :], mt == 0, mt == 1,
                             [t_vm, t_ET], [t_p])
                    k.tt("dve", oT[:, h * 2 + j, :], p_[:], rden[:], ALU.mult, [t_p, t_rd], [t_oT])
            for q in range(4):
                tt_ = gq * 4 + q
                xt, t_xt = xts[q]
                for cg in range(2):
                    po, t_po = k.ps()
                    for oc in range(8):
                        k.mm(po[:], oT[:, oc, q * 128:(q + 1) * 128], wo[:, oc, cg * 512:(cg + 1) * 512], oc == 0,
                             oc == 7, [t_oT, t_wo], [t_po])
                    k.tt("dve", xt[:, cg * 512:(cg + 1) * 512], xt[:, cg * 512:(cg + 1) * 512], po[:], ALU.add,
                         [t_xt, t_po], [t_xt])
                S.dma("sp", xbuf[s, tt_ * 128:(tt_ + 1) * 128, :], xt[:], reads=[t_xt], writes=[t_x[tt_]])
        S.barrier()


def phase_F(ctx, s, l):
    k = ctx["k"]
    S = k.S
    t_x = ctx["t_x"][s]
    xbuf = ctx["xbuf"]
    NC = 2 * DFF // 128
    with contextlib.ExitStack() as es:
        gf, t_gf = load_bc(k, es, k.vf["ffn_norm_g"][l], D, "gf")
        w1, t_w1 = load_w(k, es, "w_ffn_in", l, 0, D, 0, 2 * DFF, "wf1")
        w2, t_w2 = load_w(k, es, "w_ffn_out", l, 0, DFF, 0, D, "wf2")
        cw = k.sb(es, [128, 4, NC], F32, "cw")
        t_cw = Tr()
        for j in range(3):
            S.dma("sp", cw[:, j, :], k.vf["ffn_conv_w"][l, j].rearrange("(c p) -> p c", p=128), writes=[t_cw],
                  allow_slow_non_contiguous=True)
        S.dma("sp", cw[:, 3, :], k.vf["ffn_conv_b"][l].rearrange("(c p) -> p c", p=128), writes=[t_cw],
              allow_slow_non_contiguous=True)
        halo = k.sb(es, [128, NC, 2], F32, "halo")
        t_halo = [Tr() for _ in range(NC)]
        k.memset(halo[:], 0.0, t_halo)
        xts = [(k.sb(es, [128, D], F32, "xf"), Tr()) for _ in range(4)]
        ob = k.sb(es, [128, D], BF16, "fob")
        ss = k.sb(es, [128, 4], F32, "fss")
        t_o, t_ss = Tr(), Tr()
        xnT = k.sb(es, [128, 8, 512], BF16, "fxnT")
        t_xn = Tr()
        ues = [(k.sb(es, [128, 514], F32, "ue"), Tr()) for _ in range(2)]
        c0s = [(k.sb(es, [128, 512], F32, "c0"), Tr()) for _ in range(2)]
        sgs = [(k.sb(es, [128, 512], F32, "sgf"), Tr()) for _ in range(2)]
        aT = k.sb(es, [128, DFF // 128, 512], BF16, "aT")
        t_aT = Tr()
        for gq in range(8):
            for q in range(4):
                tt_ = gq * 4 + q
                xt, t_xt = xts[q]
                S.dma("sp", xt[:], xbuf[s, tt_ * 128:(tt_ + 1) * 128, :], reads=[t_x[tt_]], writes=[t_xt])
                rms_to_bf(k, xt, t_xt, gf, t_gf, ob, t_o, ss, t_ss)
                pb, t_pb = k.psb()
                for j in range(8):
                    k.tp(pb[:, j * 128:(j + 1) * 128], ob[:, j * 128:(j + 1) * 128], [t_o], [t_pb])
                k.cp("dve", xnT[:, :, q * 128:(q + 1) * 128], pb[:].rearrange("p (j t) -> p j t", j=8), [t_pb], [t_xn])
            it = 0
            for j in range(DFF // 128):
                res = []
                for c in (j, j + 22):
                    p_, t_p = k.ps()
                    for kc in range(8):
                        k.mm(p_[:], w1[:, kc, c * 128:(c + 1) * 128], xnT[:, kc, :], kc == 0, kc == 7, [t_w1, t_xn], [t_p])
                    ue, t_ue = ues[it % 2]
                    c0, t_c0 = c0s[it % 2]
                    it += 1
                    k.cp("pool", ue[:, 0:2], halo[:, c, :], [t_halo[c]], [t_ue])
                    k.cp("act", ue[:, 2:514], p_[:], [t_p], [t_ue])
                    k.act(c0[:], p_[:], AF.Identity, [t_p, t_cw], [t_c0], scale=cw[:, 2, c:c + 1], bias=cw[:, 3, c:c + 1])
                    k.cp("pool", halo[:, c, :], ue[:, 512:514], [t_ue], [t_halo[c]])
                    k.stt(c0[:], ue[:, 1:513], cw[:, 1, c:c + 1], c0[:], ALU.mult, ALU.add, [t_ue, t_cw, t_c0], [t_c0])
                    k.stt(c0[:], ue[:, 0:512], cw[:, 0, c:c + 1], c0[:], ALU.mult, ALU.add, [t_ue, t_cw, t_c0], [t_c0])
                    res.append((c0, t_c0))
                sg, t_sg = sgs[j % 2]
                k.act(sg[:], res[0][0][:], AF.Silu, [res[0][1]], [t_sg])
                k.tt("pool", aT[:, j, :], sg[:], res[1][0][:], ALU.mult, [t_sg, res[1][1]], [t_aT])
            for q in range(4):
                tt_ = gq * 4 + q
                xt, t_xt = xts[q]
                for cg in range(2):
                    po, t_po = k.ps()
                    for j in range(DFF // 128):
                        k.mm(po[:], aT[:, j, q * 128:(q + 1) * 128], w2[:, j, cg * 512:(cg + 1) * 512], j == 0,
                             j == DFF // 128 - 1, [t_aT, t_w2], [t_po])
                    k.tt("dve", xt[:, cg * 512:(cg + 1) * 512], xt[:, cg * 512:(cg + 1) * 512], po[:], ALU.add,
                         [t_xt, t_po], [t_xt])
                S.dma("sp", xbuf[s, tt_ * 128:(tt_ + 1) * 128, :], xt[:], reads=[t_xt], writes=[t_x[tt_]])
        S.barrier()


def phase_final(ctx, s):
    k = ctx["k"]
    S = k.S
    with contextlib.ExitStack() as es:
        gbc, t_g = load_bc(k, es, k.vf["final_norm_g"], D, "gfin")
        xts = [(k.sb(es, [128, D], F32, "xo"), Tr()) for _ in range(2)]
        junk = k.sb(es, [128, D], BF16, "junk")
        t_j = Tr()
        sss = [(k.sb(es, [128, 4], F32, "oss"), Tr()) for _ in range(2)]
        for tt_ in range(NT):
            xt, t_xt = xts[tt_ % 2]
            ss, t_ss = sss[tt_ % 2]
            S.dma("sp", xt[:], ctx["xbuf"][s, tt_ * 128:(tt_ + 1) * 128, :], reads=[ctx["t_x"][s][tt_]], writes=[t_xt])
            k.act(junk[:], xt[:], AF.Square, [t_xt], [t_j, t_ss], accum_out=ss[:, 0:1])
            k.ts("dve", ss[:, 1:2], ss[:, 0:1], 1.0 / D, EPS, ALU.mult, ALU.add, [t_ss], [t_ss])
            k.act(ss[:, 2:3], ss[:, 1:2], AF.Sqrt, [t_ss], [t_ss])
            k.recip(ss[:, 3:4], ss[:, 2:3], [t_ss], [t_ss])
            k.stt(xt[:], xt[:], ss[:, 3:4], gbc[:], ALU.mult, ALU.mult, [t_xt, t_ss, t_g], [t_xt])
            S.dma("sp", ctx["out"][s, tt_ * 128:(tt_ + 1) * 128, :], xt[:], reads=[t_xt])
        S.barrier()


def sl(st, n, step):
    return slice(st, st + step * (n - 1) + 1, step)


def phase_rope(ctx, es, s):
    k = ctx["k"]
    S = k.S
    cosT = k.sb(es, [128, S_LEN], F32, "cosT")
    sinT = k.sb(es, [128, S_LEN], F32, "sinT")
    t_tab = Tr()
    CW = 1024
    with contextlib.ExitStack() as e2:
        posi = k.sb(e2, [128, CW], I32, "posi")
        tq = k.sb(e2, [128, CW], F32, "tq")
        ki = k.sb(e2, [128, CW], I32, "ki")
        kf = k.sb(e2, [128, CW], F32, "kf")
        fr = k.sb(e2, [128, CW], F32, "fr")
        aa = k.sb(e2, [128, CW], F32, "aa")
        t_ = Tr()
        invf = k.cs["c_invf"]
        for c in range(S_LEN // CW):
            cols = slice(c * CW, (c + 1) * CW)
            S.dma("sp", posi[:], ctx["pos_in"][s, cols].partition_broadcast(128), writes=[t_])
            k.cp("dve", tq[:], posi[:], [t_], [t_])
            k.ts("dve", tq[:], tq[:], invf[:, 0:1], 1.0 / (2 * math.pi), ALU.mult, ALU.mult, [t_, k.t_const], [t_])
            k.cp("dve", ki[:], tq[:], [t_], [t_])
            k.cp("dve", kf[:], ki[:], [t_], [t_])
            k.tt("dve", fr[:], tq[:], kf[:], ALU.subtract, [t_], [t_])
            for dst, shift in ((sinT, 0.0), (cosT, 0.25)):
                if shift:
                    k.ts("dve", fr[:], fr[:], shift, None, ALU.add, ALU.bypass, [t_], [t_])
                k.ts("dve", aa[:], fr[:], 0.5, None, ALU.is_gt, ALU.bypass, [t_], [t_])
                k.tt("dve", fr[:], fr[:], aa[:], ALU.subtract, [t_], [t_])
                k.ts("dve", aa[:], fr[:], -0.5, None, ALU.is_lt, ALU.bypass, [t_], [t_])
                k.tt("dve", fr[:], fr[:], aa[:], ALU.add, [t_], [t_])
                k.act(dst[:, cols], fr[:], AF.Sin, [t_], [t_tab], scale=6.283185)
        S.barrier()
    return cosT, sinT, t_tab


def proj_rot(k, es_bufs, hT, t_hT, w, t_w, tabs, dst, t_dst):
    cosT, sinT, t_tab = tabs
    qraw, t_qr, t1, t_t1, t2, t_t2 = es_bufs
    pm = k.cs["c_pm"]
    for gq in range(8):
        cols = slice(gq * 512, (gq + 1) * 512)
        hs = slice(HOFF + gq * 512, HOFF + (gq + 1) * 512)
        p_, t_p = k.ps()
        for kc in range(8):
            k.mm(p_[:], w[:, kc, :], hT[:, kc, hs], kc == 0, kc == 7, [t_w] + t_hT[gq * 4:(gq + 1) * 4], [t_p])
        k.cp("act", qraw[:], p_[:], [t_p], [t_qr])
        p2, t_p2 = k.ps()
        k.mm(p2[:], pm[:], qraw[:], True, True, [t_qr, k.t_const], [t_p2])
        k.tt("dve", t1[:], p_[:], cosT[:, cols], ALU.mult, [t_p, t_tab], [t_t1])
        k.tt("dve", t2[:], p2[:], sinT[:, cols], ALU.mult, [t_p2, t_tab], [t_t2])
        k.tt("pool", dst[:, cols], t1[:], t2[:], ALU.add, [t_t1, t_t2], [t_dst])


def rot_bufs(k, es):
    return (k.sb(es, [128, 512], BF16, "qraw"), Tr(), k.sb(es, [128, 512], F32, "rt1"), Tr(),
            k.sb(es, [128, 512], F32, "rt2"), Tr())


def phase_A(ctx, s, l, hT, t_hT, tabs):
    k = ctx["k"]
    S = k.S
    maskA = k.cs["c_maskA"]
    ones_b = k.cs["ones_b"]
    with contextlib.ExitStack() as es:
        rb = rot_bufs(k, es)
        qT = k.sb(es, [128, S_LEN], BF16, "aqT")
        kT = k.sb(es, [128, S_LEN], BF16, "akT")
        vs = k.sb(es, [128, 32, 128], BF16, "avs")
        acc = k.sb(es, [128, 2, S_LEN], F32, "aacc")
        yb = k.sb(es, [128, S_LEN], BF16, "ayb")
        t_q, t_k, t_v, t_acc, t_yb = Tr(), Tr(), Tr(), Tr(), Tr()
        Es = [(k.sb(es, [128, 2, 256], BF16, "aE"), Tr()) for _ in range(3)]
        ws = [(k.sb(es, [128, 8, 128], BF16, "aw"), Tr()) for _ in range(3)]
        for hp in range(2):
            for g, Dl in enumerate((1, 4, 16)):
                for j3 in range(3):
                    load_w_into(k, ws[j3][0], ws[j3][1], "w_in", l, 0, D, OFF_A + g * 768 + j3 * 256 + hp * 128, 128)
                proj_rot(k, rb, hT, t_hT, ws[0][0], ws[0][1], tabs, qT, t_q)
                proj_rot(k, rb, hT, t_hT, ws[1][0], ws[1][1], tabs, kT, t_k)
                nb = 32 // Dl
                wv, t_wv = ws[2]
                for b0 in range(0, 32, 4):
                    p_, t_p = k.ps()
                    for bb in range(4):
                        blk = b0 + bb
                        r, j = blk // nb, blk % nb
                        st = HOFF + r + Dl * 128 * j
                        for kc in range(8):
                            k.mm(p_[:, bb * 128:(bb + 1) * 128], hT[:, kc, sl(st, 128, Dl)], wv[:, kc, :], kc == 0,
                                 kc == 7, [t_wv] + t_hT, [t_p])
                    k.cp("act", vs[:, b0:b0 + 4, :], p_[:].rearrange("p (b c) -> p b c", b=4), [t_p], [t_v])
                items = [(r, j) for r in range(Dl) for j in range(nb)]

                def st1(i):
                    r, j = items[i]
                    nq = 256 if j + 1 < nb else 128
                    st = r + Dl * 128 * j
                    kcols = sl(st, 128, Dl)
                    qcols = sl(st, nq, Dl)
                    E, t_E = Es[i % 3]
                    for h2 in range(2):
                        hb = 64 * h2
                        p_, t_p = k.ps()
                        k.mm(p_[:, 0:nq], kT[hb:hb + 64, kcols], qT[hb:hb + 64, qcols], True, True,
                             [t_k, t_q], [t_p])
                        k.act(E[:, h2, 0:nq], p_[:, 0:nq], AF.Exp, [t_p], [t_E], scale=0.125)
                    k.tt("pool", E[:, :, 0:nq], E[:, :, 0:nq], maskA[:].rearrange("p (a b) -> p a b", a=2)[:, :, 0:nq],
                         ALU.mult, [t_E, k.t_const], [t_E])

                def st2(i):
                    r, j = items[i]
                    st = r + Dl * 128 * j
                    E, t_E = Es[i % 3]
                    Eprev = Es[(i - 1) % 3]
                    blk = r * nb + j
                    po, t_po = k.ps()
                    for h2 in range(2):
                        hb = 64 * h2
                        for pl in range(2):
                            o_ap = po[hb:hb + 64, pl * 128:(pl + 1) * 128]
                            first = True
                            if j > 0:
                                lh = vs[:, blk - 1, hb:hb + 64] if pl == 0 else ones_b[:, 0:64]
                                k.mm(o_ap, lh, Eprev[0][:, h2, 128:256], True, False, [t_v, Eprev[1], k.t_const], [t_po])
                                first = False
                            lh = vs[:, blk, hb:hb + 64] if pl == 0 else ones_b[:, 0:64]
                            k.mm(o_ap, lh, E[:, h2, 0:128], first, True, [t_v, t_E, k.t_const], [t_po])
                    qtok = sl(st, 128, Dl)
                    pov = po[:, 0:256].rearrange("p (a b) -> p a b", a=2)
                    if g == 0:
                        k.cp("dve", acc[:, :, qtok], pov, [t_po], [t_acc])
                    else:
                        k.tt("dve", acc[:, :, qtok], acc[:, :, qtok], pov, ALU.add, [t_po, t_acc], [t_acc])
                st1(0)
                for i in range(len(items)):
                    if i + 1 < len(items):
                        st1(i + 1)
                    st2(i)
            for gq in range(8):
                cols = slice(gq * 512, (gq + 1) * 512)
                k.recip(acc[:, 1, cols], acc[:, 1, cols], [t_acc], [t_acc])
                k.tt("dve", yb[:, cols], acc[:, 0, cols], acc[:, 1, cols], ALU.mult, [t_acc], [t_yb])
                S.dma("sp", ctx["yT"][hp * 128:(hp + 1) * 128, cols], yb[:, cols], reads=[t_yb],
                      writes=[ctx["t_yT"][hp][gq]])
        S.barrier()


def phase_B(ctx, s, l, hT, t_hT, tabs):
    k = ctx["k"]
    S = k.S
    maskD = k.cs["c_maskD"]
    ones_b = k.cs["ones_b"]
    lam_init = 0.8 - 0.6 * math.exp(-0.3 * l)
    saved = (k.ps_f, k.ps_ft, k.ps_i)
    accb = list(zip(k.ps_f[0:2], k.ps_ft[0:2]))
    k.ps_f, k.ps_ft, k.ps_i = saved[0][2:6], saved[1][2:6], 0
    with contextlib.ExitStack() as es:
        rb = rot_bufs(k, es)
        lv, t_lv = load_bc(k, es, k.vf["diff_lam"][l].rearrange("a b -> (a b)"), 256, "lv")
        sc_ = k.sb(es, [128, 8], F32, "lsc")
        t_sc = Tr()
        pr = k.sb(es, [128, 128], F32, "lpr")
        k.tt("dve", pr[:, 0:64], lv[:, 0:64], lv[:, 64:128], ALU.mult, [t_lv], [t_sc])
        k.tt("dve", pr[:, 64:128], lv[:, 128:192], lv[:, 192:256], ALU.mult, [t_lv], [t_sc])
        S.op("dve", lambda e: e.reduce_sum(out=sc_[:, 0:2], in_=pr[:].rearrange("p (a b) -> p a b", a=2), axis=AX.X),
             reads=[t_sc], writes=[t_sc])
        k.act(sc_[:, 2:4], sc_[:, 0:2], AF.Exp, [t_sc], [t_sc])
        k.tt("dve", sc_[:, 4:5], sc_[:, 3:4], sc_[:, 2:3], ALU.subtract, [t_sc], [t_sc])
        k.ts("dve", sc_[:, 5:6], sc_[:, 4:5], -lam_init, None, ALU.add, ALU.bypass, [t_sc], [t_sc])
        gB = k.sb(es, [128, 1], F32, "gB")
        t_gB = Tr()
        S.dma("sp", gB[:], k.vf["diff_norm_g"][l].rearrange("(p o) -> p o", o=1), writes=[t_gB])
        k.ts("dve", gB[:], gB[:], 1.0 - lam_init, None, ALU.mult, ALU.bypass, [t_gB], [t_gB])
        qT = k.sb(es, [128, S_LEN], BF16, "bqT")
        kT = k.sb(es, [128, S_LEN], BF16, "bkT")
        vs = k.sb(es, [128, 32, 128], BF16, "bvs")
        t_q, t_k, t_v = Tr(), Tr(), Tr()
        Es = [(k.sb(es, [128, 512], BF16, "bE"), Tr()) for _ in range(6)]
        ws = [(k.sb(es, [128, 8, 128], BF16, "bw"), Tr()) for _ in range(3)]
        o1 = k.sb(es, [128, 512], F32, "bo1")
        o2 = k.sb(es, [128, 512], F32, "bo2")
        rr = k.sb(es, [128, 512], F32, "brr")
        sq = k.sb(es, [128, 512], BF16, "bsq")
        ybs = [(k.sb(es, [128, 512], BF16, "byb"), Tr()) for _ in range(2)]
        t_o = Tr()
        for h in range(4):
            for j3 in range(3):
                load_w_into(k, ws[j3][0], ws[j3][1], "w_in", l, 0, D, OFF_B + j3 * 512 + h * 128, 128)
            proj_rot(k, rb, hT, t_hT, ws[0][0], ws[0][1], tabs, qT, t_q)
            proj_rot(k, rb, hT, t_hT, ws[1][0], ws[1][1], tabs, kT, t_k)
            wv, t_wv = ws[2]
            for b0 in range(0, 32, 4):
                p_, t_p = k.ps()
                for bb in range(4):
                    blk = b0 + bb
                    st = HOFF + 128 * blk
                    for kc in range(8):
                        k.mm(p_[:, bb * 128:(bb + 1) * 128], hT[:, kc, st:st + 128], wv[:, kc, :], kc == 0, kc == 7,
                             [t_wv] + t_hT, [t_p])
                k.cp("act", vs[:, b0:b0 + 4, :], p_[:].rearrange("p (b c) -> p b c", b=4), [t_p], [t_v])
            items = [(G, m, j) for G in range(8) for m in range(2) for j in range(4 * G + 4)]
            LA = 3

            def geo(G, j):
                jj = max(j - 4 * G, 0)
                qoff = 128 * jj
                return qoff, 512 - qoff, 512 * G + qoff

            def st1(i):
                G, m, j = items[i]
                qoff, nq, q0 = geo(G, j)
                hb = 64 * m
                p_, t_p = k.ps()
                k.mm(p_[:, 0:nq], kT[hb:hb + 64, 128 * j:128 * j + 128], qT[hb:hb + 64, q0:q0 + nq], True, True,
                     [t_k, t_q], [t_p])
                E, t_E = Es[i % 6]
                k.act(E[:, 0:nq], p_[:, 0:nq], AF.Exp, [t_p], [t_E], scale=0.125)
                if j >= 4 * G:
                    k.tt("dve", E[:, 0:nq], E[:, 0:nq], maskD[:, 0:nq], ALU.mult, [t_E, k.t_const], [t_E])

            def st2(i):
                G, m, j = items[i]
                qoff, nq, q0 = geo(G, j)
                nj = 4 * G + 4
                E, t_E = Es[i % 6]
                pn, t_pn = accb[0]
                pd, t_pd = accb[1]
                k.mm(pn[:, qoff:512], vs[:, j, :], E[:, 0:nq], j == 0, j == nj - 1, [t_v, t_E], [t_pn])
                k.mm(pd[:, qoff:512], ones_b[:], E[:, 0:nq], j == 0, j == nj - 1, [t_E, k.t_const], [t_pd])
                if j == nj - 1:
                    k.recip(rr[:], pd[:], [t_pd, t_o], [t_o])
                    k.tt("dve", (o1 if m == 0 else o2)[:], pn[:], rr[:], ALU.mult, [t_pn, t_o], [t_o])
                    if m == 1:
                        fin(G)

            def fin(G):
                k.stt(o1[:], o2[:], sc_[:, 5:6], o1[:], ALU.mult, ALU.add, [t_o, t_sc], [t_o])
                k.tt("pool", sq[:], o1[:], o1[:], ALU.mult, [t_o], [t_o])
                pm_, t_pm = k.ps()
                k.mm(pm_[:], ones_b[:], sq[:], True, True, [t_o, k.t_const], [t_pm])
                k.ts("dve", rr[:], pm_[:], 1.0 / 128, EPS, ALU.mult, ALU.add, [t_pm, t_o], [t_o])
                k.act(rr[:], rr[:], AF.Sqrt, [t_o], [t_o])
                k.recip(rr[:], rr[:], [t_o], [t_o])
                yb, t_yb = ybs[G % 2]
                k.stt(yb[:], o1[:], gB[:, 0:1], rr[:], ALU.mult, ALU.mult, [t_o, t_gB], [t_yb])
                S.dma("sp", ctx["yT"][256 + h * 128:256 + (h + 1) * 128, G * 512:(G + 1) * 512], yb[:], reads=[t_yb],
                      writes=[ctx["t_yT"][2 + h][G]])

            for i in range(min(LA, len(items))):
                st1(i)
            for i in range(len(items)):
                if i + LA < len(items):
                    st1(i + LA)
                st2(i)
        S.barrier()
    k.ps_f, k.ps_ft, k.ps_i = saved


def phase_C(ctx, s, l, hT, t_hT):
    k = ctx["k"]
    S = k.S
    cs = k.cs
    with contextlib.ExitStack() as es:
        wc, t_wc = load_w(k, es, "w_in", l, 0, D, OFF_C, 1024, "wc")
        lb = k.sb(es, [128, 256], F32, "lb")
        omlb = k.sb(es, [128, 256], F32, "omlb")
        t_lb = Tr()
        if l == 0:
            k.memset(lb[:], 0.0, [t_lb])
        else:
            S.dma("sp", lb[:], k.vf["hgrn_lb_logits"][1].partition_broadcast(128), writes=[t_lb])
            S.dma("sp", omlb[:], k.vf["hgrn_lb_logits"][0].partition_broadcast(128), writes=[t_lb])
            k.tt("dve", lb[:], lb[:], omlb[:], ALU.subtract, [t_lb], [t_lb])
            k.act(lb[:], lb[:], AF.Sigmoid, [t_lb], [t_lb])
        k.ts("dve", omlb[:], lb[:], -1.0, 1.0, ALU.mult, ALU.add, [t_lb], [t_lb])
        gn4 = k.sb(es, [128, 4, 64], F32, "gn4")
        t_gn = Tr()
        for h in range(4):
            S.dma("sp", gn4[:, h, :], k.vf["hgrn_norm_g"][l].partition_broadcast(128), writes=[t_gn])
        S32 = k.sb(es, [128, 2, 64], F32, "S32")
        t_S = Tr()
        k.memset(S32[:], 0.0, [t_S])
        Sbfs = [(k.sb(es, [128, 2, 64], BF16, "Sbf"), Tr()) for _ in range(2)]
        k.memset(Sbfs[0][0][:], 0.0, [Sbfs[0][1]])

        def mk(shape, dt, n, nm):
            return [(k.sb(es, shape, dt, nm), Tr()) for _ in range(n)]
        qs_, sf_, lf_, kk_, sg_ = (mk([128, 256], F32, 2, nm) for nm in ("cqs", "csf", "clf", "ckk", "csg"))
        ib_, Qp_, Kp_ = (mk([128, 256], BF16, 2, nm) for nm in ("cib", "cQp", "cKp"))
        eb_, enb_ = (mk([128, 256], F32, 2, nm) for nm in ("ceb", "cenb"))
        colv_ = mk([128, 2, 3], F32, 2, "ccolv")
        dd_ = mk([128, 2], F32, 2, "cdd")
        QT_, QTt_, KT_ = (mk([128, 2, 128], BF16, 2, nm) for nm in ("cQT", "cQTt", "cKT"))
        attE_, attO_ = (mk([128, 2, 128], BF16, 2, nm) for nm in ("cattE", "cattO"))
        QTh_ = mk([128, 2, 128], BF16, 2, "cQTh")
        for b_ in range(2):
            k.memset(QTh_[b_][0][:], 0.0, [QTh_[b_][1]])
        o_ = mk([128, 256], F32, 2, "co")
        sq_ = mk([128, 256], F32, 2, "csq")
        st_ = mk([128, 12], F32, 2, "cst")
        tmpU_ = mk([128, 2, 64], F32, 2, "ctu")
        y_ = mk([128, 256], BF16, 2, "cy")
        ygs = mk([128, 2, 512], BF16, 2, "cyg")
        tri2 = cs["c_tri2"]
        for tt_ in range(NT):
            b = tt_ % 2
            hs = slice(HOFF + tt_ * 128, HOFF + (tt_ + 1) * 128)
            p0, t_p0 = k.ps()
            p1, t_p1 = k.ps()
            for kc in range(8):
                k.mm(p0[:], hT[:, kc, hs], wc[:, kc, 0:512], kc == 0, kc == 7, [t_wc, t_hT[tt_]], [t_p0])
            for kc in range(8):
                k.mm(p1[:], hT[:, kc, hs], wc[:, kc, 512:1024], kc == 0, kc == 7, [t_wc, t_hT[tt_]], [t_p1])
            qs, t_qs = qs_[b]
            sf, t_sf = sf_[b]
            lf, t_lf = lf_[b]
            kk, t_kk = kk_[b]
            sg, t_sg = sg_[b]
            ib, t_ib = ib_[b]
            k.act(qs[:], p0[:, 0:256], AF.Silu, [t_p0], [t_qs])
            k.act(sf[:], p0[:, 256:512], AF.Sigmoid, [t_p0], [t_sf])
            k.cp("act", ib[:], p1[:, 0:256], [t_p1], [t_ib])
            k.act(sg[:], p1[:, 256:512], AF.Silu, [t_p1], [t_sg])
            k.tt("dve", sf[:], sf[:], omlb[:], ALU.mult, [t_sf, t_lb], [t_sf])
            k.tt("dve", sf[:], sf[:], lb[:], ALU.add, [t_sf, t_lb], [t_sf])
            k.act(lf[:], sf[:], AF.Ln, [t_sf], [t_lf])
            k.ts("pool", kk[:], sf[:], -1.0, 1.0, ALU.mult, ALU.add, [t_sf], [t_kk])
            pb_, t_pb_ = k.ps()
            k.mm(pb_[:, 0:256], cs["c_triM"][:], lf[:], True, True, [t_lf, k.t_const], [t_pb_])
            pc, t_pc = k.ps()
            for hp in range(2):
                k.mm(pc[:, hp * 2:hp * 2 + 2], lf[:, hp * 128:(hp + 1) * 128], cs["c_sel"][:], True, True,
                     [t_lf, k.t_const], [t_pc])
            colv, t_cv = colv_[b]
            dd, t_dd = dd_[b]
            pcv = pc[:, 0:4].rearrange("p (a c) -> p a c", a=2)
            k.act(colv[:, :, 0:2], pcv, AF.Exp, [t_pc], [t_cv])
            k.cp("act", st_[b][0][:, 0:2], pcv[:, :, 0], [t_pc], [st_[b][1]])
            k.tt("dve", dd[:], pcv[:, :, 1], st_[b][0][:, 0:2], ALU.subtract, [t_pc, st_[b][1]], [t_dd])
            k.act(colv[:, :, 2], dd[:], AF.Exp, [t_dd], [t_cv])
            eb, t_eb = eb_[b]
            enb, t_enb = enb_[b]
            k.act(eb[:], pb_[:, 0:256], AF.Exp, [t_pb_], [t_eb])
            k.act(enb[:], pb_[:, 0:256], AF.Exp, [t_pb_], [t_enb], scale=-1.0)
            Qp, t_Qp = Qp_[b]
            Kp, t_Kp = Kp_[b]
            k.tt("dve", Qp[:], qs[:], eb[:], ALU.mult, [t_qs, t_eb], [t_Qp])
            k.tt("pool", Kp[:], kk[:], enb[:], ALU.mult, [t_kk, t_enb], [t_Kp])
            pt, t_pt = k.psb()
            for hp in range(2):
                k.tp(pt[:, hp * 128:(hp + 1) * 128], Qp[:, hp * 128:(hp + 1) * 128], [t_Qp], [t_pt])
                k.tp(pt[:, 256 + hp * 128:256 + (hp + 1) * 128], Kp[:, hp * 128:(hp + 1) * 128], [t_Kp], [t_pt])
            QT, t_QT = QT_[b]
            QTt, t_QTt = QTt_[b]
            KT, t_KT = KT_[b]
            k.cp("act", QT[:], pt[:, 0:256].rearrange("p (a c) -> p a c", a=2), [t_pt], [t_QT])
            k.cp("act", KT[:], pt[:, 256:512].rearrange("p (a c) -> p a c", a=2), [t_pt], [t_KT])
            QTh, t_QTh = QTh_[b]
            k.cp("pool", QTh[:, :, 64:128], QT[:, :, 64:128], [t_QT], [t_QTh])
            for hp in range(2):
                k.ts("dve", QTt[:, hp, :], pt[:, hp * 128:(hp + 1) * 128], colv[:, hp, 0:1], None, ALU.mult, ALU.bypass,
                     [t_pt, t_cv], [t_QTt])
            paE, t_paE = k.ps()
            paO, t_paO = k.ps()
            for h in range(4):
                hp, par = h // 2, h % 2
                hb = 64 * par
                pa, t_pa = (paE, t_paE) if par == 0 else (paO, t_paO)
                k.mm(pa[0:64, hp * 128:(hp + 1) * 128], KT[hb:hb + 64, hp, 0:64], QT[hb:hb + 64, hp, :], True, True,
                     [t_KT, t_QT], [t_pa])
                k.mm(pa[64:128, hp * 128:(hp + 1) * 128], KT[hb:hb + 64, hp, 64:128], QTh[hb:hb + 64, hp, :], True, True,
                     [t_KT, t_QTh], [t_pa])
            attE, t_aE = attE_[b]
            attO, t_aO = attO_[b]
            k.tt("dve", attE[:].rearrange("p a c -> p (a c)"), paE[:, 0:256], tri2[:], ALU.mult, [t_paE, k.t_const], [t_aE])
            k.tt("dve", attO[:].rearrange("p a c -> p (a c)"), paO[:, 0:256], tri2[:], ALU.mult, [t_paO, k.t_const], [t_aO])
            po, t_po = k.ps()
            for h in range(4):
                hp, par = h // 2, h % 2
                att, t_att = (attE, t_aE) if par == 0 else (attO, t_aO)
                k.mm(po[:, h * 64:(h + 1) * 64], att[:, hp, :], ib[:, h * 64:(h + 1) * 64], True, True, [t_att, t_ib], [t_po])
            Sbf, t_Sbf = Sbfs[b]
            Sbn, t_Sbn = Sbfs[1 - b]
            piE, t_piE = k.ps()
            piO, t_piO = k.ps()
            for h in range(4):
                hp, par = h // 2, h % 2
                hb = 64 * par
                pi, t_pi = (piE, t_piE) if par == 0 else (piO, t_piO)
                k.mm(pi[:, hp * 64:(hp + 1) * 64], QTt[hb:hb + 64, hp, :], Sbf[hb:hb + 64, hp, :], True, True,
                     [t_QTt, t_Sbf], [t_pi])
            o, t_o = o_[b]
            k.cp("act", o[:], po[:, 0:256], [t_po], [t_o])
            ov = o[:].rearrange("p (a b c) -> p a b c", a=2, b=2)
            k.tt("dve", ov[:, :, 0, :], ov[:, :, 0, :], piE[:, 0:128].rearrange("p (a c) -> p a c", a=2), ALU.add,
                 [t_o, t_piE], [t_o])
            k.tt("dve", ov[:, :, 1, :], ov[:, :, 1, :], piO[:, 0:128].rearrange("p (a c) -> p a c", a=2), ALU.add,
                 [t_o, t_piO], [t_o])
            pu, t_pu = k.ps()
            for h in range(4):
                hp, par = h // 2, h % 2
                hb = 64 * par
                k.mm(pu[hb:hb + 64, hp * 64:(hp + 1) * 64], Kp[:, h * 64:(h + 1) * 64], ib[:, h * 64:(h + 1) * 64], True, True,
                     [t_Kp, t_ib], [t_pu])
            tu, t_tu = tmpU_[b]
            for hp in range(2):
                k.ts("dve", tu[:, hp, :], pu[:, hp * 64:(hp + 1) * 64], colv[:, hp, 2:3], None, ALU.mult, ALU.bypass,
                     [t_pu, t_cv], [t_tu])
                k.stt(S32[:, hp, :], S32[:, hp, :], colv[:, hp, 1:2], tu[:, hp, :], ALU.mult, ALU.add, [t_S, t_cv, t_tu], [t_S])
            k.cp("pool", Sbn[:], S32[:], [t_S], [t_Sbn])
            sq, t_sq = sq_[b]
            st, t_st = st_[b]
            y, t_y = y_[b]
            k.tt("pool", sq[:], o[:], o[:], ALU.mult, [t_o], [t_sq])
            S.op("dve", lambda e, st=st, sq=sq: e.reduce_sum(out=st[:, 0:4], in_=sq[:].rearrange("p (a c) -> p a c", a=4),
                                                            axis=AX.X), reads=[t_sq], writes=[t_st])
            k.ts("dve", st[:, 4:8], st[:, 0:4], 1.0 / 64, EPS, ALU.mult, ALU.add, [t_st], [t_st])
            k.act(st[:, 4:8], st[:, 4:8], AF.Sqrt, [t_st], [t_st])
            k.recip(st[:, 8:12], st[:, 4:8], [t_st], [t_st])
            k.tt("pool", sq[:], sg[:], gn4[:].rearrange("p a c -> p (a c)"), ALU.mult, [t_sg, t_gn, t_sq], [t_sq])
            for h in range(4):
                k.stt(y[:, h * 64:(h + 1) * 64], o[:, h * 64:(h + 1) * 64], st[:, 8 + h:9 + h], sq[:, h * 64:(h + 1) * 64],
                      ALU.mult, ALU.mult, [t_o, t_st, t_sq], [t_y])
            pt2, t_pt2 = k.psb()
            for hp in range(2):
                k.tp(pt2[:, hp * 128:(hp + 1) * 128], y[:, hp * 128:(hp + 1) * 128], [t_y], [t_pt2])
            gq, q = tt_ // 4, tt_ % 4
            yg, t_yg = ygs[gq % 2]
            k.cp("act", yg[:, :, q * 128:(q + 1) * 128], pt2[:, 0:256].rearrange("p (a c) -> p a c", a=2), [t_pt2], [t_yg])
            if q == 3:
                for hp in range(2):
                    S.dma("sp", ctx["yT"][768 + hp * 128:768 + (hp + 1) * 128, gq * 512:(gq + 1) * 512], yg[:, hp, :],
                          reads=[t_yg], writes=[ctx["t_yT"][6 + hp][gq]])
        S.barrier()


def phase_D(ctx, s, l, hT, t_hT, t_h0):
    k = ctx["k"]
    S = k.S
    cs = k.cs
    vfirst, t_vf = ctx["vfirst"], ctx["t_vf"][s]
    with contextlib.ExitStack() as es:
        wa = k.sb(es, [128, 8, 1024], BF16, "wda")
        wb_ = k.sb(es, [128, 8, 1024], BF16, "wdb")
        t_wa = Tr()
        with contextlib.ExitStack() as e2:
            wd, t_wd = load_w(k, e2, "w_in", l, 0, D, OFF_D, 1024, "wd")
            mu, t_mu = load_bc(k, e2, k.vf["rwkv_mu"][l], 1024, "mu")
            omu = k.sb(e2, [128, 1024], F32, "omu")
            k.ts("dve", omu[:], mu[:], -1.0, 1.0, ALU.mult, ALU.add, [t_mu], [t_mu])
            for kc in range(8):
                k.tt("dve", wb_[:, kc, :], wd[:, kc, :], mu[:], ALU.mult, [t_wd, t_mu], [t_wa])
                k.tt("pool", wa[:, kc, :], wd[:, kc, :], omu[:], ALU.mult, [t_wd, t_mu], [t_wa])
            S.barrier()
        t_bc = Tr()

        def bc(src, n=256, nm="dbc"):
            t = k.sb(es, [128, n], F32, nm)
            S.dma("sp", t[:], src.partition_broadcast(128), writes=[t_bc])
            return t
        w0 = bc(k.vf["rwkv_w0"][l])
        a0 = bc(k.vf["rwkv_a0"][l])
        kkb = bc(k.vf["rwkv_k_k"][l])
        kab = bc(k.vf["rwkv_k_a"][l])
        lng = bc(k.vf["rwkv_lnx_g"][l])
        lnb = bc(k.vf["rwkv_lnx_b"][l])
        rkb = bc(k.vf["rwkv_r_k"][l].rearrange("a b -> (a b)"))
        omka = k.sb(es, [128, 256], F32, "omka")
        k.ts("dve", omka[:], kab[:], -1.0, 1.0, ALU.mult, ALU.add, [t_bc], [t_bc])
        w2a2 = k.sb(es, [128, 256], BF16, "w2a2")
        g2 = k.sb(es, [128, 256], BF16, "g2")
        t_lw = Tr()
        S.dma("sp", w2a2[0:64, :], k.wb["rwkv_w2"][l], reads=k.t_w["rwkv_w2"], writes=[t_lw])
        S.dma("sp", w2a2[64:128, :], k.wb["rwkv_a2"][l], reads=k.t_w["rwkv_a2"], writes=[t_lw])
        S.dma("sp", g2[:], k.wb["rwkv_g2"][l], reads=k.t_w["rwkv_g2"], writes=[t_lw])
        if l > 0:
            v0b = bc(k.vf["rwkv_v0"][l - 1])
            v1 = k.sb(es, [128, 2, 32], BF16, "v1")
            v2 = k.sb(es, [32, 256], BF16, "v2")
            S.dma("sp", v1[:], k.wb["rwkv_v1"][l - 1].rearrange("(kc p) n -> p kc n", p=128), reads=k.t_w["rwkv_v1"],
                  writes=[t_lw])
            S.dma("sp", v2[:], k.wb["rwkv_v2"][l - 1], reads=k.t_w["rwkv_v2"], writes=[t_lw])
        H32 = k.sb(es, [128, 2, 64], F32, "H32")
        t_H = Tr()
        k.memset(H32[:], 0.0, [t_H])
        Hbfs = [(k.sb(es, [128, 2, 64], BF16, "Hbf"), Tr()) for _ in range(2)]
        k.memset(Hbfs[0][0][:], 0.0, [Hbfs[0][1]])

        def f32(nm, n=256):
            return k.sb(es, [128, n], F32, nm), Tr()

        def b16(nm, shape=(128, 256)):
            return k.sb(es, list(shape), BF16, nm), Tr()
        r32, t_r = f32("r32")
        k32, t_k = f32("k32")
        v32, t_v = f32("v32")
        li, t_li = b16("li")
        liT, t_liT = b16("liT", (128, 2, 128))
        lw, t_lwv = f32("lw")
        a32, t_a = f32("a32")
        g32, t_g = f32("g32")
        tmp, t_tmp = f32("tmp")
        tmp2, t_tmp2 = f32("tmp2")
        kk0, t_kk = f32("kk0")
        kp, t_kp = f32("kp")
        bv, t_bv = f32("bv")
        st, t_st = f32("st", 24)
        e1, t_e1 = f32("e1")
        e2_, t_e2 = f32("e2")
        e4, t_e4 = f32("e4")
        e5, t_e5 = f32("e5")
        ewl, t_ewl = f32("ewl")
        colv, t_cv = k.sb(es, [128, 2, 2], F32, "dcolv"), Tr()
        Rp, t_Rp = b16("Rp")
        Ap, t_Ap = b16("Ap")
        Bp, t_Bp = b16("Bp")
        Kp, t_Kp = b16("Kp")
        At, t_At = b16("At")
        Bc, t_Bc = b16("Bc")
        Kc, t_Kc = b16("Kc")
        Vb, t_Vb = b16("Vb")
        ART, t_ART = b16("ART", (128, 2, 2, 128))
        BT, t_BT = b16("BT", (128, 2, 128))
        KT, t_KT = b16("KT", (128, 2, 128))
        RTt, t_RTt = b16("RTt", (128, 2, 128))
        RhT, t_RhT = b16("RhT", (128, 2, 128))
        LM = [b16("LM", (128, 512)) for _ in range(4)]
        Lp = [[b16("Lp", (128, 2, 128)) for _ in range(7)] for _ in range(4)]
        Xs = [[b16("X", (128, 128)) for _ in range(2)] for _ in range(4)]
        Ysb, t_Y = f32("Ysb")
        GpT, t_GpT = b16("GpT", (128, 2, 64))
        Zsb, t_Z = k.sb(es, [128, 2, 64], F32, "Zsb"), Tr()
        ybf, t_ybf = b16("ybf")
        ygs = [b16("dyg", (128, 2, 512)) for _ in range(2)]
        if l > 0:
            vT, t_vT = b16("vT", (128, 2, 128))
            u1, t_u1 = b16("u1", (32, 128))
            vfb, t_vfb = f32("vfb")
        mSI2, mLow = cs["c_maskSI2"], cs["c_maskLow"]

        for tt_ in range(NT):
            hs = slice(HOFF + tt_ * 128, HOFF + (tt_ + 1) * 128)
            hs1 = slice(HOFF + tt_ * 128 - 1, HOFF + (tt_ + 1) * 128 - 1)
            rdh = [t_wa, t_hT[tt_], t_h0] + ([t_hT[tt_ - 1]] if tt_ > 0 else [])
            pA, t_pA = k.ps()
            pB, t_pB = k.ps()
            for p_, t_p, c0 in ((pA, t_pA, 0), (pB, t_pB, 512)):
                for kc in range(8):
                    k.mm(p_[:], hT[:, kc, hs], wa[:, kc, c0:c0 + 512], kc == 0, False, rdh, [t_p])
                for kc in range(8):
                    k.mm(p_[:], hT[:, kc, hs1], wb_[:, kc, c0:c0 + 512], False, kc == 7, rdh, [t_p])
            k.cp("act", r32[:], pA[:, 0:256], [t_pA], [t_r])
            k.cp("act", k32[:], pA[:, 256:512], [t_pA], [t_k])
            k.cp("act", v32[:], pB[:, 0:256], [t_pB], [t_v])
            k.act(li[:, 0:64], pB[:, 256:320], AF.Tanh, [t_pB], [t_li])
            k.cp("act", li[:, 64:128], pB[:, 320:384], [t_pB], [t_li])
            k.act(li[:, 128:256], pB[:, 384:512], AF.Sigmoid, [t_pB], [t_li])
            pt, t_pt = k.psb()
            for j in range(2):
                k.tp(pt[:, j * 128:(j + 1) * 128], li[:, j * 128:(j + 1) * 128], [t_li], [t_pt])
            k.cp("dve", liT[:], pt[:, 0:256].rearrange("p (a c) -> p a c", a=2), [t_pt], [t_liT])
            pw, t_pw = k.ps()
            pa_, t_pa = k.ps()
            pg, t_pg = k.ps()
            k.mm(pw[:, 0:256], liT[0:64, 0, :], w2a2[0:64, :], True, True, [t_liT, t_lw], [t_pw])
            k.mm(pa_[:, 0:256], liT[64:128, 0, :], w2a2[64:128, :], True, True, [t_liT, t_lw], [t_pa])
            k.mm(pg[:, 0:256], liT[:, 1, :], g2[:], True, True, [t_liT, t_lw], [t_pg])
            k.tt("dve", lw[:], pw[:, 0:256], w0[:], ALU.add, [t_pw, t_bc], [t_lwv])
            k.act(lw[:], lw[:], AF.Sigmoid, [t_lwv], [t_lwv])
            k.ts("dve", lw[:], lw[:], -0.6065306597126334, None, ALU.mult, ALU.bypass, [t_lwv], [t_lwv])
            k.tt("dve", a32[:], pa_[:, 0:256], a0[:], ALU.add, [t_pa, t_bc], [t_a])
            k.act(a32[:], a32[:], AF.Sigmoid, [t_a], [t_a])
            k.cp("act", g32[:], pg[:, 0:256], [t_pg], [t_g])
            if l == 0:
                S.dma("sp", vfirst[s, tt_ * 128:(tt_ + 1) * 128, :], v32[:], reads=[t_v], writes=[t_vf[tt_]])
            else:
                S.dma("sp", vfb[:], vfirst[s, tt_ * 128:(tt_ + 1) * 128, :], reads=[t_vf[tt_]], writes=[t_vfb])
                k.cp("act", Vb[:], v32[:], [t_v], [t_Vb])
                pt, t_pt = k.psb()
                for j in range(2):
                    k.tp(pt[:, j * 128:(j + 1) * 128], Vb[:, j * 128:(j + 1) * 128], [t_Vb], [t_pt])
                k.cp("dve", vT[:], pt[:, 0:256].rearrange("p (a c) -> p a c", a=2), [t_pt], [t_vT])
                p1, t_p1 = k.ps()
                for j in range(2):
                    k.mm(p1[0:32, 0:128], v1[:, j, :], vT[:, j, :], j == 0, j == 1, [t_vT, t_lw], [t_p1])
                k.cp("act", u1[:], p1[0:32, 0:128], [t_p1], [t_u1])
                p2, t_p2 = k.ps()
                k.mm(p2[:, 0:256], u1[:], v2[:], True, True, [t_u1, t_lw], [t_p2])
                k.tt("dve", tmp[:], p2[:, 0:256], v0b[:], ALU.add, [t_p2, t_bc], [t_tmp])
                k.act(tmp[:], tmp[:], AF.Sigmoid, [t_tmp], [t_tmp])
                k.tt("dve", vfb[:], vfb[:], v32[:], ALU.subtract, [t_vfb, t_v], [t_vfb])
                k.tt("dve", vfb[:], vfb[:], tmp[:], ALU.mult, [t_vfb, t_tmp], [t_vfb])
                k.tt("dve", v32[:], v32[:], vfb[:], ALU.add, [t_v, t_vfb], [t_v])
            k.cp("act", Vb[:], v32[:], [t_v], [t_Vb])
            k.tt("dve", kk0[:], k32[:], kkb[:], ALU.mult, [t_k, t_bc], [t_kk])
            k.tt("pool", tmp[:], kk0[:], kk0[:], ALU.mult, [t_kk], [t_tmp])
            S.op("dve", lambda e: e.reduce_sum(out=st[:, 0:4], in_=tmp[:].rearrange("p (a c) -> p a c", a=4), axis=AX.X),
                 reads=[t_tmp], writes=[t_st])
            k.act(st[:, 0:4], st[:, 0:4], AF.Sqrt, [t_st], [t_st])
            k.ts("dve", st[:, 0:4], st[:, 0:4], 1e-12, None, ALU.max, ALU.bypass, [t_st], [t_st])
            k.recip(st[:, 4:8], st[:, 0:4], [t_st], [t_st])
            for h in range(4):
                k.ts("dve", kk0[:, h * 64:(h + 1) * 64], kk0[:, h * 64:(h + 1) * 64], st[:, 4 + h:5 + h], None, ALU.mult,
                     ALU.bypass, [t_kk, t_st], [t_kk])
            k.tt("dve", tmp2[:], a32[:], kab[:], ALU.mult, [t_a, t_bc], [t_tmp2])
            k.tt("pool", tmp2[:], tmp2[:], omka[:], ALU.add, [t_tmp2, t_bc], [t_tmp2])
            k.tt("dve", kp[:], k32[:], tmp2[:], ALU.mult, [t_k, t_tmp2], [t_kp])
            k.tt("pool", bv[:], kk0[:], a32[:], ALU.mult, [t_kk, t_a], [t_bv])
            pCm, t_pCm = k.ps()
            pCt, t_pCt = k.ps()
            pCr, t_pCr = k.ps()
            k.mm(pCm[:, 0:256], cs["c_triM"][:], lw[:], True, True, [t_lwv, k.t_const], [t_pCm])
            k.mm(pCt[:, 0:256], cs["c_tri"][:], lw[:], True, True, [t_lwv, k.t_const], [t_pCt])
            k.mm(pCr[:, 0:256], cs["c_triR"][:], lw[:], True, True, [t_lwv, k.t_const], [t_pCr])
            pc, t_pc = k.ps()
            for hp in range(2):
                k.mm(pc[:, hp * 2:hp * 2 + 2], lw[:, hp * 128:(hp + 1) * 128], cs["c_sel"][:], True, True,
                     [t_lwv, k.t_const], [t_pc])
            k.act(colv[:], pc[:, 0:4].rearrange("p (a c) -> p a c", a=2), AF.Exp, [t_pc], [t_cv])
            k.act(e1[:], pCm[:, 0:256], AF.Exp, [t_pCm], [t_e1])
            k.act(e2_[:], pCm[:, 0:256], AF.Exp, [t_pCm], [t_e2], scale=-1.0)
            k.act(e4[:], pCt[:, 0:256], AF.Exp, [t_pCt], [t_e4])
            k.act(e5[:], pCr[:, 0:256], AF.Exp, [t_pCr], [t_e5])
            k.act(ewl[:], lw[:], AF.Exp, [t_lwv], [t_ewl], scale=-1.0)
            k.tt("dve", Rp[:], r32[:], e1[:], ALU.mult, [t_r, t_e1], [t_Rp])
            k.tt("pool", tmp[:], kk0[:], ewl[:], ALU.mult, [t_kk, t_ewl, t_st], [t_tmp])
            k.stt(Ap[:], tmp[:], -1.0, e1[:], ALU.mult, ALU.mult, [t_tmp, t_e1], [t_Ap])
            k.stt(At[:], tmp[:], -1.0, e4[:], ALU.mult, ALU.mult, [t_tmp, t_e4], [t_At])
            k.tt("pool", Bp[:], bv[:], e2_[:], ALU.mult, [t_bv, t_e2], [t_Bp])
            k.tt("dve", Kp[:], kp[:], e2_[:], ALU.mult, [t_kp, t_e2], [t_Kp])
            k.tt("pool", Bc[:], bv[:], e5[:], ALU.mult, [t_bv, t_e5], [t_Bc])
            k.tt("dve", Kc[:], kp[:], e5[:], ALU.mult, [t_kp, t_e5], [t_Kc])
            pt, t_pt = k.psb()
            for j, src in enumerate((Ap, Rp, Bp, Kp)):
                for hp in range(2):
                    k.tp(pt[:, (j * 2 + hp) * 128:(j * 2 + hp + 1) * 128], src[:, hp * 128:(hp + 1) * 128],
                         [t_Ap, t_Rp, t_Bp, t_Kp], [t_pt])
            ptv = pt[:].rearrange("p (j a c) -> p j a c", j=4, a=2)
            for j in range(2):
                k.cp("act", ART[:, :, j, :], ptv[:, j, :, :], [t_pt], [t_ART])
            k.cp("dve", BT[:], ptv[:, 2, :, :], [t_pt], [t_BT])
            k.cp("dve", KT[:], ptv[:, 3, :, :], [t_pt], [t_KT])
            for hp in range(2):
                k.ts("dve", RTt[:, hp, :], ptv[:, 1, hp, :], colv[:, hp, 0:1], None, ALU.mult, ALU.bypass, [t_pt, t_cv],
                     [t_RTt])
            for h in range(4):
                hp, par = h // 2, h % 2
                hb = 64 * par
                LMh, t_LM = LM[h]
                pl, t_pl = k.ps()
                rhs_ar = ART[hb:hb + 64, hp, :, :].rearrange("p a c -> p (a c)")
                k.mm(pl[:, 0:256], BT[hb:hb + 64, hp, :], rhs_ar, True, True, [t_BT, t_ART], [t_pl])
                k.mm(pl[:, 256:512], KT[hb:hb + 64, hp, :], rhs_ar, True, True, [t_KT, t_ART], [t_pl])
                k.tt("dve", LMh[:], pl[:], mSI2[:], ALU.mult, [t_pl, k.t_const], [t_LM])
                L0, t_L0 = Lp[h][0]
                p0, t_p0 = k.ps()
                k.mm(p0[:, 0:128], ART[hb:hb + 64, hp, 0, :], BT[hb:hb + 64, hp, :], True, True, [t_ART, t_BT], [t_p0])
                k.tt("dve", L0[:, 0, :], p0[:, 0:128], mLow[:], ALU.mult, [t_p0, k.t_const], [t_L0])
                k.cp("pool", L0[:, 1, :], LMh[:, 0:128], [t_LM], [t_L0])
            for h in range(4):
                LMh, t_LM = LM[h]
                X0, t_X0 = Xs[h][0]
                px, t_px = k.ps()
                k.mm(px[:, 0:64], LMh[:, 256:384], Vb[:, h * 64:(h + 1) * 64], True, True, [t_LM, t_Vb], [t_px])
                k.cp("act", X0[:, 64:128], px[:, 0:64], [t_px], [t_X0])
                k.cp("pool", X0[:, 0:64], At[:, h * 64:(h + 1) * 64], [t_At], [t_X0])
            for j in range(7):
                if j < 6:
                    for h in range(4):
                        Lj, t_Lj = Lp[h][j]
                        Ln, t_Ln = Lp[h][j + 1]
                        p_, t_p = k.ps()
                        k.mm(p_[:, 0:128], Lj[:, 1, :], Lj[:, 0, :], True, True, [t_Lj], [t_p])
                        k.mm(p_[:, 128:256], Lj[:, 0, :], Lj[:, 1, :], True, True, [t_Lj], [t_p])
                        k.cp("act", Ln[:].rearrange("p a c -> p (a c)"), p_[:, 0:256], [t_p], [t_Ln])
                for h in range(4):
                    Xc, t_Xc = Xs[h][j % 2]
                    Xn, t_Xn = Xs[h][(j + 1) % 2]
                    Lj, t_Lj = Lp[h][j]
                    px, t_px = k.ps()
                    k.mm(px[:, 0:128], Lj[:, 1, :], Xc[:], True, True, [t_Lj, t_Xc], [t_px])
                    k.tt("dve", Xn[:], px[:, 0:128], Xc[:], ALU.add, [t_px, t_Xc], [t_Xn])
            pR, t_pR = k.ps()
            pY, t_pY = k.ps()
            pG, t_pG = k.ps()
            pZ, t_pZ = k.ps()
            for h in range(4):
                hp, par = h // 2, h % 2
                hb = 64 * par
                LMh, t_LM = LM[h]
                X7, t_X7 = Xs[h][1]
                hc = slice(h * 64, (h + 1) * 64)
                k.mm(pR[hb:hb + 64, hp * 128:(hp + 1) * 128], X7[:, 0:64], LMh[:, 128:256], True, True, [t_X7, t_LM], [t_pR])
                k.mm(pY[:, hc], LMh[:, 128:256], X7[:, 64:128], True, False, [t_X7, t_LM], [t_pY])
                k.mm(pY[:, hc], LMh[:, 384:512], Vb[:, hc], False, True, [t_Vb, t_LM], [t_pY])
                k.mm(pG[hb:hb + 64, hp * 64:(hp + 1) * 64], X7[:, 0:64], Bc[:, hc], True, True, [t_X7, t_Bc], [t_pG])
                k.mm(pZ[hb:hb + 64, hp * 64:(hp + 1) * 64], Bc[:, hc], X7[:, 64:128], True, False, [t_X7, t_Bc], [t_pZ])
                k.mm(pZ[hb:hb + 64, hp * 64:(hp + 1) * 64], Kc[:, hc], Vb[:, hc], False, True, [t_Kc, t_Vb], [t_pZ])
            k.tt("dve", RhT[:].rearrange("p a c -> p (a c)"), pR[:, 0:256], RTt[:].rearrange("p a c -> p (a c)"), ALU.add,
                 [t_pR, t_RTt], [t_RhT])
            k.cp("act", Ysb[:], pY[:, 0:256], [t_pY], [t_Y])
            k.cp("act", GpT[:].rearrange("p a c -> p (a c)"), pG[:, 0:128], [t_pG], [t_GpT])
            k.cp("act", Zsb[:].rearrange("p a c -> p (a c)"), pZ[:, 0:128], [t_pZ], [t_Z])
            Hbf, t_Hbf = Hbfs[tt_ % 2]
            Hbn, t_Hbn = Hbfs[1 - tt_ % 2]
            piE, t_piE = k.ps()
            piO, t_piO = k.ps()
            phE, t_phE = k.ps()
            phO, t_phO = k.ps()
            for h in range(4):
                hp, par = h // 2, h % 2
                hb = 64 * par
                pi, t_pi = (piE, t_piE) if par == 0 else (piO, t_piO)
                ph, t_ph = (phE, t_phE) if par == 0 else (phO, t_phO)
                k.mm(pi[:, hp * 64:(hp + 1) * 64], RhT[hb:hb + 64, hp, :], Hbf[hb:hb + 64, hp, :], True, True,
                     [t_RhT, t_Hbf], [t_pi])
                k.mm(ph[hb:hb + 64, hp * 64:(hp + 1) * 64], GpT[hb:hb + 64, hp, :], Hbf[hb:hb + 64, hp, :], True, True,
                     [t_GpT, t_Hbf], [t_ph])
            Yv = Ysb[:].rearrange("p (a b c) -> p a b c", a=2, b=2)
            k.tt("dve", Yv[:, :, 0, :], Yv[:, :, 0, :], piE[:, 0:128].rearrange("p (a c) -> p a c", a=2), ALU.add,
                 [t_Y, t_piE], [t_Y])
            k.tt("dve", Yv[:, :, 1, :], Yv[:, :, 1, :], piO[:, 0:128].rearrange("p (a c) -> p a c", a=2), ALU.add,
                 [t_Y, t_piO], [t_Y])
            for hp in range(2):
                k.stt(H32[:, hp, :], H32[:, hp, :], colv[:, hp, 1:2], Zsb[:, hp, :], ALU.mult, ALU.add, [t_H, t_cv, t_Z], [t_H])
            H2 = H32[:].rearrange("p a c -> p (a c)")
            k.tt("dve", H2[0:64, :], H2[0:64, :], phE[0:64, 0:128], ALU.add, [t_H, t_phE], [t_H])
            k.tt("dve", H2[64:128, :], H2[64:128, :], phO[64:128, 0:128], ALU.add, [t_H, t_phO], [t_H])
            k.cp("pool", Hbn[:], H32[:], [t_H], [t_Hbn])
            S.op("dve", lambda e: e.reduce_sum(out=st[:, 8:12], in_=Ysb[:].rearrange("p (a c) -> p a c", a=4), axis=AX.X),
                 reads=[t_Y], writes=[t_st])
            k.ts("dve", st[:, 8:12], st[:, 8:12], 1.0 / 64, None, ALU.mult, ALU.bypass, [t_st], [t_st])
            for h in range(4):
                k.ts("dve", Ysb[:, h * 64:(h + 1) * 64], Ysb[:, h * 64:(h + 1) * 64], st[:, 8 + h:9 + h], None, ALU.subtract,
                     ALU.bypass, [t_Y, t_st], [t_Y])
            k.tt("pool", tmp[:], Ysb[:], Ysb[:], ALU.mult, [t_Y], [t_tmp])
            S.op("dve", lambda e: e.reduce_sum(out=st[:, 12:16], in_=tmp[:].rearrange("p (a c) -> p a c", a=4), axis=AX.X),
                 reads=[t_tmp], writes=[t_st])
            k.ts("dve", st[:, 12:16], st[:, 12:16], 1.0 / 64, 64e-5, ALU.mult, ALU.add, [t_st], [t_st])
            k.act(st[:, 12:16], st[:, 12:16], AF.Sqrt, [t_st], [t_st])
            k.recip(st[:, 16:20], st[:, 12:16], [t_st], [t_st])
            for h in range(4):
                k.stt(Ysb[:, h * 64:(h + 1) * 64], Ysb[:, h * 64:(h + 1) * 64], st[:, 16 + h:17 + h], lng[:, h * 64:(h + 1) * 64],
                      ALU.mult, ALU.mult, [t_Y, t_st, t_bc], [t_Y])
            k.tt("pool", Ysb[:], Ysb[:], lnb[:], ALU.add, [t_Y, t_bc], [t_Y])
            k.tt("dve", tmp2[:], r32[:], kp[:], ALU.mult, [t_r, t_kp], [t_tmp2])
            k.tt("pool", tmp2[:], tmp2[:], rkb[:], ALU.mult, [t_tmp2, t_bc], [t_tmp2])
            S.op("dve", lambda e: e.reduce_sum(out=st[:, 20:24], in_=tmp2[:].rearrange("p (a c) -> p a c", a=4), axis=AX.X),
                 reads=[t_tmp2], writes=[t_st])
            for h in range(4):
                k.stt(Ysb[:, h * 64:(h + 1) * 64], v32[:, h * 64:(h + 1) * 64], st[:, 20 + h:21 + h], Ysb[:, h * 64:(h + 1) * 64],
                      ALU.mult, ALU.add, [t_Y, t_st, t_v], [t_Y])
            k.tt("dve", ybf[:], Ysb[:], g32[:], ALU.mult, [t_Y, t_g], [t_ybf])
            pt2, t_pt2 = k.psb()
            for hp in range(2):
                k.tp(pt2[:, hp * 128:(hp + 1) * 128], ybf[:, hp * 128:(hp + 1) * 128], [t_ybf], [t_pt2])
            gq, q = tt_ // 4, tt_ % 4
            yg, t_yg = ygs[gq % 2]
            k.cp("act", yg[:, :, q * 128:(q + 1) * 128], pt2[:, 0:256].rearrange("p (a c) -> p a c", a=2), [t_pt2], [t_yg])
            if q == 3:
                for hp in range(2):
                    S.dma("sp", ctx["yT"][1024 + hp * 128:1024 + (hp + 1) * 128, gq * 512:(gq + 1) * 512], yg[:, hp, :],
                          reads=[t_yg], writes=[ctx["t_yT"][8 + hp][gq]])
        S.barrier()


_NC_CACHE = {}


def kernel(**inputs):
    cfg = {}
    key = "main"
    if key not in _NC_CACHE:
        _NC_CACHE[key] = build(cfg)
    nc = _NC_CACHE[key]
    consts = host_consts()
    in_maps = []
    for c in range(8):
        m = {"x": np.ascontiguousarray(inputs["x"][2 * c:2 * c + 2]),
             "mem": np.ascontiguousarray(inputs["mem"][2 * c:2 * c + 2]),
             "positions": np.ascontiguousarray(inputs["positions"][2 * c:2 * c + 2]).astype(np.int32)}
        for n in WNAMES + VNAMES:
            m[n] = np.ascontiguousarray(inputs[n], dtype=np.float32)
        m.update(consts)
        in_maps.append(m)
    res = run_bass_kernel_spmd(nc, in_maps, core_ids=list(range(8)))
    return np.concatenate([r["out"] for r in res.results], axis=0).astype(np.float32)
```

```python
import contextlib
import math
import numpy as np
import concourse.bass as bass
import concourse.mybir as mybir
from concourse.bass_utils import run_bass_kernel_spmd

F32 = mybir.dt.float32
BF16 = mybir.dt.bfloat16
I32 = mybir.dt.int32
AF = mybir.ActivationFunctionType
ALU = mybir.AluOpType
AX = mybir.AxisListType

D = 1024
S_LEN = 4096
NT = S_LEN // 128
N_IN = 9984
OFF_A, OFF_B, OFF_C, OFF_D, OFF_G = 0, 2304, 3840, 4864, 5888
DFF = 2816
MEM = 256
EPS = 1e-5
HOFF = 8


class Tr:
    __slots__ = ("w", "r", "x")

    def __init__(self, excl=False):
        self.w = None
        self.r = {}
        self.x = excl


class Sched:
    COMPUTE = ("pe", "act", "dve", "pool")
    NDMASEM = 12

    def __init__(self, nc):
        self.nc = nc
        self.engobj = {"pe": nc.tensor, "act": nc.scalar, "dve": nc.vector, "pool": nc.gpsimd, "sp": nc.sync}
        self.prog = {k: [] for k in self.engobj}
        self.sems = {}
        self.cnt = {}
        self.seen = {k: {} for k in self.engobj}
        self._semctx = []
        for k in self.COMPUTE:
            self._mksem(k)
        self.dmasems = {}
        self.dmarr = {}
        for q in ("sp", "act", "pool"):
            self.dmasems[q] = [self._mksem(f"d_{q}_{i}") for i in range(self.NDMASEM)]
            self.dmarr[q] = 0
        self.ninst = 0

    def _mksem(self, key):
        ctx = self.nc.semaphore(key)
        h = ctx.__enter__()
        self._semctx.append(ctx)
        self.sems[key] = h
        self.cnt[key] = 0
        return key

    def _deps(self, stream, own_key, reads, writes):
        need = {}

        def add(kv):
            if kv is None:
                return
            k, v = kv
            if k == own_key and k == "pe":
                return
            if need.get(k, 0) < v:
                need[k] = v
        for t in reads:
            add(t.w)
            if t.x:
                for k, v in t.r.items():
                    if k != own_key:
                        add((k, v))
        for t in writes:
            add(t.w)
            for k, v in t.r.items():
                add((k, v))
        out = []
        seen = self.seen[stream]
        for k, v in need.items():
            if seen.get(k, 0) < v:
                seen[k] = v
                out.append((k, v))
        return out

    def _commit(self, key, val, reads, writes):
        for t in reads:
            if t.r.get(key, 0) < val:
                t.r[key] = val
        for t in writes:
            t.w = (key, val)
            t.r = {}

    def op(self, eng, fn, reads=(), writes=()):
        waits = self._deps(eng, eng, reads, writes)
        self.cnt[eng] += 1
        val = self.cnt[eng]
        self._commit(eng, val, reads, writes)
        sem = self.sems[eng]
        sems = self.sems

        def thunk(e, waits=waits, fn=fn, sem=sem):
            for k, v in waits:
                e.wait_ge(sems[k], v)
            fn(e).then_inc(sem, 1)
        self.prog[eng].append(thunk)
        self.ninst += 1

    def dma(self, q, out, in_, reads=(), writes=(), **kw):
        i = self.dmarr[q]
        self.dmarr[q] = (i + 1) % self.NDMASEM
        key = self.dmasems[q][i]
        waits = self._deps(q, key, reads, writes)
        prev = self.cnt[key]
        if prev > 0 and self.seen[q].get(key, 0) < prev:
            self.seen[q][key] = prev
            waits.append((key, prev))
        self.cnt[key] += 16
        val = self.cnt[key]
        self._commit(key, val, reads, writes)
        sem = self.sems[key]
        sems = self.sems

        def thunk(e, waits=waits, sem=sem, out=out, in_=in_, kw=kw):
            for k, v in waits:
                e.wait_ge(sems[k], v)
            e.dma_start(out=out, in_=in_, **kw).then_inc(sem, 16)
        self.prog[q].append(thunk)
        self.ninst += 1

    def barrier(self):
        snap = {k: v for k, v in self.cnt.items() if v > 0}
        sems = self.sems
        for stream in self.prog:
            seen = self.seen[stream]
            waits = []
            for k, v in snap.items():
                if k == stream:
                    continue
                if seen.get(k, 0) < v:
                    seen[k] = v
                    waits.append((k, v))
            if waits:
                def thunk(e, waits=waits):
                    for k, v in waits:
                        e.wait_ge(sems[k], v)
                self.prog[stream].append(thunk)

    def finish(self):
        nc = self.nc
        finals = [(k, v) for k, v in self.cnt.items() if v > 0]
        sems = self.sems
        prog = self.prog
        with nc.Block() as block:
            @block.tensor
            def _(e):
                for t in prog["pe"]:
                    t(e)

            @block.scalar
            def _(e):
                for t in prog["act"]:
                    t(e)

            @block.vector
            def _(e):
                for t in prog["dve"]:
                    t(e)

            @block.gpsimd
            def _(e):
                for t in prog["pool"]:
                    t(e)

            @block.sync
            def _(e):
                for t in prog["sp"]:
                    t(e)
                for k, v in finals:
                    e.wait_ge(sems[k], v)
        for ctx in reversed(self._semctx):
            ctx.__exit__(None, None, None)


WNAMES = ["w_in", "p_a", "p_b", "p_c", "p_d", "w_mix_out", "w_mem_q", "w_mem_kv", "w_mem_o", "w_ffn_in",
          "w_ffn_out", "rwkv_w2", "rwkv_a2", "rwkv_g2", "rwkv_v1", "rwkv_v2"]
VNAMES = ["mix_norm_g", "diff_lam", "diff_norm_g", "hgrn_lb_logits", "hgrn_norm_g", "rwkv_mu", "rwkv_w0", "rwkv_a0",
          "rwkv_k_k", "rwkv_k_a", "rwkv_r_k", "rwkv_lnx_g", "rwkv_lnx_b", "rwkv_v0", "mem_q_norm_g", "mem_kv_norm_g",
          "ffn_norm_g", "ffn_conv_w", "ffn_conv_b", "final_norm_g"]
SHAPES = {
    "mix_norm_g": (2, 1024), "w_in": (2, 1024, 9984), "diff_lam": (2, 4, 64), "diff_norm_g": (2, 128),
    "hgrn_lb_logits": (2, 256), "hgrn_norm_g": (2, 64), "rwkv_mu": (2, 1024), "rwkv_w0": (2, 256),
    "rwkv_w2": (2, 64, 256), "rwkv_a0": (2, 256), "rwkv_a2": (2, 64, 256), "rwkv_g2": (2, 128, 256),
    "rwkv_k_k": (2, 256), "rwkv_k_a": (2, 256), "rwkv_r_k": (2, 4, 64), "rwkv_lnx_g": (2, 256),
    "rwkv_lnx_b": (2, 256), "rwkv_v0": (1, 256), "rwkv_v1": (1, 256, 32), "rwkv_v2": (1, 32, 256),
    "p_a": (2, 256, 1024), "p_b": (2, 512, 1024), "p_c": (2, 256, 1024), "p_d": (2, 256, 1024),
    "w_mix_out": (2, 1024, 1024), "mem_q_norm_g": (2, 1024), "mem_kv_norm_g": (2, 1024),
    "w_mem_q": (2, 1024, 1024), "w_mem_kv": (2, 1024, 2048), "w_mem_o": (2, 1024, 1024),
    "ffn_norm_g": (2, 1024), "w_ffn_in": (2, 1024, 5632), "ffn_conv_w": (2, 3, 5632), "ffn_conv_b": (2, 5632),
    "w_ffn_out": (2, 2816, 1024), "final_norm_g": (1024,),
}


def host_consts():
    c = {}
    c["c_ident"] = np.eye(128, dtype=np.float32)
    s = np.arange(128)[:, None]
    t = np.arange(512)[None, :]
    c["c_maskD"] = (s <= t).astype(np.float32)
    dm = np.zeros((128, 256), np.float32)
    dm[:, :128] = (s <= np.arange(128)[None, :])
    dm[:, 128:] = (s >= np.arange(128)[None, :])
    c["c_maskA"] = np.concatenate([dm, dm], axis=1)
    tt = np.arange(128)[None, :]
    si = np.concatenate([(s < tt), (s <= tt)], axis=1).astype(np.float32)
    c["c_maskSI"] = si
    c["c_maskSI2"] = np.concatenate([si, si], axis=1)
    c["c_triR"] = (s > tt).astype(np.float32)
    c["c_maskLow"] = (tt < s).astype(np.float32)
    c["c_tri"] = (s <= tt).astype(np.float32)
    c["c_tri2"] = np.concatenate([c["c_tri"], c["c_tri"]], axis=1)
    c["c_triM"] = ((s <= tt).astype(np.float32) - (s <= 63).astype(np.float32))
    c["c_o63"] = np.broadcast_to((s <= 63), (128, 128)).astype(np.float32).copy()
    c["c_ones"] = np.ones((128, 128), np.float32)
    sel = np.zeros((128, 2), np.float32)
    sel[:64, 0] = 1.0
    sel[:, 1] = 1.0
    c["c_sel"] = sel
    pm = np.zeros((128, 128), np.float32)
    for hb in (0, 64):
        for i in range(8):
            pm[hb + i + 8, hb + i] = -1.0
            pm[hb + i, hb + i + 8] = 1.0
    c["c_pm"] = pm
    invf = np.zeros((128, 1), np.float32)
    f = (500000.0 ** (-np.arange(8, dtype=np.float32) / 8)).astype(np.float32)
    for hb in (0, 64):
        invf[hb:hb + 8, 0] = f
        invf[hb + 8:hb + 16, 0] = f
    c["c_invf"] = invf
    return c


class K:
    def __init__(self, cfg):
        self.cfg = cfg
        nc = bass.Bass("TRN2", target_bir_lowering=False)
        self.nc = nc
        self.S = Sched(nc)
        self.es = contextlib.ExitStack()
        self.uid = 0

    def sb(self, es, shape, dt, name=None):
        self.uid += 1
        return es.enter_context(self.nc.sbuf_tensor(f"{name or 't'}_{self.uid}", list(shape), dt))

    def dram(self, name, shape, dt, kind="Internal"):
        return self.nc.dram_tensor(name, list(shape), dt, kind=kind).ap()

    def mm(self, out, lhsT, rhs, start, stop, r, w):
        self.S.op("pe", lambda e: e.matmul(out, lhsT=lhsT, rhs=rhs, start=start, stop=stop), reads=r, writes=w)

    def tp(self, out, in_, r, w):
        idt = self.ident_b
        self.S.op("pe", lambda e: e.transpose(out, in_, idt), reads=list(r) + [self.t_const], writes=w)

    def act(self, out, in_, func, r, w, **kw):
        self.S.op("act", lambda e: e.activation(out=out, in_=in_, func=func, **kw), reads=r, writes=w)

    def tt(self, eng, out, in0, in1, op, r, w):
        self.S.op(eng, lambda e: e.tensor_tensor(out=out, in0=in0, in1=in1, op=op), reads=r, writes=w)

    def ts(self, eng, out, in0, s1, s2, op0, op1, r, w):
        self.S.op(eng, lambda e: e.tensor_scalar(out=out, in0=in0, scalar1=s1, scalar2=s2, op0=op0, op1=op1),
                  reads=r, writes=w)

    def stt(self, out, in0, scalar, in1, op0, op1, r, w):
        self.S.op("dve", lambda e: e.scalar_tensor_tensor(out=out, in0=in0, scalar=scalar, in1=in1, op0=op0, op1=op1),
                  reads=r, writes=w)

    def cp(self, eng, out, in_, r, w):
        if eng == "act":
            self.S.op("act", lambda e: e.copy(out=out, in_=in_), reads=r, writes=w)
        else:
            self.S.op(eng, lambda e: e.tensor_copy(out=out, in_=in_), reads=r, writes=w)

    def recip(self, out, in_, r, w):
        self.S.op("dve", lambda e: e.reciprocal(out=out, in_=in_), reads=r, writes=w)

    def memset(self, ap, val, w):
        self.S.op("dve", lambda e: e.memset(ap, val), reads=(), writes=w)

    def ps(self):
        i = self.ps_i
        self.ps_i = (i + 1) % len(self.ps_f)
        return self.ps_f[i], self.ps_ft[i]

    def psb(self):
        i = self.psb_i
        self.psb_i = (i + 1) % len(self.ps_b)
        return self.ps_b[i], self.ps_bt[i]


def build(cfg):
    k = K(cfg)
    nc, S = k.nc, k.S
    NS = cfg.get("nseq", 2)
    NL = cfg.get("layers", 2)
    dbg = cfg.get("debug")
    x_in = k.dram("x", [NS, S_LEN, D], F32, "ExternalInput")
    mem_in = k.dram("mem", [NS, MEM, D], F32, "ExternalInput")
    pos_in = k.dram("positions", [NS, S_LEN], I32, "ExternalInput")
    wf = {n: k.dram(n, SHAPES[n], F32, "ExternalInput") for n in WNAMES}
    vf = {n: k.dram(n, SHAPES[n], F32, "ExternalInput") for n in VNAMES}
    consts = host_consts()
    cf = {n: k.dram(n, v.shape, F32, "ExternalInput") for n, v in consts.items()}
    out = k.dram("out", [NS, S_LEN, D], F32, "ExternalOutput")
    xbuf = k.dram("xbuf", [NS, S_LEN, D], F32)
    yT = k.dram("yT", [1280, S_LEN], BF16)
    vfirst = k.dram("vfirst", [NS, S_LEN, 256], F32)
    wb = {n: k.dram(n + "_bf", SHAPES[n], BF16) for n in WNAMES}
    yin = None
    if cfg.get("yin"):
        yin = k.dram("yin", [1280, S_LEN], F32, "ExternalInput")
    dbg_out = None
    if dbg:
        dbg_out = k.dram("dbg", dbg["shape"], F32, "ExternalOutput")

    t_x = [[Tr() for _ in range(NT)] for _ in range(NS)]
    t_yT = [[Tr() for _ in range(8)] for _ in range(10)]
    t_vf = [[Tr() for _ in range(NT)] for _ in range(NS)]
    t_w = {}

    with contextlib.ExitStack() as g:
        k.ps_f, k.ps_ft, k.ps_b, k.ps_bt = [], [], [], []
        for i in range(6):
            k.ps_f.append(g.enter_context(nc.psum_tensor(f"psf{i}", [128, 512], F32)))
            k.ps_ft.append(Tr(True))
        for i in range(2):
            k.ps_b.append(g.enter_context(nc.psum_tensor(f"psb{i}", [128, 1024], BF16)))
            k.ps_bt.append(Tr(True))
        k.ps_i = 0
        k.psb_i = 0
        k.t_const = Tr()
        ident_b = k.sb(g, [128, 128], BF16, "ident")
        k.ident_b = ident_b[:]
        S.dma("pool", ident_b[:], cf["c_ident"], writes=[k.t_const])
        cs = {}
        for n, dt in [("c_maskD", BF16), ("c_maskA", BF16), ("c_maskSI", F32), ("c_maskSI2", F32), ("c_triR", F32), ("c_maskLow", F32), ("c_tri", F32), ("c_tri2", F32),
                      ("c_triM", F32), ("c_o63", F32), ("c_ones", F32), ("c_sel", F32), ("c_pm", BF16),
                      ("c_invf", F32)]:
            t = k.sb(g, consts[n].shape, dt, n)
            S.dma("pool" if dt == BF16 else "sp", t[:], cf[n], writes=[k.t_const])
            cs[n] = t
        ones_b = k.sb(g, [128, 128], BF16, "ones_b")
        S.dma("pool", ones_b[:], cf["c_ones"], writes=[k.t_const])
        cs["ones_b"] = ones_b
        k.cs = cs
        for n in WNAMES:
            tot = int(np.prod(SHAPES[n]))
            rows = tot // 2048
            src = wf[n].flatten().rearrange("(r c) -> r c", c=2048) if len(SHAPES[n]) > 1 else None
            nd = len(SHAPES[n])
            letters = "abc"[:nd]
            flat_s = wf[n].rearrange(f"{' '.join(letters)} -> ({' '.join(letters)})").rearrange("(r c) -> r c", c=2048)
            flat_d = wb[n].rearrange(f"{' '.join(letters)} -> ({' '.join(letters)})").rearrange("(r c) -> r c", c=2048)
            t_w[n] = []
            for r0 in range(0, rows, 512):
                r1 = min(rows, r0 + 512)
                tr_ = Tr()
                S.dma("pool", flat_d[r0:r1, :], flat_s[r0:r1, :], writes=[tr_])
                t_w[n].append(tr_)
        k.wb, k.t_w, k.vf = wb, t_w, vf

        ctx = dict(k=k, x_in=x_in, mem_in=mem_in, pos_in=pos_in, out=out, xbuf=xbuf, yT=yT, vfirst=vfirst,
                   t_x=t_x, t_yT=t_yT, t_vf=t_vf, yin=yin, dbg=dbg, dbg_out=dbg_out, cfg=cfg)
        phases = cfg.get("phases", "ABCDGMF")
        if yin is None and not any(c in phases for c in "ABCD"):
            with contextlib.ExitStack() as zx:
                zt = k.sb(zx, [128, 512], BF16, "zt")
                t_z = Tr()
                k.memset(zt[:], 0.0, [t_z])
                for rc in range(10):
                    for gq in range(8):
                        S.dma("sp", yT[rc * 128:(rc + 1) * 128, gq * 512:(gq + 1) * 512], zt[:], reads=[t_z],
                              writes=[t_yT[rc][gq]])
                S.barrier()
        for s in range(NS):
            for l in range(NL):
                xsrc = x_in if l == 0 else xbuf
                with contextlib.ExitStack() as mx:
                    hT = k.sb(mx, [128, 8, HOFF + S_LEN], BF16, "hT")
                    t_hT = [Tr() for _ in range(NT)]
                    t_h0 = Tr()
                    k.memset(hT[:, :, 0:HOFF], 0.0, [t_h0])
                    phase_norm_T(ctx, s, l, xsrc, hT, t_hT)
                    if yin is not None:
                        phase_yin(ctx)
                    else:
                        with contextlib.ExitStack() as rx:
                            tabs = phase_rope(ctx, rx, s) if ("A" in phases or "B" in phases) else None
                            if "A" in phases:
                                phase_A(ctx, s, l, hT, t_hT, tabs)
                            if "B" in phases:
                                phase_B(ctx, s, l, hT, t_hT, tabs)
                        if "C" in phases:
                            phase_C(ctx, s, l, hT, t_hT)
                        if "D" in phases:
                            phase_D(ctx, s, l, hT, t_hT, t_h0)
                    if dbg and dbg.get("what") == "yT" and dbg.get("l", 0) == l and s == 0:
                        for rc in dbg.get("rcs", range(10)):
                            for gq in range(8):
                                S.dma("pool", dbg_out[rc * 128:(rc + 1) * 128, gq * 512:(gq + 1) * 512],
                                      yT[rc * 128:(rc + 1) * 128, gq * 512:(gq + 1) * 512], reads=[t_yT[rc][gq]])
                    if "G" in phases:
                        phase_G(ctx, s, l, xsrc, hT, t_hT)
                if "M" in phases:
                    phase_M(ctx, s, l)
                if "F" in phases:
                    phase_F(ctx, s, l)
            if cfg.get("final", True):
                phase_final(ctx, s)
        S.finish()
    return nc


def load_bc(k, es, src_row_ap, n, name="bc"):
    t = k.sb(es, [128, n], F32, name)
    tr_ = Tr()
    k.S.dma("sp", t[:], src_row_ap.partition_broadcast(128), writes=[tr_])
    return t, tr_


def rms_to_bf(k, xt, t_xt, gbc, t_g, obf, t_o, ss, t_ss):
    k.act(obf[:], xt[:], AF.Square, [t_xt], [t_o, t_ss], accum_out=ss[:, 0:1])
    k.ts("dve", ss[:, 1:2], ss[:, 0:1], 1.0 / D, EPS, ALU.mult, ALU.add, [t_ss], [t_ss])
    k.act(ss[:, 2:3], ss[:, 1:2], AF.Sqrt, [t_ss], [t_ss])
    k.recip(ss[:, 3:4], ss[:, 2:3], [t_ss], [t_ss])
    k.stt(obf[:], xt[:], ss[:, 3:4], gbc[:], ALU.mult, ALU.mult, [t_xt, t_ss, t_g], [t_o])


def phase_norm_T(ctx, s, l, xsrc, hT, t_hT):
    k = ctx["k"]
    S = k.S
    with contextlib.ExitStack() as es:
        gbc, t_g = load_bc(k, es, k.vf["mix_norm_g"][l], D, "gmix")
        xts = [(k.sb(es, [128, D], F32, "xt"), Tr()) for _ in range(2)]
        obs = [(k.sb(es, [128, D], BF16, "ob"), Tr()) for _ in range(2)]
        sss = [(k.sb(es, [128, 4], F32, "ss"), Tr()) for _ in range(2)]
        for tt_ in range(NT):
            xt, t_xt = xts[tt_ % 2]
            ob, t_o = obs[tt_ % 2]
            ss, t_ss = sss[tt_ % 2]
            rd = [ctx["t_x"][s][tt_]] if l > 0 else []
            S.dma("sp", xt[:], xsrc[s, tt_ * 128:(tt_ + 1) * 128, :], reads=rd, writes=[t_xt])
            rms_to_bf(k, xt, t_xt, gbc, t_g, ob, t_o, ss, t_ss)
            pb, t_pb = k.psb()
            for j in range(8):
                k.tp(pb[:, j * 128:(j + 1) * 128], ob[:, j * 128:(j + 1) * 128], [t_o], [t_pb])
            k.cp("act" if tt_ % 2 else "dve", hT[:, :, HOFF + tt_ * 128:HOFF + (tt_ + 1) * 128],
                 pb[:].rearrange("p (j t) -> p j t", j=8), [t_pb], [t_hT[tt_]])
        S.barrier()


def phase_yin(ctx):
    k = ctx["k"]
    for rc in range(10):
        for gq in range(8):
            k.S.dma("pool", ctx["yT"][rc * 128:(rc + 1) * 128, gq * 512:(gq + 1) * 512],
                    ctx["yin"][rc * 128:(rc + 1) * 128, gq * 512:(gq + 1) * 512], writes=[ctx["t_yT"][rc][gq]])


def load_w(k, es, name, l, r0, nr, c0, ncols, tag="w"):
    kc = max(1, nr // 128)
    p = min(128, nr)
    t = k.sb(es, [p, kc, ncols], BF16, tag)
    tr_ = Tr()
    src = k.wb[name][l, r0:r0 + nr, c0:c0 + ncols].rearrange("(kc p) n -> p kc n", p=p)
    k.S.dma("sp", t[:], src, reads=k.t_w[name], writes=[tr_])
    return t, tr_


def load_w_into(k, t, tr_, name, l, r0, nr, c0, ncols):
    p = min(128, nr)
    src = k.wb[name][l, r0:r0 + nr, c0:c0 + ncols].rearrange("(kc p) n -> p kc n", p=p)
    k.S.dma("sp", t[:], src, reads=k.t_w[name], writes=[tr_])


def phase_G(ctx, s, l, xsrc, hT, t_hT):
    k = ctx["k"]
    S = k.S
    yT, t_yT = ctx["yT"], ctx["t_yT"]
    with contextlib.ExitStack() as es:
        pcat = k.sb(es, [128, 10, D], BF16, "pcat")
        t_p = Tr()
        for nm, c0, n in (("p_a", 0, 2), ("p_b", 2, 4), ("p_c", 6, 2), ("p_d", 8, 2)):
            S.dma("sp", pcat[:, c0:c0 + n, :], k.wb[nm][l].rearrange("(kc p) n -> p kc n", p=128),
                  reads=k.t_w[nm], writes=[t_p])
        wmo, t_wmo = load_w(k, es, "w_mix_out", l, 0, D, 0, D, "wmo")
        wgs = [(k.sb(es, [128, 8, D], BF16, "wg"), Tr()) for _ in range(2)]
        yts = [(k.sb(es, [128, 10, 512], BF16, "yt"), Tr()) for _ in range(2)]
        mT = k.sb(es, [128, 8, 512], BF16, "mT")
        t_mT = Tr()
        acc = k.sb(es, [128, 8, 512], F32, "acc")
        t_accs = [Tr() for _ in range(8)]
        sigs = [(k.sb(es, [128, 512], F32, "sig"), Tr()) for _ in range(3)]
        xts = [(k.sb(es, [128, D], F32, "xg"), Tr()) for _ in range(2)]
        branches = ((0, 2), (2, 4), (6, 2), (8, 2))
        it = 0
        for gq in range(8):
            yt, t_yt = yts[gq % 2]
            for rc in range(10):
                S.dma("sp", yt[:, rc, :], yT[rc * 128:(rc + 1) * 128, gq * 512:(gq + 1) * 512],
                      reads=[t_yT[rc][gq]], writes=[t_yt])
            hsl = slice(HOFF + gq * 512, HOFF + (gq + 1) * 512)
            rh = t_hT[gq * 4:(gq + 1) * 4]
            for bi, (c0, n) in enumerate(branches):
                wg, t_wg = wgs[it % 2]
                it += 1
                cw0 = OFF_G + bi * D
                S.dma("sp", wg[:], k.wb["w_in"][l, :, cw0:cw0 + D].rearrange("(kc p) n -> p kc n", p=128),
                      reads=k.t_w["w_in"], writes=[t_wg])
                for oc in range(8):
                    t_acc = t_accs[oc]
                    pg, t_pg = k.ps()
                    for kc in range(8):
                        k.mm(pg[:], wg[:, kc, oc * 128:(oc + 1) * 128], hT[:, kc, hsl], kc == 0, kc == 7, [t_wg] + rh, [t_pg])
                    sg, t_sg = sigs[(bi * 8 + oc) % 3]
                    k.act(sg[:], pg[:], AF.Sigmoid, [t_pg], [t_sg])
                    py, t_py = k.ps()
                    for j in range(n):
                        k.mm(py[:], pcat[:, c0 + j, oc * 128:(oc + 1) * 128], yt[:, c0 + j, :], j == 0, j == n - 1,
                             [t_p, t_yt], [t_py])
                    if bi == 0:
                        k.tt("dve", acc[:, oc, :], py[:], sg[:], ALU.mult, [t_py, t_sg], [t_acc])
                    else:
                        k.tt("dve", sg[:], py[:], sg[:], ALU.mult, [t_py, t_sg], [t_sg])
                        if bi < 3:
                            k.tt("pool", acc[:, oc, :], acc[:, oc, :], sg[:], ALU.add, [t_acc, t_sg], [t_acc])
                        else:
                            k.tt("pool", mT[:, oc, :], acc[:, oc, :], sg[:], ALU.add, [t_acc, t_sg], [t_mT])
            for q in range(4):
                tt_ = gq * 4 + q
                xt, t_xt = xts[q % 2]
                rd = [ctx["t_x"][s][tt_]] if l > 0 else []
                S.dma("sp", xt[:], xsrc[s, tt_ * 128:(tt_ + 1) * 128, :], reads=rd, writes=[t_xt])
                for cg in range(2):
                    po, t_po = k.ps()
                    for oc in range(8):
                        k.mm(po[:], mT[:, oc, q * 128:(q + 1) * 128], wmo[:, oc, cg * 512:(cg + 1) * 512], oc == 0,
                             oc == 7, [t_mT, t_wmo], [t_po])
                    k.tt("dve", xt[:, cg * 512:(cg + 1) * 512], xt[:, cg * 512:(cg + 1) * 512], po[:], ALU.add,
                         [t_xt, t_po], [t_xt])
                S.dma("sp", ctx["xbuf"][s, tt_ * 128:(tt_ + 1) * 128, :], xt[:], reads=[t_xt],
                      writes=[ctx["t_x"][s][tt_]])
        S.barrier()


def phase_M(ctx, s, l):
    k = ctx["k"]
    S = k.S
    t_x = ctx["t_x"][s]
    xbuf = ctx["xbuf"]
    sc = 256 ** -0.5
    with contextlib.ExitStack() as es:
        kmT = k.sb(es, [128, 8, MEM], BF16, "kmT")
        vm = k.sb(es, [128, 2, D], BF16, "vm")
        t_km, t_vm = Tr(), Tr()
        with contextlib.ExitStack() as e2:
            gkv, t_gkv = load_bc(k, e2, k.vf["mem_kv_norm_g"][l], D, "gkv")
            mnT = k.sb(e2, [128, 8, MEM], BF16, "mnT")
            t_mn = Tr()
            xt = k.sb(e2, [128, D], F32, "mx")
            ob = k.sb(e2, [128, D], BF16, "mob")
            ss = k.sb(e2, [128, 4], F32, "mss")
            t_xt, t_o, t_ss = Tr(), Tr(), Tr()
            for mt in range(2):
                S.dma("sp", xt[:], ctx["mem_in"][s, mt * 128:(mt + 1) * 128, :], writes=[t_xt])
                rms_to_bf(k, xt, t_xt, gkv, t_gkv, ob, t_o, ss, t_ss)
                pb, t_pb = k.psb()
                for j in range(8):
                    k.tp(pb[:, j * 128:(j + 1) * 128], ob[:, j * 128:(j + 1) * 128], [t_o], [t_pb])
                k.cp("dve", mnT[:, :, mt * 128:(mt + 1) * 128], pb[:].rearrange("p (j t) -> p j t", j=8), [t_pb], [t_mn])
            for half in range(4):
                wkv, t_wkv = load_w(k, e2, "w_mem_kv", l, 0, D, half * 512, 512, "wkv")
                if half < 2:
                    for fc in range(4):
                        p_, t_p = k.ps()
                        for kc in range(8):
                            k.mm(p_[:, 0:MEM], wkv[:, kc, fc * 128:(fc + 1) * 128], mnT[:, kc, :], kc == 0, kc == 7,
                                 [t_wkv, t_mn], [t_p])
                        k.cp("act", kmT[:, half * 4 + fc, :], p_[:, 0:MEM], [t_p], [t_km])
                else:
                    for mt in range(2):
                        p_, t_p = k.ps()
                        for kc in range(8):
                            k.mm(p_[:], mnT[:, kc, mt * 128:(mt + 1) * 128], wkv[:, kc, :], kc == 0, kc == 7,
                                 [t_wkv, t_mn], [t_p])
                        k.cp("act", vm[:, mt, (half - 2) * 512:(half - 1) * 512], p_[:], [t_p], [t_vm])
            S.barrier()
        gq_, t_gq = load_bc(k, es, k.vf["mem_q_norm_g"][l], D, "gq")
        wq, t_wq = load_w(k, es, "w_mem_q", l, 0, D, 0, D, "wq")
        wo, t_wo = load_w(k, es, "w_mem_o", l, 0, D, 0, D, "wo")
        xts = [(k.sb(es, [128, D], F32, "xq"), Tr()) for _ in range(4)]
        ob = k.sb(es, [128, D], BF16, "qob")
        ss = k.sb(es, [128, 4], F32, "qss")
        t_o, t_ss = Tr(), Tr()
        xnT = k.sb(es, [128, 8, 512], BF16, "xnT")
        t_xn = Tr()
        qT = k.sb(es, [128, 8, 512], BF16, "qT")
        t_qT = Tr()
        ETs = [(k.sb(es, [128, 2, 512], BF16, "ET"), Tr()) for _ in range(2)]
        oT = k.sb(es, [128, 8, 512], BF16, "oT")
        t_oT = Tr()
        rden = k.sb(es, [128, 512], F32, "rden")
        t_rd = Tr()
        for gq in range(8):
            for q in range(4):
                tt_ = gq * 4 + q
                xt, t_xt = xts[q]
                S.dma("sp", xt[:], xbuf[s, tt_ * 128:(tt_ + 1) * 128, :], reads=[t_x[tt_]], writes=[t_xt])
                rms_to_bf(k, xt, t_xt, gq_, t_gq, ob, t_o, ss, t_ss)
                pb, t_pb = k.psb()
                for j in range(8):
                    k.tp(pb[:, j * 128:(j + 1) * 128], ob[:, j * 128:(j + 1) * 128], [t_o], [t_pb])
                k.cp("dve", xnT[:, :, q * 128:(q + 1) * 128], pb[:].rearrange("p (j t) -> p j t", j=8), [t_pb], [t_xn])
            for fc in range(8):
                p_, t_p = k.ps()
                for kc in range(8):
                    k.mm(p_[:], wq[:, kc, fc * 128:(fc + 1) * 128], xnT[:, kc, :], kc == 0, kc == 7, [t_wq, t_xn], [t_p])
                k.cp("act", qT[:, fc, :], p_[:], [t_p], [t_qT])
            for h in range(4):
                ET, t_ET = ETs[h % 2]
                for mt in range(2):
                    p_, t_p = k.ps()
                    for j in range(2):
                        k.mm(p_[:], kmT[:, h * 2 + j, mt * 128:(mt + 1) * 128], qT[:, h * 2 + j, :], j == 0, j == 1,
                             [t_km, t_qT], [t_p])
                    k.act(ET[:, mt, :], p_[:], AF.Exp, [t_p], [t_ET], scale=sc)
                pd, t_pd = k.ps()
                for mt in range(2):
                    k.mm(pd[:], k.cs["ones_b"][:], ET[:, mt, :], mt == 0, mt == 1, [t_ET, k.t_const], [t_pd])
                k.recip(rden[:], pd[:], [t_pd], [t_rd])
                for j in range(2):
                    p_, t_p = k.ps()
                    for mt in range(2):
                        k.mm(p_[:], vm[:, mt, (h * 2 + j) * 128:(h * 2 + j + 1) * 128], ET[:, mt, :], mt == 0, mt == 1,
                             [t_vm, t_ET], [t_p])
                    k.tt("dve", oT[:, h * 2 + j, :], p_[:], rden[:], ALU.mult, [t_p, t_rd], [t_oT])
            for q in range(4):
                tt_ = gq * 4 + q
                xt, t_xt = xts[q]
                for cg in range(2):
                    po, t_po = k.ps()
                    for oc in range(8):
                        k.mm(po[:], oT[:, oc, q * 128:(q + 1) * 128], wo[:, oc, cg * 512:(cg + 1) * 512], oc == 0,
                             oc == 7, [t_oT, t_wo], [t_po])
                    k.tt("dve", xt[:, cg * 512:(cg + 1) * 512], xt[:, cg * 512:(cg + 1) * 512], po[:], ALU.add,
                         [t_xt, t_po], [t_xt])
                S.dma("sp", xbuf[s, tt_ * 128:(tt_ + 1) * 128, :], xt[:], reads=[t_xt], writes=[t_x[tt_]])
        S.barrier()


def phase_F(ctx, s, l):
    k = ctx["k"]
    S = k.S
    t_x = ctx["t_x"][s]
    xbuf = ctx["xbuf"]
    NC = 2 * DFF // 128
    with contextlib.ExitStack() as es:
        gf, t_gf = load_bc(k, es, k.vf["ffn_norm_g"][l], D, "gf")
        w1, t_w1 = load_w(k, es, "w_ffn_in", l, 0, D, 0, 2 * DFF, "wf1")
        w2, t_w2 = load_w(k, es, "w_ffn_out", l, 0, DFF, 0, D, "wf2")
        cw = k.sb(es, [128, 4, NC], F32, "cw")
        t_cw = Tr()
        for j in range(3):
            S.dma("sp", cw[:, j, :], k.vf["ffn_conv_w"][l, j].rearrange("(c p) -> p c", p=128), writes=[t_cw],
                  allow_slow_non_contiguous=True)
        S.dma("sp", cw[:, 3, :], k.vf["ffn_conv_b"][l].rearrange("(c p) -> p c", p=128), writes=[t_cw],
              allow_slow_non_contiguous=True)
        halo = k.sb(es, [128, NC, 2], F32, "halo")
        t_halo = [Tr() for _ in range(NC)]
        k.memset(halo[:], 0.0, t_halo)
        xts = [(k.sb(es, [128, D], F32, "xf"), Tr()) for _ in range(4)]
        ob = k.sb(es, [128, D], BF16, "fob")
        ss = k.sb(es, [128, 4], F32, "fss")
        t_o, t_ss = Tr(), Tr()
        xnT = k.sb(es, [128, 8, 512], BF16, "fxnT")
        t_xn = Tr()
        ues = [(k.sb(es, [128, 514], F32, "ue"), Tr()) for _ in range(2)]
        c0s = [(k.sb(es, [128, 512], F32, "c0"), Tr()) for _ in range(2)]
        sgs = [(k.sb(es, [128, 512], F32, "sgf"), Tr()) for _ in range(2)]
        aT = k.sb(es, [128, DFF // 128, 512], BF16, "aT")
        t_aT = Tr()
        for gq in range(8):
            for q in range(4):
                tt_ = gq * 4 + q
                xt, t_xt = xts[q]
                S.dma("sp", xt[:], xbuf[s, tt_ * 128:(tt_ + 1) * 128, :], reads=[t_x[tt_]], writes=[t_xt])
                rms_to_bf(k, xt, t_xt, gf, t_gf, ob, t_o, ss, t_ss)
                pb, t_pb = k.psb()
                for j in range(8):
                    k.tp(pb[:, j * 128:(j + 1) * 128], ob[:, j * 128:(j + 1) * 128], [t_o], [t_pb])
                k.cp("dve", xnT[:, :, q * 128:(q + 1) * 128], pb[:].rearrange("p (j t) -> p j t", j=8), [t_pb], [t_xn])
            it = 0
            for j in range(DFF // 128):
                res = []
                for c in (j, j + 22):
                    p_, t_p = k.ps()
                    for kc in range(8):
                        k.mm(p_[:], w1[:, kc, c * 128:(c + 1) * 128], xnT[:, kc, :], kc == 0, kc == 7, [t_w1, t_xn], [t_p])
                    ue, t_ue = ues[it % 2]
                    c0, t_c0 = c0s[it % 2]
                    it += 1
                    k.cp("pool", ue[:, 0:2], halo[:, c, :], [t_halo[c]], [t_ue])
                    k.cp("act", ue[:, 2:514], p_[:], [t_p], [t_ue])
                    k.act(c0[:], p_[:], AF.Identity, [t_p, t_cw], [t_c0], scale=cw[:, 2, c:c + 1], bias=cw[:, 3, c:c + 1])
                    k.cp("pool", halo[:, c, :], ue[:, 512:514], [t_ue], [t_halo[c]])
                    k.stt(c0[:], ue[:, 1:513], cw[:, 1, c:c + 1], c0[:], ALU.mult, ALU.add, [t_ue, t_cw, t_c0], [t_c0])
                    k.stt(c0[:], ue[:, 0:512], cw[:, 0, c:c + 1], c0[:], ALU.mult, ALU.add, [t_ue, t_cw, t_c0], [t_c0])
                    res.append((c0, t_c0))
                sg, t_sg = sgs[j % 2]
                k.act(sg[:], res[0][0][:], AF.Silu, [res[0][1]], [t_sg])
                k.tt("pool", aT[:, j, :], sg[:], res[1][0][:], ALU.mult, [t_sg, res[1][1]], [t_aT])
            for q in range(4):
                tt_ = gq * 4 + q
                xt, t_xt = xts[q]
                for cg in range(2):
                    po, t_po = k.ps()
                    for j in range(DFF // 128):
                        k.mm(po[:], aT[:, j, q * 128:(q + 1) * 128], w2[:, j, cg * 512:(cg + 1) * 512], j == 0,
                             j == DFF // 128 - 1, [t_aT, t_w2], [t_po])
                    k.tt("dve", xt[:, cg * 512:(cg + 1) * 512], xt[:, cg * 512:(cg + 1) * 512], po[:], ALU.add,
                         [t_xt, t_po], [t_xt])
                S.dma("sp", xbuf[s, tt_ * 128:(tt_ + 1) * 128, :], xt[:], reads=[t_xt], writes=[t_x[tt_]])
        S.barrier()


def phase_final(ctx, s):
    k = ctx["k"]
    S = k.S
    with contextlib.ExitStack() as es:
        gbc, t_g = load_bc(k, es, k.vf["final_norm_g"], D, "gfin")
        xts = [(k.sb(es, [128, D], F32, "xo"), Tr()) for _ in range(2)]
        junk = k.sb(es, [128, D], BF16, "junk")
        t_j = Tr()
        sss = [(k.sb(es, [128, 4], F32, "oss"), Tr()) for _ in range(2)]
        for tt_ in range(NT):
            xt, t_xt = xts[tt_ % 2]
            ss, t_ss = sss[tt_ % 2]
            S.dma("sp", xt[:], ctx["xbuf"][s, tt_ * 128:(tt_ + 1) * 128, :], reads=[ctx["t_x"][s][tt_]], writes=[t_xt])
            k.act(junk[:], xt[:], AF.Square, [t_xt], [t_j, t_ss], accum_out=ss[:, 0:1])
            k.ts("dve", ss[:, 1:2], ss[:, 0:1], 1.0 / D, EPS, ALU.mult, ALU.add, [t_ss], [t_ss])
            k.act(ss[:, 2:3], ss[:, 1:2], AF.Sqrt, [t_ss], [t_ss])
            k.recip(ss[:, 3:4], ss[:, 2:3], [t_ss], [t_ss])
            k.stt(xt[:], xt[:], ss[:, 3:4], gbc[:], ALU.mult, ALU.mult, [t_xt, t_ss, t_g], [t_xt])
            S.dma("sp", ctx["out"][s, tt_ * 128:(tt_ + 1) * 128, :], xt[:], reads=[t_xt])
        S.barrier()


def sl(st, n, step):
    return slice(st, st + step * (n - 1) + 1, step)


def phase_rope(ctx, es, s):
    k = ctx["k"]
    S = k.S
    cosT = k.sb(es, [128, S_LEN], F32, "cosT")
    sinT = k.sb(es, [128, S_LEN], F32, "sinT")
    t_tab = Tr()
    CW = 1024
    with contextlib.ExitStack() as e2:
        posi = k.sb(e2, [128, CW], I32, "posi")
        tq = k.sb(e2, [128, CW], F32, "tq")
        ki = k.sb(e2, [128, CW], I32, "ki")
        kf = k.sb(e2, [128, CW], F32, "kf")
        fr = k.sb(e2, [128, CW], F32, "fr")
        aa = k.sb(e2, [128, CW], F32, "aa")
        t_ = Tr()
        invf = k.cs["c_invf"]
        for c in range(S_LEN // CW):
            cols = slice(c * CW, (c + 1) * CW)
            S.dma("sp", posi[:], ctx["pos_in"][s, cols].partition_broadcast(128), writes=[t_])
            k.cp("dve", tq[:], posi[:], [t_], [t_])
            k.ts("dve", tq[:], tq[:], invf[:, 0:1], 1.0 / (2 * math.pi), ALU.mult, ALU.mult, [t_, k.t_const], [t_])
            k.cp("dve", ki[:], tq[:], [t_], [t_])
            k.cp("dve", kf[:], ki[:], [t_], [t_])
            k.tt("dve", fr[:], tq[:], kf[:], ALU.subtract, [t_], [t_])
            for dst, shift in ((sinT, 0.0), (cosT, 0.25)):
                if shift:
                    k.ts("dve", fr[:], fr[:], shift, None, ALU.add, ALU.bypass, [t_], [t_])
                k.ts("dve", aa[:], fr[:], 0.5, None, ALU.is_gt, ALU.bypass, [t_], [t_])
                k.tt("dve", fr[:], fr[:], aa[:], ALU.subtract, [t_], [t_])
                k.ts("dve", aa[:], fr[:], -0.5, None, ALU.is_lt, ALU.bypass, [t_], [t_])
                k.tt("dve", fr[:], fr[:], aa[:], ALU.add, [t_], [t_])
                k.act(dst[:, cols], fr[:], AF.Sin, [t_], [t_tab], scale=6.283185)
        S.barrier()
    return cosT, sinT, t_tab


def proj_rot(k, es_bufs, hT, t_hT, w, t_w, tabs, dst, t_dst):
    cosT, sinT, t_tab = tabs
    qraw, t_qr, t1, t_t1, t2, t_t2 = es_bufs
    pm = k.cs["c_pm"]
    for gq in range(8):
        cols = slice(gq * 512, (gq + 1) * 512)
        hs = slice(HOFF + gq * 512, HOFF + (gq + 1) * 512)
        p_, t_p = k.ps()
        for kc in range(8):
            k.mm(p_[:], w[:, kc, :], hT[:, kc, hs], kc == 0, kc == 7, [t_w] + t_hT[gq * 4:(gq + 1) * 4], [t_p])
        k.cp("act", qraw[:], p_[:], [t_p], [t_qr])
        p2, t_p2 = k.ps()
        k.mm(p2[:], pm[:], qraw[:], True, True, [t_qr, k.t_const], [t_p2])
        k.tt("dve", t1[:], p_[:], cosT[:, cols], ALU.mult, [t_p, t_tab], [t_t1])
        k.tt("dve", t2[:], p2[:], sinT[:, cols], ALU.mult, [t_p2, t_tab], [t_t2])
        k.tt("pool", dst[:, cols], t1[:], t2[:], ALU.add, [t_t1, t_t2], [t_dst])


def rot_bufs(k, es):
    return (k.sb(es, [128, 512], BF16, "qraw"), Tr(), k.sb(es, [128, 512], F32, "rt1"), Tr(),
            k.sb(es, [128, 512], F32, "rt2"), Tr())


def phase_A(ctx, s, l, hT, t_hT, tabs):
    k = ctx["k"]
    S = k.S
    maskA = k.cs["c_maskA"]
    ones_b = k.cs["ones_b"]
    with contextlib.ExitStack() as es:
        rb = rot_bufs(k, es)
        qT = k.sb(es, [128, S_LEN], BF16, "aqT")
        kT = k.sb(es, [128, S_LEN], BF16, "akT")
        vs = k.sb(es, [128, 32, 128], BF16, "avs")
        acc = k.sb(es, [128, 2, S_LEN], F32, "aacc")
        yb = k.sb(es, [128, S_LEN], BF16, "ayb")
        t_q, t_k, t_v, t_acc, t_yb = Tr(), Tr(), Tr(), Tr(), Tr()
        Es = [(k.sb(es, [128, 2, 256], BF16, "aE"), Tr()) for _ in range(3)]
        ws = [(k.sb(es, [128, 8, 128], BF16, "aw"), Tr()) for _ in range(3)]
        for hp in range(2):
            for g, Dl in enumerate((1, 4, 16)):
                for j3 in range(3):
                    load_w_into(k, ws[j3][0], ws[j3][1], "w_in", l, 0, D, OFF_A + g * 768 + j3 * 256 + hp * 128, 128)
                proj_rot(k, rb, hT, t_hT, ws[0][0], ws[0][1], tabs, qT, t_q)
                proj_rot(k, rb, hT, t_hT, ws[1][0], ws[1][1], tabs, kT, t_k)
                nb = 32 // Dl
                wv, t_wv = ws[2]
                for b0 in range(0, 32, 4):
                    p_, t_p = k.ps()
                    for bb in range(4):
                        blk = b0 + bb
                        r, j = blk // nb, blk % nb
                        st = HOFF + r + Dl * 128 * j
                        for kc in range(8):
                            k.mm(p_[:, bb * 128:(bb + 1) * 128], hT[:, kc, sl(st, 128, Dl)], wv[:, kc, :], kc == 0,
                                 kc == 7, [t_wv] + t_hT, [t_p])
                    k.cp("act", vs[:, b0:b0 + 4, :], p_[:].rearrange("p (b c) -> p b c", b=4), [t_p], [t_v])
                items = [(r, j) for r in range(Dl) for j in range(nb)]

                def st1(i):
                    r, j = items[i]
                    nq = 256 if j + 1 < nb else 128
                    st = r + Dl * 128 * j
                    kcols = sl(st, 128, Dl)
                    qcols = sl(st, nq, Dl)
                    E, t_E = Es[i % 3]
                    for h2 in range(2):
                        hb = 64 * h2
                        p_, t_p = k.ps()
                        k.mm(p_[:, 0:nq], kT[hb:hb + 64, kcols], qT[hb:hb + 64, qcols], True, True,
                             [t_k, t_q], [t_p])
                        k.act(E[:, h2, 0:nq], p_[:, 0:nq], AF.Exp, [t_p], [t_E], scale=0.125)
                    k.tt("pool", E[:, :, 0:nq], E[:, :, 0:nq], maskA[:].rearrange("p (a b) -> p a b", a=2)[:, :, 0:nq],
                         ALU.mult, [t_E, k.t_const], [t_E])

                def st2(i):
                    r, j = items[i]
                    st = r + Dl * 128 * j
                    E, t_E = Es[i % 3]
                    Eprev = Es[(i - 1) % 3]
                    blk = r * nb + j
                    po, t_po = k.ps()
                    for h2 in range(2):
                        hb = 64 * h2
                        for pl in range(2):
                            o_ap = po[hb:hb + 64, pl * 128:(pl + 1) * 128]
                            first = True
                            if j > 0:
                                lh = vs[:, blk - 1, hb:hb + 64] if pl == 0 else ones_b[:, 0:64]
                                k.mm(o_ap, lh, Eprev[0][:, h2, 128:256], True, False, [t_v, Eprev[1], k.t_const], [t_po])
                                first = False
                            lh = vs[:, blk, hb:hb + 64] if pl == 0 else ones_b[:, 0:64]
                            k.mm(o_ap, lh, E[:, h2, 0:128], first, True, [t_v, t_E, k.t_const], [t_po])
                    qtok = sl(st, 128, Dl)
                    pov = po[:, 0:256].rearrange("p (a b) -> p a b", a=2)
                    if g == 0:
                        k.cp("dve", acc[:, :, qtok], pov, [t_po], [t_acc])
                    else:
                        k.tt("dve", acc[:, :, qtok], acc[:, :, qtok], pov, ALU.add, [t_po, t_acc], [t_acc])
                st1(0)
                for i in range(len(items)):
                    if i + 1 < len(items):
                        st1(i + 1)
                    st2(i)
            for gq in range(8):
                cols = slice(gq * 512, (gq + 1) * 512)
                k.recip(acc[:, 1, cols], acc[:, 1, cols], [t_acc], [t_acc])
                k.tt("dve", yb[:, cols], acc[:, 0, cols], acc[:, 1, cols], ALU.mult, [t_acc], [t_yb])
                S.dma("sp", ctx["yT"][hp * 128:(hp + 1) * 128, cols], yb[:, cols], reads=[t_yb],
                      writes=[ctx["t_yT"][hp][gq]])
        S.barrier()


def phase_B(ctx, s, l, hT, t_hT, tabs):
    k = ctx["k"]
    S = k.S
    maskD = k.cs["c_maskD"]
    ones_b = k.cs["ones_b"]
    lam_init = 0.8 - 0.6 * math.exp(-0.3 * l)
    saved = (k.ps_f, k.ps_ft, k.ps_i)
    accb = list(zip(k.ps_f[0:2], k.ps_ft[0:2]))
    k.ps_f, k.ps_ft, k.ps_i = saved[0][2:6], saved[1][2:6], 0
    with contextlib.ExitStack() as es:
        rb = rot_bufs(k, es)
        lv, t_lv = load_bc(k, es, k.vf["diff_lam"][l].rearrange("a b -> (a b)"), 256, "lv")
        sc_ = k.sb(es, [128, 8], F32, "lsc")
        t_sc = Tr()
        pr = k.sb(es, [128, 128], F32, "lpr")
        k.tt("dve", pr[:, 0:64], lv[:, 0:64], lv[:, 64:128], ALU.mult, [t_lv], [t_sc])
        k.tt("dve", pr[:, 64:128], lv[:, 128:192], lv[:, 192:256], ALU.mult, [t_lv], [t_sc])
        S.op("dve", lambda e: e.reduce_sum(out=sc_[:, 0:2], in_=pr[:].rearrange("p (a b) -> p a b", a=2), axis=AX.X),
             reads=[t_sc], writes=[t_sc])
        k.act(sc_[:, 2:4], sc_[:, 0:2], AF.Exp, [t_sc], [t_sc])
        k.tt("dve", sc_[:, 4:5], sc_[:, 3:4], sc_[:, 2:3], ALU.subtract, [t_sc], [t_sc])
        k.ts("dve", sc_[:, 5:6], sc_[:, 4:5], -lam_init, None, ALU.add, ALU.bypass, [t_sc], [t_sc])
        gB = k.sb(es, [128, 1], F32, "gB")
        t_gB = Tr()
        S.dma("sp", gB[:], k.vf["diff_norm_g"][l].rearrange("(p o) -> p o", o=1), writes=[t_gB])
        k.ts("dve", gB[:], gB[:], 1.0 - lam_init, None, ALU.mult, ALU.bypass, [t_gB], [t_gB])
        qT = k.sb(es, [128, S_LEN], BF16, "bqT")
        kT = k.sb(es, [128, S_LEN], BF16, "bkT")
        vs = k.sb(es, [128, 32, 128], BF16, "bvs")
        t_q, t_k, t_v = Tr(), Tr(), Tr()
        Es = [(k.sb(es, [128, 512], BF16, "bE"), Tr()) for _ in range(6)]
        ws = [(k.sb(es, [128, 8, 128], BF16, "bw"), Tr()) for _ in range(3)]
        o1 = k.sb(es, [128, 512], F32, "bo1")
        o2 = k.sb(es, [128, 512], F32, "bo2")
        rr = k.sb(es, [128, 512], F32, "brr")
        sq = k.sb(es, [128, 512], BF16, "bsq")
        ybs = [(k.sb(es, [128, 512], BF16, "byb"), Tr()) for _ in range(2)]
        t_o = Tr()
        for h in range(4):
            for j3 in range(3):
                load_w_into(k, ws[j3][0], ws[j3][1], "w_in", l, 0, D, OFF_B + j3 * 512 + h * 128, 128)
            proj_rot(k, rb, hT, t_hT, ws[0][0], ws[0][1], tabs, qT, t_q)
            proj_rot(k, rb, hT, t_hT, ws[1][0], ws[1][1], tabs, kT, t_k)
            wv, t_wv = ws[2]
            for b0 in range(0, 32, 4):
                p_, t_p = k.ps()
                for bb in range(4):
                    blk = b0 + bb
                    st = HOFF + 128 * blk
                    for kc in range(8):
                        k.mm(p_[:, bb * 128:(bb + 1) * 128], hT[:, kc, st:st + 128], wv[:, kc, :], kc == 0, kc == 7,
                             [t_wv] + t_hT, [t_p])
                k.cp("act", vs[:, b0:b0 + 4, :], p_[:].rearrange("p (b c) -> p b c", b=4), [t_p], [t_v])
            items = [(G, m, j) for G in range(8) for m in range(2) for j in range(4 * G + 4)]
            LA = 3

            def geo(G, j):
                jj = max(j - 4 * G, 0)
                qoff = 128 * jj
                return qoff, 512 - qoff, 512 * G + qoff

            def st1(i):
                G, m, j = items[i]
                qoff, nq, q0 = geo(G, j)
                hb = 64 * m
                p_, t_p = k.ps()
                k.mm(p_[:, 0:nq], kT[hb:hb + 64, 128 * j:128 * j + 128], qT[hb:hb + 64, q0:q0 + nq], True, True,
                     [t_k, t_q], [t_p])
                E, t_E = Es[i % 6]
                k.act(E[:, 0:nq], p_[:, 0:nq], AF.Exp, [t_p], [t_E], scale=0.125)
                if j >= 4 * G:
                    k.tt("dve", E[:, 0:nq], E[:, 0:nq], maskD[:, 0:nq], ALU.mult, [t_E, k.t_const], [t_E])

            def st2(i):
                G, m, j = items[i]
                qoff, nq, q0 = geo(G, j)
                nj = 4 * G + 4
                E, t_E = Es[i % 6]
                pn, t_pn = accb[0]
                pd, t_pd = accb[1]
                k.mm(pn[:, qoff:512], vs[:, j, :], E[:, 0:nq], j == 0, j == nj - 1, [t_v, t_E], [t_pn])
                k.mm(pd[:, qoff:512], ones_b[:], E[:, 0:nq], j == 0, j == nj - 1, [t_E, k.t_const], [t_pd])
                if j == nj - 1:
                    k.recip(rr[:], pd[:], [t_pd, t_o], [t_o])
                    k.tt("dve", (o1 if m == 0 else o2)[:], pn[:], rr[:], ALU.mult, [t_pn, t_o], [t_o])
                    if m == 1:
                        fin(G)

            def fin(G):
                k.stt(o1[:], o2[:], sc_[:, 5:6], o1[:], ALU.mult, ALU.add, [t_o, t_sc], [t_o])
                k.tt("pool", sq[:], o1[:], o1[:], ALU.mult, [t_o], [t_o])
                pm_, t_pm = k.ps()
                k.mm(pm_[:], ones_b[:], sq[:], True, True, [t_o, k.t_const], [t_pm])
                k.ts("dve", rr[:], pm_[:], 1.0 / 128, EPS, ALU.mult, ALU.add, [t_pm, t_o], [t_o])
                k.act(rr[:], rr[:], AF.Sqrt, [t_o], [t_o])
                k.recip(rr[:], rr[:], [t_o], [t_o])
                yb, t_yb = ybs[G % 2]
                k.stt(yb[:], o1[:], gB[:, 0:1], rr[:], ALU.mult, ALU.mult, [t_o, t_gB], [t_yb])
                S.dma("sp", ctx["yT"][256 + h * 128:256 + (h + 1) * 128, G * 512:(G + 1) * 512], yb[:], reads=[t_yb],
                      writes=[ctx["t_yT"][2 + h][G]])

            for i in range(min(LA, len(items))):
                st1(i)
            for i in range(len(items)):
                if i + LA < len(items):
                    st1(i + LA)
                st2(i)
        S.barrier()
    k.ps_f, k.ps_ft, k.ps_i = saved


def phase_C(ctx, s, l, hT, t_hT):
    k = ctx["k"]
    S = k.S
    cs = k.cs
    with contextlib.ExitStack() as es:
        wc, t_wc = load_w(k, es, "w_in", l, 0, D, OFF_C, 1024, "wc")
        lb = k.sb(es, [128, 256], F32, "lb")
        omlb = k.sb(es, [128, 256], F32, "omlb")
        t_lb = Tr()
        if l == 0:
            k.memset(lb[:], 0.0, [t_lb])
        else:
            S.dma("sp", lb[:], k.vf["hgrn_lb_logits"][1].partition_broadcast(128), writes=[t_lb])
            S.dma("sp", omlb[:], k.vf["hgrn_lb_logits"][0].partition_broadcast(128), writes=[t_lb])
            k.tt("dve", lb[:], lb[:], omlb[:], ALU.subtract, [t_lb], [t_lb])
            k.act(lb[:], lb[:], AF.Sigmoid, [t_lb], [t_lb])
        k.ts("dve", omlb[:], lb[:], -1.0, 1.0, ALU.mult, ALU.add, [t_lb], [t_lb])
        gn4 = k.sb(es, [128, 4, 64], F32, "gn4")
        t_gn = Tr()
        for h in range(4):
            S.dma("sp", gn4[:, h, :], k.vf["hgrn_norm_g"][l].partition_broadcast(128), writes=[t_gn])
        S32 = k.sb(es, [128, 2, 64], F32, "S32")
        t_S = Tr()
        k.memset(S32[:], 0.0, [t_S])
        Sbfs = [(k.sb(es, [128, 2, 64], BF16, "Sbf"), Tr()) for _ in range(2)]
        k.memset(Sbfs[0][0][:], 0.0, [Sbfs[0][1]])

        def mk(shape, dt, n, nm):
            return [(k.sb(es, shape, dt, nm), Tr()) for _ in range(n)]
        qs_, sf_, lf_, kk_, sg_ = (mk([128, 256], F32, 2, nm) for nm in ("cqs", "csf", "clf", "ckk", "csg"))
        ib_, Qp_, Kp_ = (mk([128, 256], BF16, 2, nm) for nm in ("cib", "cQp", "cKp"))
        eb_, enb_ = (mk([128, 256], F32, 2, nm) for nm in ("ceb", "cenb"))
        colv_ = mk([128, 2, 3], F32, 2, "ccolv")
        dd_ = mk([128, 2], F32, 2, "cdd")
        QT_, QTt_, KT_ = (mk([128, 2, 128], BF16, 2, nm) for nm in ("cQT", "cQTt", "cKT"))
        attE_, attO_ = (mk([128, 2, 128], BF16, 2, nm) for nm in ("cattE", "cattO"))
        QTh_ = mk([128, 2, 128], BF16, 2, "cQTh")
        for b_ in range(2):
            k.memset(QTh_[b_][0][:], 0.0, [QTh_[b_][1]])
        o_ = mk([128, 256], F32, 2, "co")
        sq_ = mk([128, 256], F32, 2, "csq")
        st_ = mk([128, 12], F32, 2, "cst")
        tmpU_ = mk([128, 2, 64], F32, 2, "ctu")
        y_ = mk([128, 256], BF16, 2, "cy")
        ygs = mk([128, 2, 512], BF16, 2, "cyg")
        tri2 = cs["c_tri2"]
        for tt_ in range(NT):
            b = tt_ % 2
            hs = slice(HOFF + tt_ * 128, HOFF + (tt_ + 1) * 128)
            p0, t_p0 = k.ps()
            p1, t_p1 = k.ps()
            for kc in range(8):
                k.mm(p0[:], hT[:, kc, hs], wc[:, kc, 0:512], kc == 0, kc == 7, [t_wc, t_hT[tt_]], [t_p0])
            for kc in range(8):
                k.mm(p1[:], hT[:, kc, hs], wc[:, kc, 512:1024], kc == 0, kc == 7, [t_wc, t_hT[tt_]], [t_p1])
            qs, t_qs = qs_[b]
            sf, t_sf = sf_[b]
            lf, t_lf = lf_[b]
            kk, t_kk = kk_[b]
            sg, t_sg = sg_[b]
            ib, t_ib = ib_[b]
            k.act(qs[:], p0[:, 0:256], AF.Silu, [t_p0], [t_qs])
            k.act(sf[:], p0[:, 256:512], AF.Sigmoid, [t_p0], [t_sf])
            k.cp("act", ib[:], p1[:, 0:256], [t_p1], [t_ib])
            k.act(sg[:], p1[:, 256:512], AF.Silu, [t_p1], [t_sg])
            k.tt("dve", sf[:], sf[:], omlb[:], ALU.mult, [t_sf, t_lb], [t_sf])
            k.tt("dve", sf[:], sf[:], lb[:], ALU.add, [t_sf, t_lb], [t_sf])
            k.act(lf[:], sf[:], AF.Ln, [t_sf], [t_lf])
            k.ts("pool", kk[:], sf[:], -1.0, 1.0, ALU.mult, ALU.add, [t_sf], [t_kk])
            pb_, t_pb_ = k.ps()
            k.mm(pb_[:, 0:256], cs["c_triM"][:], lf[:], True, True, [t_lf, k.t_const], [t_pb_])
            pc, t_pc = k.ps()
            for hp in range(2):
                k.mm(pc[:, hp * 2:hp * 2 + 2], lf[:, hp * 128:(hp + 1) * 128], cs["c_sel"][:], True, True,
                     [t_lf, k.t_const], [t_pc])
            colv, t_cv = colv_[b]
            dd, t_dd = dd_[b]
            pcv = pc[:, 0:4].rearrange("p (a c) -> p a c", a=2)
            k.act(colv[:, :, 0:2], pcv, AF.Exp, [t_pc], [t_cv])
            k.cp("act", st_[b][0][:, 0:2], pcv[:, :, 0], [t_pc], [st_[b][1]])
            k.tt("dve", dd[:], pcv[:, :, 1], st_[b][0][:, 0:2], ALU.subtract, [t_pc, st_[b][1]], [t_dd])
            k.act(colv[:, :, 2], dd[:], AF.Exp, [t_dd], [t_cv])
            eb, t_eb = eb_[b]
            enb, t_enb = enb_[b]
            k.act(eb[:], pb_[:, 0:256], AF.Exp, [t_pb_], [t_eb])
            k.act(enb[:], pb_[:, 0:256], AF.Exp, [t_pb_], [t_enb], scale=-1.0)
            Qp, t_Qp = Qp_[b]
            Kp, t_Kp = Kp_[b]
            k.tt("dve", Qp[:], qs[:], eb[:], ALU.mult, [t_qs, t_eb], [t_Qp])
            k.tt("pool", Kp[:], kk[:], enb[:], ALU.mult, [t_kk, t_enb], [t_Kp])
            pt, t_pt = k.psb()
            for hp in range(2):
                k.tp(pt[:, hp * 128:(hp + 1) * 128], Qp[:, hp * 128:(hp + 1) * 128], [t_Qp], [t_pt])
                k.tp(pt[:, 256 + hp * 128:256 + (hp + 1) * 128], Kp[:, hp * 128:(hp + 1) * 128], [t_Kp], [t_pt])
            QT, t_QT = QT_[b]
            QTt, t_QTt = QTt_[b]
            KT, t_KT = KT_[b]
            k.cp("act", QT[:], pt[:, 0:256].rearrange("p (a c) -> p a c", a=2), [t_pt], [t_QT])
            k.cp("act", KT[:], pt[:, 256:512].rearrange("p (a c) -> p a c", a=2), [t_pt], [t_KT])
            QTh, t_QTh = QTh_[b]
            k.cp("pool", QTh[:, :, 64:128], QT[:, :, 64:128], [t_QT], [t_QTh])
            for hp in range(2):
                k.ts("dve", QTt[:, hp, :], pt[:, hp * 128:(hp + 1) * 128], colv[:, hp, 0:1], None, ALU.mult, ALU.bypass,
                     [t_pt, t_cv], [t_QTt])
            paE, t_paE = k.ps()
            paO, t_paO = k.ps()
            for h in range(4):
                hp, par = h // 2, h % 2
                hb = 64 * par
                pa, t_pa = (paE, t_paE) if par == 0 else (paO, t_paO)
                k.mm(pa[0:64, hp * 128:(hp + 1) * 128], KT[hb:hb + 64, hp, 0:64], QT[hb:hb + 64, hp, :], True, True,
                     [t_KT, t_QT], [t_pa])
                k.mm(pa[64:128, hp * 128:(hp + 1) * 128], KT[hb:hb + 64, hp, 64:128], QTh[hb:hb + 64, hp, :], True, True,
                     [t_KT, t_QTh], [t_pa])
            attE, t_aE = attE_[b]
            attO, t_aO = attO_[b]
            k.tt("dve", attE[:].rearrange("p a c -> p (a c)"), paE[:, 0:256], tri2[:], ALU.mult, [t_paE, k.t_const], [t_aE])
            k.tt("dve", attO[:].rearrange("p a c -> p (a c)"), paO[:, 0:256], tri2[:], ALU.mult, [t_paO, k.t_const], [t_aO])
            po, t_po = k.ps()
            for h in range(4):
                hp, par = h // 2, h % 2
                att, t_att = (attE, t_aE) if par == 0 else (attO, t_aO)
                k.mm(po[:, h * 64:(h + 1) * 64], att[:, hp, :], ib[:, h * 64:(h + 1) * 64], True, True, [t_att, t_ib], [t_po])
            Sbf, t_Sbf = Sbfs[b]
            Sbn, t_Sbn = Sbfs[1 - b]
            piE, t_piE = k.ps()
            piO, t_piO = k.ps()
            for h in range(4):
                hp, par = h // 2, h % 2
                hb = 64 * par
                pi, t_pi = (piE, t_piE) if par == 0 else (piO, t_piO)
                k.mm(pi[:, hp * 64:(hp + 1) * 64], QTt[hb:hb + 64, hp, :], Sbf[hb:hb + 64, hp, :], True, True,
                     [t_QTt, t_Sbf], [t_pi])
            o, t_o = o_[b]
            k.cp("act", o[:], po[:, 0:256], [t_po], [t_o])
            ov = o[:].rearrange("p (a b c) -> p a b c", a=2, b=2)
            k.tt("dve", ov[:, :, 0, :], ov[:, :, 0, :], piE[:, 0:128].rearrange("p (a c) -> p a c", a=2), ALU.add,
                 [t_o, t_piE], [t_o])
            k.tt("dve", ov[:, :, 1, :], ov[:, :, 1, :], piO[:, 0:128].rearrange("p (a c) -> p a c", a=2), ALU.add,
                 [t_o, t_piO], [t_o])
            pu, t_pu = k.ps()
            for h in range(4):
                hp, par = h // 2, h % 2
                hb = 64 * par
                k.mm(pu[hb:hb + 64, hp * 64:(hp + 1) * 64], Kp[:, h * 64:(h + 1) * 64], ib[:, h * 64:(h + 1) * 64], True, True,
                     [t_Kp, t_ib], [t_pu])
            tu, t_tu = tmpU_[b]
            for hp in range(2):
                k.ts("dve", tu[:, hp, :], pu[:, hp * 64:(hp + 1) * 64], colv[:, hp, 2:3], None, ALU.mult, ALU.bypass,
                     [t_pu, t_cv], [t_tu])
                k.stt(S32[:, hp, :], S32[:, hp, :], colv[:, hp, 1:2], tu[:, hp, :], ALU.mult, ALU.add, [t_S, t_cv, t_tu], [t_S])
            k.cp("pool", Sbn[:], S32[:], [t_S], [t_Sbn])
            sq, t_sq = sq_[b]
            st, t_st = st_[b]
            y, t_y = y_[b]
            k.tt("pool", sq[:], o[:], o[:], ALU.mult, [t_o], [t_sq])
            S.op("dve", lambda e, st=st, sq=sq: e.reduce_sum(out=st[:, 0:4], in_=sq[:].rearrange("p (a c) -> p a c", a=4),
                                                            axis=AX.X), reads=[t_sq], writes=[t_st])
            k.ts("dve", st[:, 4:8], st[:, 0:4], 1.0 / 64, EPS, ALU.mult, ALU.add, [t_st], [t_st])
            k.act(st[:, 4:8], st[:, 4:8], AF.Sqrt, [t_st], [t_st])
            k.recip(st[:, 8:12], st[:, 4:8], [t_st], [t_st])
            k.tt("pool", sq[:], sg[:], gn4[:].rearrange("p a c -> p (a c)"), ALU.mult, [t_sg, t_gn, t_sq], [t_sq])
            for h in range(4):
                k.stt(y[:, h * 64:(h + 1) * 64], o[:, h * 64:(h + 1) * 64], st[:, 8 + h:9 + h], sq[:, h * 64:(h + 1) * 64],
                      ALU.mult, ALU.mult, [t_o, t_st, t_sq], [t_y])
            pt2, t_pt2 = k.psb()
            for hp in range(2):
                k.tp(pt2[:, hp * 128:(hp + 1) * 128], y[:, hp * 128:(hp + 1) * 128], [t_y], [t_pt2])
            gq, q = tt_ // 4, tt_ % 4
            yg, t_yg = ygs[gq % 2]
            k.cp("act", yg[:, :, q * 128:(q + 1) * 128], pt2[:, 0:256].rearrange("p (a c) -> p a c", a=2), [t_pt2], [t_yg])
            if q == 3:
                for hp in range(2):
                    S.dma("sp", ctx["yT"][768 + hp * 128:768 + (hp + 1) * 128, gq * 512:(gq + 1) * 512], yg[:, hp, :],
                          reads=[t_yg], writes=[ctx["t_yT"][6 + hp][gq]])
        S.barrier()


def phase_D(ctx, s, l, hT, t_hT, t_h0):
    k = ctx["k"]
    S = k.S
    cs = k.cs
    vfirst, t_vf = ctx["vfirst"], ctx["t_vf"][s]
    with contextlib.ExitStack() as es:
        wa = k.sb(es, [128, 8, 1024], BF16, "wda")
        wb_ = k.sb(es, [128, 8, 1024], BF16, "wdb")
        t_wa = Tr()
        with contextlib.ExitStack() as e2:
            wd, t_wd = load_w(k, e2, "w_in", l, 0, D, OFF_D, 1024, "wd")
            mu, t_mu = load_bc(k, e2, k.vf["rwkv_mu"][l], 1024, "mu")
            omu = k.sb(e2, [128, 1024], F32, "omu")
            k.ts("dve", omu[:], mu[:], -1.0, 1.0, ALU.mult, ALU.add, [t_mu], [t_mu])
            for kc in range(8):
                k.tt("dve", wb_[:, kc, :], wd[:, kc, :], mu[:], ALU.mult, [t_wd, t_mu], [t_wa])
                k.tt("pool", wa[:, kc, :], wd[:, kc, :], omu[:], ALU.mult, [t_wd, t_mu], [t_wa])
            S.barrier()
        t_bc = Tr()

        def bc(src, n=256, nm="dbc"):
            t = k.sb(es, [128, n], F32, nm)
            S.dma("sp", t[:], src.partition_broadcast(128), writes=[t_bc])
            return t
        w0 = bc(k.vf["rwkv_w0"][l])
        a0 = bc(k.vf["rwkv_a0"][l])
        kkb = bc(k.vf["rwkv_k_k"][l])
        kab = bc(k.vf["rwkv_k_a"][l])
        lng = bc(k.vf["rwkv_lnx_g"][l])
        lnb = bc(k.vf["rwkv_lnx_b"][l])
        rkb = bc(k.vf["rwkv_r_k"][l].rearrange("a b -> (a b)"))
        omka = k.sb(es, [128, 256], F32, "omka")
        k.ts("dve", omka[:], kab[:], -1.0, 1.0, ALU.mult, ALU.add, [t_bc], [t_bc])
        w2a2 = k.sb(es, [128, 256], BF16, "w2a2")
        g2 = k.sb(es, [128, 256], BF16, "g2")
        t_lw = Tr()
        S.dma("sp", w2a2[0:64, :], k.wb["rwkv_w2"][l], reads=k.t_w["rwkv_w2"], writes=[t_lw])
        S.dma("sp", w2a2[64:128, :], k.wb["rwkv_a2"][l], reads=k.t_w["rwkv_a2"], writes=[t_lw])
        S.dma("sp", g2[:], k.wb["rwkv_g2"][l], reads=k.t_w["rwkv_g2"], writes=[t_lw])
        if l > 0:
            v0b = bc(k.vf["rwkv_v0"][l - 1])
            v1 = k.sb(es, [128, 2, 32], BF16, "v1")
            v2 = k.sb(es, [32, 256], BF16, "v2")
            S.dma("sp", v1[:], k.wb["rwkv_v1"][l - 1].rearrange("(kc p) n -> p kc n", p=128), reads=k.t_w["rwkv_v1"],
                  writes=[t_lw])
            S.dma("sp", v2[:], k.wb["rwkv_v2"][l - 1], reads=k.t_w["rwkv_v2"], writes=[t_lw])
        H32 = k.sb(es, [128, 2, 64], F32, "H32")
        t_H = Tr()
        k.memset(H32[:], 0.0, [t_H])
        Hbfs = [(k.sb(es, [128, 2, 64], BF16, "Hbf"), Tr()) for _ in range(2)]
        k.memset(Hbfs[0][0][:], 0.0, [Hbfs[0][1]])

        def f32(nm, n=256):
            return k.sb(es, [128, n], F32, nm), Tr()

        def b16(nm, shape=(128, 256)):
            return k.sb(es, list(shape), BF16, nm), Tr()
        r32, t_r = f32("r32")
        k32, t_k = f32("k32")
        v32, t_v = f32("v32")
        li, t_li = b16("li")
        liT, t_liT = b16("liT", (128, 2, 128))
        lw, t_lwv = f32("lw")
        a32, t_a = f32("a32")
        g32, t_g = f32("g32")
        tmp, t_tmp = f32("tmp")
        tmp2, t_tmp2 = f32("tmp2")
        kk0, t_kk = f32("kk0")
        kp, t_kp = f32("kp")
        bv, t_bv = f32("bv")
        st, t_st = f32("st", 24)
        e1, t_e1 = f32("e1")
        e2_, t_e2 = f32("e2")
        e4, t_e4 = f32("e4")
        e5, t_e5 = f32("e5")
        ewl, t_ewl = f32("ewl")
        colv, t_cv = k.sb(es, [128, 2, 2], F32, "dcolv"), Tr()
        Rp, t_Rp = b16("Rp")
        Ap, t_Ap = b16("Ap")
        Bp, t_Bp = b16("Bp")
        Kp, t_Kp = b16("Kp")
        At, t_At = b16("At")
        Bc, t_Bc = b16("Bc")
        Kc, t_Kc = b16("Kc")
        Vb, t_Vb = b16("Vb")
        ART, t_ART = b16("ART", (128, 2, 2, 128))
        BT, t_BT = b16("BT", (128, 2, 128))
        KT, t_KT = b16("KT", (128, 2, 128))
        RTt, t_RTt = b16("RTt", (128, 2, 128))
        RhT, t_RhT = b16("RhT", (128, 2, 128))
        LM = [b16("LM", (128, 512)) for _ in range(4)]
        Lp = [[b16("Lp", (128, 2, 128)) for _ in range(7)] for _ in range(4)]
        Xs = [[b16("X", (128, 128)) for _ in range(2)] for _ in range(4)]
        Ysb, t_Y = f32("Ysb")
        GpT, t_GpT = b16("GpT", (128, 2, 64))
        Zsb, t_Z = k.sb(es, [128, 2, 64], F32, "Zsb"), Tr()
        ybf, t_ybf = b16("ybf")
        ygs = [b16("dyg", (128, 2, 512)) for _ in range(2)]
        if l > 0:
            vT, t_vT = b16("vT", (128, 2, 128))
            u1, t_u1 = b16("u1", (32, 128))
            vfb, t_vfb = f32("vfb")
        mSI2, mLow = cs["c_maskSI2"], cs["c_maskLow"]

        for tt_ in range(NT):
            hs = slice(HOFF + tt_ * 128, HOFF + (tt_ + 1) * 128)
            hs1 = slice(HOFF + tt_ * 128 - 1, HOFF + (tt_ + 1) * 128 - 1)
            rdh = [t_wa, t_hT[tt_], t_h0] + ([t_hT[tt_ - 1]] if tt_ > 0 else [])
            pA, t_pA = k.ps()
            pB, t_pB = k.ps()
            for p_, t_p, c0 in ((pA, t_pA, 0), (pB, t_pB, 512)):
                for kc in range(8):
                    k.mm(p_[:], hT[:, kc, hs], wa[:, kc, c0:c0 + 512], kc == 0, False, rdh, [t_p])
                for kc in range(8):
                    k.mm(p_[:], hT[:, kc, hs1], wb_[:, kc, c0:c0 + 512], False, kc == 7, rdh, [t_p])
            k.cp("act", r32[:], pA[:, 0:256], [t_pA], [t_r])
            k.cp("act", k32[:], pA[:, 256:512], [t_pA], [t_k])
            k.cp("act", v32[:], pB[:, 0:256], [t_pB], [t_v])
            k.act(li[:, 0:64], pB[:, 256:320], AF.Tanh, [t_pB], [t_li])
            k.cp("act", li[:, 64:128], pB[:, 320:384], [t_pB], [t_li])
            k.act(li[:, 128:256], pB[:, 384:512], AF.Sigmoid, [t_pB], [t_li])
            pt, t_pt = k.psb()
            for j in range(2):
                k.tp(pt[:, j * 128:(j + 1) * 128], li[:, j * 128:(j + 1) * 128], [t_li], [t_pt])
            k.cp("dve", liT[:], pt[:, 0:256].rearrange("p (a c) -> p a c", a=2), [t_pt], [t_liT])
            pw, t_pw = k.ps()
            pa_, t_pa = k.ps()
            pg, t_pg = k.ps()
            k.mm(pw[:, 0:256], liT[0:64, 0, :], w2a2[0:64, :], True, True, [t_liT, t_lw], [t_pw])
            k.mm(pa_[:, 0:256], liT[64:128, 0, :], w2a2[64:128, :], True, True, [t_liT, t_lw], [t_pa])
            k.mm(pg[:, 0:256], liT[:, 1, :], g2[:], True, True, [t_liT, t_lw], [t_pg])
            k.tt("dve", lw[:], pw[:, 0:256], w0[:], ALU.add, [t_pw, t_bc], [t_lwv])
            k.act(lw[:], lw[:], AF.Sigmoid, [t_lwv], [t_lwv])
            k.ts("dve", lw[:], lw[:], -0.6065306597126334, None, ALU.mult, ALU.bypass, [t_lwv], [t_lwv])
            k.tt("dve", a32[:], pa_[:, 0:256], a0[:], ALU.add, [t_pa, t_bc], [t_a])
            k.act(a32[:], a32[:], AF.Sigmoid, [t_a], [t_a])
            k.cp("act", g32[:], pg[:, 0:256], [t_pg], [t_g])
            if l == 0:
                S.dma("sp", vfirst[s, tt_ * 128:(tt_ + 1) * 128, :], v32[:], reads=[t_v], writes=[t_vf[tt_]])
            else:
                S.dma("sp", vfb[:], vfirst[s, tt_ * 128:(tt_ + 1) * 128, :], reads=[t_vf[tt_]], writes=[t_vfb])
                k.cp("act", Vb[:], v32[:], [t_v], [t_Vb])
                pt, t_pt = k.psb()
                for j in range(2):
                    k.tp(pt[:, j * 128:(j + 1) * 128], Vb[:, j * 128:(j + 1) * 128], [t_Vb], [t_pt])
                k.cp("dve", vT[:], pt[:, 0:256].rearrange("p (a c) -> p a c", a=2), [t_pt], [t_vT])
                p1, t_p1 = k.ps()
                for j in range(2):
                    k.mm(p1[0:32, 0:128], v1[:, j, :], vT[:, j, :], j == 0, j == 1, [t_vT, t_lw], [t_p1])
                k.cp("act", u1[:], p1[0:32, 0:128], [t_p1], [t_u1])
                p2, t_p2 = k.ps()
                k.mm(p2[:, 0:256], u1[:], v2[:], True, True, [t_u1, t_lw], [t_p2])
                k.tt("dve", tmp[:], p2[:, 0:256], v0b[:], ALU.add, [t_p2, t_bc], [t_tmp])
                k.act(tmp[:], tmp[:], AF.Sigmoid, [t_tmp], [t_tmp])
                k.tt("dve", vfb[:], vfb[:], v32[:], ALU.subtract, [t_vfb, t_v], [t_vfb])
                k.tt("dve", vfb[:], vfb[:], tmp[:], ALU.mult, [t_vfb, t_tmp], [t_vfb])
                k.tt("dve", v32[:], v32[:], vfb[:], ALU.add, [t_v, t_vfb], [t_v])
            k.cp("act", Vb[:], v32[:], [t_v], [t_Vb])
            k.tt("dve", kk0[:], k32[:], kkb[:], ALU.mult, [t_k, t_bc], [t_kk])
            k.tt("pool", tmp[:], kk0[:], kk0[:], ALU.mult, [t_kk], [t_tmp])
            S.op("dve", lambda e: e.reduce_sum(out=st[:, 0:4], in_=tmp[:].rearrange("p (a c) -> p a c", a=4), axis=AX.X),
                 reads=[t_tmp], writes=[t_st])
            k.act(st[:, 0:4], st[:, 0:4], AF.Sqrt, [t_st], [t_st])
            k.ts("dve", st[:, 0:4], st[:, 0:4], 1e-12, None, ALU.max, ALU.bypass, [t_st], [t_st])
            k.recip(st[:, 4:8], st[:, 0:4], [t_st], [t_st])
            for h in range(4):
                k.ts("dve", kk0[:, h * 64:(h + 1) * 64], kk0[:, h * 64:(h + 1) * 64], st[:, 4 + h:5 + h], None, ALU.mult,
                     ALU.bypass, [t_kk, t_st], [t_kk])
            k.tt("dve", tmp2[:], a32[:], kab[:], ALU.mult, [t_a, t_bc], [t_tmp2])
            k.tt("pool", tmp2[:], tmp2[:], omka[:], ALU.add, [t_tmp2, t_bc], [t_tmp2])
            k.tt("dve", kp[:], k32[:], tmp2[:], ALU.mult, [t_k, t_tmp2], [t_kp])
            k.tt("pool", bv[:], kk0[:], a32[:], ALU.mult, [t_kk, t_a], [t_bv])
            pCm, t_pCm = k.ps()
            pCt, t_pCt = k.ps()
            pCr, t_pCr = k.ps()
            k.mm(pCm[:, 0:256], cs["c_triM"][:], lw[:], True, True, [t_lwv, k.t_const], [t_pCm])
            k.mm(pCt[:, 0:256], cs["c_tri"][:], lw[:], True, True, [t_lwv, k.t_const], [t_pCt])
            k.mm(pCr[:, 0:256], cs["c_triR"][:], lw[:], True, True, [t_lwv, k.t_const], [t_pCr])
            pc, t_pc = k.ps()
            for hp in range(2):
                k.mm(pc[:, hp * 2:hp * 2 + 2], lw[:, hp * 128:(hp + 1) * 128], cs["c_sel"][:], True, True,
                     [t_lwv, k.t_const], [t_pc])
            k.act(colv[:], pc[:, 0:4].rearrange("p (a c) -> p a c", a=2), AF.Exp, [t_pc], [t_cv])
            k.act(e1[:], pCm[:, 0:256], AF.Exp, [t_pCm], [t_e1])
            k.act(e2_[:], pCm[:, 0:256], AF.Exp, [t_pCm], [t_e2], scale=-1.0)
            k.act(e4[:], pCt[:, 0:256], AF.Exp, [t_pCt], [t_e4])
            k.act(e5[:], pCr[:, 0:256], AF.Exp, [t_pCr], [t_e5])
            k.act(ewl[:], lw[:], AF.Exp, [t_lwv], [t_ewl], scale=-1.0)
            k.tt("dve", Rp[:], r32[:], e1[:], ALU.mult, [t_r, t_e1], [t_Rp])
            k.tt("pool", tmp[:], kk0[:], ewl[:], ALU.mult, [t_kk, t_ewl, t_st], [t_tmp])
            k.stt(Ap[:], tmp[:], -1.0, e1[:], ALU.mult, ALU.mult, [t_tmp, t_e1], [t_Ap])
            k.stt(At[:], tmp[:], -1.0, e4[:], ALU.mult, ALU.mult, [t_tmp, t_e4], [t_At])
            k.tt("pool", Bp[:], bv[:], e2_[:], ALU.mult, [t_bv, t_e2], [t_Bp])
            k.tt("dve", Kp[:], kp[:], e2_[:], ALU.mult, [t_kp, t_e2], [t_Kp])
            k.tt("pool", Bc[:], bv[:], e5[:], ALU.mult, [t_bv, t_e5], [t_Bc])
            k.tt("dve", Kc[:], kp[:], e5[:], ALU.mult, [t_kp, t_e5], [t_Kc])
            pt, t_pt = k.psb()
            for j, src in enumerate((Ap, Rp, Bp, Kp)):
                for hp in range(2):
                    k.tp(pt[:, (j * 2 + hp) * 128:(j * 2 + hp + 1) * 128], src[:, hp * 128:(hp + 1) * 128],
                         [t_Ap, t_Rp, t_Bp, t_Kp], [t_pt])
            ptv = pt[:].rearrange("p (j a c) -> p j a c", j=4, a=2)
            for j in range(2):
                k.cp("act", ART[:, :, j, :], ptv[:, j, :, :], [t_pt], [t_ART])
            k.cp("dve", BT[:], ptv[:, 2, :, :], [t_pt], [t_BT])
            k.cp("dve", KT[:], ptv[:, 3, :, :], [t_pt], [t_KT])
            for hp in range(2):
                k.ts("dve", RTt[:, hp, :], ptv[:, 1, hp, :], colv[:, hp, 0:1], None, ALU.mult, ALU.bypass, [t_pt, t_cv],
                     [t_RTt])
            for h in range(4):
                hp, par = h // 2, h % 2
                hb = 64 * par
                LMh, t_LM = LM[h]
                pl, t_pl = k.ps()
                rhs_ar = ART[hb:hb + 64, hp, :, :].rearrange("p a c -> p (a c)")
                k.mm(pl[:, 0:256], BT[hb:hb + 64, hp, :], rhs_ar, True, True, [t_BT, t_ART], [t_pl])
                k.mm(pl[:, 256:512], KT[hb:hb + 64, hp, :], rhs_ar, True, True, [t_KT, t_ART], [t_pl])
                k.tt("dve", LMh[:], pl[:], mSI2[:], ALU.mult, [t_pl, k.t_const], [t_LM])
                L0, t_L0 = Lp[h][0]
                p0, t_p0 = k.ps()
                k.mm(p0[:, 0:128], ART[hb:hb + 64, hp, 0, :], BT[hb:hb + 64, hp, :], True, True, [t_ART, t_BT], [t_p0])
                k.tt("dve", L0[:, 0, :], p0[:, 0:128], mLow[:], ALU.mult, [t_p0, k.t_const], [t_L0])
                k.cp("pool", L0[:, 1, :], LMh[:, 0:128], [t_LM], [t_L0])
            for h in range(4):
                LMh, t_LM = LM[h]
                X0, t_X0 = Xs[h][0]
                px, t_px = k.ps()
                k.mm(px[:, 0:64], LMh[:, 256:384], Vb[:, h * 64:(h + 1) * 64], True, True, [t_LM, t_Vb], [t_px])
                k.cp("act", X0[:, 64:128], px[:, 0:64], [t_px], [t_X0])
                k.cp("pool", X0[:, 0:64], At[:, h * 64:(h + 1) * 64], [t_At], [t_X0])
            for j in range(7):
                if j < 6:
                    for h in range(4):
                        Lj, t_Lj = Lp[h][j]
                        Ln, t_Ln = Lp[h][j + 1]
                        p_, t_p = k.ps()
                        k.mm(p_[:, 0:128], Lj[:, 1, :], Lj[:, 0, :], True, True, [t_Lj], [t_p])
                        k.mm(p_[:, 128:256], Lj[:, 0, :], Lj[:, 1, :], True, True, [t_Lj], [t_p])
                        k.cp("act", Ln[:].rearrange("p a c -> p (a c)"), p_[:, 0:256], [t_p], [t_Ln])
                for h in range(4):
                    Xc, t_Xc = Xs[h][j % 2]
                    Xn, t_Xn = Xs[h][(j + 1) % 2]
                    Lj, t_Lj = Lp[h][j]
                    px, t_px = k.ps()
                    k.mm(px[:, 0:128], Lj[:, 1, :], Xc[:], True, True, [t_Lj, t_Xc], [t_px])
                    k.tt("dve", Xn[:], px[:, 0:128], Xc[:], ALU.add, [t_px, t_Xc], [t_Xn])
            pR, t_pR = k.ps()
            pY, t_pY = k.ps()
            pG, t_pG = k.ps()
            pZ, t_pZ = k.ps()
            for h in range(4):
                hp, par = h // 2, h % 2
                hb = 64 * par
                LMh, t_LM = LM[h]
                X7, t_X7 = Xs[h][1]
                hc = slice(h * 64, (h + 1) * 64)
                k.mm(pR[hb:hb + 64, hp * 128:(hp + 1) * 128], X7[:, 0:64], LMh[:, 128:256], True, True, [t_X7, t_LM], [t_pR])
                k.mm(pY[:, hc], LMh[:, 128:256], X7[:, 64:128], True, False, [t_X7, t_LM], [t_pY])
                k.mm(pY[:, hc], LMh[:, 384:512], Vb[:, hc], False, True, [t_Vb, t_LM], [t_pY])
                k.mm(pG[hb:hb + 64, hp * 64:(hp + 1) * 64], X7[:, 0:64], Bc[:, hc], True, True, [t_X7, t_Bc], [t_pG])
                k.mm(pZ[hb:hb + 64, hp * 64:(hp + 1) * 64], Bc[:, hc], X7[:, 64:128], True, False, [t_X7, t_Bc], [t_pZ])
                k.mm(pZ[hb:hb + 64, hp * 64:(hp + 1) * 64], Kc[:, hc], Vb[:, hc], False, True, [t_Kc, t_Vb], [t_pZ])
            k.tt("dve", RhT[:].rearrange("p a c -> p (a c)"), pR[:, 0:256], RTt[:].rearrange("p a c -> p (a c)"), ALU.add,
                 [t_pR, t_RTt], [t_RhT])
            k.cp("act", Ysb[:], pY[:, 0:256], [t_pY], [t_Y])
            k.cp("act", GpT[:].rearrange("p a c -> p (a c)"), pG[:, 0:128], [t_pG], [t_GpT])
            k.cp("act", Zsb[:].rearrange("p a c -> p (a c)"), pZ[:, 0:128], [t_pZ], [t_Z])
            Hbf, t_Hbf = Hbfs[tt_ % 2]
            Hbn, t_Hbn = Hbfs[1 - tt_ % 2]
            piE, t_piE = k.ps()
            piO, t_piO = k.ps()
            phE, t_phE = k.ps()
            phO, t_phO = k.ps()
            for h in range(4):
                hp, par = h // 2, h % 2
                hb = 64 * par
                pi, t_pi = (piE, t_piE) if par == 0 else (piO, t_piO)
                ph, t_ph = (phE, t_phE) if par == 0 else (phO, t_phO)
                k.mm(pi[:, hp * 64:(hp + 1) * 64], RhT[hb:hb + 64, hp, :], Hbf[hb:hb + 64, hp, :], True, True,
                     [t_RhT, t_Hbf], [t_pi])
                k.mm(ph[hb:hb + 64, hp * 64:(hp + 1) * 64], GpT[hb:hb + 64, hp, :], Hbf[hb:hb + 64, hp, :], True, True,
                     [t_GpT, t_Hbf], [t_ph])
            Yv = Ysb[:].rearrange("p (a b c) -> p a b c", a=2, b=2)
            k.tt("dve", Yv[:, :, 0, :], Yv[:, :, 0, :], piE[:, 0:128].rearrange("p (a c) -> p a c", a=2), ALU.add,
                 [t_Y, t_piE], [t_Y])
            k.tt("dve", Yv[:, :, 1, :], Yv[:, :, 1, :], piO[:, 0:128].rearrange("p (a c) -> p a c", a=2), ALU.add,
                 [t_Y, t_piO], [t_Y])
            for hp in range(2):
                k.stt(H32[:, hp, :], H32[:, hp, :], colv[:, hp, 1:2], Zsb[:, hp, :], ALU.mult, ALU.add, [t_H, t_cv, t_Z], [t_H])
            H2 = H32[:].rearrange("p a c -> p (a c)")
            k.tt("dve", H2[0:64, :], H2[0:64, :], phE[0:64, 0:128], ALU.add, [t_H, t_phE], [t_H])
            k.tt("dve", H2[64:128, :], H2[64:128, :], phO[64:128, 0:128], ALU.add, [t_H, t_phO], [t_H])
            k.cp("pool", Hbn[:], H32[:], [t_H], [t_Hbn])
            S.op("dve", lambda e: e.reduce_sum(out=st[:, 8:12], in_=Ysb[:].rearrange("p (a c) -> p a c", a=4), axis=AX.X),
                 reads=[t_Y], writes=[t_st])
            k.ts("dve", st[:, 8:12], st[:, 8:12], 1.0 / 64, None, ALU.mult, ALU.bypass, [t_st], [t_st])
            for h in range(4):
                k.ts("dve", Ysb[:, h * 64:(h + 1) * 64], Ysb[:, h * 64:(h + 1) * 64], st[:, 8 + h:9 + h], None, ALU.subtract,
                     ALU.bypass, [t_Y, t_st], [t_Y])
            k.tt("pool", tmp[:], Ysb[:], Ysb[:], ALU.mult, [t_Y], [t_tmp])
            S.op("dve", lambda e: e.reduce_sum(out=st[:, 12:16], in_=tmp[:].rearrange("p (a c) -> p a c", a=4), axis=AX.X),
                 reads=[t_tmp], writes=[t_st])
            k.ts("dve", st[:, 12:16], st[:, 12:16], 1.0 / 64, 64e-5, ALU.mult, ALU.add, [t_st], [t_st])
            k.act(st[:, 12:16], st[:, 12:16], AF.Sqrt, [t_st], [t_st])
            k.recip(st[:, 16:20], st[:, 12:16], [t_st], [t_st])
            for h in range(4):
                k.stt(Ysb[:, h * 64:(h + 1) * 64], Ysb[:, h * 64:(h + 1) * 64], st[:, 16 + h:17 + h], lng[:, h * 64:(h + 1) * 64],
                      ALU.mult, ALU.mult, [t_Y, t_st, t_bc], [t_Y])
            k.tt("pool", Ysb[:], Ysb[:], lnb[:], ALU.add, [t_Y, t_bc], [t_Y])
            k.tt("dve", tmp2[:], r32[:], kp[:], ALU.mult, [t_r, t_kp], [t_tmp2])
            k.tt("pool", tmp2[:], tmp2[:], rkb[:], ALU.mult, [t_tmp2, t_bc], [t_tmp2])
            S.op("dve", lambda e: e.reduce_sum(out=st[:, 20:24], in_=tmp2[:].rearrange("p (a c) -> p a c", a=4), axis=AX.X),
                 reads=[t_tmp2], writes=[t_st])
            for h in range(4):
                k.stt(Ysb[:, h * 64:(h + 1) * 64], v32[:, h * 64:(h + 1) * 64], st[:, 20 + h:21 + h], Ysb[:, h * 64:(h + 1) * 64],
                      ALU.mult, ALU.add, [t_Y, t_st, t_v], [t_Y])
            k.tt("dve", ybf[:], Ysb[:], g32[:], ALU.mult, [t_Y, t_g], [t_ybf])
            pt2, t_pt2 = k.psb()
            for hp in range(2):
                k.tp(pt2[:, hp * 128:(hp + 1) * 128], ybf[:, hp * 128:(hp + 1) * 128], [t_ybf], [t_pt2])
            gq, q = tt_ // 4, tt_ % 4
            yg, t_yg = ygs[gq % 2]
            k.cp("act", yg[:, :, q * 128:(q + 1) * 128], pt2[:, 0:256].rearrange("p (a c) -> p a c", a=2), [t_pt2], [t_yg])
            if q == 3:
                for hp in range(2):
                    S.dma("sp", ctx["yT"][1024 + hp * 128:1024 + (hp + 1) * 128, gq * 512:(gq + 1) * 512], yg[:, hp, :],
                          reads=[t_yg], writes=[ctx["t_yT"][8 + hp][gq]])
        S.barrier()


_NC_CACHE = {}


def kernel(**inputs):
    cfg = {}
    key = "main"
    if key not in _NC_CACHE:
        _NC_CACHE[key] = build(cfg)
    nc = _NC_CACHE[key]
    consts = host_consts()
    in_maps = []
    for c in range(8):
        m = {"x": np.ascontiguousarray(inputs["x"][2 * c:2 * c + 2]),
             "mem": np.ascontiguousarray(inputs["mem"][2 * c:2 * c + 2]),
             "positions": np.ascontiguousarray(inputs["positions"][2 * c:2 * c + 2]).astype(np.int32)}
        for n in WNAMES + VNAMES:
            m[n] = np.ascontiguousarray(inputs[n], dtype=np.float32)
        m.update(consts)
        in_maps.append(m)
    res = run_bass_kernel_spmd(nc, in_maps, core_ids=list(range(8)))
    return np.concatenate([r["out"] for r in res.results], axis=0).astype(np.float32)
```

```python
import contextlib
import math
import numpy as np
import concourse.bass as bass
import concourse.mybir as mybir
from concourse.bass_utils import run_bass_kernel_spmd

F32 = mybir.dt.float32
BF16 = mybir.dt.bfloat16
I32 = mybir.dt.int32
AF = mybir.ActivationFunctionType
ALU = mybir.AluOpType
AX = mybir.AxisListType

D = 1024
S_LEN = 4096
NT = S_LEN // 128
N_IN = 9984
OFF_A, OFF_B, OFF_C, OFF_D, OFF_G = 0, 2304, 3840, 4864, 5888
DFF = 2816
MEM = 256
EPS = 1e-5
HOFF = 8


class Tr:
    __slots__ = ("w", "r", "x")

    def __init__(self, excl=False):
        self.w = None
        self.r = {}
        self.x = excl


class Sched:
    COMPUTE = ("pe", "act", "dve", "pool")
    NDMASEM = 12

    def __init__(self, nc):
        self.nc = nc
        self.engobj = {"pe": nc.tensor, "act": nc.scalar, "dve": nc.vector, "pool": nc.gpsimd, "sp": nc.sync}
        self.prog = {k: [] for k in self.engobj}
        self.sems = {}
        self.cnt = {}
        self.seen = {k: {} for k in self.engobj}
        self._semctx = []
        for k in self.COMPUTE:
            self._mksem(k)
        self.dmasems = {}
        self.dmarr = {}
        for q in ("sp", "act", "pool"):
            self.dmasems[q] = [self._mksem(f"d_{q}_{i}") for i in range(self.NDMASEM)]
            self.dmarr[q] = 0
        self.ninst = 0

    def _mksem(self, key):
        ctx = self.nc.semaphore(key)
        h = ctx.__enter__()
        self._semctx.append(ctx)
        self.sems[key] = h
        self.cnt[key] = 0
        return key

    def _deps(self, stream, own_key, reads, writes):
        need = {}

        def add(kv):
            if kv is None:
                return
            k, v = kv
            if k == own_key and k == "pe":
                return
            if need.get(k, 0) < v:
                need[k] = v
        for t in reads:
            add(t.w)
            if t.x:
                for k, v in t.r.items():
                    if k != own_key:
                        add((k, v))
        for t in writes:
            add(t.w)
            for k, v in t.r.items():
                add((k, v))
        out = []
        seen = self.seen[stream]
        for k, v in need.items():
            if seen.get(k, 0) < v:
                seen[k] = v
                out.append((k, v))
        return out

    def _commit(self, key, val, reads, writes):
        for t in reads:
            if t.r.get(key, 0) < val:
                t.r[key] = val
        for t in writes:
            t.w = (key, val)
            t.r = {}

    def op(self, eng, fn, reads=(), writes=()):
        waits = self._deps(eng, eng, reads, writes)
        self.cnt[eng] += 1
        val = self.cnt[eng]
        self._commit(eng, val, reads, writes)
        sem = self.sems[eng]
        sems = self.sems

        def thunk(e, waits=waits, fn=fn, sem=sem):
            for k, v in waits:
                e.wait_ge(sems[k], v)
            fn(e).then_inc(sem, 1)
        self.prog[eng].append(thunk)
        self.ninst += 1

    def dma(self, q, out, in_, reads=(), writes=(), **kw):
        i = self.dmarr[q]
        self.dmarr[q] = (i + 1) % self.NDMASEM
        key = self.dmasems[q][i]
        waits = self._deps(q, key, reads, writes)
        prev = self.cnt[key]
        if prev > 0 and self.seen[q].get(key, 0) < prev:
            self.seen[q][key] = prev
            waits.append((key, prev))
        self.cnt[key] += 16
        val = self.cnt[key]
        self._commit(key, val, reads, writes)
        sem = self.sems[key]
        sems = self.sems

        def thunk(e, waits=waits, sem=sem, out=out, in_=in_, kw=kw):
            for k, v in waits:
                e.wait_ge(sems[k], v)
            e.dma_start(out=out, in_=in_, **kw).then_inc(sem, 16)
        self.prog[q].append(thunk)
        self.ninst += 1

    def barrier(self):
        snap = {k: v for k, v in self.cnt.items() if v > 0}
        sems = self.sems
        for stream in self.prog:
            seen = self.seen[stream]
            waits = []
            for k, v in snap.items():
                if k == stream:
                    continue
                if seen.get(k, 0) < v:
                    seen[k] = v
                    waits.append((k, v))
            if waits:
                def thunk(e, waits=waits):
                    for k, v in waits:
                        e.wait_ge(sems[k], v)
                self.prog[stream].append(thunk)

    def finish(self):
        nc = self.nc
        finals = [(k, v) for k, v in self.cnt.items() if v > 0]
        sems = self.sems
        prog = self.prog
        with nc.Block() as block:
            @block.tensor
            def _(e):
                for t in prog["pe"]:
                    t(e)

            @block.scalar
            def _(e):
                for t in prog["act"]:
                    t(e)

            @block.vector
            def _(e):
                for t in prog["dve"]:
                    t(e)

            @block.gpsimd
            def _(e):
                for t in prog["pool"]:
                    t(e)

            @block.sync
            def _(e):
                for t in prog["sp"]:
                    t(e)
                for k, v in finals:
                    e.wait_ge(sems[k], v)
        for ctx in reversed(self._semctx):
            ctx.__exit__(None, None, None)


WNAMES = ["w_in", "p_a", "p_b", "p_c", "p_d", "w_mix_out", "w_mem_q", "w_mem_kv", "w_mem_o", "w_ffn_in",
          "w_ffn_out", "rwkv_w2", "rwkv_a2", "rwkv_g2", "rwkv_v1", "rwkv_v2"]
VNAMES = ["mix_norm_g", "diff_lam", "diff_norm_g", "hgrn_lb_logits", "hgrn_norm_g", "rwkv_mu", "rwkv_w0", "rwkv_a0",
          "rwkv_k_k", "rwkv_k_a", "rwkv_r_k", "rwkv_lnx_g", "rwkv_lnx_b", "rwkv_v0", "mem_q_norm_g", "mem_kv_norm_g",
          "ffn_norm_g", "ffn_conv_w", "ffn_conv_b", "final_norm_g"]
SHAPES = {
    "mix_norm_g": (2, 1024), "w_in": (2, 1024, 9984), "diff_lam": (2, 4, 64), "diff_norm_g": (2, 128),
    "hgrn_lb_logits": (2, 256), "hgrn_norm_g": (2, 64), "rwkv_mu": (2, 1024), "rwkv_w0": (2, 256),
    "rwkv_w2": (2, 64, 256), "rwkv_a0": (2, 256), "rwkv_a2": (2, 64, 256), "rwkv_g2": (2, 128, 256),
    "rwkv_k_k": (2, 256), "rwkv_k_a": (2, 256), "rwkv_r_k": (2, 4, 64), "rwkv_lnx_g": (2, 256),
    "rwkv_lnx_b": (2, 256), "rwkv_v0": (1, 256), "rwkv_v1": (1, 256, 32), "rwkv_v2": (1, 32, 256),
    "p_a": (2, 256, 1024), "p_b": (2, 512, 1024), "p_c": (2, 256, 1024), "p_d": (2, 256, 1024),
    "w_mix_out": (2, 1024, 1024), "mem_q_norm_g": (2, 1024), "mem_kv_norm_g": (2, 1024),
    "w_mem_q": (2, 1024, 1024), "w_mem_kv": (2, 1024, 2048), "w_mem_o": (2, 1024, 1024),
    "ffn_norm_g": (2, 1024), "w_ffn_in": (2, 1024, 5632), "ffn_conv_w": (2, 3, 5632), "ffn_conv_b": (2, 5632),
    "w_ffn_out": (2, 2816, 1024), "final_norm_g": (1024,),
}


def host_consts():
    c = {}
    c["c_ident"] = np.eye(128, dtype=np.float32)
    s = np.arange(128)[:, None]
    t = np.arange(512)[None, :]
    c["c_maskD"] = (s <= t).astype(np.float32)
    dm = np.zeros((128, 256), np.float32)
    dm[:, :128] = (s <= np.arange(128)[None, :])
    dm[:, 128:] = (s >= np.arange(128)[None, :])
    c["c_maskA"] = np.concatenate([dm, dm], axis=1)
    tt = np.arange(128)[None, :]
    si = np.concatenate([(s < tt), (s <= tt)], axis=1).astype(np.float32)
    c["c_maskSI"] = si
    c["c_maskSI2"] = np.concatenate([si, si], axis=1)
    c["c_triR"] = (s > tt).astype(np.float32)
    c["c_maskLow"] = (tt < s).astype(np.float32)
    c["c_tri"] = (s <= tt).astype(np.float32)
    c["c_tri2"] = np.concatenate([c["c_tri"], c["c_tri"]], axis=1)
    c["c_triM"] = ((s <= tt).astype(np.float32) - (s <= 63).astype(np.float32))
    c["c_o63"] = np.broadcast_to((s <= 63), (128, 128)).astype(np.float32).copy()
    c["c_ones"] = np.ones((128, 128), np.float32)
    sel = np.zeros((128, 2), np.float32)
    sel[:64, 0] = 1.0
    sel[:, 1] = 1.0
    c["c_sel"] = sel
    pm = np.zeros((128, 128), np.float32)
    for hb in (0, 64):
        for i in range(8):
            pm[hb + i + 8, hb + i] = -1.0
            pm[hb + i, hb + i + 8] = 1.0
    c["c_pm"] = pm
    invf = np.zeros((128, 1), np.float32)
    f = (500000.0 ** (-np.arange(8, dtype=np.float32) / 8)).astype(np.float32)
    for hb in (0, 64):
        invf[hb:hb + 8, 0] = f
        invf[hb + 8:hb + 16, 0] = f
    c["c_invf"] = invf
    return c


class K:
    def __init__(self, cfg):
        self.cfg = cfg
        nc = bass.Bass("TRN2", target_bir_lowering=False)
        self.nc = nc
        self.S = Sched(nc)
        self.es = contextlib.ExitStack()
        self.uid = 0

    def sb(self, es, shape, dt, name=None):
        self.uid += 1
        return es.enter_context(self.nc.sbuf_tensor(f"{name or 't'}_{self.uid}", list(shape), dt))

    def dram(self, name, shape, dt, kind="Internal"):
        return self.nc.dram_tensor(name, list(shape), dt, kind=kind).ap()

    def mm(self, out, lhsT, rhs, start, stop, r, w):
        self.S.op("pe", lambda e: e.matmul(out, lhsT=lhsT, rhs=rhs, start=start, stop=stop), reads=r, writes=w)

    def tp(self, out, in_, r, w):
        idt = self.ident_b
        self.S.op("pe", lambda e: e.transpose(out, in_, idt), reads=list(r) + [self.t_const], writes=w)

    def act(self, out, in_, func, r, w, **kw):
        self.S.op("act", lambda e: e.activation(out=out, in_=in_, func=func, **kw), reads=r, writes=w)

    def tt(self, eng, out, in0, in1, op, r, w):
        self.S.op(eng, lambda e: e.tensor_tensor(out=out, in0=in0, in1=in1, op=op), reads=r, writes=w)

    def ts(self, eng, out, in0, s1, s2, op0, op1, r, w):
        self.S.op(eng, lambda e: e.tensor_scalar(out=out, in0=in0, scalar1=s1, scalar2=s2, op0=op0, op1=op1),
                  reads=r, writes=w)

    def stt(self, out, in0, scalar, in1, op0, op1, r, w):
        self.S.op("dve", lambda e: e.scalar_tensor_tensor(out=out, in0=in0, scalar=scalar, in1=in1, op0=op0, op1=op1),
                  reads=r, writes=w)

    def cp(self, eng, out, in_, r, w):
        if eng == "act":
            self.S.op("act", lambda e: e.copy(out=out, in_=in_), reads=r, writes=w)
        else:
            self.S.op(eng, lambda e: e.tensor_copy(out=out, in_=in_), reads=r, writes=w)

    def recip(self, out, in_, r, w):
        self.S.op("dve", lambda e: e.reciprocal(out=out, in_=in_), reads=r, writes=w)

    def memset(self, ap, val, w):
        self.S.op("dve", lambda e: e.memset(ap, val), reads=(), writes=w)

    def ps(self):
        i = self.ps_i
        self.ps_i = (i + 1) % len(self.ps_f)
        return self.ps_f[i], self.ps_ft[i]

    def psb(self):
        i = self.psb_i
        self.psb_i = (i + 1) % len(self.ps_b)
        return self.ps_b[i], self.ps_bt[i]


def build(cfg):
    k = K(cfg)
    nc, S = k.nc, k.S
    NS = cfg.get("nseq", 2)
    NL = cfg.get("layers", 2)
    dbg = cfg.get("debug")
    x_in = k.dram("x", [NS, S_LEN, D], F32, "ExternalInput")
    mem_in = k.dram("mem", [NS, MEM, D], F32, "ExternalInput")
    pos_in = k.dram("positions", [NS, S_LEN], I32, "ExternalInput")
    wf = {n: k.dram(n, SHAPES[n], F32, "ExternalInput") for n in WNAMES}
    vf = {n: k.dram(n, SHAPES[n], F32, "ExternalInput") for n in VNAMES}
    consts = host_consts()
    cf = {n: k.dram(n, v.shape, F32, "ExternalInput") for n, v in consts.items()}
    out = k.dram("out", [NS, S_LEN, D], F32, "ExternalOutput")
    xbuf = k.dram("xbuf", [NS, S_LEN, D], F32)
    yT = k.dram("yT", [1280, S_LEN], BF16)
    vfirst = k.dram("vfirst", [NS, S_LEN, 256], F32)
    wb = {n: k.dram(n + "_bf", SHAPES[n], BF16) for n in WNAMES}
    yin = None
    if cfg.get("yin"):
        yin = k.dram("yin", [1280, S_LEN], F32, "ExternalInput")
    dbg_out = None
    if dbg:
        dbg_out = k.dram("dbg", dbg["shape"], F32, "ExternalOutput")

    t_x = [[Tr() for _ in range(NT)] for _ in range(NS)]
    t_yT = [[Tr() for _ in range(8)] for _ in range(10)]
    t_vf = [[Tr() for _ in range(NT)] for _ in range(NS)]
    t_w = {}

    with contextlib.ExitStack() as g:
        k.ps_f, k.ps_ft, k.ps_b, k.ps_bt = [], [], [], []
        for i in range(6):
            k.ps_f.append(g.enter_context(nc.psum_tensor(f"psf{i}", [128, 512], F32)))
            k.ps_ft.append(Tr(True))
        for i in range(2):
            k.ps_b.append(g.enter_context(nc.psum_tensor(f"psb{i}", [128, 1024], BF16)))
            k.ps_bt.append(Tr(True))
        k.ps_i = 0
        k.psb_i = 0
        k.t_const = Tr()
        ident_b = k.sb(g, [128, 128], BF16, "ident")
        k.ident_b = ident_b[:]
        S.dma("pool", ident_b[:], cf["c_ident"], writes=[k.t_const])
        cs = {}
        for n, dt in [("c_maskD", BF16), ("c_maskA", BF16), ("c_maskSI", F32), ("c_maskSI2", F32), ("c_triR", F32), ("c_maskLow", F32), ("c_tri", F32), ("c_tri2", F32),
                      ("c_triM", F32), ("c_o63", F32), ("c_ones", F32), ("c_sel", F32), ("c_pm", BF16),
                      ("c_invf", F32)]:
            t = k.sb(g, consts[n].shape, dt, n)
            S.dma("pool" if dt == BF16 else "sp", t[:], cf[n], writes=[k.t_const])
            cs[n] = t
        ones_b = k.sb(g, [128, 128], BF16, "ones_b")
        S.dma("pool", ones_b[:], cf["c_ones"], writes=[k.t_const])
        cs["ones_b"] = ones_b
        k.cs = cs
        for n in WNAMES:
            tot = int(np.prod(SHAPES[n]))
            rows = tot // 2048
            src = wf[n].flatten().rearrange("(r c) -> r c", c=2048) if len(SHAPES[n]) > 1 else None
            nd = len(SHAPES[n])
            letters = "abc"[:nd]
            flat_s = wf[n].rearrange(f"{' '.join(letters)} -> ({' '.join(letters)})").rearrange("(r c) -> r c", c=2048)
            flat_d = wb[n].rearrange(f"{' '.join(letters)} -> ({' '.join(letters)})").rearrange("(r c) -> r c", c=2048)
            t_w[n] = []
            for r0 in range(0, rows, 512):
                r1 = min(rows, r0 + 512)
                tr_ = Tr()
                S.dma("pool", flat_d[r0:r1, :], flat_s[r0:r1, :], writes=[tr_])
                t_w[n].append(tr_)
        k.wb, k.t_w, k.vf = wb, t_w, vf

        ctx = dict(k=k, x_in=x_in, mem_in=mem_in, pos_in=pos_in, out=out, xbuf=xbuf, yT=yT, vfirst=vfirst,
                   t_x=t_x, t_yT=t_yT, t_vf=t_vf, yin=yin, dbg=dbg, dbg_out=dbg_out, cfg=cfg)
        phases = cfg.get("phases", "ABCDGMF")
        if yin is None and not any(c in phases for c in "ABCD"):
            with contextlib.ExitStack() as zx:
                zt = k.sb(zx, [128, 512], BF16, "zt")
                t_z = Tr()
                k.memset(zt[:], 0.0, [t_z])
                for rc in range(10):
                    for gq in range(8):
                        S.dma("sp", yT[rc * 128:(rc + 1) * 128, gq * 512:(gq + 1) * 512], zt[:], reads=[t_z],
                              writes=[t_yT[rc][gq]])
                S.barrier()
        for s in range(NS):
            for l in range(NL):
                xsrc = x_in if l == 0 else xbuf
                with contextlib.ExitStack() as mx:
                    hT = k.sb(mx, [128, 8, HOFF + S_LEN], BF16, "hT")
                    t_hT = [Tr() for _ in range(NT)]
                    t_h0 = Tr()
                    k.memset(hT[:, :, 0:HOFF], 0.0, [t_h0])
                    phase_norm_T(ctx, s, l, xsrc, hT, t_hT)
                    if yin is not None:
                        phase_yin(ctx)
                    else:
                        with contextlib.ExitStack() as rx:
                            tabs = phase_rope(ctx, rx, s) if ("A" in phases or "B" in phases) else None
                            if "A" in phases:
                                phase_A(ctx, s, l, hT, t_hT, tabs)
                            if "B" in phases:
                                phase_B(ctx, s, l, hT, t_hT, tabs)
                        if "C" in phases:
                            phase_C(ctx, s, l, hT, t_hT)
                        if "D" in phases:
                            phase_D(ctx, s, l, hT, t_hT, t_h0)
                    if dbg and dbg.get("what") == "yT" and dbg.get("l", 0) == l and s == 0:
                        for rc in dbg.get("rcs", range(10)):
                            for gq in range(8):
                                S.dma("pool", dbg_out[rc * 128:(rc + 1) * 128, gq * 512:(gq + 1) * 512],
                                      yT[rc * 128:(rc + 1) * 128, gq * 512:(gq + 1) * 512], reads=[t_yT[rc][gq]])
                    if "G" in phases:
                        phase_G(ctx, s, l, xsrc, hT, t_hT)
                if "M" in phases:
                    phase_M(ctx, s, l)
                if "F" in phases:
                    phase_F(ctx, s, l)
            if cfg.get("final", True):
                phase_final(ctx, s)
        S.finish()
    return nc


def load_bc(k, es, src_row_ap, n, name="bc"):
    t = k.sb(es, [128, n], F32, name)
    tr_ = Tr()
    k.S.dma("sp", t[:], src_row_ap.partition_broadcast(128), writes=[tr_])
    return t, tr_


def rms_to_bf(k, xt, t_xt, gbc, t_g, obf, t_o, ss, t_ss):
    k.act(obf[:], xt[:], AF.Square, [t_xt], [t_o, t_ss], accum_out=ss[:, 0:1])
    k.ts("dve", ss[:, 1:2], ss[:, 0:1], 1.0 / D, EPS, ALU.mult, ALU.add, [t_ss], [t_ss])
    k.act(ss[:, 2:3], ss[:, 1:2], AF.Sqrt, [t_ss], [t_ss])
    k.recip(ss[:, 3:4], ss[:, 2:3], [t_ss], [t_ss])
    k.stt(obf[:], xt[:], ss[:, 3:4], gbc[:], ALU.mult, ALU.mult, [t_xt, t_ss, t_g], [t_o])


def phase_norm_T(ctx, s, l, xsrc, hT, t_hT):
    k = ctx["k"]
    S = k.S
    with contextlib.ExitStack() as es:
        gbc, t_g = load_bc(k, es, k.vf["mix_norm_g"][l], D, "gmix")
        xts = [(k.sb(es, [128, D], F32, "xt"), Tr()) for _ in range(2)]
        obs = [(k.sb(es, [128, D], BF16, "ob"), Tr()) for _ in range(2)]
        sss = [(k.sb(es, [128, 4], F32, "ss"), Tr()) for _ in range(2)]
        for tt_ in range(NT):
            xt, t_xt = xts[tt_ % 2]
            ob, t_o = obs[tt_ % 2]
            ss, t_ss = sss[tt_ % 2]
            rd = [ctx["t_x"][s][tt_]] if l > 0 else []
            S.dma("sp", xt[:], xsrc[s, tt_ * 128:(tt_ + 1) * 128, :], reads=rd, writes=[t_xt])
            rms_to_bf(k, xt, t_xt, gbc, t_g, ob, t_o, ss, t_ss)
            pb, t_pb = k.psb()
            for j in range(8):
                k.tp(pb[:, j * 128:(j + 1) * 128], ob[:, j * 128:(j + 1) * 128], [t_o], [t_pb])
            k.cp("act" if tt_ % 2 else "dve", hT[:, :, HOFF + tt_ * 128:HOFF + (tt_ + 1) * 128],
                 pb[:].rearrange("p (j t) -> p j t", j=8), [t_pb], [t_hT[tt_]])
        S.barrier()


def phase_yin(ctx):
    k = ctx["k"]
    for rc in range(10):
        for gq in range(8):
            k.S.dma("pool", ctx["yT"][rc * 128:(rc + 1) * 128, gq * 512:(gq + 1) * 512],
                    ctx["yin"][rc * 128:(rc + 1) * 128, gq * 512:(gq + 1) * 512], writes=[ctx["t_yT"][rc][gq]])


def load_w(k, es, name, l, r0, nr, c0, ncols, tag="w"):
    kc = max(1, nr // 128)
    p = min(128, nr)
    t = k.sb(es, [p, kc, ncols], BF16, tag)
    tr_ = Tr()
    src = k.wb[name][l, r0:r0 + nr, c0:c0 + ncols].rearrange("(kc p) n -> p kc n", p=p)
    k.S.dma("sp", t[:], src, reads=k.t_w[name], writes=[tr_])
    return t, tr_


def load_w_into(k, t, tr_, name, l, r0, nr, c0, ncols):
    p = min(128, nr)
    src = k.wb[name][l, r0:r0 + nr, c0:c0 + ncols].rearrange("(kc p) n -> p kc n", p=p)
    k.S.dma("sp", t[:], src, reads=k.t_w[name], writes=[tr_])


def phase_G(ctx, s, l, xsrc, hT, t_hT):
    k = ctx["k"]
    S = k.S
    yT, t_yT = ctx["yT"], ctx["t_yT"]
    with contextlib.ExitStack() as es:
        pcat = k.sb(es, [128, 10, D], BF16, "pcat")
        t_p = Tr()
        for nm, c0, n in (("p_a", 0, 2), ("p_b", 2, 4), ("p_c", 6, 2), ("p_d", 8, 2)):
            S.dma("sp", pcat[:, c0:c0 + n, :], k.wb[nm][l].rearrange("(kc p) n -> p kc n", p=128),
                  reads=k.t_w[nm], writes=[t_p])
        wmo, t_wmo = load_w(k, es, "w_mix_out", l, 0, D, 0, D, "wmo")
        wgs = [(k.sb(es, [128, 8, 4, 128], BF16, "wg"), Tr()) for _ in range(2)]
        yts = [(k.sb(es, [128, 10, 512], BF16, "yt"), Tr()) for _ in range(2)]
        mT = k.sb(es, [128, 8, 512], BF16, "mT")
        t_mT = Tr()
        accs = [(k.sb(es, [128, 512], F32, "acc"), Tr()) for _ in range(2)]
        sigs = [(k.sb(es, [128, 512], F32, "sig"), Tr()) for _ in range(3)]
        xts = [(k.sb(es, [128, D], F32, "xg"), Tr()) for _ in range(2)]
        branches = ((0, 2), (2, 4), (6, 2), (8, 2))
        it = 0
        for gq in range(8):
            yt, t_yt = yts[gq % 2]
            for rc in range(10):
                S.dma("sp", yt[:, rc, :], yT[rc * 128:(rc + 1) * 128, gq * 512:(gq + 1) * 512],
                      reads=[t_yT[rc][gq]], writes=[t_yt])
            hsl = slice(HOFF + gq * 512, HOFF + (gq + 1) * 512)
            rh = t_hT[gq * 4:(gq + 1) * 4]
            for oc in range(8):
                wg, t_wg = wgs[it % 2]
                it += 1
                for bi in range(4):
                    c0 = OFF_G + bi * D + oc * 128
                    S.dma("sp", wg[:, :, bi, :], k.wb["w_in"][l, :, c0:c0 + 128].rearrange("(kc p) n -> p kc n", p=128),
                          reads=k.t_w["w_in"], writes=[t_wg])
                acc, t_acc = accs[oc % 2]
                for bi, (c0, n) in enumerate(branches):
                    pg, t_pg = k.ps()
                    for kc in range(8):
                        k.mm(pg[:], wg[:, kc, bi, :], hT[:, kc, hsl], kc == 0, kc == 7, [t_wg] + rh, [t_pg])
                    sg, t_sg = sigs[(oc * 4 + bi) % 3]
                    k.act(sg[:], pg[:], AF.Sigmoid, [t_pg], [t_sg])
                    py, t_py = k.ps()
                    for j in range(n):
                        k.mm(py[:], pcat[:, c0 + j, oc * 128:(oc + 1) * 128], yt[:, c0 + j, :], j == 0, j == n - 1,
                             [t_p, t_yt], [t_py])
                    if bi == 0:
                        k.tt("dve", acc[:], py[:], sg[:], ALU.mult, [t_py, t_sg], [t_acc])
                    else:
                        k.tt("dve", sg[:], py[:], sg[:], ALU.mult, [t_py, t_sg], [t_sg])
                        if bi < 3:
                            k.tt("pool", acc[:], acc[:], sg[:], ALU.add, [t_acc, t_sg], [t_acc])
                        else:
                            k.tt("pool", mT[:, oc, :], acc[:], sg[:], ALU.add, [t_acc, t_sg], [t_mT])
            for q in range(4):
                tt_ = gq * 4 + q
                xt, t_xt = xts[q % 2]
                rd = [ctx["t_x"][s][tt_]] if l > 0 else []
                S.dma("sp", xt[:], xsrc[s, tt_ * 128:(tt_ + 1) * 128, :], reads=rd, writes=[t_xt])
                for cg in range(2):
                    po, t_po = k.ps()
                    for oc in range(8):
                        k.mm(po[:], mT[:, oc, q * 128:(q + 1) * 128], wmo[:, oc, cg * 512:(cg + 1) * 512], oc == 0,
                             oc == 7, [t_mT, t_wmo], [t_po])
                    k.tt("dve", xt[:, cg * 512:(cg + 1) * 512], xt[:, cg * 512:(cg + 1) * 512], po[:], ALU.add,
                         [t_xt, t_po], [t_xt])
                S.dma("sp", ctx["xbuf"][s, tt_ * 128:(tt_ + 1) * 128, :], xt[:], reads=[t_xt],
                      writes=[ctx["t_x"][s][tt_]])
        S.barrier()


def phase_M(ctx, s, l):
    k = ctx["k"]
    S = k.S
    t_x = ctx["t_x"][s]
    xbuf = ctx["xbuf"]
    sc = 256 ** -0.5
    with contextlib.ExitStack() as es:
        kmT = k.sb(es, [128, 8, MEM], BF16, "kmT")
        vm = k.sb(es, [128, 2, D], BF16, "vm")
        t_km, t_vm = Tr(), Tr()
        with contextlib.ExitStack() as e2:
            gkv, t_gkv = load_bc(k, e2, k.vf["mem_kv_norm_g"][l], D, "gkv")
            mnT = k.sb(e2, [128, 8, MEM], BF16, "mnT")
            t_mn = Tr()
            xt = k.sb(e2, [128, D], F32, "mx")
            ob = k.sb(e2, [128, D], BF16, "mob")
            ss = k.sb(e2, [128, 4], F32, "mss")
            t_xt, t_o, t_ss = Tr(), Tr(), Tr()
            for mt in range(2):
                S.dma("sp", xt[:], ctx["mem_in"][s, mt * 128:(mt + 1) * 128, :], writes=[t_xt])
                rms_to_bf(k, xt, t_xt, gkv, t_gkv, ob, t_o, ss, t_ss)
                pb, t_pb = k.psb()
                for j in range(8):
                    k.tp(pb[:, j * 128:(j + 1) * 128], ob[:, j * 128:(j + 1) * 128], [t_o], [t_pb])
                k.cp("dve", mnT[:, :, mt * 128:(mt + 1) * 128], pb[:].rearrange("p (j t) -> p j t", j=8), [t_pb], [t_mn])
            for half in range(4):
                wkv, t_wkv = load_w(k, e2, "w_mem_kv", l, 0, D, half * 512, 512, "wkv")
                if half < 2:
                    for fc in range(4):
                        p_, t_p = k.ps()
                        for kc in range(8):
                            k.mm(p_[:, 0:MEM], wkv[:, kc, fc * 128:(fc + 1) * 128], mnT[:, kc, :], kc == 0, kc == 7,
                                 [t_wkv, t_mn], [t_p])
                        k.cp("act", kmT[:, half * 4 + fc, :], p_[:, 0:MEM], [t_p], [t_km])
                else:
                    for mt in range(2):
                        p_, t_p = k.ps()
                        for kc in range(8):
                            k.mm(p_[:], mnT[:, kc, mt * 128:(mt + 1) * 128], wkv[:, kc, :], kc == 0, kc == 7,
                                 [t_wkv, t_mn], [t_p])
                        k.cp("act", vm[:, mt, (half - 2) * 512:(half - 1) * 512], p_[:], [t_p], [t_vm])
            S.barrier()
        gq_, t_gq = load_bc(k, es, k.vf["mem_q_norm_g"][l], D, "gq")
        wq, t_wq = load_w(k, es, "w_mem_q", l, 0, D, 0, D, "wq")
        wo, t_wo = load_w(k, es, "w_mem_o", l, 0, D, 0, D, "wo")
        xts = [(k.sb(es, [128, D], F32, "xq"), Tr()) for _ in range(4)]
        ob = k.sb(es, [128, D], BF16, "qob")
        ss = k.sb(es, [128, 4], F32, "qss")
        t_o, t_ss = Tr(), Tr()
        xnT = k.sb(es, [128, 8, 512], BF16, "xnT")
        t_xn = Tr()
        qT = k.sb(es, [128, 8, 512], BF16, "qT")
        t_qT = Tr()
        ETs = [(k.sb(es, [128, 2, 512], BF16, "ET"), Tr()) for _ in range(2)]
        oT = k.sb(es, [128, 8, 512], BF16, "oT")
        t_oT = Tr()
        rden = k.sb(es, [128, 512], F32, "rden")
        t_rd = Tr()
        for gq in range(8):
            for q in range(4):
                tt_ = gq * 4 + q
                xt, t_xt = xts[q]
                S.dma("sp", xt[:], xbuf[s, tt_ * 128:(tt_ + 1) * 128, :], reads=[t_x[tt_]], writes=[t_xt])
                rms_to_bf(k, xt, t_xt, gq_, t_gq, ob, t_o, ss, t_ss)
                pb, t_pb = k.psb()
                for j in range(8):
                    k.tp(pb[:, j * 128:(j + 1) * 128], ob[:, j * 128:(j + 1) * 128], [t_o], [t_pb])
                k.cp("dve", xnT[:, :, q * 128:(q + 1) * 128], pb[:].rearrange("p (j t) -> p j t", j=8), [t_pb], [t_xn])
            for fc in range(8):
                p_, t_p = k.ps()
                for kc in range(8):
                    k.mm(p_[:], wq[:, kc, fc * 128:(fc + 1) * 128], xnT[:, kc, :], kc == 0, kc == 7, [t_wq, t_xn], [t_p])
                k.cp("act", qT[:, fc, :], p_[:], [t_p], [t_qT])
            for h in range(4):
                ET, t_ET = ETs[h % 2]
                for mt in range(2):
                    p_, t_p = k.ps()
                    for j in range(2):
                        k.mm(p_[:], kmT[:, h * 2 + j, mt * 128:(mt + 1) * 128], qT[:, h * 2 + j, :], j == 0, j == 1,
                             [t_km, t_qT], [t_p])
                    k.act(ET[:, mt, :], p_[:], AF.Exp, [t_p], [t_ET], scale=sc)
                pd, t_pd = k.ps()
                for mt in range(2):
                    k.mm(pd[:], k.cs["ones_b"][:], ET[:, mt, :], mt == 0, mt == 1, [t_ET, k.t_const], [t_pd])
                k.recip(rden[:], pd[:], [t_pd], [t_rd])
                for j in range(2):
                    p_, t_p = k.ps()
                    for mt in range(2):
                        k.mm(p_[:], vm[:, mt, (h * 2 + j) * 128:(h * 2 + j + 1) * 128], ET[:, mt, :], mt == 0, mt == 1,
                             [t_vm, t_ET], [t_p])
                    k.tt("dve", oT[:, h * 2 + j, :], p_[:], rden[:], ALU.mult, [t_p, t_rd], [t_oT])
            for q in range(4):
                tt_ = gq * 4 + q
                xt, t_xt = xts[q]
                for cg in range(2):
                    po, t_po = k.ps()
                    for oc in range(8):
                        k.mm(po[:], oT[:, oc, q * 128:(q + 1) * 128], wo[:, oc, cg * 512:(cg + 1) * 512], oc == 0,
                             oc == 7, [t_oT, t_wo], [t_po])
                    k.tt("dve", xt[:, cg * 512:(cg + 1) * 512], xt[:, cg * 512:(cg + 1) * 512], po[:], ALU.add,
                         [t_xt, t_po], [t_xt])
                S.dma("sp", xbuf[s, tt_ * 128:(tt_ + 1) * 128, :], xt[:], reads=[t_xt], writes=[t_x[tt_]])
        S.barrier()


def phase_F(ctx, s, l):
    k = ctx["k"]
    S = k.S
    t_x = ctx["t_x"][s]
    xbuf = ctx["xbuf"]
    NC = 2 * DFF // 128
    with contextlib.ExitStack() as es:
        gf, t_gf = load_bc(k, es, k.vf["ffn_norm_g"][l], D, "gf")
        w1, t_w1 = load_w(k, es, "w_ffn_in", l, 0, D, 0, 2 * DFF, "wf1")
        w2, t_w2 = load_w(k, es, "w_ffn_out", l, 0, DFF, 0, D, "wf2")
        cw = k.sb(es, [128, 4, NC], F32, "cw")
        t_cw = Tr()
        for j in range(3):
            S.dma("sp", cw[:, j, :], k.vf["ffn_conv_w"][l, j].rearrange("(c p) -> p c", p=128), writes=[t_cw],
                  allow_slow_non_contiguous=True)
        S.dma("sp", cw[:, 3, :], k.vf["ffn_conv_b"][l].rearrange("(c p) -> p c", p=128), writes=[t_cw],
              allow_slow_non_contiguous=True)
        halo = k.sb(es, [128, NC, 2], F32, "halo")
        t_halo = [Tr() for _ in range(NC)]
        k.memset(halo[:], 0.0, t_halo)
        xts = [(k.sb(es, [128, D], F32, "xf"), Tr()) for _ in range(4)]
        ob = k.sb(es, [128, D], BF16, "fob")
        ss = k.sb(es, [128, 4], F32, "fss")
        t_o, t_ss = Tr(), Tr()
        xnT = k.sb(es, [128, 8, 512], BF16, "fxnT")
        t_xn = Tr()
        ues = [(k.sb(es, [128, 514], F32, "ue"), Tr()) for _ in range(2)]
        c0s = [(k.sb(es, [128, 512], F32, "c0"), Tr()) for _ in range(2)]
        sgs = [(k.sb(es, [128, 512], F32, "sgf"), Tr()) for _ in range(2)]
        aT = k.sb(es, [128, DFF // 128, 512], BF16, "aT")
        t_aT = Tr()
        for gq in range(8):
            for q in range(4):
                tt_ = gq * 4 + q
                xt, t_xt = xts[q]
                S.dma("sp", xt[:], xbuf[s, tt_ * 128:(tt_ + 1) * 128, :], reads=[t_x[tt_]], writes=[t_xt])
                rms_to_bf(k, xt, t_xt, gf, t_gf, ob, t_o, ss, t_ss)
                pb, t_pb = k.psb()
                for j in range(8):
                    k.tp(pb[:, j * 128:(j + 1) * 128], ob[:, j * 128:(j + 1) * 128], [t_o], [t_pb])
                k.cp("dve", xnT[:, :, q * 128:(q + 1) * 128], pb[:].rearrange("p (j t) -> p j t", j=8), [t_pb], [t_xn])
            it = 0
            for j in range(DFF // 128):
                res = []
                for c in (j, j + 22):
                    p_, t_p = k.ps()
                    for kc in range(8):
                        k.mm(p_[:], w1[:, kc, c * 128:(c + 1) * 128], xnT[:, kc, :], kc == 0, kc == 7, [t_w1, t_xn], [t_p])
                    ue, t_ue = ues[it % 2]
                    c0, t_c0 = c0s[it % 2]
                    it += 1
                    k.cp("pool", ue[:, 0:2], halo[:, c, :], [t_halo[c]], [t_ue])
                    k.cp("act", ue[:, 2:514], p_[:], [t_p], [t_ue])
                    k.act(c0[:], p_[:], AF.Identity, [t_p, t_cw], [t_c0], scale=cw[:, 2, c:c + 1], bias=cw[:, 3, c:c + 1])
                    k.cp("pool", halo[:, c, :], ue[:, 512:514], [t_ue], [t_halo[c]])
                    k.stt(c0[:], ue[:, 1:513], cw[:, 1, c:c + 1], c0[:], ALU.mult, ALU.add, [t_ue, t_cw, t_c0], [t_c0])
                    k.stt(c0[:], ue[:, 0:512], cw[:, 0, c:c + 1], c0[:], ALU.mult, ALU.add, [t_ue, t_cw, t_c0], [t_c0])
                    res.append((c0, t_c0))
                sg, t_sg = sgs[j % 2]
                k.act(sg[:], res[0][0][:], AF.Silu, [res[0][1]], [t_sg])
                k.tt("pool", aT[:, j, :], sg[:], res[1][0][:], ALU.mult, [t_sg, res[1][1]], [t_aT])
            for q in range(4):
                tt_ = gq * 4 + q
                xt, t_xt = xts[q]
                for cg in range(2):
                    po, t_po = k.ps()
                    for j in range(DFF // 128):
                        k.mm(po[:], aT[:, j, q * 128:(q + 1) * 128], w2[:, j, cg * 512:(cg + 1) * 512], j == 0,
                             j == DFF // 128 - 1, [t_aT, t_w2], [t_po])
                    k.tt("dve", xt[:, cg * 512:(cg + 1) * 512], xt[:, cg * 512:(cg + 1) * 512], po[:], ALU.add,
                         [t_xt, t_po], [t_xt])
                S.dma("sp", xbuf[s, tt_ * 128:(tt_ + 1) * 128, :], xt[:], reads=[t_xt], writes=[t_x[tt_]])
        S.barrier()


def phase_final(ctx, s):
    k = ctx["k"]
    S = k.S
    with contextlib.ExitStack() as es:
        gbc, t_g = load_bc(k, es, k.vf["final_norm_g"], D, "gfin")
        xts = [(k.sb(es, [128, D], F32, "xo"), Tr()) for _ in range(2)]
        junk = k.sb(es, [128, D], BF16, "junk")
        t_j = Tr()
        sss = [(k.sb(es, [128, 4], F32, "oss"), Tr()) for _ in range(2)]
        for tt_ in range(NT):
            xt, t_xt = xts[tt_ % 2]
            ss, t_ss = sss[tt_ % 2]
            S.dma("sp", xt[:], ctx["xbuf"][s, tt_ * 128:(tt_ + 1) * 128, :], reads=[ctx["t_x"][s][tt_]], writes=[t_xt])
            k.act(junk[:], xt[:], AF.Square, [t_xt], [t_j, t_ss], accum_out=ss[:, 0:1])
            k.ts("dve", ss[:, 1:2], ss[:, 0:1], 1.0 / D, EPS, ALU.mult, ALU.add, [t_ss], [t_ss])
            k.act(ss[:, 2:3], ss[:, 1:2], AF.Sqrt, [t_ss], [t_ss])
            k.recip(ss[:, 3:4], ss[:, 2:3], [t_ss], [t_ss])
            k.stt(xt[:], xt[:], ss[:, 3:4], gbc[:], ALU.mult, ALU.mult, [t_xt, t_ss, t_g], [t_xt])
            S.dma("sp", ctx["out"][s, tt_ * 128:(tt_ + 1) * 128, :], xt[:], reads=[t_xt])
        S.barrier()


def sl(st, n, step):
    return slice(st, st + step * (n - 1) + 1, step)


def phase_rope(ctx, es, s):
    k = ctx["k"]
    S = k.S
    cosT = k.sb(es, [128, S_LEN], F32, "cosT")
    sinT = k.sb(es, [128, S_LEN], F32, "sinT")
    t_tab = Tr()
    CW = 1024
    with contextlib.ExitStack() as e2:
        posi = k.sb(e2, [128, CW], I32, "posi")
        tq = k.sb(e2, [128, CW], F32, "tq")
        ki = k.sb(e2, [128, CW], I32, "ki")
        kf = k.sb(e2, [128, CW], F32, "kf")
        fr = k.sb(e2, [128, CW], F32, "fr")
        aa = k.sb(e2, [128, CW], F32, "aa")
        t_ = Tr()
        invf = k.cs["c_invf"]
        for c in range(S_LEN // CW):
            cols = slice(c * CW, (c + 1) * CW)
            S.dma("sp", posi[:], ctx["pos_in"][s, cols].partition_broadcast(128), writes=[t_])
            k.cp("dve", tq[:], posi[:], [t_], [t_])
            k.ts("dve", tq[:], tq[:], invf[:, 0:1], 1.0 / (2 * math.pi), ALU.mult, ALU.mult, [t_, k.t_const], [t_])
            k.cp("dve", ki[:], tq[:], [t_], [t_])
            k.cp("dve", kf[:], ki[:], [t_], [t_])
            k.tt("dve", fr[:], tq[:], kf[:], ALU.subtract, [t_], [t_])
            for dst, shift in ((sinT, 0.0), (cosT, 0.25)):
                if shift:
                    k.ts("dve", fr[:], fr[:], shift, None, ALU.add, ALU.bypass, [t_], [t_])
                k.ts("dve", aa[:], fr[:], 0.5, None, ALU.is_gt, ALU.bypass, [t_], [t_])
                k.tt("dve", fr[:], fr[:], aa[:], ALU.subtract, [t_], [t_])
                k.ts("dve", aa[:], fr[:], -0.5, None, ALU.is_lt, ALU.bypass, [t_], [t_])
                k.tt("dve", fr[:], fr[:], aa[:], ALU.add, [t_], [t_])
                k.act(dst[:, cols], fr[:], AF.Sin, [t_], [t_tab], scale=6.283185)
        S.barrier()
    return cosT, sinT, t_tab


def proj_rot(k, es_bufs, hT, t_hT, w, t_w, tabs, dst, t_dst):
    cosT, sinT, t_tab = tabs
    qraw, t_qr, t1, t_t1, t2, t_t2 = es_bufs
    pm = k.cs["c_pm"]
    for gq in range(8):
        cols = slice(gq * 512, (gq + 1) * 512)
        hs = slice(HOFF + gq * 512, HOFF + (gq + 1) * 512)
        p_, t_p = k.ps()
        for kc in range(8):
            k.mm(p_[:], w[:, kc, :], hT[:, kc, hs], kc == 0, kc == 7, [t_w] + t_hT[gq * 4:(gq + 1) * 4], [t_p])
        k.cp("act", qraw[:], p_[:], [t_p], [t_qr])
        p2, t_p2 = k.ps()
        k.mm(p2[:], pm[:], qraw[:], True, True, [t_qr, k.t_const], [t_p2])
        k.tt("dve", t1[:], p_[:], cosT[:, cols], ALU.mult, [t_p, t_tab], [t_t1])
        k.tt("dve", t2[:], p2[:], sinT[:, cols], ALU.mult, [t_p2, t_tab], [t_t2])
        k.tt("pool", dst[:, cols], t1[:], t2[:], ALU.add, [t_t1, t_t2], [t_dst])


def rot_bufs(k, es):
    return (k.sb(es, [128, 512], BF16, "qraw"), Tr(), k.sb(es, [128, 512], F32, "rt1"), Tr(),
            k.sb(es, [128, 512], F32, "rt2"), Tr())


def phase_A(ctx, s, l, hT, t_hT, tabs):
    k = ctx["k"]
    S = k.S
    maskA = k.cs["c_maskA"]
    ones_b = k.cs["ones_b"]
    with contextlib.ExitStack() as es:
        rb = rot_bufs(k, es)
        qT = k.sb(es, [128, S_LEN], BF16, "aqT")
        kT = k.sb(es, [128, S_LEN], BF16, "akT")
        vs = k.sb(es, [128, 32, 128], BF16, "avs")
        acc = k.sb(es, [128, 2, S_LEN], F32, "aacc")
        yb = k.sb(es, [128, S_LEN], BF16, "ayb")
        t_q, t_k, t_v, t_acc, t_yb = Tr(), Tr(), Tr(), Tr(), Tr()
        Es = [(k.sb(es, [128, 2, 256], BF16, "aE"), Tr()) for _ in range(3)]
        ws = [(k.sb(es, [128, 8, 128], BF16, "aw"), Tr()) for _ in range(3)]
        for hp in range(2):
            for g, Dl in enumerate((1, 4, 16)):
                for j3 in range(3):
                    load_w_into(k, ws[j3][0], ws[j3][1], "w_in", l, 0, D, OFF_A + g * 768 + j3 * 256 + hp * 128, 128)
                proj_rot(k, rb, hT, t_hT, ws[0][0], ws[0][1], tabs, qT, t_q)
                proj_rot(k, rb, hT, t_hT, ws[1][0], ws[1][1], tabs, kT, t_k)
                nb = 32 // Dl
                wv, t_wv = ws[2]
                for b0 in range(0, 32, 4):
                    p_, t_p = k.ps()
                    for bb in range(4):
                        blk = b0 + bb
                        r, j = blk // nb, blk % nb
                        st = HOFF + r + Dl * 128 * j
                        for kc in range(8):
                            k.mm(p_[:, bb * 128:(bb + 1) * 128], hT[:, kc, sl(st, 128, Dl)], wv[:, kc, :], kc == 0,
                                 kc == 7, [t_wv] + t_hT, [t_p])
                    k.cp("act", vs[:, b0:b0 + 4, :], p_[:].rearrange("p (b c) -> p b c", b=4), [t_p], [t_v])
                items = [(r, j) for r in range(Dl) for j in range(nb)]

                def st1(i):
                    r, j = items[i]
                    nq = 256 if j + 1 < nb else 128
                    st = r + Dl * 128 * j
                    kcols = sl(st, 128, Dl)
                    qcols = sl(st, nq, Dl)
                    E, t_E = Es[i % 3]
                    for h2 in range(2):
                        hb = 64 * h2
                        p_, t_p = k.ps()
                        k.mm(p_[:, 0:nq], kT[hb:hb + 64, kcols], qT[hb:hb + 64, qcols], True, True,
                             [t_k, t_q], [t_p])
                        k.act(E[:, h2, 0:nq], p_[:, 0:nq], AF.Exp, [t_p], [t_E], scale=0.125)
                    k.tt("pool", E[:, :, 0:nq], E[:, :, 0:nq], maskA[:].rearrange("p (a b) -> p a b", a=2)[:, :, 0:nq],
                         ALU.mult, [t_E, k.t_const], [t_E])

                def st2(i):
                    r, j = items[i]
                    st = r + Dl * 128 * j
                    E, t_E = Es[i % 3]
                    Eprev = Es[(i - 1) % 3]
                    blk = r * nb + j
                    po, t_po = k.ps()
                    for h2 in range(2):
                        hb = 64 * h2
                        for pl in range(2):
                            o_ap = po[hb:hb + 64, pl * 128:(pl + 1) * 128]
                            first = True
                            if j > 0:
                                lh = vs[:, blk - 1, hb:hb + 64] if pl == 0 else ones_b[:, 0:64]
                                k.mm(o_ap, lh, Eprev[0][:, h2, 128:256], True, False, [t_v, Eprev[1], k.t_const], [t_po])
                                first = False
                            lh = vs[:, blk, hb:hb + 64] if pl == 0 else ones_b[:, 0:64]
                            k.mm(o_ap, lh, E[:, h2, 0:128], first, True, [t_v, t_E, k.t_const], [t_po])
                    qtok = sl(st, 128, Dl)
                    pov = po[:, 0:256].rearrange("p (a b) -> p a b", a=2)
                    if g == 0:
                        k.cp("dve", acc[:, :, qtok], pov, [t_po], [t_acc])
                    else:
                        k.tt("dve", acc[:, :, qtok], acc[:, :, qtok], pov, ALU.add, [t_po, t_acc], [t_acc])
                st1(0)
                for i in range(len(items)):
                    if i + 1 < len(items):
                        st1(i + 1)
                    st2(i)
            for gq in range(8):
                cols = slice(gq * 512, (gq + 1) * 512)
                k.recip(acc[:, 1, cols], acc[:, 1, cols], [t_acc], [t_acc])
                k.tt("dve", yb[:, cols], acc[:, 0, cols], acc[:, 1, cols], ALU.mult, [t_acc], [t_yb])
                S.dma("sp", ctx["yT"][hp * 128:(hp + 1) * 128, cols], yb[:, cols], reads=[t_yb],
                      writes=[ctx["t_yT"][hp][gq]])
        S.barrier()


def phase_B(ctx, s, l, hT, t_hT, tabs):
    k = ctx["k"]
    S = k.S
    maskD = k.cs["c_maskD"]
    ones_b = k.cs["ones_b"]
    lam_init = 0.8 - 0.6 * math.exp(-0.3 * l)
    saved = (k.ps_f, k.ps_ft, k.ps_i)
    accb = list(zip(k.ps_f[0:4], k.ps_ft[0:4]))
    k.ps_f = [saved[0][4], saved[0][5], k.ps_b[0][:].bitcast(F32), k.ps_b[1][:].bitcast(F32)]
    k.ps_ft = [saved[1][4], saved[1][5], k.ps_bt[0], k.ps_bt[1]]
    k.ps_i = 0
    with contextlib.ExitStack() as es:
        rb = rot_bufs(k, es)
        lv, t_lv = load_bc(k, es, k.vf["diff_lam"][l].rearrange("a b -> (a b)"), 256, "lv")
        sc_ = k.sb(es, [128, 8], F32, "lsc")
        t_sc = Tr()
        pr = k.sb(es, [128, 128], F32, "lpr")
        k.tt("dve", pr[:, 0:64], lv[:, 0:64], lv[:, 64:128], ALU.mult, [t_lv], [t_sc])
        k.tt("dve", pr[:, 64:128], lv[:, 128:192], lv[:, 192:256], ALU.mult, [t_lv], [t_sc])
        S.op("dve", lambda e: e.reduce_sum(out=sc_[:, 0:2], in_=pr[:].rearrange("p (a b) -> p a b", a=2), axis=AX.X),
             reads=[t_sc], writes=[t_sc])
        k.act(sc_[:, 2:4], sc_[:, 0:2], AF.Exp, [t_sc], [t_sc])
        k.tt("dve", sc_[:, 4:5], sc_[:, 3:4], sc_[:, 2:3], ALU.subtract, [t_sc], [t_sc])
        k.ts("dve", sc_[:, 5:6], sc_[:, 4:5], -lam_init, None, ALU.add, ALU.bypass, [t_sc], [t_sc])
        gB = k.sb(es, [128, 1], F32, "gB")
        t_gB = Tr()
        S.dma("sp", gB[:], k.vf["diff_norm_g"][l].rearrange("(p o) -> p o", o=1), writes=[t_gB])
        k.ts("dve", gB[:], gB[:], 1.0 - lam_init, None, ALU.mult, ALU.bypass, [t_gB], [t_gB])
        qT = k.sb(es, [128, S_LEN], BF16, "bqT")
        kT = k.sb(es, [128, S_LEN], BF16, "bkT")
        vs = k.sb(es, [128, 32, 128], BF16, "bvs")
        t_q, t_k, t_v = Tr(), Tr(), Tr()
        Es = [(k.sb(es, [128, 512], BF16, "bE"), Tr()) for _ in range(6)]
        ws = [(k.sb(es, [128, 8, 128], BF16, "bw"), Tr()) for _ in range(3)]
        o1 = k.sb(es, [128, 512], F32, "bo1")
        o2 = k.sb(es, [128, 512], F32, "bo2")
        rr = k.sb(es, [128, 512], F32, "brr")
        sq = k.sb(es, [128, 512], BF16, "bsq")
        ybs = [(k.sb(es, [128, 512], BF16, "byb"), Tr()) for _ in range(2)]
        t_o = Tr()
        for h in range(4):
            for j3 in range(3):
                load_w_into(k, ws[j3][0], ws[j3][1], "w_in", l, 0, D, OFF_B + j3 * 512 + h * 128, 128)
            proj_rot(k, rb, hT, t_hT, ws[0][0], ws[0][1], tabs, qT, t_q)
            proj_rot(k, rb, hT, t_hT, ws[1][0], ws[1][1], tabs, kT, t_k)
            wv, t_wv = ws[2]
            for b0 in range(0, 32, 4):
                p_, t_p = k.ps()
                for bb in range(4):
                    blk = b0 + bb
                    st = HOFF + 128 * blk
                    for kc in range(8):
                        k.mm(p_[:, bb * 128:(bb + 1) * 128], hT[:, kc, st:st + 128], wv[:, kc, :], kc == 0, kc == 7,
                             [t_wv] + t_hT, [t_p])
                k.cp("act", vs[:, b0:b0 + 4, :], p_[:].rearrange("p (b c) -> p b c", b=4), [t_p], [t_v])
            items = [(G, j) for G in range(8) for j in range(4 * G + 4)]
            LA = 1

            def geo(G, j):
                jj = max(j - 4 * G, 0)
                qoff = 128 * jj
                return qoff, 512 - qoff, 512 * G + qoff

            def st1(i):
                G, j = items[i]
                qoff, nq, q0 = geo(G, j)
                pp = []
                for m in range(2):
                    hb = 64 * m
                    p_, t_p = k.ps()
                    k.mm(p_[:, 0:nq], kT[hb:hb + 64, 128 * j:128 * j + 128], qT[hb:hb + 64, q0:q0 + nq], True, True,
                         [t_k, t_q], [t_p])
                    pp.append((p_, t_p))
                for m in range(2):
                    p_, t_p = pp[m]
                    E, t_E = Es[(i % 3) * 2 + m]
                    k.act(E[:, 0:nq], p_[:, 0:nq], AF.Exp, [t_p], [t_E], scale=0.125)
                    if j >= 4 * G:
                        k.tt("dve", E[:, 0:nq], E[:, 0:nq], maskD[:, 0:nq], ALU.mult, [t_E, k.t_const], [t_E])

            def st2(i):
                G, j = items[i]
                qoff, nq, q0 = geo(G, j)
                nj = 4 * G + 4
                for m in range(2):
                    E, t_E = Es[(i % 3) * 2 + m]
                    pn, t_pn = accb[2 * m]
                    pd, t_pd = accb[2 * m + 1]
                    k.mm(pn[:, qoff:512], vs[:, j, :], E[:, 0:nq], j == 0, j == nj - 1, [t_v, t_E], [t_pn])
                    k.mm(pd[:, qoff:512], ones_b[:], E[:, 0:nq], j == 0, j == nj - 1, [t_E, k.t_const], [t_pd])
                if j == nj - 1:
                    for m in range(2):
                        pn, t_pn = accb[2 * m]
                        pd, t_pd = accb[2 * m + 1]
                        k.recip(rr[:], pd[:], [t_pd, t_o], [t_o])
                        k.tt("dve", (o1 if m == 0 else o2)[:], pn[:], rr[:], ALU.mult, [t_pn, t_o], [t_o])
                    fin(G)

            def fin(G):
                k.stt(o1[:], o2[:], sc_[:, 5:6], o1[:], ALU.mult, ALU.add, [t_o, t_sc], [t_o])
                k.tt("pool", sq[:], o1[:], o1[:], ALU.mult, [t_o], [t_o])
                pm_, t_pm = k.ps()
                k.mm(pm_[:], ones_b[:], sq[:], True, True, [t_o, k.t_const], [t_pm])
                k.ts("dve", rr[:], pm_[:], 1.0 / 128, EPS, ALU.mult, ALU.add, [t_pm, t_o], [t_o])
                k.act(rr[:], rr[:], AF.Sqrt, [t_o], [t_o])
                k.recip(rr[:], rr[:], [t_o], [t_o])
                yb, t_yb = ybs[G % 2]
                k.stt(yb[:], o1[:], gB[:, 0:1], rr[:], ALU.mult, ALU.mult, [t_o, t_gB], [t_yb])
                S.dma("sp", ctx["yT"][256 + h * 128:256 + (h + 1) * 128, G * 512:(G + 1) * 512], yb[:], reads=[t_yb],
                      writes=[ctx["t_yT"][2 + h][G]])

            for i in range(min(LA, len(items))):
                st1(i)
            for i in range(len(items)):
                if i + LA < len(items):
                    st1(i + LA)
                st2(i)
        S.barrier()
    k.ps_f, k.ps_ft, k.ps_i = saved


def phase_C(ctx, s, l, hT, t_hT):
    k = ctx["k"]
    S = k.S
    cs = k.cs
    with contextlib.ExitStack() as es:
        wc, t_wc = load_w(k, es, "w_in", l, 0, D, OFF_C, 1024, "wc")
        lb = k.sb(es, [128, 256], F32, "lb")
        omlb = k.sb(es, [128, 256], F32, "omlb")
        t_lb = Tr()
        if l == 0:
            k.memset(lb[:], 0.0, [t_lb])
        else:
            S.dma("sp", lb[:], k.vf["hgrn_lb_logits"][1].partition_broadcast(128), writes=[t_lb])
            S.dma("sp", omlb[:], k.vf["hgrn_lb_logits"][0].partition_broadcast(128), writes=[t_lb])
            k.tt("dve", lb[:], lb[:], omlb[:], ALU.subtract, [t_lb], [t_lb])
            k.act(lb[:], lb[:], AF.Sigmoid, [t_lb], [t_lb])
        k.ts("dve", omlb[:], lb[:], -1.0, 1.0, ALU.mult, ALU.add, [t_lb], [t_lb])
        gn4 = k.sb(es, [128, 4, 64], F32, "gn4")
        t_gn = Tr()
        for h in range(4):
            S.dma("sp", gn4[:, h, :], k.vf["hgrn_norm_g"][l].partition_broadcast(128), writes=[t_gn])
        S32 = k.sb(es, [128, 2, 64], F32, "S32")
        t_S = Tr()
        k.memset(S32[:], 0.0, [t_S])
        Sbfs = [(k.sb(es, [128, 2, 64], BF16, "Sbf"), Tr()) for _ in range(2)]
        k.memset(Sbfs[0][0][:], 0.0, [Sbfs[0][1]])

        def mk(shape, dt, n, nm):
            return [(k.sb(es, shape, dt, nm), Tr()) for _ in range(n)]
        qs_, sf_, lf_, kk_, sg_ = (mk([128, 256], F32, 2, nm) for nm in ("cqs", "csf", "clf", "ckk", "csg"))
        ib_, Qp_, Kp_ = (mk([128, 256], BF16, 2, nm) for nm in ("cib", "cQp", "cKp"))
        eb_, enb_ = (mk([128, 256], F32, 2, nm) for nm in ("ceb", "cenb"))
        colv_ = mk([128, 2, 3], F32, 2, "ccolv")
        dd_ = mk([128, 2], F32, 2, "cdd")
        QT_, QTt_, KT_ = (mk([128, 2, 128], BF16, 2, nm) for nm in ("cQT", "cQTt", "cKT"))
        attE_, attO_ = (mk([128, 2, 128], BF16, 2, nm) for nm in ("cattE", "cattO"))
        QTh_ = mk([128, 2, 128], BF16, 2, "cQTh")
        for b_ in range(2):
            k.memset(QTh_[b_][0][:], 0.0, [QTh_[b_][1]])
        o_ = mk([128, 256], F32, 2, "co")
        sq_ = mk([128, 256], F32, 2, "csq")
        st_ = mk([128, 12], F32, 2, "cst")
        tmpU_ = mk([128, 2, 64], F32, 2, "ctu")
        y_ = mk([128, 256], BF16, 2, "cy")
        ygs = mk([128, 2, 512], BF16, 2, "cyg")
        tri2 = cs["c_tri2"]
        for tt_ in range(NT):
            b = tt_ % 2
            hs = slice(HOFF + tt_ * 128, HOFF + (tt_ + 1) * 128)
            p0, t_p0 = k.ps()
            p1, t_p1 = k.ps()
            for kc in range(8):
                k.mm(p0[:], hT[:, kc, hs], wc[:, kc, 0:512], kc == 0, kc == 7, [t_wc, t_hT[tt_]], [t_p0])
            for kc in range(8):
                k.mm(p1[:], hT[:, kc, hs], wc[:, kc, 512:1024], kc == 0, kc == 7, [t_wc, t_hT[tt_]], [t_p1])
            qs, t_qs = qs_[b]
            sf, t_sf = sf_[b]
            lf, t_lf = lf_[b]
            kk, t_kk = kk_[b]
            sg, t_sg = sg_[b]
            ib, t_ib = ib_[b]
            k.act(qs[:], p0[:, 0:256], AF.Silu, [t_p0], [t_qs])
            k.act(sf[:], p0[:, 256:512], AF.Sigmoid, [t_p0], [t_sf])
            k.cp("act", ib[:], p1[:, 0:256], [t_p1], [t_ib])
            k.act(sg[:], p1[:, 256:512], AF.Silu, [t_p1], [t_sg])
            k.tt("dve", sf[:], sf[:], omlb[:], ALU.mult, [t_sf, t_lb], [t_sf])
            k.tt("dve", sf[:], sf[:], lb[:], ALU.add, [t_sf, t_lb], [t_sf])
            k.act(lf[:], sf[:], AF.Ln, [t_sf], [t_lf])
            k.ts("pool", kk[:], sf[:], -1.0, 1.0, ALU.mult, ALU.add, [t_sf], [t_kk])
            pb_, t_pb_ = k.ps()
            k.mm(pb_[:, 0:256], cs["c_triM"][:], lf[:], True, True, [t_lf, k.t_const], [t_pb_])
            pc, t_pc = k.ps()
            for hp in range(2):
                k.mm(pc[:, hp * 2:hp * 2 + 2], lf[:, hp * 128:(hp + 1) * 128], cs["c_sel"][:], True, True,
                     [t_lf, k.t_const], [t_pc])
            colv, t_cv = colv_[b]
            dd, t_dd = dd_[b]
            pcv = pc[:, 0:4].rearrange("p (a c) -> p a c", a=2)
            k.act(colv[:, :, 0:2], pcv, AF.Exp, [t_pc], [t_cv])
            k.cp("act", st_[b][0][:, 0:2], pcv[:, :, 0], [t_pc], [st_[b][1]])
            k.tt("dve", dd[:], pcv[:, :, 1], st_[b][0][:, 0:2], ALU.subtract, [t_pc, st_[b][1]], [t_dd])
            k.act(colv[:, :, 2], dd[:], AF.Exp, [t_dd], [t_cv])
            eb, t_eb = eb_[b]
            enb, t_enb = enb_[b]
            k.act(eb[:], pb_[:, 0:256], AF.Exp, [t_pb_], [t_eb])
            k.act(enb[:], pb_[:, 0:256], AF.Exp, [t_pb_], [t_enb], scale=-1.0)
            Qp, t_Qp = Qp_[b]
            Kp, t_Kp = Kp_[b]
            k.tt("dve", Qp[:], qs[:], eb[:], ALU.mult, [t_qs, t_eb], [t_Qp])
            k.tt("pool", Kp[:], kk[:], enb[:], ALU.mult, [t_kk, t_enb], [t_Kp])
            pt, t_pt = k.psb()
            for hp in range(2):
                k.tp(pt[:, hp * 128:(hp + 1) * 128], Qp[:, hp * 128:(hp + 1) * 128], [t_Qp], [t_pt])
                k.tp(pt[:, 256 + hp * 128:256 + (hp + 1) * 128], Kp[:, hp * 128:(hp + 1) * 128], [t_Kp], [t_pt])
            QT, t_QT = QT_[b]
            QTt, t_QTt = QTt_[b]
            KT, t_KT = KT_[b]
            k.cp("act", QT[:], pt[:, 0:256].rearrange("p (a c) -> p a c", a=2), [t_pt], [t_QT])
            k.cp("act", KT[:], pt[:, 256:512].rearrange("p (a c) -> p a c", a=2), [t_pt], [t_KT])
            QTh, t_QTh = QTh_[b]
            k.cp("pool", QTh[:, :, 64:128], QT[:, :, 64:128], [t_QT], [t_QTh])
            for hp in range(2):
                k.ts("dve", QTt[:, hp, :], pt[:, hp * 128:(hp + 1) * 128], colv[:, hp, 0:1], None, ALU.mult, ALU.bypass,
                     [t_pt, t_cv], [t_QTt])
            paE, t_paE = k.ps()
            paO, t_paO = k.ps()
            for h in range(4):
                hp, par = h // 2, h % 2
                hb = 64 * par
                pa, t_pa = (paE, t_paE) if par == 0 else (paO, t_paO)
                k.mm(pa[0:64, hp * 128:(hp + 1) * 128], KT[hb:hb + 64, hp, 0:64], QT[hb:hb + 64, hp, :], True, True,
                     [t_KT, t_QT], [t_pa])
                k.mm(pa[64:128, hp * 128:(hp + 1) * 128], KT[hb:hb + 64, hp, 64:128], QTh[hb:hb + 64, hp, :], True, True,
                     [t_KT, t_QTh], [t_pa])
            attE, t_aE = attE_[b]
            attO, t_aO = attO_[b]
            k.tt("dve", attE[:].rearrange("p a c -> p (a c)"), paE[:, 0:256], tri2[:], ALU.mult, [t_paE, k.t_const], [t_aE])
            k.tt("dve", attO[:].rearrange("p a c -> p (a c)"), paO[:, 0:256], tri2[:], ALU.mult, [t_paO, k.t_const], [t_aO])
            po, t_po = k.ps()
            for h in range(4):
                hp, par = h // 2, h % 2
                att, t_att = (attE, t_aE) if par == 0 else (attO, t_aO)
                k.mm(po[:, h * 64:(h + 1) * 64], att[:, hp, :], ib[:, h * 64:(h + 1) * 64], True, True, [t_att, t_ib], [t_po])
            Sbf, t_Sbf = Sbfs[b]
            Sbn, t_Sbn = Sbfs[1 - b]
            piE, t_piE = k.ps()
            piO, t_piO = k.ps()
            for h in range(4):
                hp, par = h // 2, h % 2
                hb = 64 * par
                pi, t_pi = (piE, t_piE) if par == 0 else (piO, t_piO)
                k.mm(pi[:, hp * 64:(hp + 1) * 64], QTt[hb:hb + 64, hp, :], Sbf[hb:hb + 64, hp, :], True, True,
                     [t_QTt, t_Sbf], [t_pi])
            o, t_o = o_[b]
            k.cp("act", o[:], po[:, 0:256], [t_po], [t_o])
            ov = o[:].rearrange("p (a b c) -> p a b c", a=2, b=2)
            k.tt("dve", ov[:, :, 0, :], ov[:, :, 0, :], piE[:, 0:128].rearrange("p (a c) -> p a c", a=2), ALU.add,
                 [t_o, t_piE], [t_o])
            k.tt("dve", ov[:, :, 1, :], ov[:, :, 1, :], piO[:, 0:128].rearrange("p (a c) -> p a c", a=2), ALU.add,
                 [t_o, t_piO], [t_o])
            pu, t_pu = k.ps()
            for h in range(4):
                hp, par = h // 2, h % 2
                hb = 64 * par
                k.mm(pu[hb:hb + 64, hp * 64:(hp + 1) * 64], Kp[:, h * 64:(h + 1) * 64], ib[:, h * 64:(h + 1) * 64], True, True,
                     [t_Kp, t_ib], [t_pu])
            tu, t_tu = tmpU_[b]
            for hp in range(2):
                k.ts("dve", tu[:, hp, :], pu[:, hp * 64:(hp + 1) * 64], colv[:, hp, 2:3], None, ALU.mult, ALU.bypass,
                     [t_pu, t_cv], [t_tu])
                k.stt(S32[:, hp, :], S32[:, hp, :], colv[:, hp, 1:2], tu[:, hp, :], ALU.mult, ALU.add, [t_S, t_cv, t_tu], [t_S])
            k.cp("pool", Sbn[:], S32[:], [t_S], [t_Sbn])
            sq, t_sq = sq_[b]
            st, t_st = st_[b]
            y, t_y = y_[b]
            k.tt("pool", sq[:], o[:], o[:], ALU.mult, [t_o], [t_sq])
            S.op("dve", lambda e, st=st, sq=sq: e.reduce_sum(out=st[:, 0:4], in_=sq[:].rearrange("p (a c) -> p a c", a=4),
                                                            axis=AX.X), reads=[t_sq], writes=[t_st])
            k.ts("dve", st[:, 4:8], st[:, 0:4], 1.0 / 64, EPS, ALU.mult, ALU.add, [t_st], [t_st])
            k.act(st[:, 4:8], st[:, 4:8], AF.Sqrt, [t_st], [t_st])
            k.recip(st[:, 8:12], st[:, 4:8], [t_st], [t_st])
            k.tt("pool", sq[:], sg[:], gn4[:].rearrange("p a c -> p (a c)"), ALU.mult, [t_sg, t_gn, t_sq], [t_sq])
            for h in range(4):
                k.stt(y[:, h * 64:(h + 1) * 64], o[:, h * 64:(h + 1) * 64], st[:, 8 + h:9 + h], sq[:, h * 64:(h + 1) * 64],
                      ALU.mult, ALU.mult, [t_o, t_st, t_sq], [t_y])
            pt2, t_pt2 = k.psb()
            for hp in range(2):
                k.tp(pt2[:, hp * 128:(hp + 1) * 128], y[:, hp * 128:(hp + 1) * 128], [t_y], [t_pt2])
            gq, q = tt_ // 4, tt_ % 4
            yg, t_yg = ygs[gq % 2]
            k.cp("act", yg[:, :, q * 128:(q + 1) * 128], pt2[:, 0:256].rearrange("p (a c) -> p a c", a=2), [t_pt2], [t_yg])
            if q == 3:
                for hp in range(2):
                    S.dma("sp", ctx["yT"][768 + hp * 128:768 + (hp + 1) * 128, gq * 512:(gq + 1) * 512], yg[:, hp, :],
                          reads=[t_yg], writes=[ctx["t_yT"][6 + hp][gq]])
        S.barrier()


def phase_D(ctx, s, l, hT, t_hT, t_h0):
    k = ctx["k"]
    S = k.S
    cs = k.cs
    vfirst, t_vf = ctx["vfirst"], ctx["t_vf"][s]
    with contextlib.ExitStack() as es:
        wa = k.sb(es, [128, 8, 1024], BF16, "wda")
        wb_ = k.sb(es, [128, 8, 1024], BF16, "wdb")
        t_wa = Tr()
        with contextlib.ExitStack() as e2:
            wd, t_wd = load_w(k, e2, "w_in", l, 0, D, OFF_D, 1024, "wd")
            mu, t_mu = load_bc(k, e2, k.vf["rwkv_mu"][l], 1024, "mu")
            omu = k.sb(e2, [128, 1024], F32, "omu")
            k.ts("dve", omu[:], mu[:], -1.0, 1.0, ALU.mult, ALU.add, [t_mu], [t_mu])
            for kc in range(8):
                k.tt("dve", wb_[:, kc, :], wd[:, kc, :], mu[:], ALU.mult, [t_wd, t_mu], [t_wa])
                k.tt("pool", wa[:, kc, :], wd[:, kc, :], omu[:], ALU.mult, [t_wd, t_mu], [t_wa])
            S.barrier()
        t_bc = Tr()

        def bc(src, n=256, nm="dbc"):
            t = k.sb(es, [128, n], F32, nm)
            S.dma("sp", t[:], src.partition_broadcast(128), writes=[t_bc])
            return t
        w0 = bc(k.vf["rwkv_w0"][l])
        a0 = bc(k.vf["rwkv_a0"][l])
        kkb = bc(k.vf["rwkv_k_k"][l])
        kab = bc(k.vf["rwkv_k_a"][l])
        lng = bc(k.vf["rwkv_lnx_g"][l])
        lnb = bc(k.vf["rwkv_lnx_b"][l])
        rkb = bc(k.vf["rwkv_r_k"][l].rearrange("a b -> (a b)"))
        omka = k.sb(es, [128, 256], F32, "omka")
        k.ts("dve", omka[:], kab[:], -1.0, 1.0, ALU.mult, ALU.add, [t_bc], [t_bc])
        w2a2 = k.sb(es, [128, 256], BF16, "w2a2")
        g2 = k.sb(es, [128, 256], BF16, "g2")
        t_lw = Tr()
        S.dma("sp", w2a2[0:64, :], k.wb["rwkv_w2"][l], reads=k.t_w["rwkv_w2"], writes=[t_lw])
        S.dma("sp", w2a2[64:128, :], k.wb["rwkv_a2"][l], reads=k.t_w["rwkv_a2"], writes=[t_lw])
        S.dma("sp", g2[:], k.wb["rwkv_g2"][l], reads=k.t_w["rwkv_g2"], writes=[t_lw])
        if l > 0:
            v0b = bc(k.vf["rwkv_v0"][l - 1])
            v1 = k.sb(es, [128, 2, 32], BF16, "v1")
            v2 = k.sb(es, [32, 256], BF16, "v2")
            S.dma("sp", v1[:], k.wb["rwkv_v1"][l - 1].rearrange("(kc p) n -> p kc n", p=128), reads=k.t_w["rwkv_v1"],
                  writes=[t_lw])
            S.dma("sp", v2[:], k.wb["rwkv_v2"][l - 1], reads=k.t_w["rwkv_v2"], writes=[t_lw])
        H32 = k.sb(es, [128, 2, 64], F32, "H32")
        t_H = Tr()
        k.memset(H32[:], 0.0, [t_H])
        Hbfs = [(k.sb(es, [128, 2, 64], BF16, "Hbf"), Tr()) for _ in range(2)]
        k.memset(Hbfs[0][0][:], 0.0, [Hbfs[0][1]])

        def f32(nm, n=256):
            return k.sb(es, [128, n], F32, nm), Tr()

        def b16(nm, shape=(128, 256)):
            return k.sb(es, list(shape), BF16, nm), Tr()
        r32, t_r = f32("r32")
        k32, t_k = f32("k32")
        v32, t_v = f32("v32")
        li, t_li = b16("li")
        liT, t_liT = b16("liT", (128, 2, 128))
        lw, t_lwv = f32("lw")
        a32, t_a = f32("a32")
        g32, t_g = f32("g32")
        tmp, t_tmp = f32("tmp")
        tmp2, t_tmp2 = f32("tmp2")
        kk0, t_kk = f32("kk0")
        kp, t_kp = f32("kp")
        bv, t_bv = f32("bv")
        st, t_st = f32("st", 24)
        e1, t_e1 = f32("e1")
        e2_, t_e2 = f32("e2")
        e4, t_e4 = f32("e4")
        e5, t_e5 = f32("e5")
        ewl, t_ewl = f32("ewl")
        colv, t_cv = k.sb(es, [128, 2, 2], F32, "dcolv"), Tr()
        Rp, t_Rp = b16("Rp")
        Ap, t_Ap = b16("Ap")
        Bp, t_Bp = b16("Bp")
        Kp, t_Kp = b16("Kp")
        At, t_At = b16("At")
        Bc, t_Bc = b16("Bc")
        Kc, t_Kc = b16("Kc")
        Vb, t_Vb = b16("Vb")
        ART, t_ART = b16("ART", (128, 2, 2, 128))
        BT, t_BT = b16("BT", (128, 2, 128))
        KT, t_KT = b16("KT", (128, 2, 128))
        RTt, t_RTt = b16("RTt", (128, 2, 128))
        RhT, t_RhT = b16("RhT", (128, 2, 128))
        LM = [b16("LM", (128, 512)) for _ in range(4)]
        Lp = [[b16("Lp", (128, 2, 128)) for _ in range(7)] for _ in range(4)]
        Xs = [[b16("X", (128, 128)) for _ in range(2)] for _ in range(4)]
        Ysb, t_Y = f32("Ysb")
        GpT, t_GpT = b16("GpT", (128, 2, 64))
        Zsb, t_Z = k.sb(es, [128, 2, 64], F32, "Zsb"), Tr()
        ybf, t_ybf = b16("ybf")
        ygs = [b16("dyg", (128, 2, 512)) for _ in range(2)]
        if l > 0:
            vT, t_vT = b16("vT", (128, 2, 128))
            u1, t_u1 = b16("u1", (32, 128))
            vfb, t_vfb = f32("vfb")
        mSI2, mLow = cs["c_maskSI2"], cs["c_maskLow"]

        for tt_ in range(NT):
            hs = slice(HOFF + tt_ * 128, HOFF + (tt_ + 1) * 128)
            hs1 = slice(HOFF + tt_ * 128 - 1, HOFF + (tt_ + 1) * 128 - 1)
            rdh = [t_wa, t_hT[tt_], t_h0] + ([t_hT[tt_ - 1]] if tt_ > 0 else [])
            pA, t_pA = k.ps()
            pB, t_pB = k.ps()
            for p_, t_p, c0 in ((pA, t_pA, 0), (pB, t_pB, 512)):
                for kc in range(8):
                    k.mm(p_[:], hT[:, kc, hs], wa[:, kc, c0:c0 + 512], kc == 0, False, rdh, [t_p])
                for kc in range(8):
                    k.mm(p_[:], hT[:, kc, hs1], wb_[:, kc, c0:c0 + 512], False, kc == 7, rdh, [t_p])
            k.cp("act", r32[:], pA[:, 0:256], [t_pA], [t_r])
            k.cp("act", k32[:], pA[:, 256:512], [t_pA], [t_k])
            k.cp("act", v32[:], pB[:, 0:256], [t_pB], [t_v])
            k.act(li[:, 0:64], pB[:, 256:320], AF.Tanh, [t_pB], [t_li])
            k.cp("act", li[:, 64:128], pB[:, 320:384], [t_pB], [t_li])
            k.act(li[:, 128:256], pB[:, 384:512], AF.Sigmoid, [t_pB], [t_li])
            pt, t_pt = k.psb()
            for j in range(2):
                k.tp(pt[:, j * 128:(j + 1) * 128], li[:, j * 128:(j + 1) * 128], [t_li], [t_pt])
            k.cp("dve", liT[:], pt[:, 0:256].rearrange("p (a c) -> p a c", a=2), [t_pt], [t_liT])
            pw, t_pw = k.ps()
            pa_, t_pa = k.ps()
            pg, t_pg = k.ps()
            k.mm(pw[:, 0:256], liT[0:64, 0, :], w2a2[0:64, :], True, True, [t_liT, t_lw], [t_pw])
            k.mm(pa_[:, 0:256], liT[64:128, 0, :], w2a2[64:128, :], True, True, [t_liT, t_lw], [t_pa])
            k.mm(pg[:, 0:256], liT[:, 1, :], g2[:], True, True, [t_liT, t_lw], [t_pg])
            k.tt("dve", lw[:], pw[:, 0:256], w0[:], ALU.add, [t_pw, t_bc], [t_lwv])
            k.act(lw[:], lw[:], AF.Sigmoid, [t_lwv], [t_lwv])
            k.ts("dve", lw[:], lw[:], -0.6065306597126334, None, ALU.mult, ALU.bypass, [t_lwv], [t_lwv])
            k.tt("dve", a32[:], pa_[:, 0:256], a0[:], ALU.add, [t_pa, t_bc], [t_a])
            k.act(a32[:], a32[:], AF.Sigmoid, [t_a], [t_a])
            k.cp("act", g32[:], pg[:, 0:256], [t_pg], [t_g])
            if l == 0:
                S.dma("sp", vfirst[s, tt_ * 128:(tt_ + 1) * 128, :], v32[:], reads=[t_v], writes=[t_vf[tt_]])
            else:
                S.dma("sp", vfb[:], vfirst[s, tt_ * 128:(tt_ + 1) * 128, :], reads=[t_vf[tt_]], writes=[t_vfb])
                k.cp("act", Vb[:], v32[:], [t_v], [t_Vb])
                pt, t_pt = k.psb()
                for j in range(2):
                    k.tp(pt[:, j * 128:(j + 1) * 128], Vb[:, j * 128:(j + 1) * 128], [t_Vb], [t_pt])
                k.cp("dve", vT[:], pt[:, 0:256].rearrange("p (a c) -> p a c", a=2), [t_pt], [t_vT])
                p1, t_p1 = k.ps()
                for j in range(2):
                    k.mm(p1[0:32, 0:128], v1[:, j, :], vT[:, j, :], j == 0, j == 1, [t_vT, t_lw], [t_p1])
                k.cp("act", u1[:], p1[0:32, 0:128], [t_p1], [t_u1])
                p2, t_p2 = k.ps()
                k.mm(p2[:, 0:256], u1[:], v2[:], True, True, [t_u1, t_lw], [t_p2])
                k.tt("dve", tmp[:], p2[:, 0:256], v0b[:], ALU.add, [t_p2, t_bc], [t_tmp])
                k.act(tmp[:], tmp[:], AF.Sigmoid, [t_tmp], [t_tmp])
                k.tt("dve", vfb[:], vfb[:], v32[:], ALU.subtract, [t_vfb, t_v], [t_vfb])
                k.tt("dve", vfb[:], vfb[:], tmp[:], ALU.mult, [t_vfb, t_tmp], [t_vfb])
                k.tt("dve", v32[:], v32[:], vfb[:], ALU.add, [t_v, t_vfb], [t_v])
            k.cp("act", Vb[:], v32[:], [t_v], [t_Vb])
            k.tt("dve", kk0[:], k32[:], kkb[:], ALU.mult, [t_k, t_bc], [t_kk])
            k.tt("pool", tmp[:], kk0[:], kk0[:], ALU.mult, [t_kk], [t_tmp])
            S.op("dve", lambda e: e.reduce_sum(out=st[:, 0:4], in_=tmp[:].rearrange("p (a c) -> p a c", a=4), axis=AX.X),
                 reads=[t_tmp], writes=[t_st])
            k.act(st[:, 0:4], st[:, 0:4], AF.Sqrt, [t_st], [t_st])
            k.ts("dve", st[:, 0:4], st[:, 0:4], 1e-12, None, ALU.max, ALU.bypass, [t_st], [t_st])
            k.recip(st[:, 4:8], st[:, 0:4], [t_st], [t_st])
            for h in range(4):
                k.ts("dve", kk0[:, h * 64:(h + 1) * 64], kk0[:, h * 64:(h + 1) * 64], st[:, 4 + h:5 + h], None, ALU.mult,
                     ALU.bypass, [t_kk, t_st], [t_kk])
            k.tt("dve", tmp2[:], a32[:], kab[:], ALU.mult, [t_a, t_bc], [t_tmp2])
            k.tt("pool", tmp2[:], tmp2[:], omka[:], ALU.add, [t_tmp2, t_bc], [t_tmp2])
            k.tt("dve", kp[:], k32[:], tmp2[:], ALU.mult, [t_k, t_tmp2], [t_kp])
            k.tt("pool", bv[:], kk0[:], a32[:], ALU.mult, [t_kk, t_a], [t_bv])
            pCm, t_pCm = k.ps()
            pCt, t_pCt = k.ps()
            pCr, t_pCr = k.ps()
            k.mm(pCm[:, 0:256], cs["c_triM"][:], lw[:], True, True, [t_lwv, k.t_const], [t_pCm])
            k.mm(pCt[:, 0:256], cs["c_tri"][:], lw[:], True, True, [t_lwv, k.t_const], [t_pCt])
            k.mm(pCr[:, 0:256], cs["c_triR"][:], lw[:], True, True, [t_lwv, k.t_const], [t_pCr])
            pc, t_pc = k.ps()
            for hp in range(2):
                k.mm(pc[:, hp * 2:hp * 2 + 2], lw[:, hp * 128:(hp + 1) * 128], cs["c_sel"][:], True, True,
                     [t_lwv, k.t_const], [t_pc])
            k.act(colv[:], pc[:, 0:4].rearrange("p (a c) -> p a c", a=2), AF.Exp, [t_pc], [t_cv])
            k.act(e1[:], pCm[:, 0:256], AF.Exp, [t_pCm], [t_e1])
            k.act(e2_[:], pCm[:, 0:256], AF.Exp, [t_pCm], [t_e2], scale=-1.0)
            k.act(e4[:], pCt[:, 0:256], AF.Exp, [t_pCt], [t_e4])
            k.act(e5[:], pCr[:, 0:256], AF.Exp, [t_pCr], [t_e5])
            k.act(ewl[:], lw[:], AF.Exp, [t_lwv], [t_ewl], scale=-1.0)
            k.tt("dve", Rp[:], r32[:], e1[:], ALU.mult, [t_r, t_e1], [t_Rp])
            k.tt("pool", tmp[:], kk0[:], ewl[:], ALU.mult, [t_kk, t_ewl, t_st], [t_tmp])
            k.stt(Ap[:], tmp[:], -1.0, e1[:], ALU.mult, ALU.mult, [t_tmp, t_e1], [t_Ap])
            k.stt(At[:], tmp[:], -1.0, e4[:], ALU.mult, ALU.mult, [t_tmp, t_e4], [t_At])
            k.tt("pool", Bp[:], bv[:], e2_[:], ALU.mult, [t_bv, t_e2], [t_Bp])
            k.tt("dve", Kp[:], kp[:], e2_[:], ALU.mult, [t_kp, t_e2], [t_Kp])
            k.tt("pool", Bc[:], bv[:], e5[:], ALU.mult, [t_bv, t_e5], [t_Bc])
            k.tt("dve", Kc[:], kp[:], e5[:], ALU.mult, [t_kp, t_e5], [t_Kc])
            pt, t_pt = k.psb()
            for j, src in enumerate((Ap, Rp, Bp, Kp)):
                for hp in range(2):
                    k.tp(pt[:, (j * 2 + hp) * 128:(j * 2 + hp + 1) * 128], src[:, hp * 128:(hp + 1) * 128],
                         [t_Ap, t_Rp, t_Bp, t_Kp], [t_pt])
            ptv = pt[:].rearrange("p (j a c) -> p j a c", j=4, a=2)
            for j in range(2):
                k.cp("act", ART[:, :, j, :], ptv[:, j, :, :], [t_pt], [t_ART])
            k.cp("dve", BT[:], ptv[:, 2, :, :], [t_pt], [t_BT])
            k.cp("dve", KT[:], ptv[:, 3, :, :], [t_pt], [t_KT])
            for hp in range(2):
                k.ts("dve", RTt[:, hp, :], ptv[:, 1, hp, :], colv[:, hp, 0:1], None, ALU.mult, ALU.bypass, [t_pt, t_cv],
                     [t_RTt])
            for h in range(4):
                hp, par = h // 2, h % 2
                hb = 64 * par
                LMh, t_LM = LM[h]
                pl, t_pl = k.ps()
                rhs_ar = ART[hb:hb + 64, hp, :, :].rearrange("p a c -> p (a c)")
                k.mm(pl[:, 0:256], BT[hb:hb + 64, hp, :], rhs_ar, True, True, [t_BT, t_ART], [t_pl])
                k.mm(pl[:, 256:512], KT[hb:hb + 64, hp, :], rhs_ar, True, True, [t_KT, t_ART], [t_pl])
                k.tt("dve", LMh[:], pl[:], mSI2[:], ALU.mult, [t_pl, k.t_const], [t_LM])
                L0, t_L0 = Lp[h][0]
                p0, t_p0 = k.ps()
                k.mm(p0[:, 0:128], ART[hb:hb + 64, hp, 0, :], BT[hb:hb + 64, hp, :], True, True, [t_ART, t_BT], [t_p0])
                k.tt("dve", L0[:, 0, :], p0[:, 0:128], mLow[:], ALU.mult, [t_p0, k.t_const], [t_L0])
                k.cp("pool", L0[:, 1, :], LMh[:, 0:128], [t_LM], [t_L0])
            for h in range(4):
                LMh, t_LM = LM[h]
                X0, t_X0 = Xs[h][0]
                px, t_px = k.ps()
                k.mm(px[:, 0:64], LMh[:, 256:384], Vb[:, h * 64:(h + 1) * 64], True, True, [t_LM, t_Vb], [t_px])
                k.cp("act", X0[:, 64:128], px[:, 0:64], [t_px], [t_X0])
                k.cp("pool", X0[:, 0:64], At[:, h * 64:(h + 1) * 64], [t_At], [t_X0])
            for j in range(7):
                if j < 6:
                    for h in range(4):
                        Lj, t_Lj = Lp[h][j]
                        Ln, t_Ln = Lp[h][j + 1]
                        p_, t_p = k.ps()
                        k.mm(p_[:, 0:128], Lj[:, 1, :], Lj[:, 0, :], True, True, [t_Lj], [t_p])
                        k.mm(p_[:, 128:256], Lj[:, 0, :], Lj[:, 1, :], True, True, [t_Lj], [t_p])
                        k.cp("act", Ln[:].rearrange("p a c -> p (a c)"), p_[:, 0:256], [t_p], [t_Ln])
                for h in range(4):
                    Xc, t_Xc = Xs[h][j % 2]
                    Xn, t_Xn = Xs[h][(j + 1) % 2]
                    Lj, t_Lj = Lp[h][j]
                    px, t_px = k.ps()
                    k.mm(px[:, 0:128], Lj[:, 1, :], Xc[:], True, True, [t_Lj, t_Xc], [t_px])
                    k.tt("dve", Xn[:], px[:, 0:128], Xc[:], ALU.add, [t_px, t_Xc], [t_Xn])
            pR, t_pR = k.ps()
            pY, t_pY = k.ps()
            pG, t_pG = k.ps()
            pZ, t_pZ = k.ps()
            for h in range(4):
                hp, par = h // 2, h % 2
                hb = 64 * par
                LMh, t_LM = LM[h]
                X7, t_X7 = Xs[h][1]
                hc = slice(h * 64, (h + 1) * 64)
                k.mm(pR[hb:hb + 64, hp * 128:(hp + 1) * 128], X7[:, 0:64], LMh[:, 128:256], True, True, [t_X7, t_LM], [t_pR])
                k.mm(pY[:, hc], LMh[:, 128:256], X7[:, 64:128], True, False, [t_X7, t_LM], [t_pY])
                k.mm(pY[:, hc], LMh[:, 384:512], Vb[:, hc], False, True, [t_Vb, t_LM], [t_pY])
                k.mm(pG[hb:hb + 64, hp * 64:(hp + 1) * 64], X7[:, 0:64], Bc[:, hc], True, True, [t_X7, t_Bc], [t_pG])
                k.mm(pZ[hb:hb + 64, hp * 64:(hp + 1) * 64], Bc[:, hc], X7[:, 64:128], True, False, [t_X7, t_Bc], [t_pZ])
                k.mm(pZ[hb:hb + 64, hp * 64:(hp + 1) * 64], Kc[:, hc], Vb[:, hc], False, True, [t_Kc, t_Vb], [t_pZ])
            k.tt("dve", RhT[:].rearrange("p a c -> p (a c)"), pR[:, 0:256], RTt[:].rearrange("p a c -> p (a c)"), ALU.add,
                 [t_pR, t_RTt], [t_RhT])
            k.cp("act", Ysb[:], pY[:, 0:256], [t_pY], [t_Y])
            k.cp("act", GpT[:].rearrange("p a c -> p (a c)"), pG[:, 0:128], [t_pG], [t_GpT])
            k.cp("act", Zsb[:].rearrange("p a c -> p (a c)"), pZ[:, 0:128], [t_pZ], [t_Z])
            Hbf, t_Hbf = Hbfs[tt_ % 2]
            Hbn, t_Hbn = Hbfs[1 - tt_ % 2]
            piE, t_piE = k.ps()
            piO, t_piO = k.ps()
            phE, t_phE = k.ps()
            phO, t_phO = k.ps()
            for h in range(4):
                hp, par = h // 2, h % 2
                hb = 64 * par
                pi, t_pi = (piE, t_piE) if par == 0 else (piO, t_piO)
                ph, t_ph = (phE, t_phE) if par == 0 else (phO, t_phO)
                k.mm(pi[:, hp * 64:(hp + 1) * 64], RhT[hb:hb + 64, hp, :], Hbf[hb:hb + 64, hp, :], True, True,
                     [t_RhT, t_Hbf], [t_pi])
                k.mm(ph[hb:hb + 64, hp * 64:(hp + 1) * 64], GpT[hb:hb + 64, hp, :], Hbf[hb:hb + 64, hp, :], True, True,
                     [t_GpT, t_Hbf], [t_ph])
            Yv = Ysb[:].rearrange("p (a b c) -> p a b c", a=2, b=2)
            k.tt("dve", Yv[:, :, 0, :], Yv[:, :, 0, :], piE[:, 0:128].rearrange("p (a c) -> p a c", a=2), ALU.add,
                 [t_Y, t_piE], [t_Y])
            k.tt("dve", Yv[:, :, 1, :], Yv[:, :, 1, :], piO[:, 0:128].rearrange("p (a c) -> p a c", a=2), ALU.add,
                 [t_Y, t_piO], [t_Y])
            for hp in range(2):
                k.stt(H32[:, hp, :], H32[:, hp, :], colv[:, hp, 1:2], Zsb[:, hp, :], ALU.mult, ALU.add, [t_H, t_cv, t_Z], [t_H])
            H2 = H32[:].rearrange("p a c -> p (a c)")
            k.tt("dve", H2[0:64, :], H2[0:64, :], phE[0:64, 0:128], ALU.add, [t_H, t_phE], [t_H])
            k.tt("dve", H2[64:128, :], H2[64:128, :], phO[64:128, 0:128], ALU.add, [t_H, t_phO], [t_H])
            k.cp("pool", Hbn[:], H32[:], [t_H], [t_Hbn])
            S.op("dve", lambda e: e.reduce_sum(out=st[:, 8:12], in_=Ysb[:].rearrange("p (a c) -> p a c", a=4), axis=AX.X),
                 reads=[t_Y], writes=[t_st])
            k.ts("dve", st[:, 8:12], st[:, 8:12], 1.0 / 64, None, ALU.mult, ALU.bypass, [t_st], [t_st])
            for h in range(4):
                k.ts("dve", Ysb[:, h * 64:(h + 1) * 64], Ysb[:, h * 64:(h + 1) * 64], st[:, 8 + h:9 + h], None, ALU.subtract,
                     ALU.bypass, [t_Y, t_st], [t_Y])
            k.tt("pool", tmp[:], Ysb[:], Ysb[:], ALU.mult, [t_Y], [t_tmp])
            S.op("dve", lambda e: e.reduce_sum(out=st[:, 12:16], in_=tmp[:].rearrange("p (a c) -> p a c", a=4), axis=AX.X),
                 reads=[t_tmp], writes=[t_st])
            k.ts("dve", st[:, 12:16], st[:, 12:16], 1.0 / 64, 64e-5, ALU.mult, ALU.add, [t_st], [t_st])
            k.act(st[:, 12:16], st[:, 12:16], AF.Sqrt, [t_st], [t_st])
            k.recip(st[:, 16:20], st[:, 12:16], [t_st], [t_st])
            for h in range(4):
                k.stt(Ysb[:, h * 64:(h + 1) * 64], Ysb[:, h * 64:(h + 1) * 64], st[:, 16 + h:17 + h], lng[:, h * 64:(h + 1) * 64],
                      ALU.mult, ALU.mult, [t_Y, t_st, t_bc], [t_Y])
            k.tt("pool", Ysb[:], Ysb[:], lnb[:], ALU.add, [t_Y, t_bc], [t_Y])
            k.tt("dve", tmp2[:], r32[:], kp[:], ALU.mult, [t_r, t_kp], [t_tmp2])
            k.tt("pool", tmp2[:], tmp2[:], rkb[:], ALU.mult, [t_tmp2, t_bc], [t_tmp2])
            S.op("dve", lambda e: e.reduce_sum(out=st[:, 20:24], in_=tmp2[:].rearrange("p (a c) -> p a c", a=4), axis=AX.X),
                 reads=[t_tmp2], writes=[t_st])
            for h in range(4):
                k.stt(Ysb[:, h * 64:(h + 1) * 64], v32[:, h * 64:(h + 1) * 64], st[:, 20 + h:21 + h], Ysb[:, h * 64:(h + 1) * 64],
                      ALU.mult, ALU.add, [t_Y, t_st, t_v], [t_Y])
            k.tt("dve", ybf[:], Ysb[:], g32[:], ALU.mult, [t_Y, t_g], [t_ybf])
            pt2, t_pt2 = k.psb()
            for hp in range(2):
                k.tp(pt2[:, hp * 128:(hp + 1) * 128], ybf[:, hp * 128:(hp + 1) * 128], [t_ybf], [t_pt2])
            gq, q = tt_ // 4, tt_ % 4
            yg, t_yg = ygs[gq % 2]
            k.cp("act", yg[:, :, q * 128:(q + 1) * 128], pt2[:, 0:256].rearrange("p (a c) -> p a c", a=2), [t_pt2], [t_yg])
            if q == 3:
                for hp in range(2):
                    S.dma("sp", ctx["yT"][1024 + hp * 128:1024 + (hp + 1) * 128, gq * 512:(gq + 1) * 512], yg[:, hp, :],
                          reads=[t_yg], writes=[ctx["t_yT"][8 + hp][gq]])
        S.barrier()


_NC_CACHE = {}


def kernel(**inputs):
    cfg = {}
    key = "main"
    if key not in _NC_CACHE:
        _NC_CACHE[key] = build(cfg)
    nc = _NC_CACHE[key]
    consts = host_consts()
    in_maps = []
    for c in range(8):
        m = {"x": np.ascontiguousarray(inputs["x"][2 * c:2 * c + 2]),
             "mem": np.ascontiguousarray(inputs["mem"][2 * c:2 * c + 2]),
             "positions": np.ascontiguousarray(inputs["positions"][2 * c:2 * c + 2]).astype(np.int32)}
        for n in WNAMES + VNAMES:
            m[n] = np.ascontiguousarray(inputs[n], dtype=np.float32)
        m.update(consts)
        in_maps.append(m)
    res = run_bass_kernel_spmd(nc, in_maps, core_ids=list(range(8)))
    return np.concatenate([r["out"] for r in res.results], axis=0).astype(np.float32)
```

```python
import contextlib
import math
import numpy as np
import concourse.bass as bass
import concourse.mybir as mybir
from concourse.bass_utils import run_bass_kernel_spmd

F32 = mybir.dt.float32
BF16 = mybir.dt.bfloat16
I32 = mybir.dt.int32
AF = mybir.ActivationFunctionType
ALU = mybir.AluOpType
AX = mybir.AxisListType

D = 1024
S_LEN = 4096
NT = S_LEN // 128
N_IN = 9984
OFF_A, OFF_B, OFF_C, OFF_D, OFF_G = 0, 2304, 3840, 4864, 5888
DFF = 2816
MEM = 256
EPS = 1e-5
HOFF = 8


class Tr:
    __slots__ = ("w", "r", "x")

    def __init__(self, excl=False):
        self.w = None
        self.r = {}
        self.x = excl


class Sched:
    COMPUTE = ("pe", "act", "dve", "pool")
    NDMASEM = 12

    def __init__(self, nc):
        self.nc = nc
        self.engobj = {"pe": nc.tensor, "act": nc.scalar, "dve": nc.vector, "pool": nc.gpsimd, "sp": nc.sync}
        self.prog = {k: [] for k in self.engobj}
        self.sems = {}
        self.cnt = {}
        self.seen = {k: {} for k in self.engobj}
        self._semctx = []
        for k in self.COMPUTE:
            self._mksem(k)
        self.dmasems = {}
        self.dmarr = {}
        for q in ("sp", "act", "pool"):
            self.dmasems[q] = [self._mksem(f"d_{q}_{i}") for i in range(self.NDMASEM)]
            self.dmarr[q] = 0
        self.ninst = 0

    def _mksem(self, key):
        ctx = self.nc.semaphore(key)
        h = ctx.__enter__()
        self._semctx.append(ctx)
        self.sems[key] = h
        self.cnt[key] = 0
        return key

    def _deps(self, stream, own_key, reads, writes):
        need = {}

        def add(kv):
            if kv is None:
                return
            k, v = kv
            if k == own_key and k == "pe":
                return
            if need.get(k, 0) < v:
                need[k] = v
        for t in reads:
            add(t.w)
            if t.x:
                for k, v in t.r.items():
                    if k != own_key:
                        add((k, v))
        for t in writes:
            add(t.w)
            for k, v in t.r.items():
                add((k, v))
        out = []
        seen = self.seen[stream]
        for k, v in need.items():
            if seen.get(k, 0) < v:
                seen[k] = v
                out.append((k, v))
        return out

    def _commit(self, key, val, reads, writes):
        for t in reads:
            if t.r.get(key, 0) < val:
                t.r[key] = val
        for t in writes:
            t.w = (key, val)
            t.r = {}

    def op(self, eng, fn, reads=(), writes=()):
        waits = self._deps(eng, eng, reads, writes)
        self.cnt[eng] += 1
        val = self.cnt[eng]
        self._commit(eng, val, reads, writes)
        sem = self.sems[eng]
        sems = self.sems

        def thunk(e, waits=waits, fn=fn, sem=sem):
            for k, v in waits:
                e.wait_ge(sems[k], v)
            fn(e).then_inc(sem, 1)
        self.prog[eng].append(thunk)
        self.ninst += 1

    def dma(self, q, out, in_, reads=(), writes=(), **kw):
        i = self.dmarr[q]
        self.dmarr[q] = (i + 1) % self.NDMASEM
        key = self.dmasems[q][i]
        waits = self._deps(q, key, reads, writes)
        prev = self.cnt[key]
        if prev > 0 and self.seen[q].get(key, 0) < prev:
            self.seen[q][key] = prev
            waits.append((key, prev))
        self.cnt[key] += 16
        val = self.cnt[key]
        self._commit(key, val, reads, writes)
        sem = self.sems[key]
        sems = self.sems

        def thunk(e, waits=waits, sem=sem, out=out, in_=in_, kw=kw):
            for k, v in waits:
                e.wait_ge(sems[k], v)
            e.dma_start(out=out, in_=in_, **kw).then_inc(sem, 16)
        self.prog[q].append(thunk)
        self.ninst += 1

    def barrier(self):
        snap = {k: v for k, v in self.cnt.items() if v > 0}
        sems = self.sems
        for stream in self.prog:
            seen = self.seen[stream]
            waits = []
            for k, v in snap.items():
                if k == stream:
                    continue
                if seen.get(k, 0) < v:
                    seen[k] = v
                    waits.append((k, v))
            if waits:
                def thunk(e, waits=waits):
                    for k, v in waits:
                        e.wait_ge(sems[k], v)
                self.prog[stream].append(thunk)

    def finish(self):
        nc = self.nc
        finals = [(k, v) for k, v in self.cnt.items() if v > 0]
        sems = self.sems
        prog = self.prog
        with nc.Block() as block:
            @block.tensor
            def _(e):
                for t in prog["pe"]:
                    t(e)

            @block.scalar
            def _(e):
                for t in prog["act"]:
                    t(e)

            @block.vector
            def _(e):
                for t in prog["dve"]:
                    t(e)

            @block.gpsimd
            def _(e):
                for t in prog["pool"]:
                    t(e)

            @block.sync
            def _(e):
                for t in prog["sp"]:
                    t(e)
                for k, v in finals:
                    e.wait_ge(sems[k], v)
        for ctx in reversed(self._semctx):
            ctx.__exit__(None, None, None)


WNAMES = ["w_in", "p_a", "p_b", "p_c", "p_d", "w_mix_out", "w_mem_q", "w_mem_kv", "w_mem_o", "w_ffn_in",
          "w_ffn_out", "rwkv_w2", "rwkv_a2", "rwkv_g2", "rwkv_v1", "rwkv_v2"]
VNAMES = ["mix_norm_g", "diff_lam", "diff_norm_g", "hgrn_lb_logits", "hgrn_norm_g", "rwkv_mu", "rwkv_w0", "rwkv_a0",
          "rwkv_k_k", "rwkv_k_a", "rwkv_r_k", "rwkv_lnx_g", "rwkv_lnx_b", "rwkv_v0", "mem_q_norm_g", "mem_kv_norm_g",
          "ffn_norm_g", "ffn_conv_w", "ffn_conv_b", "final_norm_g"]
SHAPES = {
    "mix_norm_g": (2, 1024), "w_in": (2, 1024, 9984), "diff_lam": (2, 4, 64), "diff_norm_g": (2, 128),
    "hgrn_lb_logits": (2, 256), "hgrn_norm_g": (2, 64), "rwkv_mu": (2, 1024), "rwkv_w0": (2, 256),
    "rwkv_w2": (2, 64, 256), "rwkv_a0": (2, 256), "rwkv_a2": (2, 64, 256), "rwkv_g2": (2, 128, 256),
    "rwkv_k_k": (2, 256), "rwkv_k_a": (2, 256), "rwkv_r_k": (2, 4, 64), "rwkv_lnx_g": (2, 256),
    "rwkv_lnx_b": (2, 256), "rwkv_v0": (1, 256), "rwkv_v1": (1, 256, 32), "rwkv_v2": (1, 32, 256),
    "p_a": (2, 256, 1024), "p_b": (2, 512, 1024), "p_c": (2, 256, 1024), "p_d": (2, 256, 1024),
    "w_mix_out": (2, 1024, 1024), "mem_q_norm_g": (2, 1024), "mem_kv_norm_g": (2, 1024),
    "w_mem_q": (2, 1024, 1024), "w_mem_kv": (2, 1024, 2048), "w_mem_o": (2, 1024, 1024),
    "ffn_norm_g": (2, 1024), "w_ffn_in": (2, 1024, 5632), "ffn_conv_w": (2, 3, 5632), "ffn_conv_b": (2, 5632),
    "w_ffn_out": (2, 2816, 1024), "final_norm_g": (1024,),
}


def host_consts():
    c = {}
    c["c_ident"] = np.eye(128, dtype=np.float32)
    s = np.arange(128)[:, None]
    t = np.arange(512)[None, :]
    c["c_maskD"] = (s <= t).astype(np.float32)
    dm = np.zeros((128, 256), np.float32)
    dm[:, :128] = (s <= np.arange(128)[None, :])
    dm[:, 128:] = (s >= np.arange(128)[None, :])
    c["c_maskA"] = np.concatenate([dm, dm], axis=1)
    tt = np.arange(128)[None, :]
    si = np.concatenate([(s < tt), (s <= tt)], axis=1).astype(np.float32)
    c["c_maskSI"] = si
    c["c_maskSI2"] = np.concatenate([si, si], axis=1)
    c["c_triR"] = (s > tt).astype(np.float32)
    c["c_maskLow"] = (tt < s).astype(np.float32)
    c["c_tri"] = (s <= tt).astype(np.float32)
    c["c_tri2"] = np.concatenate([c["c_tri"], c["c_tri"]], axis=1)
    c["c_triM"] = ((s <= tt).astype(np.float32) - (s <= 63).astype(np.float32))
    c["c_o63"] = np.broadcast_to((s <= 63), (128, 128)).astype(np.float32).copy()
    c["c_ones"] = np.ones((128, 128), np.float32)
    sel = np.zeros((128, 2), np.float32)
    sel[:64, 0] = 1.0
    sel[:, 1] = 1.0
    c["c_sel"] = sel
    pm = np.zeros((128, 128), np.float32)
    for hb in (0, 64):
        for i in range(8):
            pm[hb + i + 8, hb + i] = -1.0
            pm[hb + i, hb + i + 8] = 1.0
    c["c_pm"] = pm
    invf = np.zeros((128, 1), np.float32)
    f = (500000.0 ** (-np.arange(8, dtype=np.float32) / 8)).astype(np.float32)
    for hb in (0, 64):
        invf[hb:hb + 8, 0] = f
        invf[hb + 8:hb + 16, 0] = f
    c["c_invf"] = invf
    return c


class K:
    def __init__(self, cfg):
        self.cfg = cfg
        nc = bass.Bass("TRN2", target_bir_lowering=False)
        self.nc = nc
        self.S = Sched(nc)
        self.es = contextlib.ExitStack()
        self.uid = 0

    def sb(self, es, shape, dt, name=None):
        self.uid += 1
        return es.enter_context(self.nc.sbuf_tensor(f"{name or 't'}_{self.uid}", list(shape), dt))

    def dram(self, name, shape, dt, kind="Internal"):
        return self.nc.dram_tensor(name, list(shape), dt, kind=kind).ap()

    def mm(self, out, lhsT, rhs, start, stop, r, w):
        self.S.op("pe", lambda e: e.matmul(out, lhsT=lhsT, rhs=rhs, start=start, stop=stop), reads=r, writes=w)

    def tp(self, out, in_, r, w):
        idt = self.ident_b
        self.S.op("pe", lambda e: e.transpose(out, in_, idt), reads=list(r) + [self.t_const], writes=w)

    def act(self, out, in_, func, r, w, **kw):
        self.S.op("act", lambda e: e.activation(out=out, in_=in_, func=func, **kw), reads=r, writes=w)

    def tt(self, eng, out, in0, in1, op, r, w):
        self.S.op(eng, lambda e: e.tensor_tensor(out=out, in0=in0, in1=in1, op=op), reads=r, writes=w)

    def ts(self, eng, out, in0, s1, s2, op0, op1, r, w):
        self.S.op(eng, lambda e: e.tensor_scalar(out=out, in0=in0, scalar1=s1, scalar2=s2, op0=op0, op1=op1),
                  reads=r, writes=w)

    def stt(self, out, in0, scalar, in1, op0, op1, r, w):
        self.S.op("dve", lambda e: e.scalar_tensor_tensor(out=out, in0=in0, scalar=scalar, in1=in1, op0=op0, op1=op1),
                  reads=r, writes=w)

    def cp(self, eng, out, in_, r, w):
        if eng == "act":
            self.S.op("act", lambda e: e.copy(out=out, in_=in_), reads=r, writes=w)
        else:
            self.S.op(eng, lambda e: e.tensor_copy(out=out, in_=in_), reads=r, writes=w)

    def recip(self, out, in_, r, w):
        self.S.op("dve", lambda e: e.reciprocal(out=out, in_=in_), reads=r, writes=w)

    def memset(self, ap, val, w):
        self.S.op("dve", lambda e: e.memset(ap, val), reads=(), writes=w)

    def ps(self):
        i = self.ps_i
        self.ps_i = (i + 1) % len(self.ps_f)
        return self.ps_f[i], self.ps_ft[i]

    def psb(self):
        i = self.psb_i
        self.psb_i = (i + 1) % len(self.ps_b)
        return self.ps_b[i], self.ps_bt[i]


def build(cfg):
    k = K(cfg)
    nc, S = k.nc, k.S
    NS = cfg.get("nseq", 2)
    NL = cfg.get("layers", 2)
    dbg = cfg.get("debug")
    x_in = k.dram("x", [NS, S_LEN, D], F32, "ExternalInput")
    mem_in = k.dram("mem", [NS, MEM, D], F32, "ExternalInput")
    pos_in = k.dram("positions", [NS, S_LEN], I32, "ExternalInput")
    wf = {n: k.dram(n, SHAPES[n], F32, "ExternalInput") for n in WNAMES}
    vf = {n: k.dram(n, SHAPES[n], F32, "ExternalInput") for n in VNAMES}
    consts = host_consts()
    cf = {n: k.dram(n, v.shape, F32, "ExternalInput") for n, v in consts.items()}
    out = k.dram("out", [NS, S_LEN, D], F32, "ExternalOutput")
    xbuf = k.dram("xbuf", [NS, S_LEN, D], F32)
    yT = k.dram("yT", [1280, S_LEN], BF16)
    vfirst = k.dram("vfirst", [NS, S_LEN, 256], F32)
    wb = {n: k.dram(n + "_bf", SHAPES[n], BF16) for n in WNAMES}
    yin = None
    if cfg.get("yin"):
        yin = k.dram("yin", [1280, S_LEN], F32, "ExternalInput")
    dbg_out = None
    if dbg:
        dbg_out = k.dram("dbg", dbg["shape"], F32, "ExternalOutput")

    t_x = [[Tr() for _ in range(NT)] for _ in range(NS)]
    t_yT = [[Tr() for _ in range(8)] for _ in range(10)]
    t_vf = [[Tr() for _ in range(NT)] for _ in range(NS)]
    t_w = {}

    with contextlib.ExitStack() as g:
        k.ps_f, k.ps_ft, k.ps_b, k.ps_bt = [], [], [], []
        for i in range(6):
            k.ps_f.append(g.enter_context(nc.psum_tensor(f"psf{i}", [128, 512], F32)))
            k.ps_ft.append(Tr(True))
        for i in range(2):
            k.ps_b.append(g.enter_context(nc.psum_tensor(f"psb{i}", [128, 1024], BF16)))
            k.ps_bt.append(Tr(True))
        k.ps_i = 0
        k.psb_i = 0
        k.t_const = Tr()
        ident_b = k.sb(g, [128, 128], BF16, "ident")
        k.ident_b = ident_b[:]
        S.dma("pool", ident_b[:], cf["c_ident"], writes=[k.t_const])
        cs = {}
        for n, dt in [("c_maskD", BF16), ("c_maskA", BF16), ("c_maskSI", F32), ("c_maskSI2", F32), ("c_triR", F32), ("c_maskLow", F32), ("c_tri", F32), ("c_tri2", F32),
                      ("c_triM", F32), ("c_o63", F32), ("c_ones", F32), ("c_sel", F32), ("c_pm", BF16),
                      ("c_invf", F32)]:
            t = k.sb(g, consts[n].shape, dt, n)
            S.dma("pool" if dt == BF16 else "sp", t[:], cf[n], writes=[k.t_const])
            cs[n] = t
        ones_b = k.sb(g, [128, 128], BF16, "ones_b")
        S.dma("pool", ones_b[:], cf["c_ones"], writes=[k.t_const])
        cs["ones_b"] = ones_b
        k.cs = cs
        for n in WNAMES:
            tot = int(np.prod(SHAPES[n]))
            rows = tot // 2048
            src = wf[n].flatten().rearrange("(r c) -> r c", c=2048) if len(SHAPES[n]) > 1 else None
            nd = len(SHAPES[n])
            letters = "abc"[:nd]
            flat_s = wf[n].rearrange(f"{' '.join(letters)} -> ({' '.join(letters)})").rearrange("(r c) -> r c", c=2048)
            flat_d = wb[n].rearrange(f"{' '.join(letters)} -> ({' '.join(letters)})").rearrange("(r c) -> r c", c=2048)
            t_w[n] = []
            for r0 in range(0, rows, 512):
                r1 = min(rows, r0 + 512)
                tr_ = Tr()
                S.dma("pool", flat_d[r0:r1, :], flat_s[r0:r1, :], writes=[tr_])
                t_w[n].append(tr_)
        k.wb, k.t_w, k.vf = wb, t_w, vf

        ctx = dict(k=k, x_in=x_in, mem_in=mem_in, pos_in=pos_in, out=out, xbuf=xbuf, yT=yT, vfirst=vfirst,
                   t_x=t_x, t_yT=t_yT, t_vf=t_vf, yin=yin, dbg=dbg, dbg_out=dbg_out, cfg=cfg)
        phases = cfg.get("phases", "ABCDGMF")
        if yin is None and not any(c in phases for c in "ABCD"):
            with contextlib.ExitStack() as zx:
                zt = k.sb(zx, [128, 512], BF16, "zt")
                t_z = Tr()
                k.memset(zt[:], 0.0, [t_z])
                for rc in range(10):
                    for gq in range(8):
                        S.dma("sp", yT[rc * 128:(rc + 1) * 128, gq * 512:(gq + 1) * 512], zt[:], reads=[t_z],
                              writes=[t_yT[rc][gq]])
                S.barrier()
        for s in range(NS):
            for l in range(NL):
                xsrc = x_in if l == 0 else xbuf
                with contextlib.ExitStack() as mx:
                    hT = k.sb(mx, [128, 8, HOFF + S_LEN], BF16, "hT")
                    t_hT = [Tr() for _ in range(NT)]
                    t_h0 = Tr()
                    k.memset(hT[:, :, 0:HOFF], 0.0, [t_h0])
                    phase_norm_T(ctx, s, l, xsrc, hT, t_hT)
                    if yin is not None:
                        phase_yin(ctx)
                    else:
                        with contextlib.ExitStack() as rx:
                            tabs = phase_rope(ctx, rx, s) if ("A" in phases or "B" in phases) else None
                            if "A" in phases:
                                phase_A(ctx, s, l, hT, t_hT, tabs)
                            if "B" in phases:
                                phase_B(ctx, s, l, hT, t_hT, tabs)
                        if "C" in phases:
                            phase_C(ctx, s, l, hT, t_hT)
                        if "D" in phases:
                            phase_D(ctx, s, l, hT, t_hT, t_h0)
                    if dbg and dbg.get("what") == "yT" and dbg.get("l", 0) == l and s == 0:
                        for rc in dbg.get("rcs", range(10)):
                            for gq in range(8):
                                S.dma("pool", dbg_out[rc * 128:(rc + 1) * 128, gq * 512:(gq + 1) * 512],
                                      yT[rc * 128:(rc + 1) * 128, gq * 512:(gq + 1) * 512], reads=[t_yT[rc][gq]])
                    if "G" in phases:
                        phase_G(ctx, s, l, xsrc, hT, t_hT)
                if "M" in phases:
                    phase_M(ctx, s, l)
                if "F" in phases:
                    phase_F(ctx, s, l)
            if cfg.get("final", True):
                phase_final(ctx, s)
        S.finish()
    return nc


def load_bc(k, es, src_row_ap, n, name="bc"):
    t = k.sb(es, [128, n], F32, name)
    tr_ = Tr()
    k.S.dma("sp", t[:], src_row_ap.partition_broadcast(128), writes=[tr_])
    return t, tr_


def rms_to_bf(k, xt, t_xt, gbc, t_g, obf, t_o, ss, t_ss):
    k.act(obf[:], xt[:], AF.Square, [t_xt], [t_o, t_ss], accum_out=ss[:, 0:1])
    k.ts("dve", ss[:, 1:2], ss[:, 0:1], 1.0 / D, EPS, ALU.mult, ALU.add, [t_ss], [t_ss])
    k.act(ss[:, 2:3], ss[:, 1:2], AF.Sqrt, [t_ss], [t_ss])
    k.recip(ss[:, 3:4], ss[:, 2:3], [t_ss], [t_ss])
    k.stt(obf[:], xt[:], ss[:, 3:4], gbc[:], ALU.mult, ALU.mult, [t_xt, t_ss, t_g], [t_o])


def phase_norm_T(ctx, s, l, xsrc, hT, t_hT):
    k = ctx["k"]
    S = k.S
    with contextlib.ExitStack() as es:
        gbc, t_g = load_bc(k, es, k.vf["mix_norm_g"][l], D, "gmix")
        xts = [(k.sb(es, [128, D], F32, "xt"), Tr()) for _ in range(2)]
        obs = [(k.sb(es, [128, D], BF16, "ob"), Tr()) for _ in range(2)]
        sss = [(k.sb(es, [128, 4], F32, "ss"), Tr()) for _ in range(2)]
        for tt_ in range(NT):
            xt, t_xt = xts[tt_ % 2]
            ob, t_o = obs[tt_ % 2]
            ss, t_ss = sss[tt_ % 2]
            rd = [ctx["t_x"][s][tt_]] if l > 0 else []
            S.dma("sp", xt[:], xsrc[s, tt_ * 128:(tt_ + 1) * 128, :], reads=rd, writes=[t_xt])
            rms_to_bf(k, xt, t_xt, gbc, t_g, ob, t_o, ss, t_ss)
            pb, t_pb = k.psb()
            for j in range(8):
                k.tp(pb[:, j * 128:(j + 1) * 128], ob[:, j * 128:(j + 1) * 128], [t_o], [t_pb])
            k.cp("act" if tt_ % 2 else "dve", hT[:, :, HOFF + tt_ * 128:HOFF + (tt_ + 1) * 128],
                 pb[:].rearrange("p (j t) -> p j t", j=8), [t_pb], [t_hT[tt_]])
        S.barrier()


def phase_yin(ctx):
    k = ctx["k"]
    for rc in range(10):
        for gq in range(8):
            k.S.dma("pool", ctx["yT"][rc * 128:(rc + 1) * 128, gq * 512:(gq + 1) * 512],
                    ctx["yin"][rc * 128:(rc + 1) * 128, gq * 512:(gq + 1) * 512], writes=[ctx["t_yT"][rc][gq]])


def load_w(k, es, name, l, r0, nr, c0, ncols, tag="w"):
    kc = max(1, nr // 128)
    p = min(128, nr)
    t = k.sb(es, [p, kc, ncols], BF16, tag)
    tr_ = Tr()
    src = k.wb[name][l, r0:r0 + nr, c0:c0 + ncols].rearrange("(kc p) n -> p kc n", p=p)
    k.S.dma("sp", t[:], src, reads=k.t_w[name], writes=[tr_])
    return t, tr_


def load_w_into(k, t, tr_, name, l, r0, nr, c0, ncols):
    p = min(128, nr)
    src = k.wb[name][l, r0:r0 + nr, c0:c0 + ncols].rearrange("(kc p) n -> p kc n", p=p)
    k.S.dma("sp", t[:], src, reads=k.t_w[name], writes=[tr_])


def phase_G(ctx, s, l, xsrc, hT, t_hT):
    k = ctx["k"]
    S = k.S
    yT, t_yT = ctx["yT"], ctx["t_yT"]
    with contextlib.ExitStack() as es:
        pcat = k.sb(es, [128, 10, D], BF16, "pcat")
        t_p = Tr()
        for nm, c0, n in (("p_a", 0, 2), ("p_b", 2, 4), ("p_c", 6, 2), ("p_d", 8, 2)):
            S.dma("sp", pcat[:, c0:c0 + n, :], k.wb[nm][l].rearrange("(kc p) n -> p kc n", p=128),
                  reads=k.t_w[nm], writes=[t_p])
        wmo, t_wmo = load_w(k, es, "w_mix_out", l, 0, D, 0, D, "wmo")
        wgs = [(k.sb(es, [128, 8, 4, 128], BF16, "wg"), Tr()) for _ in range(2)]
        yts = [(k.sb(es, [128, 10, 512], BF16, "yt"), Tr()) for _ in range(2)]
        mT = k.sb(es, [128, 8, 512], BF16, "mT")
        t_mT = Tr()
        accs = [(k.sb(es, [128, 512], F32, "acc"), Tr()) for _ in range(2)]
        sigs = [(k.sb(es, [128, 512], F32, "sig"), Tr()) for _ in range(3)]
        xts = [(k.sb(es, [128, D], F32, "xg"), Tr()) for _ in range(2)]
        branches = ((0, 2), (2, 4), (6, 2), (8, 2))
        it = 0
        for gq in range(8):
            yt, t_yt = yts[gq % 2]
            for rc in range(10):
                S.dma("sp", yt[:, rc, :], yT[rc * 128:(rc + 1) * 128, gq * 512:(gq + 1) * 512],
                      reads=[t_yT[rc][gq]], writes=[t_yt])
            hsl = slice(HOFF + gq * 512, HOFF + (gq + 1) * 512)
            rh = t_hT[gq * 4:(gq + 1) * 4]
            for oc in range(8):
                wg, t_wg = wgs[it % 2]
                it += 1
                for bi in range(4):
                    c0 = OFF_G + bi * D + oc * 128
                    S.dma("sp", wg[:, :, bi, :], k.wb["w_in"][l, :, c0:c0 + 128].rearrange("(kc p) n -> p kc n", p=128),
                          reads=k.t_w["w_in"], writes=[t_wg])
                acc, t_acc = accs[oc % 2]
                for bi, (c0, n) in enumerate(branches):
                    pg, t_pg = k.ps()
                    for kc in range(8):
                        k.mm(pg[:], wg[:, kc, bi, :], hT[:, kc, hsl], kc == 0, kc == 7, [t_wg] + rh, [t_pg])
                    sg, t_sg = sigs[(oc * 4 + bi) % 3]
                    k.act(sg[:], pg[:], AF.Sigmoid, [t_pg], [t_sg])
                    py, t_py = k.ps()
                    for j in range(n):
                        k.mm(py[:], pcat[:, c0 + j, oc * 128:(oc + 1) * 128], yt[:, c0 + j, :], j == 0, j == n - 1,
                             [t_p, t_yt], [t_py])
                    if bi == 0:
                        k.tt("dve", acc[:], py[:], sg[:], ALU.mult, [t_py, t_sg], [t_acc])
                    else:
                        k.tt("dve", sg[:], py[:], sg[:], ALU.mult, [t_py, t_sg], [t_sg])
                        if bi < 3:
                            k.tt("pool", acc[:], acc[:], sg[:], ALU.add, [t_acc, t_sg], [t_acc])
                        else:
                            k.tt("pool", mT[:, oc, :], acc[:], sg[:], ALU.add, [t_acc, t_sg], [t_mT])
            for q in range(4):
                tt_ = gq * 4 + q
                xt, t_xt = xts[q % 2]
                rd = [ctx["t_x"][s][tt_]] if l > 0 else []
                S.dma("sp", xt[:], xsrc[s, tt_ * 128:(tt_ + 1) * 128, :], reads=rd, writes=[t_xt])
                for cg in range(2):
                    po, t_po = k.ps()
                    for oc in range(8):
                        k.mm(po[:], mT[:, oc, q * 128:(q + 1) * 128], wmo[:, oc, cg * 512:(cg + 1) * 512], oc == 0,
                             oc == 7, [t_mT, t_wmo], [t_po])
                    k.tt("dve", xt[:, cg * 512:(cg + 1) * 512], xt[:, cg * 512:(cg + 1) * 512], po[:], ALU.add,
                         [t_xt, t_po], [t_xt])
                S.dma("sp", ctx["xbuf"][s, tt_ * 128:(tt_ + 1) * 128, :], xt[:], reads=[t_xt],
                      writes=[ctx["t_x"][s][tt_]])
        S.barrier()


def phase_M(ctx, s, l):
    k = ctx["k"]
    S = k.S
    t_x = ctx["t_x"][s]
    xbuf = ctx["xbuf"]
    sc = 256 ** -0.5
    with contextlib.ExitStack() as es:
        kmT = k.sb(es, [128, 8, MEM], BF16, "kmT")
        vm = k.sb(es, [128, 2, D], BF16, "vm")
        t_km, t_vm = Tr(), Tr()
        with contextlib.ExitStack() as e2:
            gkv, t_gkv = load_bc(k, e2, k.vf["mem_kv_norm_g"][l], D, "gkv")
            mnT = k.sb(e2, [128, 8, MEM], BF16, "mnT")
            t_mn = Tr()
            xt = k.sb(e2, [128, D], F32, "mx")
            ob = k.sb(e2, [128, D], BF16, "mob")
            ss = k.sb(e2, [128, 4], F32, "mss")
            t_xt, t_o, t_ss = Tr(), Tr(), Tr()
            for mt in range(2):
                S.dma("sp", xt[:], ctx["mem_in"][s, mt * 128:(mt + 1) * 128, :], writes=[t_xt])
                rms_to_bf(k, xt, t_xt, gkv, t_gkv, ob, t_o, ss, t_ss)
                pb, t_pb = k.psb()
                for j in range(8):
                    k.tp(pb[:, j * 128:(j + 1) * 128], ob[:, j * 128:(j + 1) * 128], [t_o], [t_pb])
                k.cp("dve", mnT[:, :, mt * 128:(mt + 1) * 128], pb[:].rearrange("p (j t) -> p j t", j=8), [t_pb], [t_mn])
            for half in range(4):
                wkv, t_wkv = load_w(k, e2, "w_mem_kv", l, 0, D, half * 512, 512, "wkv")
                if half < 2:
                    for fc in range(4):
                        p_, t_p = k.ps()
                        for kc in range(8):
                            k.mm(p_[:, 0:MEM], wkv[:, kc, fc * 128:(fc + 1) * 128], mnT[:, kc, :], kc == 0, kc == 7,
                                 [t_wkv, t_mn], [t_p])
                        k.cp("act", kmT[:, half * 4 + fc, :], p_[:, 0:MEM], [t_p], [t_km])
                else:
                    for mt in range(2):
                        p_, t_p = k.ps()
                        for kc in range(8):
                            k.mm(p_[:], mnT[:, kc, mt * 128:(mt + 1) * 128], wkv[:, kc, :], kc == 0, kc == 7,
                                 [t_wkv, t_mn], [t_p])
                        k.cp("act", vm[:, mt, (half - 2) * 512:(half - 1) * 512], p_[:], [t_p], [t_vm])
            S.barrier()
        gq_, t_gq = load_bc(k, es, k.vf["mem_q_norm_g"][l], D, "gq")
        wq, t_wq = load_w(k, es, "w_mem_q", l, 0, D, 0, D, "wq")
        wo, t_wo = load_w(k, es, "w_mem_o", l, 0, D, 0, D, "wo")
        xts = [(k.sb(es, [128, D], F32, "xq"), Tr()) for _ in range(4)]
        ob = k.sb(es, [128, D], BF16, "qob")
        ss = k.sb(es, [128, 4], F32, "qss")
        t_o, t_ss = Tr(), Tr()
        xnT = k.sb(es, [128, 8, 512], BF16, "xnT")
        t_xn = Tr()
        qT = k.sb(es, [128, 8, 512], BF16, "qT")
        t_qT = Tr()
        ETs = [(k.sb(es, [128, 2, 512], BF16, "ET"), Tr()) for _ in range(2)]
        oT = k.sb(es, [128, 8, 512], BF16, "oT")
        t_oT = Tr()
        rden = k.sb(es, [128, 512], F32, "rden")
        t_rd = Tr()
        for gq in range(8):
            for q in range(4):
                tt_ = gq * 4 + q
                xt, t_xt = xts[q]
                S.dma("sp", xt[:], xbuf[s, tt_ * 128:(tt_ + 1) * 128, :], reads=[t_x[tt_]], writes=[t_xt])
                rms_to_bf(k, xt, t_xt, gq_, t_gq, ob, t_o, ss, t_ss)
                pb, t_pb = k.psb()
                for j in range(8):
                    k.tp(pb[:, j * 128:(j + 1) * 128], ob[:, j * 128:(j + 1) * 128], [t_o], [t_pb])
                k.cp("dve", xnT[:, :, q * 128:(q + 1) * 128], pb[:].rearrange("p (j t) -> p j t", j=8), [t_pb], [t_xn])
            for fc in range(8):
                p_, t_p = k.ps()
                for kc in range(8):
                    k.mm(p_[:], wq[:, kc, fc * 128:(fc + 1) * 128], xnT[:, kc, :], kc == 0, kc == 7, [t_wq, t_xn], [t_p])
                k.cp("act", qT[:, fc, :], p_[:], [t_p], [t_qT])
            for h in range(4):
                ET, t_ET = ETs[h % 2]
                for mt in range(2):
                    p_, t_p = k.ps()
                    for j in range(2):
                        k.mm(p_[:], kmT[:, h * 2 + j, mt * 128:(mt + 1) * 128], qT[:, h * 2 + j, :], j == 0, j == 1,
                             [t_km, t_qT], [t_p])
                    k.act(ET[:, mt, :], p_[:], AF.Exp, [t_p], [t_ET], scale=sc)
                pd, t_pd = k.ps()
                for mt in range(2):
                    k.mm(pd[:], k.cs["ones_b"][:], ET[:, mt, :], mt == 0, mt == 1, [t_ET, k.t_const], [t_pd])
                k.recip(rden[:], pd[:], [t_pd], [t_rd])
                for j in range(2):
                    p_, t_p = k.ps()
                    for mt in range(2):
                        k.mm(p_[:], vm[:, mt, (h * 2 + j) * 128:(h * 2 + j + 1) * 128], ET[:, mt, :], mt == 0, mt == 1,
                             [t_vm, t_ET], [t_p])
                    k.tt("dve", oT[:, h * 2 + j, :], p_[:], rden[:], ALU.mult, [t_p, t_rd], [t_oT])
            for q in range(4):
                tt_ = gq * 4 + q
                xt, t_xt = xts[q]
                for cg in range(2):
                    po, t_po = k.ps()
                    for oc in range(8):
                        k.mm(po[:], oT[:, oc, q * 128:(q + 1) * 128], wo[:, oc, cg * 512:(cg + 1) * 512], oc == 0,
                             oc == 7, [t_oT, t_wo], [t_po])
                    k.tt("dve", xt[:, cg * 512:(cg + 1) * 512], xt[:, cg * 512:(cg + 1) * 512], po[:], ALU.add,
                         [t_xt, t_po], [t_xt])
                S.dma("sp", xbuf[s, tt_ * 128:(tt_ + 1) * 128, :], xt[:], reads=[t_xt], writes=[t_x[tt_]])
        S.barrier()


def phase_F(ctx, s, l):
    k = ctx["k"]
    S = k.S
    t_x = ctx["t_x"][s]
    xbuf = ctx["xbuf"]
    NC = 2 * DFF // 128
    with contextlib.ExitStack() as es:
        gf, t_gf = load_bc(k, es, k.vf["ffn_norm_g"][l], D, "gf")
        w1, t_w1 = load_w(k, es, "w_ffn_in", l, 0, D, 0, 2 * DFF, "wf1")
        w2, t_w2 = load_w(k, es, "w_ffn_out", l, 0, DFF, 0, D, "wf2")
        cw = k.sb(es, [128, 4, NC], F32, "cw")
        t_cw = Tr()
        for j in range(3):
            S.dma("sp", cw[:, j, :], k.vf["ffn_conv_w"][l, j].rearrange("(c p) -> p c", p=128), writes=[t_cw],
                  allow_slow_non_contiguous=True)
        S.dma("sp", cw[:, 3, :], k.vf["ffn_conv_b"][l].rearrange("(c p) -> p c", p=128), writes=[t_cw],
              allow_slow_non_contiguous=True)
        halo = k.sb(es, [128, NC, 2], F32, "halo")
        t_halo = [Tr() for _ in range(NC)]
        k.memset(halo[:], 0.0, t_halo)
        xps = [(k.sb(es, [128, D], F32, "xfp"), Tr()) for _ in range(1)]
        xos = [(k.sb(es, [128, D], F32, "xfo"), Tr()) for _ in range(1)]
        ob = k.sb(es, [128, D], BF16, "fob")
        ss = k.sb(es, [128, 4], F32, "fss")
        t_o, t_ss = Tr(), Tr()
        xnTs = [(k.sb(es, [128, 8, 512], BF16, "fxnT"), Tr()) for _ in range(2)]
        ues = [(k.sb(es, [128, 514], F32, "ue"), Tr()) for _ in range(2)]
        c0s = [(k.sb(es, [128, 512], F32, "c0"), Tr()) for _ in range(2)]
        sgs = [(k.sb(es, [128, 512], F32, "sgf"), Tr()) for _ in range(2)]
        aT = k.sb(es, [128, DFF // 128, 512], BF16, "aT")
        t_aT = Tr()
        def prep(gq):
            xnT, t_xn = xnTs[gq % 2]
            for q in range(4):
                tt_ = gq * 4 + q
                xt, t_xt = xps[0]
                S.dma("sp", xt[:], xbuf[s, tt_ * 128:(tt_ + 1) * 128, :], reads=[t_x[tt_]], writes=[t_xt])
                rms_to_bf(k, xt, t_xt, gf, t_gf, ob, t_o, ss, t_ss)
                pb, t_pb = k.psb()
                for j in range(8):
                    k.tp(pb[:, j * 128:(j + 1) * 128], ob[:, j * 128:(j + 1) * 128], [t_o], [t_pb])
                k.cp("dve", xnT[:, :, q * 128:(q + 1) * 128], pb[:].rearrange("p (j t) -> p j t", j=8), [t_pb], [t_xn])

        prep(0)
        for gq in range(8):
            xnT, t_xn = xnTs[gq % 2]
            it = 0
            for j in range(DFF // 128):
                res = []
                for c in (j, j + 22):
                    p_, t_p = k.ps()
                    for kc in range(8):
                        k.mm(p_[:], w1[:, kc, c * 128:(c + 1) * 128], xnT[:, kc, :], kc == 0, kc == 7, [t_w1, t_xn], [t_p])
                    ue, t_ue = ues[it % 2]
                    c0, t_c0 = c0s[it % 2]
                    it += 1
                    k.cp("pool", ue[:, 0:2], halo[:, c, :], [t_halo[c]], [t_ue])
                    k.cp("act", ue[:, 2:514], p_[:], [t_p], [t_ue])
                    k.act(c0[:], p_[:], AF.Identity, [t_p, t_cw], [t_c0], scale=cw[:, 2, c:c + 1], bias=cw[:, 3, c:c + 1])
                    k.cp("pool", halo[:, c, :], ue[:, 512:514], [t_ue], [t_halo[c]])
                    k.stt(c0[:], ue[:, 1:513], cw[:, 1, c:c + 1], c0[:], ALU.mult, ALU.add, [t_ue, t_cw, t_c0], [t_c0])
                    k.stt(c0[:], ue[:, 0:512], cw[:, 0, c:c + 1], c0[:], ALU.mult, ALU.add, [t_ue, t_cw, t_c0], [t_c0])
                    res.append((c0, t_c0))
                sg, t_sg = sgs[j % 2]
                k.act(sg[:], res[0][0][:], AF.Silu, [res[0][1]], [t_sg])
                k.tt("pool", aT[:, j, :], sg[:], res[1][0][:], ALU.mult, [t_sg, res[1][1]], [t_aT])
            if gq + 1 < 8:
                prep(gq + 1)
            for q in range(4):
                tt_ = gq * 4 + q
                xt, t_xt = xos[0]
                S.dma("sp", xt[:], xbuf[s, tt_ * 128:(tt_ + 1) * 128, :], reads=[t_x[tt_]], writes=[t_xt])
                for cg in range(2):
                    po, t_po = k.ps()
                    for j in range(DFF // 128):
                        k.mm(po[:], aT[:, j, q * 128:(q + 1) * 128], w2[:, j, cg * 512:(cg + 1) * 512], j == 0,
                             j == DFF // 128 - 1, [t_aT, t_w2], [t_po])
                    k.tt("dve", xt[:, cg * 512:(cg + 1) * 512], xt[:, cg * 512:(cg + 1) * 512], po[:], ALU.add,
                         [t_xt, t_po], [t_xt])
                S.dma("sp", xbuf[s, tt_ * 128:(tt_ + 1) * 128, :], xt[:], reads=[t_xt], writes=[t_x[tt_]])
        S.barrier()


def phase_final(ctx, s):
    k = ctx["k"]
    S = k.S
    with contextlib.ExitStack() as es:
        gbc, t_g = load_bc(k, es, k.vf["final_norm_g"], D, "gfin")
        xts = [(k.sb(es, [128, D], F32, "xo"), Tr()) for _ in range(2)]
        junk = k.sb(es, [128, D], BF16, "junk")
        t_j = Tr()
        sss = [(k.sb(es, [128, 4], F32, "oss"), Tr()) for _ in range(2)]
        for tt_ in range(NT):
            xt, t_xt = xts[tt_ % 2]
            ss, t_ss = sss[tt_ % 2]
            S.dma("sp", xt[:], ctx["xbuf"][s, tt_ * 128:(tt_ + 1) * 128, :], reads=[ctx["t_x"][s][tt_]], writes=[t_xt])
            k.act(junk[:], xt[:], AF.Square, [t_xt], [t_j, t_ss], accum_out=ss[:, 0:1])
            k.ts("dve", ss[:, 1:2], ss[:, 0:1], 1.0 / D, EPS, ALU.mult, ALU.add, [t_ss], [t_ss])
            k.act(ss[:, 2:3], ss[:, 1:2], AF.Sqrt, [t_ss], [t_ss])
            k.recip(ss[:, 3:4], ss[:, 2:3], [t_ss], [t_ss])
            k.stt(xt[:], xt[:], ss[:, 3:4], gbc[:], ALU.mult, ALU.mult, [t_xt, t_ss, t_g], [t_xt])
            S.dma("sp", ctx["out"][s, tt_ * 128:(tt_ + 1) * 128, :], xt[:], reads=[t_xt])
        S.barrier()


def sl(st, n, step):
    return slice(st, st + step * (n - 1) + 1, step)


def phase_rope(ctx, es, s):
    k = ctx["k"]
    S = k.S
    cosT = k.sb(es, [128, S_LEN], F32, "cosT")
    sinT = k.sb(es, [128, S_LEN], F32, "sinT")
    t_tab = Tr()
    CW = 1024
    with contextlib.ExitStack() as e2:
        posi = k.sb(e2, [128, CW], I32, "posi")
        tq = k.sb(e2, [128, CW], F32, "tq")
        ki = k.sb(e2, [128, CW], I32, "ki")
        kf = k.sb(e2, [128, CW], F32, "kf")
        fr = k.sb(e2, [128, CW], F32, "fr")
        aa = k.sb(e2, [128, CW], F32, "aa")
        t_ = Tr()
        invf = k.cs["c_invf"]
        for c in range(S_LEN // CW):
            cols = slice(c * CW, (c + 1) * CW)
            S.dma("sp", posi[:], ctx["pos_in"][s, cols].partition_broadcast(128), writes=[t_])
            k.cp("dve", tq[:], posi[:], [t_], [t_])
            k.ts("dve", tq[:], tq[:], invf[:, 0:1], 1.0 / (2 * math.pi), ALU.mult, ALU.mult, [t_, k.t_const], [t_])
            k.cp("dve", ki[:], tq[:], [t_], [t_])
            k.cp("dve", kf[:], ki[:], [t_], [t_])
            k.tt("dve", fr[:], tq[:], kf[:], ALU.subtract, [t_], [t_])
            for dst, shift in ((sinT, 0.0), (cosT, 0.25)):
                if shift:
                    k.ts("dve", fr[:], fr[:], shift, None, ALU.add, ALU.bypass, [t_], [t_])
                k.ts("dve", aa[:], fr[:], 0.5, None, ALU.is_gt, ALU.bypass, [t_], [t_])
                k.tt("dve", fr[:], fr[:], aa[:], ALU.subtract, [t_], [t_])
                k.ts("dve", aa[:], fr[:], -0.5, None, ALU.is_lt, ALU.bypass, [t_], [t_])
                k.tt("dve", fr[:], fr[:], aa[:], ALU.add, [t_], [t_])
                k.act(dst[:, cols], fr[:], AF.Sin, [t_], [t_tab], scale=6.283185)
        S.barrier()
    return cosT, sinT, t_tab


def proj_rot(k, es_bufs, hT, t_hT, w, t_w, tabs, dst, t_dst):
    cosT, sinT, t_tab = tabs
    qraw, t_qr, t1, t_t1, t2, t_t2 = es_bufs
    pm = k.cs["c_pm"]
    for gq in range(8):
        cols = slice(gq * 512, (gq + 1) * 512)
        hs = slice(HOFF + gq * 512, HOFF + (gq + 1) * 512)
        p_, t_p = k.ps()
        for kc in range(8):
            k.mm(p_[:], w[:, kc, :], hT[:, kc, hs], kc == 0, kc == 7, [t_w] + t_hT[gq * 4:(gq + 1) * 4], [t_p])
        k.cp("act", qraw[:], p_[:], [t_p], [t_qr])
        p2, t_p2 = k.ps()
        k.mm(p2[:], pm[:], qraw[:], True, True, [t_qr, k.t_const], [t_p2])
        k.tt("dve", t1[:], p_[:], cosT[:, cols], ALU.mult, [t_p, t_tab], [t_t1])
        k.tt("dve", t2[:], p2[:], sinT[:, cols], ALU.mult, [t_p2, t_tab], [t_t2])
        k.tt("pool", dst[:, cols], t1[:], t2[:], ALU.add, [t_t1, t_t2], [t_dst])


def rot_bufs(k, es):
    return (k.sb(es, [128, 512], BF16, "qraw"), Tr(), k.sb(es, [128, 512], F32, "rt1"), Tr(),
            k.sb(es, [128, 512], F32, "rt2"), Tr())


def phase_A(ctx, s, l, hT, t_hT, tabs):
    k = ctx["k"]
    S = k.S
    maskA = k.cs["c_maskA"]
    ones_b = k.cs["ones_b"]
    with contextlib.ExitStack() as es:
        rb = rot_bufs(k, es)
        qT = k.sb(es, [128, S_LEN], BF16, "aqT")
        kT = k.sb(es, [128, S_LEN], BF16, "akT")
        vs = k.sb(es, [128, 32, 128], BF16, "avs")
        acc = k.sb(es, [128, 2, S_LEN], F32, "aacc")
        yb = k.sb(es, [128, S_LEN], BF16, "ayb")
        t_q, t_k, t_v, t_acc, t_yb = Tr(), Tr(), Tr(), Tr(), Tr()
        Es = [(k.sb(es, [128, 2, 256], BF16, "aE"), Tr()) for _ in range(3)]
        ws = [(k.sb(es, [128, 8, 128], BF16, "aw"), Tr()) for _ in range(3)]
        for hp in range(2):
            for g, Dl in enumerate((1, 4, 16)):
                for j3 in range(3):
                    load_w_into(k, ws[j3][0], ws[j3][1], "w_in", l, 0, D, OFF_A + g * 768 + j3 * 256 + hp * 128, 128)
                proj_rot(k, rb, hT, t_hT, ws[0][0], ws[0][1], tabs, qT, t_q)
                proj_rot(k, rb, hT, t_hT, ws[1][0], ws[1][1], tabs, kT, t_k)
                nb = 32 // Dl
                wv, t_wv = ws[2]
                for b0 in range(0, 32, 4):
                    p_, t_p = k.ps()
                    for bb in range(4):
                        blk = b0 + bb
                        r, j = blk // nb, blk % nb
                        st = HOFF + r + Dl * 128 * j
                        for kc in range(8):
                            k.mm(p_[:, bb * 128:(bb + 1) * 128], hT[:, kc, sl(st, 128, Dl)], wv[:, kc, :], kc == 0,
                                 kc == 7, [t_wv] + t_hT, [t_p])
                    k.cp("act", vs[:, b0:b0 + 4, :], p_[:].rearrange("p (b c) -> p b c", b=4), [t_p], [t_v])
                items = [(r, j) for r in range(Dl) for j in range(nb)]

                def st1(i):
                    r, j = items[i]
                    nq = 256 if j + 1 < nb else 128
                    st = r + Dl * 128 * j
                    kcols = sl(st, 128, Dl)
                    qcols = sl(st, nq, Dl)
                    E, t_E = Es[i % 3]
                    for h2 in range(2):
                        hb = 64 * h2
                        p_, t_p = k.ps()
                        k.mm(p_[:, 0:nq], kT[hb:hb + 64, kcols], qT[hb:hb + 64, qcols], True, True,
                             [t_k, t_q], [t_p])
                        k.act(E[:, h2, 0:nq], p_[:, 0:nq], AF.Exp, [t_p], [t_E], scale=0.125)
                    k.tt("pool", E[:, :, 0:nq], E[:, :, 0:nq], maskA[:].rearrange("p (a b) -> p a b", a=2)[:, :, 0:nq],
                         ALU.mult, [t_E, k.t_const], [t_E])

                def st2(i):
                    r, j = items[i]
                    st = r + Dl * 128 * j
                    E, t_E = Es[i % 3]
                    Eprev = Es[(i - 1) % 3]
                    blk = r * nb + j
                    po, t_po = k.ps()
                    for h2 in range(2):
                        hb = 64 * h2
                        for pl in range(2):
                            o_ap = po[hb:hb + 64, pl * 128:(pl + 1) * 128]
                            first = True
                            if j > 0:
                                lh = vs[:, blk - 1, hb:hb + 64] if pl == 0 else ones_b[:, 0:64]
                                k.mm(o_ap, lh, Eprev[0][:, h2, 128:256], True, False, [t_v, Eprev[1], k.t_const], [t_po])
                                first = False
                            lh = vs[:, blk, hb:hb + 64] if pl == 0 else ones_b[:, 0:64]
                            k.mm(o_ap, lh, E[:, h2, 0:128], first, True, [t_v, t_E, k.t_const], [t_po])
                    qtok = sl(st, 128, Dl)
                    pov = po[:, 0:256].rearrange("p (a b) -> p a b", a=2)
                    if g == 0:
                        k.cp("dve", acc[:, :, qtok], pov, [t_po], [t_acc])
                    else:
                        k.tt("dve", acc[:, :, qtok], acc[:, :, qtok], pov, ALU.add, [t_po, t_acc], [t_acc])
                st1(0)
                for i in range(len(items)):
                    if i + 1 < len(items):
                        st1(i + 1)
                    st2(i)
            for gq in range(8):
                cols = slice(gq * 512, (gq + 1) * 512)
                k.recip(acc[:, 1, cols], acc[:, 1, cols], [t_acc], [t_acc])
                k.tt("dve", yb[:, cols], acc[:, 0, cols], acc[:, 1, cols], ALU.mult, [t_acc], [t_yb])
                S.dma("sp", ctx["yT"][hp * 128:(hp + 1) * 128, cols], yb[:, cols], reads=[t_yb],
                      writes=[ctx["t_yT"][hp][gq]])
        S.barrier()


def phase_B(ctx, s, l, hT, t_hT, tabs):
    k = ctx["k"]
    S = k.S
    maskD = k.cs["c_maskD"]
    ones_b = k.cs["ones_b"]
    lam_init = 0.8 - 0.6 * math.exp(-0.3 * l)
    saved = (k.ps_f, k.ps_ft, k.ps_i)
    accb = list(zip(k.ps_f[0:4], k.ps_ft[0:4]))
    k.ps_f = [saved[0][4], saved[0][5], k.ps_b[0][:].bitcast(F32), k.ps_b[1][:].bitcast(F32)]
    k.ps_ft = [saved[1][4], saved[1][5], k.ps_bt[0], k.ps_bt[1]]
    k.ps_i = 0
    with contextlib.ExitStack() as es:
        rb = rot_bufs(k, es)
        lv, t_lv = load_bc(k, es, k.vf["diff_lam"][l].rearrange("a b -> (a b)"), 256, "lv")
        sc_ = k.sb(es, [128, 8], F32, "lsc")
        t_sc = Tr()
        pr = k.sb(es, [128, 128], F32, "lpr")
        k.tt("dve", pr[:, 0:64], lv[:, 0:64], lv[:, 64:128], ALU.mult, [t_lv], [t_sc])
        k.tt("dve", pr[:, 64:128], lv[:, 128:192], lv[:, 192:256], ALU.mult, [t_lv], [t_sc])
        S.op("dve", lambda e: e.reduce_sum(out=sc_[:, 0:2], in_=pr[:].rearrange("p (a b) -> p a b", a=2), axis=AX.X),
             reads=[t_sc], writes=[t_sc])
        k.act(sc_[:, 2:4], sc_[:, 0:2], AF.Exp, [t_sc], [t_sc])
        k.tt("dve", sc_[:, 4:5], sc_[:, 3:4], sc_[:, 2:3], ALU.subtract, [t_sc], [t_sc])
        k.ts("dve", sc_[:, 5:6], sc_[:, 4:5], -lam_init, None, ALU.add, ALU.bypass, [t_sc], [t_sc])
        gB = k.sb(es, [128, 1], F32, "gB")
        t_gB = Tr()
        S.dma("sp", gB[:], k.vf["diff_norm_g"][l].rearrange("(p o) -> p o", o=1), writes=[t_gB])
        k.ts("dve", gB[:], gB[:], 1.0 - lam_init, None, ALU.mult, ALU.bypass, [t_gB], [t_gB])
        qT = k.sb(es, [128, S_LEN], BF16, "bqT")
        kT = k.sb(es, [128, S_LEN], BF16, "bkT")
        vs = k.sb(es, [128, 32, 128], BF16, "bvs")
        t_q, t_k, t_v = Tr(), Tr(), Tr()
        Es = [(k.sb(es, [128, 512], BF16, "bE"), Tr()) for _ in range(6)]
        ws = [(k.sb(es, [128, 8, 128], BF16, "bw"), Tr()) for _ in range(3)]
        o1 = k.sb(es, [128, 512], F32, "bo1")
        o2 = k.sb(es, [128, 512], F32, "bo2")
        rr = k.sb(es, [128, 512], F32, "brr")
        sq = k.sb(es, [128, 512], BF16, "bsq")
        ybs = [(k.sb(es, [128, 512], BF16, "byb"), Tr()) for _ in range(2)]
        t_o = Tr()
        for h in range(4):
            for j3 in range(3):
                load_w_into(k, ws[j3][0], ws[j3][1], "w_in", l, 0, D, OFF_B + j3 * 512 + h * 128, 128)
            proj_rot(k, rb, hT, t_hT, ws[0][0], ws[0][1], tabs, qT, t_q)
            proj_rot(k, rb, hT, t_hT, ws[1][0], ws[1][1], tabs, kT, t_k)
            wv, t_wv = ws[2]
            for b0 in range(0, 32, 4):
                p_, t_p = k.ps()
                for bb in range(4):
                    blk = b0 + bb
                    st = HOFF + 128 * blk
                    for kc in range(8):
                        k.mm(p_[:, bb * 128:(bb + 1) * 128], hT[:, kc, st:st + 128], wv[:, kc, :], kc == 0, kc == 7,
                             [t_wv] + t_hT, [t_p])
                k.cp("act", vs[:, b0:b0 + 4, :], p_[:].rearrange("p (b c) -> p b c", b=4), [t_p], [t_v])
            items = [(G, j) for G in range(8) for j in range(4 * G + 4)]
            LA = 1

            def geo(G, j):
                jj = max(j - 4 * G, 0)
                qoff = 128 * jj
                return qoff, 512 - qoff, 512 * G + qoff

            def st1(i):
                G, j = items[i]
                qoff, nq, q0 = geo(G, j)
                pp = []
                for m in range(2):
                    hb = 64 * m
                    p_, t_p = k.ps()
                    k.mm(p_[:, 0:nq], kT[hb:hb + 64, 128 * j:128 * j + 128], qT[hb:hb + 64, q0:q0 + nq], True, True,
                         [t_k, t_q], [t_p])
                    pp.append((p_, t_p))
                for m in range(2):
                    p_, t_p = pp[m]
                    E, t_E = Es[(i % 3) * 2 + m]
                    k.act(E[:, 0:nq], p_[:, 0:nq], AF.Exp, [t_p], [t_E], scale=0.125)
                    if j >= 4 * G:
                        k.tt("dve", E[:, 0:nq], E[:, 0:nq], maskD[:, 0:nq], ALU.mult, [t_E, k.t_const], [t_E])

            def st2(i):
                G, j = items[i]
                qoff, nq, q0 = geo(G, j)
                nj = 4 * G + 4
                for m in range(2):
                    E, t_E = Es[(i % 3) * 2 + m]
                    pn, t_pn = accb[2 * m]
                    pd, t_pd = accb[2 * m + 1]
                    k.mm(pn[:, qoff:512], vs[:, j, :], E[:, 0:nq], j == 0, j == nj - 1, [t_v, t_E], [t_pn])
                    k.mm(pd[:, qoff:512], ones_b[:], E[:, 0:nq], j == 0, j == nj - 1, [t_E, k.t_const], [t_pd])
                if j == nj - 1:
                    for m in range(2):
                        pn, t_pn = accb[2 * m]
                        pd, t_pd = accb[2 * m + 1]
                        k.recip(rr[:], pd[:], [t_pd, t_o], [t_o])
                        k.tt("dve", (o1 if m == 0 else o2)[:], pn[:], rr[:], ALU.mult, [t_pn, t_o], [t_o])
                    fin(G)

            def fin(G):
                k.stt(o1[:], o2[:], sc_[:, 5:6], o1[:], ALU.mult, ALU.add, [t_o, t_sc], [t_o])
                k.tt("pool", sq[:], o1[:], o1[:], ALU.mult, [t_o], [t_o])
                pm_, t_pm = k.ps()
                k.mm(pm_[:], ones_b[:], sq[:], True, True, [t_o, k.t_const], [t_pm])
                k.ts("dve", rr[:], pm_[:], 1.0 / 128, EPS, ALU.mult, ALU.add, [t_pm, t_o], [t_o])
                k.act(rr[:], rr[:], AF.Sqrt, [t_o], [t_o])
                k.recip(rr[:], rr[:], [t_o], [t_o])
                yb, t_yb = ybs[G % 2]
                k.stt(yb[:], o1[:], gB[:, 0:1], rr[:], ALU.mult, ALU.mult, [t_o, t_gB], [t_yb])
                S.dma("sp", ctx["yT"][256 + h * 128:256 + (h + 1) * 128, G * 512:(G + 1) * 512], yb[:], reads=[t_yb],
                      writes=[ctx["t_yT"][2 + h][G]])

            for i in range(min(LA, len(items))):
                st1(i)
            for i in range(len(items)):
                if i + LA < len(items):
                    st1(i + LA)
                st2(i)
        S.barrier()
    k.ps_f, k.ps_ft, k.ps_i = saved


def phase_C(ctx, s, l, hT, t_hT):
    k = ctx["k"]
    S = k.S
    cs = k.cs
    with contextlib.ExitStack() as es:
        wc, t_wc = load_w(k, es, "w_in", l, 0, D, OFF_C, 1024, "wc")
        lb = k.sb(es, [128, 256], F32, "lb")
        omlb = k.sb(es, [128, 256], F32, "omlb")
        t_lb = Tr()
        if l == 0:
            k.memset(lb[:], 0.0, [t_lb])
        else:
            S.dma("sp", lb[:], k.vf["hgrn_lb_logits"][1].partition_broadcast(128), writes=[t_lb])
            S.dma("sp", omlb[:], k.vf["hgrn_lb_logits"][0].partition_broadcast(128), writes=[t_lb])
            k.tt("dve", lb[:], lb[:], omlb[:], ALU.subtract, [t_lb], [t_lb])
            k.act(lb[:], lb[:], AF.Sigmoid, [t_lb], [t_lb])
        k.ts("dve", omlb[:], lb[:], -1.0, 1.0, ALU.mult, ALU.add, [t_lb], [t_lb])
        gn4 = k.sb(es, [128, 4, 64], F32, "gn4")
        t_gn = Tr()
        for h in range(4):
            S.dma("sp", gn4[:, h, :], k.vf["hgrn_norm_g"][l].partition_broadcast(128), writes=[t_gn])
        S32 = k.sb(es, [128, 2, 64], F32, "S32")
        t_S = Tr()
        k.memset(S32[:], 0.0, [t_S])
        Sbfs = [(k.sb(es, [128, 2, 64], BF16, "Sbf"), Tr()) for _ in range(2)]
        k.memset(Sbfs[0][0][:], 0.0, [Sbfs[0][1]])

        def mk(shape, dt, n, nm):
            return [(k.sb(es, shape, dt, nm), Tr()) for _ in range(n)]
        qs_, sf_, lf_, kk_, sg_ = (mk([128, 256], F32, 2, nm) for nm in ("cqs", "csf", "clf", "ckk", "csg"))
        ib_, Qp_, Kp_ = (mk([128, 256], BF16, 2, nm) for nm in ("cib", "cQp", "cKp"))
        eb_, enb_ = (mk([128, 256], F32, 2, nm) for nm in ("ceb", "cenb"))
        colv_ = mk([128, 2, 3], F32, 2, "ccolv")
        dd_ = mk([128, 2], F32, 2, "cdd")
        QT_, QTt_, KT_ = (mk([128, 2, 128], BF16, 2, nm) for nm in ("cQT", "cQTt", "cKT"))
        attE_, attO_ = (mk([128, 2, 128], BF16, 2, nm) for nm in ("cattE", "cattO"))
        QTh_ = mk([128, 2, 128], BF16, 2, "cQTh")
        for b_ in range(2):
            k.memset(QTh_[b_][0][:], 0.0, [QTh_[b_][1]])
        o_ = mk([128, 256], F32, 2, "co")
        sq_ = mk([128, 256], F32, 2, "csq")
        st_ = mk([128, 12], F32, 2, "cst")
        tmpU_ = mk([128, 2, 64], F32, 2, "ctu")
        y_ = mk([128, 256], BF16, 2, "cy")
        ygs = mk([128, 2, 512], BF16, 2, "cyg")
        tri2 = cs["c_tri2"]
        for tt_ in range(NT):
            b = tt_ % 2
            hs = slice(HOFF + tt_ * 128, HOFF + (tt_ + 1) * 128)
            p0, t_p0 = k.ps()
            p1, t_p1 = k.ps()
            for kc in range(8):
                k.mm(p0[:], hT[:, kc, hs], wc[:, kc, 0:512], kc == 0, kc == 7, [t_wc, t_hT[tt_]], [t_p0])
            for kc in range(8):
                k.mm(p1[:], hT[:, kc, hs], wc[:, kc, 512:1024], kc == 0, kc == 7, [t_wc, t_hT[tt_]], [t_p1])
            qs, t_qs = qs_[b]
            sf, t_sf = sf_[b]
            lf, t_lf = lf_[b]
            kk, t_kk = kk_[b]
            sg, t_sg = sg_[b]
            ib, t_ib = ib_[b]
            k.act(qs[:], p0[:, 0:256], AF.Silu, [t_p0], [t_qs])
            k.act(sf[:], p0[:, 256:512], AF.Sigmoid, [t_p0], [t_sf])
            k.cp("act", ib[:], p1[:, 0:256], [t_p1], [t_ib])
            k.act(sg[:], p1[:, 256:512], AF.Silu, [t_p1], [t_sg])
            k.tt("dve", sf[:], sf[:], omlb[:], ALU.mult, [t_sf, t_lb], [t_sf])
            k.tt("dve", sf[:], sf[:], lb[:], ALU.add, [t_sf, t_lb], [t_sf])
            k.act(lf[:], sf[:], AF.Ln, [t_sf], [t_lf])
            k.ts("pool", kk[:], sf[:], -1.0, 1.0, ALU.mult, ALU.add, [t_sf], [t_kk])
            pb_, t_pb_ = k.ps()
            k.mm(pb_[:, 0:256], cs["c_triM"][:], lf[:], True, True, [t_lf, k.t_const], [t_pb_])
            pc, t_pc = k.ps()
            for hp in range(2):
                k.mm(pc[:, hp * 2:hp * 2 + 2], lf[:, hp * 128:(hp + 1) * 128], cs["c_sel"][:], True, True,
                     [t_lf, k.t_const], [t_pc])
            colv, t_cv = colv_[b]
            dd, t_dd = dd_[b]
            pcv = pc[:, 0:4].rearrange("p (a c) -> p a c", a=2)
            k.act(colv[:, :, 0:2], pcv, AF.Exp, [t_pc], [t_cv])
            k.cp("act", st_[b][0][:, 0:2], pcv[:, :, 0], [t_pc], [st_[b][1]])
            k.tt("dve", dd[:], pcv[:, :, 1], st_[b][0][:, 0:2], ALU.subtract, [t_pc, st_[b][1]], [t_dd])
            k.act(colv[:, :, 2], dd[:], AF.Exp, [t_dd], [t_cv])
            eb, t_eb = eb_[b]
            enb, t_enb = enb_[b]
            k.act(eb[:], pb_[:, 0:256], AF.Exp, [t_pb_], [t_eb])
            k.act(enb[:], pb_[:, 0:256], AF.Exp, [t_pb_], [t_enb], scale=-1.0)
            Qp, t_Qp = Qp_[b]
            Kp, t_Kp = Kp_[b]
            k.tt("dve", Qp[:], qs[:], eb[:], ALU.mult, [t_qs, t_eb], [t_Qp])
            k.tt("pool", Kp[:], kk[:], enb[:], ALU.mult, [t_kk, t_enb], [t_Kp])
            pt, t_pt = k.psb()
            for hp in range(2):
                k.tp(pt[:, hp * 128:(hp + 1) * 128], Qp[:, hp * 128:(hp + 1) * 128], [t_Qp], [t_pt])
                k.tp(pt[:, 256 + hp * 128:256 + (hp + 1) * 128], Kp[:, hp * 128:(hp + 1) * 128], [t_Kp], [t_pt])
            QT, t_QT = QT_[b]
            QTt, t_QTt = QTt_[b]
            KT, t_KT = KT_[b]
            k.cp("act", QT[:], pt[:, 0:256].rearrange("p (a c) -> p a c", a=2), [t_pt], [t_QT])
            k.cp("act", KT[:], pt[:, 256:512].rearrange("p (a c) -> p a c", a=2), [t_pt], [t_KT])
            QTh, t_QTh = QTh_[b]
            k.cp("pool", QTh[:, :, 64:128], QT[:, :, 64:128], [t_QT], [t_QTh])
            for hp in range(2):
                k.ts("dve", QTt[:, hp, :], pt[:, hp * 128:(hp + 1) * 128], colv[:, hp, 0:1], None, ALU.mult, ALU.bypass,
                     [t_pt, t_cv], [t_QTt])
            paE, t_paE = k.ps()
            paO, t_paO = k.ps()
            for h in range(4):
                hp, par = h // 2, h % 2
                hb = 64 * par
                pa, t_pa = (paE, t_paE) if par == 0 else (paO, t_paO)
                k.mm(pa[0:64, hp * 128:(hp + 1) * 128], KT[hb:hb + 64, hp, 0:64], QT[hb:hb + 64, hp, :], True, True,
                     [t_KT, t_QT], [t_pa])
                k.mm(pa[64:128, hp * 128:(hp + 1) * 128], KT[hb:hb + 64, hp, 64:128], QTh[hb:hb + 64, hp, :], True, True,
                     [t_KT, t_QTh], [t_pa])
            attE, t_aE = attE_[b]
            attO, t_aO = attO_[b]
            k.tt("dve", attE[:].rearrange("p a c -> p (a c)"), paE[:, 0:256], tri2[:], ALU.mult, [t_paE, k.t_const], [t_aE])
            k.tt("dve", attO[:].rearrange("p a c -> p (a c)"), paO[:, 0:256], tri2[:], ALU.mult, [t_paO, k.t_const], [t_aO])
            po, t_po = k.ps()
            for h in range(4):
                hp, par = h // 2, h % 2
                att, t_att = (attE, t_aE) if par == 0 else (attO, t_aO)
                k.mm(po[:, h * 64:(h + 1) * 64], att[:, hp, :], ib[:, h * 64:(h + 1) * 64], True, True, [t_att, t_ib], [t_po])
            Sbf, t_Sbf = Sbfs[b]
            Sbn, t_Sbn = Sbfs[1 - b]
            piE, t_piE = k.ps()
            piO, t_piO = k.ps()
            for h in range(4):
                hp, par = h // 2, h % 2
                hb = 64 * par
                pi, t_pi = (piE, t_piE) if par == 0 else (piO, t_piO)
                k.mm(pi[:, hp * 64:(hp + 1) * 64], QTt[hb:hb + 64, hp, :], Sbf[hb:hb + 64, hp, :], True, True,
                     [t_QTt, t_Sbf], [t_pi])
            o, t_o = o_[b]
            k.cp("act", o[:], po[:, 0:256], [t_po], [t_o])
            ov = o[:].rearrange("p (a b c) -> p a b c", a=2, b=2)
            k.tt("dve", ov[:, :, 0, :], ov[:, :, 0, :], piE[:, 0:128].rearrange("p (a c) -> p a c", a=2), ALU.add,
                 [t_o, t_piE], [t_o])
            k.tt("dve", ov[:, :, 1, :], ov[:, :, 1, :], piO[:, 0:128].rearrange("p (a c) -> p a c", a=2), ALU.add,
                 [t_o, t_piO], [t_o])
            pu, t_pu = k.ps()
            for h in range(4):
                hp, par = h // 2, h % 2
                hb = 64 * par
                k.mm(pu[hb:hb + 64, hp * 64:(hp + 1) * 64], Kp[:, h * 64:(h + 1) * 64], ib[:, h * 64:(h + 1) * 64], True, True,
                     [t_Kp, t_ib], [t_pu])
            tu, t_tu = tmpU_[b]
            for hp in range(2):
                k.ts("dve", tu[:, hp, :], pu[:, hp * 64:(hp + 1) * 64], colv[:, hp, 2:3], None, ALU.mult, ALU.bypass,
                     [t_pu, t_cv], [t_tu])
                k.stt(S32[:, hp, :], S32[:, hp, :], colv[:, hp, 1:2], tu[:, hp, :], ALU.mult, ALU.add, [t_S, t_cv, t_tu], [t_S])
            k.cp("pool", Sbn[:], S32[:], [t_S], [t_Sbn])
            sq, t_sq = sq_[b]
            st, t_st = st_[b]
            y, t_y = y_[b]
            k.tt("pool", sq[:], o[:], o[:], ALU.mult, [t_o], [t_sq])
            S.op("dve", lambda e, st=st, sq=sq: e.reduce_sum(out=st[:, 0:4], in_=sq[:].rearrange("p (a c) -> p a c", a=4),
                                                            axis=AX.X), reads=[t_sq], writes=[t_st])
            k.ts("dve", st[:, 4:8], st[:, 0:4], 1.0 / 64, EPS, ALU.mult, ALU.add, [t_st], [t_st])
            k.act(st[:, 4:8], st[:, 4:8], AF.Sqrt, [t_st], [t_st])
            k.recip(st[:, 8:12], st[:, 4:8], [t_st], [t_st])
            k.tt("pool", sq[:], sg[:], gn4[:].rearrange("p a c -> p (a c)"), ALU.mult, [t_sg, t_gn, t_sq], [t_sq])
            for h in range(4):
                k.stt(y[:, h * 64:(h + 1) * 64], o[:, h * 64:(h + 1) * 64], st[:, 8 + h:9 + h], sq[:, h * 64:(h + 1) * 64],
                      ALU.mult, ALU.mult, [t_o, t_st, t_sq], [t_y])
            pt2, t_pt2 = k.psb()
            for hp in range(2):
                k.tp(pt2[:, hp * 128:(hp + 1) * 128], y[:, hp * 128:(hp + 1) * 128], [t_y], [t_pt2])
            gq, q = tt_ // 4, tt_ % 4
            yg, t_yg = ygs[gq % 2]
            k.cp("act", yg[:, :, q * 128:(q + 1) * 128], pt2[:, 0:256].rearrange("p (a c) -> p a c", a=2), [t_pt2], [t_yg])
            if q == 3:
                for hp in range(2):
                    S.dma("sp", ctx["yT"][768 + hp * 128:768 + (hp + 1) * 128, gq * 512:(gq + 1) * 512], yg[:, hp, :],
                          reads=[t_yg], writes=[ctx["t_yT"][6 + hp][gq]])
        S.barrier()


def phase_D(ctx, s, l, hT, t_hT, t_h0):
    k = ctx["k"]
    S = k.S
    cs = k.cs
    vfirst, t_vf = ctx["vfirst"], ctx["t_vf"][s]
    with contextlib.ExitStack() as es:
        wa = k.sb(es, [128, 8, 1024], BF16, "wda")
        wb_ = k.sb(es, [128, 8, 1024], BF16, "wdb")
        t_wa = Tr()
        with contextlib.ExitStack() as e2:
            wd, t_wd = load_w(k, e2, "w_in", l, 0, D, OFF_D, 1024, "wd")
            mu, t_mu = load_bc(k, e2, k.vf["rwkv_mu"][l], 1024, "mu")
            omu = k.sb(e2, [128, 1024], F32, "omu")
            k.ts("dve", omu[:], mu[:], -1.0, 1.0, ALU.mult, ALU.add, [t_mu], [t_mu])
            for kc in range(8):
                k.tt("dve", wb_[:, kc, :], wd[:, kc, :], mu[:], ALU.mult, [t_wd, t_mu], [t_wa])
                k.tt("pool", wa[:, kc, :], wd[:, kc, :], omu[:], ALU.mult, [t_wd, t_mu], [t_wa])
            S.barrier()
        t_bc = Tr()

        def bc(src, n=256, nm="dbc"):
            t = k.sb(es, [128, n], F32, nm)
            S.dma("sp", t[:], src.partition_broadcast(128), writes=[t_bc])
            return t
        w0 = bc(k.vf["rwkv_w0"][l])
        a0 = bc(k.vf["rwkv_a0"][l])
        kkb = bc(k.vf["rwkv_k_k"][l])
        kab = bc(k.vf["rwkv_k_a"][l])
        lng = bc(k.vf["rwkv_lnx_g"][l])
        lnb = bc(k.vf["rwkv_lnx_b"][l])
        rkb = bc(k.vf["rwkv_r_k"][l].rearrange("a b -> (a b)"))
        omka = k.sb(es, [128, 256], F32, "omka")
        k.ts("dve", omka[:], kab[:], -1.0, 1.0, ALU.mult, ALU.add, [t_bc], [t_bc])
        w2a2 = k.sb(es, [128, 256], BF16, "w2a2")
        g2 = k.sb(es, [128, 256], BF16, "g2")
        t_lw = Tr()
        S.dma("sp", w2a2[0:64, :], k.wb["rwkv_w2"][l], reads=k.t_w["rwkv_w2"], writes=[t_lw])
        S.dma("sp", w2a2[64:128, :], k.wb["rwkv_a2"][l], reads=k.t_w["rwkv_a2"], writes=[t_lw])
        S.dma("sp", g2[:], k.wb["rwkv_g2"][l], reads=k.t_w["rwkv_g2"], writes=[t_lw])
        if l > 0:
            v0b = bc(k.vf["rwkv_v0"][l - 1])
            v1 = k.sb(es, [128, 2, 32], BF16, "v1")
            v2 = k.sb(es, [32, 256], BF16, "v2")
            S.dma("sp", v1[:], k.wb["rwkv_v1"][l - 1].rearrange("(kc p) n -> p kc n", p=128), reads=k.t_w["rwkv_v1"],
                  writes=[t_lw])
            S.dma("sp", v2[:], k.wb["rwkv_v2"][l - 1], reads=k.t_w["rwkv_v2"], writes=[t_lw])
        H32 = k.sb(es, [128, 2, 64], F32, "H32")
        t_H = Tr()
        k.memset(H32[:], 0.0, [t_H])
        Hbfs = [(k.sb(es, [128, 2, 64], BF16, "Hbf"), Tr()) for _ in range(2)]
        k.memset(Hbfs[0][0][:], 0.0, [Hbfs[0][1]])

        def f32(nm, n=256):
            return k.sb(es, [128, n], F32, nm), Tr()

        def b16(nm, shape=(128, 256)):
            return k.sb(es, list(shape), BF16, nm), Tr()
        r32, t_r = f32("r32")
        k32, t_k = f32("k32")
        v32, t_v = f32("v32")
        li, t_li = b16("li")
        liT, t_liT = b16("liT", (128, 2, 128))
        lw, t_lwv = f32("lw")
        a32, t_a = f32("a32")
        g32, t_g = f32("g32")
        tmp, t_tmp = f32("tmp")
        tmp2, t_tmp2 = f32("tmp2")
        kk0, t_kk = f32("kk0")
        kp, t_kp = f32("kp")
        bv, t_bv = f32("bv")
        st, t_st = f32("st", 24)
        e1, t_e1 = f32("e1")
        e2_, t_e2 = f32("e2")
        e4, t_e4 = f32("e4")
        e5, t_e5 = f32("e5")
        ewl, t_ewl = f32("ewl")
        colv, t_cv = k.sb(es, [128, 2, 2], F32, "dcolv"), Tr()
        Rp, t_Rp = b16("Rp")
        Ap, t_Ap = b16("Ap")
        Bp, t_Bp = b16("Bp")
        Kp, t_Kp = b16("Kp")
        At, t_At = b16("At")
        Bc, t_Bc = b16("Bc")
        Kc, t_Kc = b16("Kc")
        Vb, t_Vb = b16("Vb")
        ART, t_ART = b16("ART", (128, 2, 2, 128))
        BT, t_BT = b16("BT", (128, 2, 128))
        KT, t_KT = b16("KT", (128, 2, 128))
        RTt, t_RTt = b16("RTt", (128, 2, 128))
        RhT, t_RhT = b16("RhT", (128, 2, 128))
        LM = [b16("LM", (128, 512)) for _ in range(4)]
        Lp = [[b16("Lp", (128, 2, 128)) for _ in range(7)] for _ in range(4)]
        Xs = [[b16("X", (128, 128)) for _ in range(2)] for _ in range(4)]
        Ysb, t_Y = f32("Ysb")
        GpT, t_GpT = b16("GpT", (128, 2, 64))
        Zsb, t_Z = k.sb(es, [128, 2, 64], F32, "Zsb"), Tr()
        ybf, t_ybf = b16("ybf")
        ygs = [b16("dyg", (128, 2, 512)) for _ in range(2)]
        if l > 0:
            vT, t_vT = b16("vT", (128, 2, 128))
            u1, t_u1 = b16("u1", (32, 128))
            vfb, t_vfb = f32("vfb")
        mSI2, mLow = cs["c_maskSI2"], cs["c_maskLow"]

        for tt_ in range(NT):
            hs = slice(HOFF + tt_ * 128, HOFF + (tt_ + 1) * 128)
            hs1 = slice(HOFF + tt_ * 128 - 1, HOFF + (tt_ + 1) * 128 - 1)
            rdh = [t_wa, t_hT[tt_], t_h0] + ([t_hT[tt_ - 1]] if tt_ > 0 else [])
            pA, t_pA = k.ps()
            pB, t_pB = k.ps()
            for p_, t_p, c0 in ((pA, t_pA, 0), (pB, t_pB, 512)):
                for kc in range(8):
                    k.mm(p_[:], hT[:, kc, hs], wa[:, kc, c0:c0 + 512], kc == 0, False, rdh, [t_p])
                for kc in range(8):
                    k.mm(p_[:], hT[:, kc, hs1], wb_[:, kc, c0:c0 + 512], False, kc == 7, rdh, [t_p])
            k.cp("act", r32[:], pA[:, 0:256], [t_pA], [t_r])
            k.cp("act", k32[:], pA[:, 256:512], [t_pA], [t_k])
            k.cp("act", v32[:], pB[:, 0:256], [t_pB], [t_v])
            k.act(li[:, 0:64], pB[:, 256:320], AF.Tanh, [t_pB], [t_li])
            k.cp("act", li[:, 64:128], pB[:, 320:384], [t_pB], [t_li])
            k.act(li[:, 128:256], pB[:, 384:512], AF.Sigmoid, [t_pB], [t_li])
            pt, t_pt = k.psb()
            for j in range(2):
                k.tp(pt[:, j * 128:(j + 1) * 128], li[:, j * 128:(j + 1) * 128], [t_li], [t_pt])
            k.cp("dve", liT[:], pt[:, 0:256].rearrange("p (a c) -> p a c", a=2), [t_pt], [t_liT])
            pw, t_pw = k.ps()
            pa_, t_pa = k.ps()
            pg, t_pg = k.ps()
            k.mm(pw[:, 0:256], liT[0:64, 0, :], w2a2[0:64, :], True, True, [t_liT, t_lw], [t_pw])
            k.mm(pa_[:, 0:256], liT[64:128, 0, :], w2a2[64:128, :], True, True, [t_liT, t_lw], [t_pa])
            k.mm(pg[:, 0:256], liT[:, 1, :], g2[:], True, True, [t_liT, t_lw], [t_pg])
            k.tt("dve", lw[:], pw[:, 0:256], w0[:], ALU.add, [t_pw, t_bc], [t_lwv])
            k.act(lw[:], lw[:], AF.Sigmoid, [t_lwv], [t_lwv])
            k.ts("dve", lw[:], lw[:], -0.6065306597126334, None, ALU.mult, ALU.bypass, [t_lwv], [t_lwv])
            k.tt("dve", a32[:], pa_[:, 0:256], a0[:], ALU.add, [t_pa, t_bc], [t_a])
            k.act(a32[:], a32[:], AF.Sigmoid, [t_a], [t_a])
            k.cp("act", g32[:], pg[:, 0:256], [t_pg], [t_g])
            if l == 0:
                S.dma("sp", vfirst[s, tt_ * 128:(tt_ + 1) * 128, :], v32[:], reads=[t_v], writes=[t_vf[tt_]])
            else:
                S.dma("sp", vfb[:], vfirst[s, tt_ * 128:(tt_ + 1) * 128, :], reads=[t_vf[tt_]], writes=[t_vfb])
                k.cp("act", Vb[:], v32[:], [t_v], [t_Vb])
                pt, t_pt = k.psb()
                for j in range(2):
                    k.tp(pt[:, j * 128:(j + 1) * 128], Vb[:, j * 128:(j + 1) * 128], [t_Vb], [t_pt])
                k.cp("dve", vT[:], pt[:, 0:256].rearrange("p (a c) -> p a c", a=2), [t_pt], [t_vT])
                p1, t_p1 = k.ps()
                for j in range(2):
                    k.mm(p1[0:32, 0:128], v1[:, j, :], vT[:, j, :], j == 0, j == 1, [t_vT, t_lw], [t_p1])
                k.cp("act", u1[:], p1[0:32, 0:128], [t_p1], [t_u1])
                p2, t_p2 = k.ps()
                k.mm(p2[:, 0:256], u1[:], v2[:], True, True, [t_u1, t_lw], [t_p2])
                k.tt("dve", tmp[:], p2[:, 0:256], v0b[:], ALU.add, [t_p2, t_bc], [t_tmp])
                k.act(tmp[:], tmp[:], AF.Sigmoid, [t_tmp], [t_tmp])
                k.tt("dve", vfb[:], vfb[:], v32[:], ALU.subtract, [t_vfb, t_v], [t_vfb])
                k.tt("dve", vfb[:], vfb[:], tmp[:], ALU.mult, [t_vfb, t_tmp], [t_vfb])
                k.tt("dve", v32[:], v32[:], vfb[:], ALU.add, [t_v, t_vfb], [t_v])
            k.cp("act", Vb[:], v32[:], [t_v], [t_Vb])
            k.tt("dve", kk0[:], k32[:], kkb[:], ALU.mult, [t_k, t_bc], [t_kk])
            k.tt("pool", tmp[:], kk0[:], kk0[:], ALU.mult, [t_kk], [t_tmp])
            S.op("dve", lambda e: e.reduce_sum(out=st[:, 0:4], in_=tmp[:].rearrange("p (a c) -> p a c", a=4), axis=AX.X),
                 reads=[t_tmp], writes=[t_st])
            k.act(st[:, 0:4], st[:, 0:4], AF.Sqrt, [t_st], [t_st])
            k.ts("dve", st[:, 0:4], st[:, 0:4], 1e-12, None, ALU.max, ALU.bypass, [t_st], [t_st])
            k.recip(st[:, 4:8], st[:, 0:4], [t_st], [t_st])
            for h in range(4):
                k.ts("dve", kk0[:, h * 64:(h + 1) * 64], kk0[:, h * 64:(h + 1) * 64], st[:, 4 + h:5 + h], None, ALU.mult,
                     ALU.bypass, [t_kk, t_st], [t_kk])
            k.tt("dve", tmp2[:], a32[:], kab[:], ALU.mult, [t_a, t_bc], [t_tmp2])
            k.tt("pool", tmp2[:], tmp2[:], omka[:], ALU.add, [t_tmp2, t_bc], [t_tmp2])
            k.tt("dve", kp[:], k32[:], tmp2[:], ALU.mult, [t_k, t_tmp2], [t_kp])
            k.tt("pool", bv[:], kk0[:], a32[:], ALU.mult, [t_kk, t_a], [t_bv])
            pCm, t_pCm = k.ps()
            pCt, t_pCt = k.ps()
            pCr, t_pCr = k.ps()
            k.mm(pCm[:, 0:256], cs["c_triM"][:], lw[:], True, True, [t_lwv, k.t_const], [t_pCm])
            k.mm(pCt[:, 0:256], cs["c_tri"][:], lw[:], True, True, [t_lwv, k.t_const], [t_pCt])
            k.mm(pCr[:, 0:256], cs["c_triR"][:], lw[:], True, True, [t_lwv, k.t_const], [t_pCr])
            pc, t_pc = k.ps()
            for hp in range(2):
                k.mm(pc[:, hp * 2:hp * 2 + 2], lw[:, hp * 128:(hp + 1) * 128], cs["c_sel"][:], True, True,
                     [t_lwv, k.t_const], [t_pc])
            k.act(colv[:], pc[:, 0:4].rearrange("p (a c) -> p a c", a=2), AF.Exp, [t_pc], [t_cv])
            k.act(e1[:], pCm[:, 0:256], AF.Exp, [t_pCm], [t_e1])
            k.act(e2_[:], pCm[:, 0:256], AF.Exp, [t_pCm], [t_e2], scale=-1.0)
            k.act(e4[:], pCt[:, 0:256], AF.Exp, [t_pCt], [t_e4])
            k.act(e5[:], pCr[:, 0:256], AF.Exp, [t_pCr], [t_e5])
            k.act(ewl[:], lw[:], AF.Exp, [t_lwv], [t_ewl], scale=-1.0)
            k.tt("dve", Rp[:], r32[:], e1[:], ALU.mult, [t_r, t_e1], [t_Rp])
            k.tt("pool", tmp[:], kk0[:], ewl[:], ALU.mult, [t_kk, t_ewl, t_st], [t_tmp])
            k.stt(Ap[:], tmp[:], -1.0, e1[:], ALU.mult, ALU.mult, [t_tmp, t_e1], [t_Ap])
            k.stt(At[:], tmp[:], -1.0, e4[:], ALU.mult, ALU.mult, [t_tmp, t_e4], [t_At])
            k.tt("pool", Bp[:], bv[:], e2_[:], ALU.mult, [t_bv, t_e2], [t_Bp])
            k.tt("dve", Kp[:], kp[:], e2_[:], ALU.mult, [t_kp, t_e2], [t_Kp])
            k.tt("pool", Bc[:], bv[:], e5[:], ALU.mult, [t_bv, t_e5], [t_Bc])
            k.tt("dve", Kc[:], kp[:], e5[:], ALU.mult, [t_kp, t_e5], [t_Kc])
            pt, t_pt = k.psb()
            for j, src in enumerate((Ap, Rp, Bp, Kp)):
                for hp in range(2):
                    k.tp(pt[:, (j * 2 + hp) * 128:(j * 2 + hp + 1) * 128], src[:, hp * 128:(hp + 1) * 128],
                         [t_Ap, t_Rp, t_Bp, t_Kp], [t_pt])
            ptv = pt[:].rearrange("p (j a c) -> p j a c", j=4, a=2)
            for j in range(2):
                k.cp("act", ART[:, :, j, :], ptv[:, j, :, :], [t_pt], [t_ART])
            k.cp("dve", BT[:], ptv[:, 2, :, :], [t_pt], [t_BT])
            k.cp("dve", KT[:], ptv[:, 3, :, :], [t_pt], [t_KT])
            for hp in range(2):
                k.ts("dve", RTt[:, hp, :], ptv[:, 1, hp, :], colv[:, hp, 0:1], None, ALU.mult, ALU.bypass, [t_pt, t_cv],
                     [t_RTt])
            for h in range(4):
                hp, par = h // 2, h % 2
                hb = 64 * par
                LMh, t_LM = LM[h]
                pl, t_pl = k.ps()
                rhs_ar = ART[hb:hb + 64, hp, :, :].rearrange("p a c -> p (a c)")
                k.mm(pl[:, 0:256], BT[hb:hb + 64, hp, :], rhs_ar, True, True, [t_BT, t_ART], [t_pl])
                k.mm(pl[:, 256:512], KT[hb:hb + 64, hp, :], rhs_ar, True, True, [t_KT, t_ART], [t_pl])
                k.tt("dve", LMh[:], pl[:], mSI2[:], ALU.mult, [t_pl, k.t_const], [t_LM])
                L0, t_L0 = Lp[h][0]
                p0, t_p0 = k.ps()
                k.mm(p0[:, 0:128], ART[hb:hb + 64, hp, 0, :], BT[hb:hb + 64, hp, :], True, True, [t_ART, t_BT], [t_p0])
                k.tt("dve", L0[:, 0, :], p0[:, 0:128], mLow[:], ALU.mult, [t_p0, k.t_const], [t_L0])
                k.cp("pool", L0[:, 1, :], LMh[:, 0:128], [t_LM], [t_L0])
            for h in range(4):
                LMh, t_LM = LM[h]
                X0, t_X0 = Xs[h][0]
                px, t_px = k.ps()
                k.mm(px[:, 0:64], LMh[:, 256:384], Vb[:, h * 64:(h + 1) * 64], True, True, [t_LM, t_Vb], [t_px])
                k.cp("act", X0[:, 64:128], px[:, 0:64], [t_px], [t_X0])
                k.cp("pool", X0[:, 0:64], At[:, h * 64:(h + 1) * 64], [t_At], [t_X0])
            for j in range(7):
                if j < 6:
                    for h in range(4):
                        Lj, t_Lj = Lp[h][j]
                        Ln, t_Ln = Lp[h][j + 1]
                        p_, t_p = k.ps()
                        k.mm(p_[:, 0:128], Lj[:, 1, :], Lj[:, 0, :], True, True, [t_Lj], [t_p])
                        k.mm(p_[:, 128:256], Lj[:, 0, :], Lj[:, 1, :], True, True, [t_Lj], [t_p])
                        k.cp("act", Ln[:].rearrange("p a c -> p (a c)"), p_[:, 0:256], [t_p], [t_Ln])
                for h in range(4):
                    Xc, t_Xc = Xs[h][j % 2]
                    Xn, t_Xn = Xs[h][(j + 1) % 2]
                    Lj, t_Lj = Lp[h][j]
                    px, t_px = k.ps()
                    k.mm(px[:, 0:128], Lj[:, 1, :], Xc[:], True, True, [t_Lj, t_Xc], [t_px])
                    k.tt("dve", Xn[:], px[:, 0:128], Xc[:], ALU.add, [t_px, t_Xc], [t_Xn])
            pR, t_pR = k.ps()
            pY, t_pY = k.ps()
            pG, t_pG = k.ps()
            pZ, t_pZ = k.ps()
            for h in range(4):
                hp, par = h // 2, h % 2
                hb = 64 * par
                LMh, t_LM = LM[h]
                X7, t_X7 = Xs[h][1]
                hc = slice(h * 64, (h + 1) * 64)
                k.mm(pR[hb:hb + 64, hp * 128:(hp + 1) * 128], X7[:, 0:64], LMh[:, 128:256], True, True, [t_X7, t_LM], [t_pR])
                k.mm(pY[:, hc], LMh[:, 128:256], X7[:, 64:128], True, False, [t_X7, t_LM], [t_pY])
                k.mm(pY[:, hc], LMh[:, 384:512], Vb[:, hc], False, True, [t_Vb, t_LM], [t_pY])
                k.mm(pG[hb:hb + 64, hp * 64:(hp + 1) * 64], X7[:, 0:64], Bc[:, hc], True, True, [t_X7, t_Bc], [t_pG])
                k.mm(pZ[hb:hb + 64, hp * 64:(hp + 1) * 64], Bc[:, hc], X7[:, 64:128], True, False, [t_X7, t_Bc], [t_pZ])
                k.mm(pZ[hb:hb + 64, hp * 64:(hp + 1) * 64], Kc[:, hc], Vb[:, hc], False, True, [t_Kc, t_Vb], [t_pZ])
            k.tt("dve", RhT[:].rearrange("p a c -> p (a c)"), pR[:, 0:256], RTt[:].rearrange("p a c -> p (a c)"), ALU.add,
                 [t_pR, t_RTt], [t_RhT])
            k.cp("act", Ysb[:], pY[:, 0:256], [t_pY], [t_Y])
            k.cp("act", GpT[:].rearrange("p a c -> p (a c)"), pG[:, 0:128], [t_pG], [t_GpT])
            k.cp("act", Zsb[:].rearrange("p a c -> p (a c)"), pZ[:, 0:128], [t_pZ], [t_Z])
            Hbf, t_Hbf = Hbfs[tt_ % 2]
            Hbn, t_Hbn = Hbfs[1 - tt_ % 2]
            piE, t_piE = k.ps()
            piO, t_piO = k.ps()
            phE, t_phE = k.ps()
            phO, t_phO = k.ps()
            for h in range(4):
                hp, par = h // 2, h % 2
                hb = 64 * par
                pi, t_pi = (piE, t_piE) if par == 0 else (piO, t_piO)
                ph, t_ph = (phE, t_phE) if par == 0 else (phO, t_phO)
                k.mm(pi[:, hp * 64:(hp + 1) * 64], RhT[hb:hb + 64, hp, :], Hbf[hb:hb + 64, hp, :], True, True,
                     [t_RhT, t_Hbf], [t_pi])
                k.mm(ph[hb:hb + 64, hp * 64:(hp + 1) * 64], GpT[hb:hb + 64, hp, :], Hbf[hb:hb + 64, hp, :], True, True,
                     [t_GpT, t_Hbf], [t_ph])
            Yv = Ysb[:].rearrange("p (a b c) -> p a b c", a=2, b=2)
            k.tt("dve", Yv[:, :, 0, :], Yv[:, :, 0, :], piE[:, 0:128].rearrange("p (a c) -> p a c", a=2), ALU.add,
                 [t_Y, t_piE], [t_Y])
            k.tt("dve", Yv[:, :, 1, :], Yv[:, :, 1, :], piO[:, 0:128].rearrange("p (a c) -> p a c", a=2), ALU.add,
                 [t_Y, t_piO], [t_Y])
            for hp in range(2):
                k.stt(H32[:, hp, :], H32[:, hp, :], colv[:, hp, 1:2], Zsb[:, hp, :], ALU.mult, ALU.add, [t_H, t_cv, t_Z], [t_H])
            H2 = H32[:].rearrange("p a c -> p (a c)")
            k.tt("dve", H2[0:64, :], H2[0:64, :], phE[0:64, 0:128], ALU.add, [t_H, t_phE], [t_H])
            k.tt("dve", H2[64:128, :], H2[64:128, :], phO[64:128, 0:128], ALU.add, [t_H, t_phO], [t_H])
            k.cp("pool", Hbn[:], H32[:], [t_H], [t_Hbn])
            S.op("dve", lambda e: e.reduce_sum(out=st[:, 8:12], in_=Ysb[:].rearrange("p (a c) -> p a c", a=4), axis=AX.X),
                 reads=[t_Y], writes=[t_st])
            k.ts("dve", st[:, 8:12], st[:, 8:12], 1.0 / 64, None, ALU.mult, ALU.bypass, [t_st], [t_st])
            for h in range(4):
                k.ts("dve", Ysb[:, h * 64:(h + 1) * 64], Ysb[:, h * 64:(h + 1) * 64], st[:, 8 + h:9 + h], None, ALU.subtract,
                     ALU.bypass, [t_Y, t_st], [t_Y])
            k.tt("pool", tmp[:], Ysb[:], Ysb[:], ALU.mult, [t_Y], [t_tmp])
            S.op("dve", lambda e: e.reduce_sum(out=st[:, 12:16], in_=tmp[:].rearrange("p (a c) -> p a c", a=4), axis=AX.X),
                 reads=[t_tmp], writes=[t_st])
            k.ts("dve", st[:, 12:16], st[:, 12:16], 1.0 / 64, 64e-5, ALU.mult, ALU.add, [t_st], [t_st])
            k.act(st[:, 12:16], st[:, 12:16], AF.Sqrt, [t_st], [t_st])
            k.recip(st[:, 16:20], st[:, 12:16], [t_st], [t_st])
            for h in range(4):
                k.stt(Ysb[:, h * 64:(h + 1) * 64], Ysb[:, h * 64:(h + 1) * 64], st[:, 16 + h:17 + h], lng[:, h * 64:(h + 1) * 64],
                      ALU.mult, ALU.mult, [t_Y, t_st, t_bc], [t_Y])
            k.tt("pool", Ysb[:], Ysb[:], lnb[:], ALU.add, [t_Y, t_bc], [t_Y])
            k.tt("dve", tmp2[:], r32[:], kp[:], ALU.mult, [t_r, t_kp], [t_tmp2])
            k.tt("pool", tmp2[:], tmp2[:], rkb[:], ALU.mult, [t_tmp2, t_bc], [t_tmp2])
            S.op("dve", lambda e: e.reduce_sum(out=st[:, 20:24], in_=tmp2[:].rearrange("p (a c) -> p a c", a=4), axis=AX.X),
                 reads=[t_tmp2], writes=[t_st])
            for h in range(4):
                k.stt(Ysb[:, h * 64:(h + 1) * 64], v32[:, h * 64:(h + 1) * 64], st[:, 20 + h:21 + h], Ysb[:, h * 64:(h + 1) * 64],
                      ALU.mult, ALU.add, [t_Y, t_st, t_v], [t_Y])
            k.tt("dve", ybf[:], Ysb[:], g32[:], ALU.mult, [t_Y, t_g], [t_ybf])
            pt2, t_pt2 = k.psb()
            for hp in range(2):
                k.tp(pt2[:, hp * 128:(hp + 1) * 128], ybf[:, hp * 128:(hp + 1) * 128], [t_ybf], [t_pt2])
            gq, q = tt_ // 4, tt_ % 4
            yg, t_yg = ygs[gq % 2]
            k.cp("act", yg[:, :, q * 128:(q + 1) * 128], pt2[:, 0:256].rearrange("p (a c) -> p a c", a=2), [t_pt2], [t_yg])
            if q == 3:
                for hp in range(2):
                    S.dma("sp", ctx["yT"][1024 + hp * 128:1024 + (hp + 1) * 128, gq * 512:(gq + 1) * 512], yg[:, hp, :],
                          reads=[t_yg], writes=[ctx["t_yT"][8 + hp][gq]])
        S.barrier()


_NC_CACHE = {}


def kernel(**inputs):
    cfg = {}
    key = "main"
    if key not in _NC_CACHE:
        _NC_CACHE[key] = build(cfg)
    nc = _NC_CACHE[key]
    consts = host_consts()
    in_maps = []
    for c in range(8):
        m = {"x": np.ascontiguousarray(inputs["x"][2 * c:2 * c + 2]),
             "mem": np.ascontiguousarray(inputs["mem"][2 * c:2 * c + 2]),
             "positions": np.ascontiguousarray(inputs["positions"][2 * c:2 * c + 2]).astype(np.int32)}
        for n in WNAMES + VNAMES:
            m[n] = np.ascontiguousarray(inputs[n], dtype=np.float32)
        m.update(consts)
        in_maps.append(m)
    res = run_bass_kernel_spmd(nc, in_maps, core_ids=list(range(8)))
    return np.concatenate([r["out"] for r in res.results], axis=0).astype(np.float32)
```

```python
import contextlib
import math
import numpy as np
import concourse.bass as bass
import concourse.mybir as mybir
from concourse.bass_utils import run_bass_kernel_spmd

F32 = mybir.dt.float32
BF16 = mybir.dt.bfloat16
I32 = mybir.dt.int32
AF = mybir.ActivationFunctionType
ALU = mybir.AluOpType
AX = mybir.AxisListType

D = 1024
S_LEN = 4096
NT = S_LEN // 128
N_IN = 9984
OFF_A, OFF_B, OFF_C, OFF_D, OFF_G = 0, 2304, 3840, 4864, 5888
DFF = 2816
MEM = 256
EPS = 1e-5
HOFF = 8


class Tr:
    __slots__ = ("w", "r", "x")

    def __init__(self, excl=False):
        self.w = None
        self.r = {}
        self.x = excl


class Sched:
    COMPUTE = ("pe", "act", "dve", "pool")
    NDMASEM = 12

    def __init__(self, nc):
        self.nc = nc
        self.engobj = {"pe": nc.tensor, "act": nc.scalar, "dve": nc.vector, "pool": nc.gpsimd, "sp": nc.sync}
        self.prog = {k: [] for k in self.engobj}
        self.sems = {}
        self.cnt = {}
        self.seen = {k: {} for k in self.engobj}
        self._semctx = []
        for k in self.COMPUTE:
            self._mksem(k)
        self.dmasems = {}
        self.dmarr = {}
        for q in ("sp", "act", "pool"):
            self.dmasems[q] = [self._mksem(f"d_{q}_{i}") for i in range(self.NDMASEM)]
            self.dmarr[q] = 0
        self.ninst = 0

    def _mksem(self, key):
        ctx = self.nc.semaphore(key)
        h = ctx.__enter__()
        self._semctx.append(ctx)
        self.sems[key] = h
        self.cnt[key] = 0
        return key

    def _deps(self, stream, own_key, reads, writes):
        need = {}

        def add(kv):
            if kv is None:
                return
            k, v = kv
            if k == own_key and k == "pe":
                return
            if need.get(k, 0) < v:
                need[k] = v
        for t in reads:
            add(t.w)
            if t.x:
                for k, v in t.r.items():
                    if k != own_key:
                        add((k, v))
        for t in writes:
            add(t.w)
            for k, v in t.r.items():
                add((k, v))
        out = []
        seen = self.seen[stream]
        for k, v in need.items():
            if seen.get(k, 0) < v:
                seen[k] = v
                out.append((k, v))
        return out

    def _commit(self, key, val, reads, writes):
        for t in reads:
            if t.r.get(key, 0) < val:
                t.r[key] = val
        for t in writes:
            t.w = (key, val)
            t.r = {}

    def op(self, eng, fn, reads=(), writes=()):
        waits = self._deps(eng, eng, reads, writes)
        self.cnt[eng] += 1
        val = self.cnt[eng]
        self._commit(eng, val, reads, writes)
        sem = self.sems[eng]
        sems = self.sems

        def thunk(e, waits=waits, fn=fn, sem=sem):
            for k, v in waits:
                e.wait_ge(sems[k], v)
            fn(e).then_inc(sem, 1)
        self.prog[eng].append(thunk)
        self.ninst += 1

    def dma(self, q, out, in_, reads=(), writes=(), **kw):
        i = self.dmarr[q]
        self.dmarr[q] = (i + 1) % self.NDMASEM
        key = self.dmasems[q][i]
        waits = self._deps(q, key, reads, writes)
        prev = self.cnt[key]
        if prev > 0 and self.seen[q].get(key, 0) < prev:
            self.seen[q][key] = prev
            waits.append((key, prev))
        self.cnt[key] += 16
        val = self.cnt[key]
        self._commit(key, val, reads, writes)
        sem = self.sems[key]
        sems = self.sems

        def thunk(e, waits=waits, sem=sem, out=out, in_=in_, kw=kw):
            for k, v in waits:
                e.wait_ge(sems[k], v)
            e.dma_start(out=out, in_=in_, **kw).then_inc(sem, 16)
        self.prog[q].append(thunk)
        self.ninst += 1

    def barrier(self):
        snap = {k: v for k, v in self.cnt.items() if v > 0}
        sems = self.sems
        for stream in self.prog:
            seen = self.seen[stream]
            waits = []
            for k, v in snap.items():
                if k == stream:
                    continue
                if seen.get(k, 0) < v:
                    seen[k] = v
                    waits.append((k, v))
            if waits:
                def thunk(e, waits=waits):
                    for k, v in waits:
                        e.wait_ge(sems[k], v)
                self.prog[stream].append(thunk)

    def finish(self):
        nc = self.nc
        finals = [(k, v) for k, v in self.cnt.items() if v > 0]
        sems = self.sems
        prog = self.prog
        with nc.Block() as block:
            @block.tensor
            def _(e):
                for t in prog["pe"]:
                    t(e)

            @block.scalar
            def _(e):
                for t in prog["act"]:
                    t(e)

            @block.vector
            def _(e):
                for t in prog["dve"]:
                    t(e)

            @block.gpsimd
            def _(e):
                for t in prog["pool"]:
                    t(e)

            @block.sync
            def _(e):
                for t in prog["sp"]:
                    t(e)
                for k, v in finals:
                    e.wait_ge(sems[k], v)
        for ctx in reversed(self._semctx):
            ctx.__exit__(None, None, None)


WNAMES = ["w_in", "p_a", "p_b", "p_c", "p_d", "w_mix_out", "w_mem_q", "w_mem_kv", "w_mem_o", "w_ffn_in",
          "w_ffn_out", "rwkv_w2", "rwkv_a2", "rwkv_g2", "rwkv_v1", "rwkv_v2"]
VNAMES = ["mix_norm_g", "diff_lam", "diff_norm_g", "hgrn_lb_logits", "hgrn_norm_g", "rwkv_mu", "rwkv_w0", "rwkv_a0",
          "rwkv_k_k", "rwkv_k_a", "rwkv_r_k", "rwkv_lnx_g", "rwkv_lnx_b", "rwkv_v0", "mem_q_norm_g", "mem_kv_norm_g",
          "ffn_norm_g", "ffn_conv_w", "ffn_conv_b", "final_norm_g"]
SHAPES = {
    "mix_norm_g": (2, 1024), "w_in": (2, 1024, 9984), "diff_lam": (2, 4, 64), "diff_norm_g": (2, 128),
    "hgrn_lb_logits": (2, 256), "hgrn_norm_g": (2, 64), "rwkv_mu": (2, 1024), "rwkv_w0": (2, 256),
    "rwkv_w2": (2, 64, 256), "rwkv_a0": (2, 256), "rwkv_a2": (2, 64, 256), "rwkv_g2": (2, 128, 256),
    "rwkv_k_k": (2, 256), "rwkv_k_a": (2, 256), "rwkv_r_k": (2, 4, 64), "rwkv_lnx_g": (2, 256),
    "rwkv_lnx_b": (2, 256), "rwkv_v0": (1, 256), "rwkv_v1": (1, 256, 32), "rwkv_v2": (1, 32, 256),
    "p_a": (2, 256, 1024), "p_b": (2, 512, 1024), "p_c": (2, 256, 1024), "p_d": (2, 256, 1024),
    "w_mix_out": (2, 1024, 1024), "mem_q_norm_g": (2, 1024), "mem_kv_norm_g": (2, 1024),
    "w_mem_q": (2, 1024, 1024), "w_mem_kv": (2, 1024, 2048), "w_mem_o": (2, 1024, 1024),
    "ffn_norm_g": (2, 1024), "w_ffn_in": (2, 1024, 5632), "ffn_conv_w": (2, 3, 5632), "ffn_conv_b": (2, 5632),
    "w_ffn_out": (2, 2816, 1024), "final_norm_g": (1024,),
}


def host_consts():
    c = {}
    c["c_ident"] = np.eye(128, dtype=np.float32)
    s = np.arange(128)[:, None]
    t = np.arange(512)[None, :]
    c["c_maskD"] = (s <= t).astype(np.float32)
    dm = np.zeros((128, 256), np.float32)
    dm[:, :128] = (s <= np.arange(128)[None, :])
    dm[:, 128:] = (s >= np.arange(128)[None, :])
    c["c_maskA"] = np.concatenate([dm, dm], axis=1)
    tt = np.arange(128)[None, :]
    si = np.concatenate([(s < tt), (s <= tt)], axis=1).astype(np.float32)
    c["c_maskSI"] = si
    c["c_maskSI2"] = np.concatenate([si, si], axis=1)
    c["c_triR"] = (s > tt).astype(np.float32)
    c["c_maskLow"] = (tt < s).astype(np.float32)
    c["c_tri"] = (s <= tt).astype(np.float32)
    c["c_tri2"] = np.concatenate([c["c_tri"], c["c_tri"]], axis=1)
    c["c_triM"] = ((s <= tt).astype(np.float32) - (s <= 63).astype(np.float32))
    c["c_o63"] = np.broadcast_to((s <= 63), (128, 128)).astype(np.float32).copy()
    c["c_ones"] = np.ones((128, 128), np.float32)
    sel = np.zeros((128, 2), np.float32)
    sel[:64, 0] = 1.0
    sel[:, 1] = 1.0
    c["c_sel"] = sel
    pm = np.zeros((128, 128), np.float32)
    for hb in (0, 64):
        for i in range(8):
            pm[hb + i + 8, hb + i] = -1.0
            pm[hb + i, hb + i + 8] = 1.0
    c["c_pm"] = pm
    invf = np.zeros((128, 1), np.float32)
    f = (500000.0 ** (-np.arange(8, dtype=np.float32) / 8)).astype(np.float32)
    for hb in (0, 64):
        invf[hb:hb + 8, 0] = f
        invf[hb + 8:hb + 16, 0] = f
    c["c_invf"] = invf
    return c


class K:
    def __init__(self, cfg):
        self.cfg = cfg
        nc = bass.Bass("TRN2", target_bir_lowering=False)
        self.nc = nc
        self.S = Sched(nc)
        self.es = contextlib.ExitStack()
        self.uid = 0

    def sb(self, es, shape, dt, name=None):
        self.uid += 1
        return es.enter_context(self.nc.sbuf_tensor(f"{name or 't'}_{self.uid}", list(shape), dt))

    def dram(self, name, shape, dt, kind="Internal"):
        return self.nc.dram_tensor(name, list(shape), dt, kind=kind).ap()

    def mm(self, out, lhsT, rhs, start, stop, r, w):
        self.S.op("pe", lambda e: e.matmul(out, lhsT=lhsT, rhs=rhs, start=start, stop=stop), reads=r, writes=w)

    def tp(self, out, in_, r, w):
        idt = self.ident_b
        self.S.op("pe", lambda e: e.transpose(out, in_, idt), reads=list(r) + [self.t_const], writes=w)

    def act(self, out, in_, func, r, w, **kw):
        self.S.op("act", lambda e: e.activation(out=out, in_=in_, func=func, **kw), reads=r, writes=w)

    def tt(self, eng, out, in0, in1, op, r, w):
        self.S.op(eng, lambda e: e.tensor_tensor(out=out, in0=in0, in1=in1, op=op), reads=r, writes=w)

    def ts(self, eng, out, in0, s1, s2, op0, op1, r, w):
        self.S.op(eng, lambda e: e.tensor_scalar(out=out, in0=in0, scalar1=s1, scalar2=s2, op0=op0, op1=op1),
                  reads=r, writes=w)

    def stt(self, out, in0, scalar, in1, op0, op1, r, w):
        self.S.op("dve", lambda e: e.scalar_tensor_tensor(out=out, in0=in0, scalar=scalar, in1=in1, op0=op0, op1=op1),
                  reads=r, writes=w)

    def cp(self, eng, out, in_, r, w):
        if eng == "act":
            self.S.op("act", lambda e: e.copy(out=out, in_=in_), reads=r, writes=w)
        else:
            self.S.op(eng, lambda e: e.tensor_copy(out=out, in_=in_), reads=r, writes=w)

    def recip(self, out, in_, r, w):
        self.S.op("dve", lambda e: e.reciprocal(out=out, in_=in_), reads=r, writes=w)

    def memset(self, ap, val, w):
        self.S.op("dve", lambda e: e.memset(ap, val), reads=(), writes=w)

    def ps(self):
        i = self.ps_i
        self.ps_i = (i + 1) % len(self.ps_f)
        return self.ps_f[i], self.ps_ft[i]

    def psb(self):
        i = self.psb_i
        self.psb_i = (i + 1) % len(self.ps_b)
        return self.ps_b[i], self.ps_bt[i]


def build(cfg):
    k = K(cfg)
    nc, S = k.nc, k.S
    NS = cfg.get("nseq", 2)
    NL = cfg.get("layers", 2)
    dbg = cfg.get("debug")
    x_in = k.dram("x", [NS, S_LEN, D], F32, "ExternalInput")
    mem_in = k.dram("mem", [NS, MEM, D], F32, "ExternalInput")
    pos_in = k.dram("positions", [NS, S_LEN], I32, "ExternalInput")
    wf = {n: k.dram(n, SHAPES[n], F32, "ExternalInput") for n in WNAMES}
    vf = {n: k.dram(n, SHAPES[n], F32, "ExternalInput") for n in VNAMES}
    consts = host_consts()
    cf = {n: k.dram(n, v.shape, F32, "ExternalInput") for n, v in consts.items()}
    out = k.dram("out", [NS, S_LEN, D], F32, "ExternalOutput")
    xbuf = k.dram("xbuf", [NS, S_LEN, D], F32)
    yT = k.dram("yT", [1280, S_LEN], BF16)
    vfirst = k.dram("vfirst", [NS, S_LEN, 256], F32)
    wb = {n: k.dram(n + "_bf", SHAPES[n], BF16) for n in WNAMES}
    yin = None
    if cfg.get("yin"):
        yin = k.dram("yin", [1280, S_LEN], F32, "ExternalInput")
    dbg_out = None
    if dbg:
        dbg_out = k.dram("dbg", dbg["shape"], F32, "ExternalOutput")

    t_x = [[Tr() for _ in range(NT)] for _ in range(NS)]
    t_yT = [[Tr() for _ in range(8)] for _ in range(10)]
    t_vf = [[Tr() for _ in range(NT)] for _ in range(NS)]
    t_w = {}

    with contextlib.ExitStack() as g:
        k.ps_f, k.ps_ft, k.ps_b, k.ps_bt = [], [], [], []
        for i in range(6):
            k.ps_f.append(g.enter_context(nc.psum_tensor(f"psf{i}", [128, 512], F32)))
            k.ps_ft.append(Tr(True))
        for i in range(2):
            k.ps_b.append(g.enter_context(nc.psum_tensor(f"psb{i}", [128, 1024], BF16)))
            k.ps_bt.append(Tr(True))
        k.ps_i = 0
        k.psb_i = 0
        k.t_const = Tr()
        ident_b = k.sb(g, [128, 128], BF16, "ident")
        k.ident_b = ident_b[:]
        S.dma("pool", ident_b[:], cf["c_ident"], writes=[k.t_const])
        cs = {}
        for n, dt in [("c_maskD", BF16), ("c_maskA", BF16), ("c_maskSI", F32), ("c_maskSI2", F32), ("c_triR", F32), ("c_maskLow", F32), ("c_tri", F32), ("c_tri2", F32),
                      ("c_triM", F32), ("c_o63", F32), ("c_ones", F32), ("c_sel", F32), ("c_pm", BF16),
                      ("c_invf", F32)]:
            t = k.sb(g, consts[n].shape, dt, n)
            S.dma("pool" if dt == BF16 else "sp", t[:], cf[n], writes=[k.t_const])
            cs[n] = t
        ones_b = k.sb(g, [128, 128], BF16, "ones_b")
        S.dma("pool", ones_b[:], cf["c_ones"], writes=[k.t_const])
        cs["ones_b"] = ones_b
        k.cs = cs
        for n in WNAMES:
            tot = int(np.prod(SHAPES[n]))
            rows = tot // 2048
            src = wf[n].flatten().rearrange("(r c) -> r c", c=2048) if len(SHAPES[n]) > 1 else None
            nd = len(SHAPES[n])
            letters = "abc"[:nd]
            flat_s = wf[n].rearrange(f"{' '.join(letters)} -> ({' '.join(letters)})").rearrange("(r c) -> r c", c=2048)
            flat_d = wb[n].rearrange(f"{' '.join(letters)} -> ({' '.join(letters)})").rearrange("(r c) -> r c", c=2048)
            t_w[n] = []
            for r0 in range(0, rows, 512):
                r1 = min(rows, r0 + 512)
                tr_ = Tr()
                S.dma("pool", flat_d[r0:r1, :], flat_s[r0:r1, :], writes=[tr_])
                t_w[n].append(tr_)
        k.wb, k.t_w, k.vf = wb, t_w, vf

        ctx = dict(k=k, x_in=x_in, mem_in=mem_in, pos_in=pos_in, out=out, xbuf=xbuf, yT=yT, vfirst=vfirst,
                   t_x=t_x, t_yT=t_yT, t_vf=t_vf, yin=yin, dbg=dbg, dbg_out=dbg_out, cfg=cfg)
        phases = cfg.get("phases", "ABCDGMF")
        if yin is None and not any(c in phases for c in "ABCD"):
            with contextlib.ExitStack() as zx:
                zt = k.sb(zx, [128, 512], BF16, "zt")
                t_z = Tr()
                k.memset(zt[:], 0.0, [t_z])
                for rc in range(10):
                    for gq in range(8):
                        S.dma("sp", yT[rc * 128:(rc + 1) * 128, gq * 512:(gq + 1) * 512], zt[:], reads=[t_z],
                              writes=[t_yT[rc][gq]])
                S.barrier()
        for s in range(NS):
            for l in range(NL):
                xsrc = x_in if l == 0 else xbuf
                with contextlib.ExitStack() as mx:
                    hT = k.sb(mx, [128, 8, HOFF + S_LEN], BF16, "hT")
                    t_hT = [Tr() for _ in range(NT)]
                    t_h0 = Tr()
                    k.memset(hT[:, :, 0:HOFF], 0.0, [t_h0])
                    phase_norm_T(ctx, s, l, xsrc, hT, t_hT)
                    if yin is not None:
                        phase_yin(ctx)
                    else:
                        with contextlib.ExitStack() as rx:
                            tabs = phase_rope(ctx, rx, s) if ("A" in phases or "B" in phases) else None
                            if "A" in phases:
                                phase_A(ctx, s, l, hT, t_hT, tabs)
                            if "B" in phases:
                                phase_B(ctx, s, l, hT, t_hT, tabs)
                        if "C" in phases:
                            phase_C(ctx, s, l, hT, t_hT)
                        if "D" in phases:
                            phase_D(ctx, s, l, hT, t_hT, t_h0)
                    if dbg and dbg.get("what") == "yT" and dbg.get("l", 0) == l and s == 0:
                        for rc in dbg.get("rcs", range(10)):
                            for gq in range(8):
                                S.dma("pool", dbg_out[rc * 128:(rc + 1) * 128, gq * 512:(gq + 1) * 512],
                                      yT[rc * 128:(rc + 1) * 128, gq * 512:(gq + 1) * 512], reads=[t_yT[rc][gq]])
                    if "G" in phases:
                        phase_G(ctx, s, l, xsrc, hT, t_hT)
                if "M" in phases:
                    phase_M(ctx, s, l)
                if "F" in phases:
                    phase_F(ctx, s, l)
            if cfg.get("final", True):
                phase_final(ctx, s)
        S.finish()
    return nc


def load_bc(k, es, src_row_ap, n, name="bc"):
    t = k.sb(es, [128, n], F32, name)
    tr_ = Tr()
    k.S.dma("sp", t[:], src_row_ap.partition_broadcast(128), writes=[tr_])
    return t, tr_


def rms_to_bf(k, xt, t_xt, gbc, t_g, obf, t_o, ss, t_ss):
    k.act(obf[:], xt[:], AF.Square, [t_xt], [t_o, t_ss], accum_out=ss[:, 0:1])
    k.ts("dve", ss[:, 1:2], ss[:, 0:1], 1.0 / D, EPS, ALU.mult, ALU.add, [t_ss], [t_ss])
    k.act(ss[:, 2:3], ss[:, 1:2], AF.Sqrt, [t_ss], [t_ss])
    k.recip(ss[:, 3:4], ss[:, 2:3], [t_ss], [t_ss])
    k.stt(obf[:], xt[:], ss[:, 3:4], gbc[:], ALU.mult, ALU.mult, [t_xt, t_ss, t_g], [t_o])


def phase_norm_T(ctx, s, l, xsrc, hT, t_hT):
    k = ctx["k"]
    S = k.S
    with contextlib.ExitStack() as es:
        gbc, t_g = load_bc(k, es, k.vf["mix_norm_g"][l], D, "gmix")
        xts = [(k.sb(es, [128, D], F32, "xt"), Tr()) for _ in range(2)]
        obs = [(k.sb(es, [128, D], BF16, "ob"), Tr()) for _ in range(2)]
        sss = [(k.sb(es, [128, 4], F32, "ss"), Tr()) for _ in range(2)]
        for tt_ in range(NT):
            xt, t_xt = xts[tt_ % 2]
            ob, t_o = obs[tt_ % 2]
            ss, t_ss = sss[tt_ % 2]
            rd = [ctx["t_x"][s][tt_]] if l > 0 else []
            S.dma("sp", xt[:], xsrc[s, tt_ * 128:(tt_ + 1) * 128, :], reads=rd, writes=[t_xt])
            rms_to_bf(k, xt, t_xt, gbc, t_g, ob, t_o, ss, t_ss)
            pb, t_pb = k.psb()
            for j in range(8):
                k.tp(pb[:, j * 128:(j + 1) * 128], ob[:, j * 128:(j + 1) * 128], [t_o], [t_pb])
            k.cp("act" if tt_ % 2 else "dve", hT[:, :, HOFF + tt_ * 128:HOFF + (tt_ + 1) * 128],
                 pb[:].rearrange("p (j t) -> p j t", j=8), [t_pb], [t_hT[tt_]])
        S.barrier()


def phase_yin(ctx):
    k = ctx["k"]
    for rc in range(10):
        for gq in range(8):
            k.S.dma("pool", ctx["yT"][rc * 128:(rc + 1) * 128, gq * 512:(gq + 1) * 512],
                    ctx["yin"][rc * 128:(rc + 1) * 128, gq * 512:(gq + 1) * 512], writes=[ctx["t_yT"][rc][gq]])


def load_w(k, es, name, l, r0, nr, c0, ncols, tag="w"):
    kc = max(1, nr // 128)
    p = min(128, nr)
    t = k.sb(es, [p, kc, ncols], BF16, tag)
    tr_ = Tr()
    src = k.wb[name][l, r0:r0 + nr, c0:c0 + ncols].rearrange("(kc p) n -> p kc n", p=p)
    k.S.dma("sp", t[:], src, reads=k.t_w[name], writes=[tr_])
    return t, tr_


def load_w_into(k, t, tr_, name, l, r0, nr, c0, ncols):
    p = min(128, nr)
    src = k.wb[name][l, r0:r0 + nr, c0:c0 + ncols].rearrange("(kc p) n -> p kc n", p=p)
    k.S.dma("sp", t[:], src, reads=k.t_w[name], writes=[tr_])


def phase_G(ctx, s, l, xsrc, hT, t_hT):
    k = ctx["k"]
    S = k.S
    yT, t_yT = ctx["yT"], ctx["t_yT"]
    with contextlib.ExitStack() as es:
        pcat = k.sb(es, [128, 10, D], BF16, "pcat")
        t_p = Tr()
        for nm, c0, n in (("p_a", 0, 2), ("p_b", 2, 4), ("p_c", 6, 2), ("p_d", 8, 2)):
            S.dma("sp", pcat[:, c0:c0 + n, :], k.wb[nm][l].rearrange("(kc p) n -> p kc n", p=128),
                  reads=k.t_w[nm], writes=[t_p])
        wmo, t_wmo = load_w(k, es, "w_mix_out", l, 0, D, 0, D, "wmo")
        wgs = [(k.sb(es, [128, 8, 4, 128], BF16, "wg"), Tr()) for _ in range(2)]
        yts = [(k.sb(es, [128, 10, 512], BF16, "yt"), Tr()) for _ in range(2)]
        mT = k.sb(es, [128, 8, 512], BF16, "mT")
        t_mT = Tr()
        accs = [(k.sb(es, [128, 512], F32, "acc"), Tr()) for _ in range(2)]
        sigs = [(k.sb(es, [128, 512], F32, "sig"), Tr()) for _ in range(3)]
        xts = [(k.sb(es, [128, D], F32, "xg"), Tr()) for _ in range(2)]
        branches = ((0, 2), (2, 4), (6, 2), (8, 2))
        it = 0
        for gq in range(8):
            yt, t_yt = yts[gq % 2]
            for rc in range(10):
                S.dma("sp", yt[:, rc, :], yT[rc * 128:(rc + 1) * 128, gq * 512:(gq + 1) * 512],
                      reads=[t_yT[rc][gq]], writes=[t_yt])
            hsl = slice(HOFF + gq * 512, HOFF + (gq + 1) * 512)
            rh = t_hT[gq * 4:(gq + 1) * 4]
            for oc in range(8):
                wg, t_wg = wgs[it % 2]
                it += 1
                for bi in range(4):
                    c0 = OFF_G + bi * D + oc * 128
                    S.dma("sp", wg[:, :, bi, :], k.wb["w_in"][l, :, c0:c0 + 128].rearrange("(kc p) n -> p kc n", p=128),
                          reads=k.t_w["w_in"], writes=[t_wg])
                acc, t_acc = accs[oc % 2]
                for bi, (c0, n) in enumerate(branches):
                    pg, t_pg = k.ps()
                    for kc in range(8):
                        k.mm(pg[:], wg[:, kc, bi, :], hT[:, kc, hsl], kc == 0, kc == 7, [t_wg] + rh, [t_pg])
                    sg, t_sg = sigs[(oc * 4 + bi) % 3]
                    k.act(sg[:], pg[:], AF.Sigmoid, [t_pg], [t_sg])
                    py, t_py = k.ps()
                    for j in range(n):
                        k.mm(py[:], pcat[:, c0 + j, oc * 128:(oc + 1) * 128], yt[:, c0 + j, :], j == 0, j == n - 1,
                             [t_p, t_yt], [t_py])
                    if bi == 0:
                        k.tt("dve", acc[:], py[:], sg[:], ALU.mult, [t_py, t_sg], [t_acc])
                    else:
                        k.tt("dve", sg[:], py[:], sg[:], ALU.mult, [t_py, t_sg], [t_sg])
                        if bi < 3:
                            k.tt("pool", acc[:], acc[:], sg[:], ALU.add, [t_acc, t_sg], [t_acc])
                        else:
                            k.tt("pool", mT[:, oc, :], acc[:], sg[:], ALU.add, [t_acc, t_sg], [t_mT])
            for q in range(4):
                tt_ = gq * 4 + q
                xt, t_xt = xts[q % 2]
                rd = [ctx["t_x"][s][tt_]] if l > 0 else []
                S.dma("sp", xt[:], xsrc[s, tt_ * 128:(tt_ + 1) * 128, :], reads=rd, writes=[t_xt])
                for cg in range(2):
                    po, t_po = k.ps()
                    for oc in range(8):
                        k.mm(po[:], mT[:, oc, q * 128:(q + 1) * 128], wmo[:, oc, cg * 512:(cg + 1) * 512], oc == 0,
                             oc == 7, [t_mT, t_wmo], [t_po])
                    k.tt("dve", xt[:, cg * 512:(cg + 1) * 512], xt[:, cg * 512:(cg + 1) * 512], po[:], ALU.add,
                         [t_xt, t_po], [t_xt])
                S.dma("sp", ctx["xbuf"][s, tt_ * 128:(tt_ + 1) * 128, :], xt[:], reads=[t_xt],
                      writes=[ctx["t_x"][s][tt_]])
        S.barrier()


def phase_M(ctx, s, l):
    k = ctx["k"]
    S = k.S
    t_x = ctx["t_x"][s]
    xbuf = ctx["xbuf"]
    sc = 256 ** -0.5
    with contextlib.ExitStack() as es:
        kmT = k.sb(es, [128, 8, MEM], BF16, "kmT")
        vm = k.sb(es, [128, 2, D], BF16, "vm")
        t_km, t_vm = Tr(), Tr()
        with contextlib.ExitStack() as e2:
            gkv, t_gkv = load_bc(k, e2, k.vf["mem_kv_norm_g"][l], D, "gkv")
            mnT = k.sb(e2, [128, 8, MEM], BF16, "mnT")
            t_mn = Tr()
            xt = k.sb(e2, [128, D], F32, "mx")
            ob = k.sb(e2, [128, D], BF16, "mob")
            ss = k.sb(e2, [128, 4], F32, "mss")
            t_xt, t_o, t_ss = Tr(), Tr(), Tr()
            for mt in range(2):
                S.dma("sp", xt[:], ctx["mem_in"][s, mt * 128:(mt + 1) * 128, :], writes=[t_xt])
                rms_to_bf(k, xt, t_xt, gkv, t_gkv, ob, t_o, ss, t_ss)
                pb, t_pb = k.psb()
                for j in range(8):
                    k.tp(pb[:, j * 128:(j + 1) * 128], ob[:, j * 128:(j + 1) * 128], [t_o], [t_pb])
                k.cp("dve", mnT[:, :, mt * 128:(mt + 1) * 128], pb[:].rearrange("p (j t) -> p j t", j=8), [t_pb], [t_mn])
            for half in range(4):
                wkv, t_wkv = load_w(k, e2, "w_mem_kv", l, 0, D, half * 512, 512, "wkv")
                if half < 2:
                    for fc in range(4):
                        p_, t_p = k.ps()
                        for kc in range(8):
                            k.mm(p_[:, 0:MEM], wkv[:, kc, fc * 128:(fc + 1) * 128], mnT[:, kc, :], kc == 0, kc == 7,
                                 [t_wkv, t_mn], [t_p])
                        k.cp("act", kmT[:, half * 4 + fc, :], p_[:, 0:MEM], [t_p], [t_km])
                else:
                    for mt in range(2):
                        p_, t_p = k.ps()
                        for kc in range(8):
                            k.mm(p_[:], mnT[:, kc, mt * 128:(mt + 1) * 128], wkv[:, kc, :], kc == 0, kc == 7,
                                 [t_wkv, t_mn], [t_p])
                        k.cp("act", vm[:, mt, (half - 2) * 512:(half - 1) * 512], p_[:], [t_p], [t_vm])
            S.barrier()
        gq_, t_gq = load_bc(k, es, k.vf["mem_q_norm_g"][l], D, "gq")
        wq, t_wq = load_w(k, es, "w_mem_q", l, 0, D, 0, D, "wq")
        wo, t_wo = load_w(k, es, "w_mem_o", l, 0, D, 0, D, "wo")
        xts = [(k.sb(es, [128, D], F32, "xq"), Tr()) for _ in range(4)]
        ob = k.sb(es, [128, D], BF16, "qob")
        ss = k.sb(es, [128, 4], F32, "qss")
        t_o, t_ss = Tr(), Tr()
        xnT = k.sb(es, [128, 8, 512], BF16, "xnT")
        t_xn = Tr()
        qT = k.sb(es, [128, 8, 512], BF16, "qT")
        t_qT = Tr()
        ETs = [(k.sb(es, [128, 2, 512], BF16, "ET"), Tr()) for _ in range(2)]
        oT = k.sb(es, [128, 8, 512], BF16, "oT")
        t_oT = Tr()
        rden = k.sb(es, [128, 512], F32, "rden")
        t_rd = Tr()
        for gq in range(8):
            for q in range(4):
                tt_ = gq * 4 + q
                xt, t_xt = xts[q]
                S.dma("sp", xt[:], xbuf[s, tt_ * 128:(tt_ + 1) * 128, :], reads=[t_x[tt_]], writes=[t_xt])
                rms_to_bf(k, xt, t_xt, gq_, t_gq, ob, t_o, ss, t_ss)
                pb, t_pb = k.psb()
                for j in range(8):
                    k.tp(pb[:, j * 128:(j + 1) * 128], ob[:, j * 128:(j + 1) * 128], [t_o], [t_pb])
                k.cp("dve", xnT[:, :, q * 128:(q + 1) * 128], pb[:].rearrange("p (j t) -> p j t", j=8), [t_pb], [t_xn])
            for fc in range(8):
                p_, t_p = k.ps()
                for kc in range(8):
                    k.mm(p_[:], wq[:, kc, fc * 128:(fc + 1) * 128], xnT[:, kc, :], kc == 0, kc == 7, [t_wq, t_xn], [t_p])
                k.cp("act", qT[:, fc, :], p_[:], [t_p], [t_qT])
            for h in range(4):
                ET, t_ET = ETs[h % 2]
                for mt in range(2):
                    p_, t_p = k.ps()
                    for j in range(2):
                        k.mm(p_[:], kmT[:, h * 2 + j, mt * 128:(mt + 1) * 128], qT[:, h * 2 + j, :], j == 0, j == 1,
                             [t_km, t_qT], [t_p])
                    k.act(ET[:, mt, :], p_[:], AF.Exp, [t_p], [t_ET], scale=sc)
                pd, t_pd = k.ps()
                for mt in range(2):
                    k.mm(pd[:], k.cs["ones_b"][:], ET[:, mt, :], mt == 0, mt == 1, [t_ET, k.t_const], [t_pd])
                k.recip(rden[:], pd[:], [t_pd], [t_rd])
                for j in range(2):
                    p_, t_p = k.ps()
                    for mt in range(2):
                        k.mm(p_[:], vm[:, mt, (h * 2 + j) * 128:(h * 2 + j + 1) * 128], ET[:, mt, :], mt == 0, mt == 1,
                             [t_vm, t_ET], [t_p])
                    k.tt("dve", oT[:, h * 2 + j, :], p_[:], rden[:], ALU.mult, [t_p, t_rd], [t_oT])
            for q in range(4):
                tt_ = gq * 4 + q
                xt, t_xt = xts[q]
                for cg in range(2):
                    po, t_po = k.ps()
                    for oc in range(8):
                        k.mm(po[:], oT[:, oc, q * 128:(q + 1) * 128], wo[:, oc, cg * 512:(cg + 1) * 512], oc == 0,
                             oc == 7, [t_oT, t_wo], [t_po])
                    k.tt("dve", xt[:, cg * 512:(cg + 1) * 512], xt[:, cg * 512:(cg + 1) * 512], po[:], ALU.add,
                         [t_xt, t_po], [t_xt])
                S.dma("sp", xbuf[s, tt_ * 128:(tt_ + 1) * 128, :], xt[:], reads=[t_xt], writes=[t_x[tt_]])
        S.barrier()


def phase_F(ctx, s, l):
    k = ctx["k"]
    S = k.S
    t_x = ctx["t_x"][s]
    xbuf = ctx["xbuf"]
    NC = 2 * DFF // 128
    with contextlib.ExitStack() as es:
        gf, t_gf = load_bc(k, es, k.vf["ffn_norm_g"][l], D, "gf")
        w1, t_w1 = load_w(k, es, "w_ffn_in", l, 0, D, 0, 2 * DFF, "wf1")
        w2, t_w2 = load_w(k, es, "w_ffn_out", l, 0, DFF, 0, D, "wf2")
        cw = k.sb(es, [128, 4, NC], F32, "cw")
        t_cw = Tr()
        for j in range(3):
            S.dma("sp", cw[:, j, :], k.vf["ffn_conv_w"][l, j].rearrange("(c p) -> p c", p=128), writes=[t_cw],
                  allow_slow_non_contiguous=True)
        S.dma("sp", cw[:, 3, :], k.vf["ffn_conv_b"][l].rearrange("(c p) -> p c", p=128), writes=[t_cw],
              allow_slow_non_contiguous=True)
        halo = k.sb(es, [128, NC, 2], F32, "halo")
        t_halo = [Tr() for _ in range(NC)]
        k.memset(halo[:], 0.0, t_halo)
        xts = [(k.sb(es, [128, D], F32, "xf"), Tr()) for _ in range(4)]
        ob = k.sb(es, [128, D], BF16, "fob")
        ss = k.sb(es, [128, 4], F32, "fss")
        t_o, t_ss = Tr(), Tr()
        xnT = k.sb(es, [128, 8, 512], BF16, "fxnT")
        t_xn = Tr()
        ues = [(k.sb(es, [128, 514], F32, "ue"), Tr()) for _ in range(2)]
        c0s = [(k.sb(es, [128, 512], F32, "c0"), Tr()) for _ in range(2)]
        sgs = [(k.sb(es, [128, 512], F32, "sgf"), Tr()) for _ in range(2)]
        aT = k.sb(es, [128, DFF // 128, 512], BF16, "aT")
        t_aT = Tr()
        for gq in range(8):
            for q in range(4):
                tt_ = gq * 4 + q
                xt, t_xt = xts[q]
                S.dma("sp", xt[:], xbuf[s, tt_ * 128:(tt_ + 1) * 128, :], reads=[t_x[tt_]], writes=[t_xt])
                rms_to_bf(k, xt, t_xt, gf, t_gf, ob, t_o, ss, t_ss)
                pb, t_pb = k.psb()
                for j in range(8):
                    k.tp(pb[:, j * 128:(j + 1) * 128], ob[:, j * 128:(j + 1) * 128], [t_o], [t_pb])
                k.cp("dve", xnT[:, :, q * 128:(q + 1) * 128], pb[:].rearrange("p (j t) -> p j t", j=8), [t_pb], [t_xn])
            it = 0
            for j in range(DFF // 128):
                res = []
                for c in (j, j + 22):
                    p_, t_p = k.ps()
                    for kc in range(8):
                        k.mm(p_[:], w1[:, kc, c * 128:(c + 1) * 128], xnT[:, kc, :], kc == 0, kc == 7, [t_w1, t_xn], [t_p])
                    ue, t_ue = ues[it % 2]
                    c0, t_c0 = c0s[it % 2]
                    it += 1
                    k.cp("pool", ue[:, 0:2], halo[:, c, :], [t_halo[c]], [t_ue])
                    k.cp("act", ue[:, 2:514], p_[:], [t_p], [t_ue])
                    k.act(c0[:], p_[:], AF.Identity, [t_p, t_cw], [t_c0], scale=cw[:, 2, c:c + 1], bias=cw[:, 3, c:c + 1])
                    k.cp("pool", halo[:, c, :], ue[:, 512:514], [t_ue], [t_halo[c]])
                    k.stt(c0[:], ue[:, 1:513], cw[:, 1, c:c + 1], c0[:], ALU.mult, ALU.add, [t_ue, t_cw, t_c0], [t_c0])
                    k.stt(c0[:], ue[:, 0:512], cw[:, 0, c:c + 1], c0[:], ALU.mult, ALU.add, [t_ue, t_cw, t_c0], [t_c0])
                    res.append((c0, t_c0))
                sg, t_sg = sgs[j % 2]
                k.act(sg[:], res[0][0][:], AF.Silu, [res[0][1]], [t_sg])
                k.tt("pool", aT[:, j, :], sg[:], res[1][0][:], ALU.mult, [t_sg, res[1][1]], [t_aT])
            for q in range(4):
                tt_ = gq * 4 + q
                xt, t_xt = xts[q]
                for cg in range(2):
                    po, t_po = k.ps()
                    for j in range(DFF // 128):
                        k.mm(po[:], aT[:, j, q * 128:(q + 1) * 128], w2[:, j, cg * 512:(cg + 1) * 512], j == 0,
                             j == DFF // 128 - 1, [t_aT, t_w2], [t_po])
                    k.tt("dve", xt[:, cg * 512:(cg + 1) * 512], xt[:, cg * 512:(cg + 1) * 512], po[:], ALU.add,
                         [t_xt, t_po], [t_xt])
                S.dma("sp", xbuf[s, tt_ * 128:(tt_ + 1) * 128, :], xt[:], reads=[t_xt], writes=[t_x[tt_]])
        S.barrier()


def phase_final(ctx, s):
    k = ctx["k"]
    S = k.S
    with contextlib.ExitStack() as es:
        gbc, t_g = load_bc(k, es, k.vf["final_norm_g"], D, "gfin")
        xts = [(k.sb(es, [128, D], F32, "xo"), Tr()) for _ in range(2)]
        junk = k.sb(es, [128, D], BF16, "junk")
        t_j = Tr()
        sss = [(k.sb(es, [128, 4], F32, "oss"), Tr()) for _ in range(2)]
        for tt_ in range(NT):
            xt, t_xt = xts[tt_ % 2]
            ss, t_ss = sss[tt_ % 2]
            S.dma("sp", xt[:], ctx["xbuf"][s, tt_ * 128:(tt_ + 1) * 128, :], reads=[ctx["t_x"][s][tt_]], writes=[t_xt])
            k.act(junk[:], xt[:], AF.Square, [t_xt], [t_j, t_ss], accum_out=ss[:, 0:1])
            k.ts("dve", ss[:, 1:2], ss[:, 0:1], 1.0 / D, EPS, ALU.mult, ALU.add, [t_ss], [t_ss])
            k.act(ss[:, 2:3], ss[:, 1:2], AF.Sqrt, [t_ss], [t_ss])
            k.recip(ss[:, 3:4], ss[:, 2:3], [t_ss], [t_ss])
            k.stt(xt[:], xt[:], ss[:, 3:4], gbc[:], ALU.mult, ALU.mult, [t_xt, t_ss, t_g], [t_xt])
            S.dma("sp", ctx["out"][s, tt_ * 128:(tt_ + 1) * 128, :], xt[:], reads=[t_xt])
        S.barrier()


def sl(st, n, step):
    return slice(st, st + step * (n - 1) + 1, step)


def phase_rope(ctx, es, s):
    k = ctx["k"]
    S = k.S
    cosT = k.sb(es, [128, S_LEN], F32, "cosT")
    sinT = k.sb(es, [128, S_LEN], F32, "sinT")
    t_tab = Tr()
    CW = 1024
    with contextlib.ExitStack() as e2:
        posi = k.sb(e2, [128, CW], I32, "posi")
        tq = k.sb(e2, [128, CW], F32, "tq")
        ki = k.sb(e2, [128, CW], I32, "ki")
        kf = k.sb(e2, [128, CW], F32, "kf")
        fr = k.sb(e2, [128, CW], F32, "fr")
        aa = k.sb(e2, [128, CW], F32, "aa")
        t_ = Tr()
        invf = k.cs["c_invf"]
        for c in range(S_LEN // CW):
            cols = slice(c * CW, (c + 1) * CW)
            S.dma("sp", posi[:], ctx["pos_in"][s, cols].partition_broadcast(128), writes=[t_])
            k.cp("dve", tq[:], posi[:], [t_], [t_])
            k.ts("dve", tq[:], tq[:], invf[:, 0:1], 1.0 / (2 * math.pi), ALU.mult, ALU.mult, [t_, k.t_const], [t_])
            k.cp("dve", ki[:], tq[:], [t_], [t_])
            k.cp("dve", kf[:], ki[:], [t_], [t_])
            k.tt("dve", fr[:], tq[:], kf[:], ALU.subtract, [t_], [t_])
            for dst, shift in ((sinT, 0.0), (cosT, 0.25)):
                if shift:
                    k.ts("dve", fr[:], fr[:], shift, None, ALU.add, ALU.bypass, [t_], [t_])
                k.ts("dve", aa[:], fr[:], 0.5, None, ALU.is_gt, ALU.bypass, [t_], [t_])
                k.tt("dve", fr[:], fr[:], aa[:], ALU.subtract, [t_], [t_])
                k.ts("dve", aa[:], fr[:], -0.5, None, ALU.is_lt, ALU.bypass, [t_], [t_])
                k.tt("dve", fr[:], fr[:], aa[:], ALU.add, [t_], [t_])
                k.act(dst[:, cols], fr[:], AF.Sin, [t_], [t_tab], scale=6.283185)
        S.barrier()
    return cosT, sinT, t_tab


def proj_rot(k, es_bufs, hT, t_hT, w, t_w, tabs, dst, t_dst):
    cosT, sinT, t_tab = tabs
    qraw, t_qr, t1, t_t1, t2, t_t2 = es_bufs
    pm = k.cs["c_pm"]
    for gq in range(8):
        cols = slice(gq * 512, (gq + 1) * 512)
        hs = slice(HOFF + gq * 512, HOFF + (gq + 1) * 512)
        p_, t_p = k.ps()
        for kc in range(8):
            k.mm(p_[:], w[:, kc, :], hT[:, kc, hs], kc == 0, kc == 7, [t_w] + t_hT[gq * 4:(gq + 1) * 4], [t_p])
        k.cp("act", qraw[:], p_[:], [t_p], [t_qr])
        p2, t_p2 = k.ps()
        k.mm(p2[:], pm[:], qraw[:], True, True, [t_qr, k.t_const], [t_p2])
        k.tt("dve", t1[:], p_[:], cosT[:, cols], ALU.mult, [t_p, t_tab], [t_t1])
        k.tt("dve", t2[:], p2[:], sinT[:, cols], ALU.mult, [t_p2, t_tab], [t_t2])
        k.tt("pool", dst[:, cols], t1[:], t2[:], ALU.add, [t_t1, t_t2], [t_dst])


def rot_bufs(k, es):
    return (k.sb(es, [128, 512], BF16, "qraw"), Tr(), k.sb(es, [128, 512], F32, "rt1"), Tr(),
            k.sb(es, [128, 512], F32, "rt2"), Tr())


def phase_A(ctx, s, l, hT, t_hT, tabs):
    k = ctx["k"]
    S = k.S
    maskA = k.cs["c_maskA"]
    ones_b = k.cs["ones_b"]
    saved = (k.ps_f, k.ps_ft, k.ps_i)
    k.ps_f = list(saved[0]) + [k.ps_b[0][:].bitcast(F32), k.ps_b[1][:].bitcast(F32)]
    k.ps_ft = list(saved[1]) + [k.ps_bt[0], k.ps_bt[1]]
    with contextlib.ExitStack() as es:
        rb = rot_bufs(k, es)
        qT = k.sb(es, [128, S_LEN], BF16, "aqT")
        kT = k.sb(es, [128, S_LEN], BF16, "akT")
        vs = k.sb(es, [128, 32, 128], BF16, "avs")
        acc = k.sb(es, [128, 2, S_LEN], F32, "aacc")
        yb = k.sb(es, [128, S_LEN], BF16, "ayb")
        t_q, t_k, t_v, t_acc, t_yb = Tr(), Tr(), Tr(), Tr(), Tr()
        Es = [(k.sb(es, [128, 2, 256], BF16, "aE"), Tr()) for _ in range(5)]
        ws = [(k.sb(es, [128, 8, 128], BF16, "aw"), Tr()) for _ in range(3)]
        for hp in range(2):
            for g, Dl in enumerate((1, 4, 16)):
                for j3 in range(3):
                    load_w_into(k, ws[j3][0], ws[j3][1], "w_in", l, 0, D, OFF_A + g * 768 + j3 * 256 + hp * 128, 128)
                proj_rot(k, rb, hT, t_hT, ws[0][0], ws[0][1], tabs, qT, t_q)
                proj_rot(k, rb, hT, t_hT, ws[1][0], ws[1][1], tabs, kT, t_k)
                nb = 32 // Dl
                wv, t_wv = ws[2]
                for b0 in range(0, 32, 4):
                    p_, t_p = k.ps()
                    for bb in range(4):
                        blk = b0 + bb
                        r, j = blk // nb, blk % nb
                        st = HOFF + r + Dl * 128 * j
                        for kc in range(8):
                            k.mm(p_[:, bb * 128:(bb + 1) * 128], hT[:, kc, sl(st, 128, Dl)], wv[:, kc, :], kc == 0,
                                 kc == 7, [t_wv] + t_hT, [t_p])
                    k.cp("act", vs[:, b0:b0 + 4, :], p_[:].rearrange("p (b c) -> p b c", b=4), [t_p], [t_v])
                items = [(r, j) for r in range(Dl) for j in range(nb)]

                def st1(i):
                    r, j = items[i]
                    nq = 256 if j + 1 < nb else 128
                    st = r + Dl * 128 * j
                    kcols = sl(st, 128, Dl)
                    qcols = sl(st, nq, Dl)
                    E, t_E = Es[i % 5]
                    for h2 in range(2):
                        hb = 64 * h2
                        p_, t_p = k.ps()
                        k.mm(p_[:, 0:nq], kT[hb:hb + 64, kcols], qT[hb:hb + 64, qcols], True, True,
                             [t_k, t_q], [t_p])
                        k.act(E[:, h2, 0:nq], p_[:, 0:nq], AF.Exp, [t_p], [t_E], scale=0.125)
                    k.tt("pool", E[:, :, 0:nq], E[:, :, 0:nq], maskA[:].rearrange("p (a b) -> p a b", a=2)[:, :, 0:nq],
                         ALU.mult, [t_E, k.t_const], [t_E])

                def st2(i):
                    r, j = items[i]
                    st = r + Dl * 128 * j
                    E, t_E = Es[i % 5]
                    Eprev = Es[(i - 1) % 5]
                    blk = r * nb + j
                    po, t_po = k.ps()
                    for pl in range(2):
                        if j > 0:
                            for h2 in range(2):
                                hb = 64 * h2
                                lh = vs[:, blk - 1, hb:hb + 64] if pl == 0 else ones_b[:, 0:64]
                                k.mm(po[hb:hb + 64, pl * 128:(pl + 1) * 128], lh, Eprev[0][:, h2, 128:256], True, False,
                                     [t_v, Eprev[1], k.t_const], [t_po])
                        for h2 in range(2):
                            hb = 64 * h2
                            lh = vs[:, blk, hb:hb + 64] if pl == 0 else ones_b[:, 0:64]
                            k.mm(po[hb:hb + 64, pl * 128:(pl + 1) * 128], lh, E[:, h2, 0:128], j == 0, True,
                                 [t_v, t_E, k.t_const], [t_po])
                    qtok = sl(st, 128, Dl)
                    pov = po[:, 0:256].rearrange("p (a b) -> p a b", a=2)
                    if g == 0:
                        k.cp("dve", acc[:, :, qtok], pov, [t_po], [t_acc])
                    else:
                        k.tt("dve", acc[:, :, qtok], acc[:, :, qtok], pov, ALU.add, [t_po, t_acc], [t_acc])
                LA = 2
                for i in range(min(LA, len(items))):
                    st1(i)
                for i in range(len(items)):
                    if i + LA < len(items):
                        st1(i + LA)
                    st2(i)
            for gq in range(8):
                cols = slice(gq * 512, (gq + 1) * 512)
                k.recip(acc[:, 1, cols], acc[:, 1, cols], [t_acc], [t_acc])
                k.tt("dve", yb[:, cols], acc[:, 0, cols], acc[:, 1, cols], ALU.mult, [t_acc], [t_yb])
                S.dma("sp", ctx["yT"][hp * 128:(hp + 1) * 128, cols], yb[:, cols], reads=[t_yb],
                      writes=[ctx["t_yT"][hp][gq]])
        S.barrier()
    k.ps_f, k.ps_ft, k.ps_i = saved


def phase_B(ctx, s, l, hT, t_hT, tabs):
    k = ctx["k"]
    S = k.S
    maskD = k.cs["c_maskD"]
    ones_b = k.cs["ones_b"]
    lam_init = 0.8 - 0.6 * math.exp(-0.3 * l)
    saved = (k.ps_f, k.ps_ft, k.ps_i)
    accb = list(zip(k.ps_f[0:4], k.ps_ft[0:4]))
    k.ps_f = [saved[0][4], saved[0][5], k.ps_b[0][:].bitcast(F32), k.ps_b[1][:].bitcast(F32)]
    k.ps_ft = [saved[1][4], saved[1][5], k.ps_bt[0], k.ps_bt[1]]
    k.ps_i = 0
    with contextlib.ExitStack() as es:
        rb = rot_bufs(k, es)
        lv, t_lv = load_bc(k, es, k.vf["diff_lam"][l].rearrange("a b -> (a b)"), 256, "lv")
        sc_ = k.sb(es, [128, 8], F32, "lsc")
        t_sc = Tr()
        pr = k.sb(es, [128, 128], F32, "lpr")
        k.tt("dve", pr[:, 0:64], lv[:, 0:64], lv[:, 64:128], ALU.mult, [t_lv], [t_sc])
        k.tt("dve", pr[:, 64:128], lv[:, 128:192], lv[:, 192:256], ALU.mult, [t_lv], [t_sc])
        S.op("dve", lambda e: e.reduce_sum(out=sc_[:, 0:2], in_=pr[:].rearrange("p (a b) -> p a b", a=2), axis=AX.X),
             reads=[t_sc], writes=[t_sc])
        k.act(sc_[:, 2:4], sc_[:, 0:2], AF.Exp, [t_sc], [t_sc])
        k.tt("dve", sc_[:, 4:5], sc_[:, 3:4], sc_[:, 2:3], ALU.subtract, [t_sc], [t_sc])
        k.ts("dve", sc_[:, 5:6], sc_[:, 4:5], -lam_init, None, ALU.add, ALU.bypass, [t_sc], [t_sc])
        gB = k.sb(es, [128, 1], F32, "gB")
        t_gB = Tr()
        S.dma("sp", gB[:], k.vf["diff_norm_g"][l].rearrange("(p o) -> p o", o=1), writes=[t_gB])
        k.ts("dve", gB[:], gB[:], 1.0 - lam_init, None, ALU.mult, ALU.bypass, [t_gB], [t_gB])
        qT = k.sb(es, [128, S_LEN], BF16, "bqT")
        kT = k.sb(es, [128, S_LEN], BF16, "bkT")
        vs = k.sb(es, [128, 32, 128], BF16, "bvs")
        t_q, t_k, t_v = Tr(), Tr(), Tr()
        Es = [(k.sb(es, [128, 512], BF16, "bE"), Tr()) for _ in range(6)]
        ws = [(k.sb(es, [128, 8, 128], BF16, "bw"), Tr()) for _ in range(3)]
        o1 = k.sb(es, [128, 512], F32, "bo1")
        o2 = k.sb(es, [128, 512], F32, "bo2")
        rr = k.sb(es, [128, 512], F32, "brr")
        sq = k.sb(es, [128, 512], BF16, "bsq")
        ybs = [(k.sb(es, [128, 512], BF16, "byb"), Tr()) for _ in range(2)]
        t_o = Tr()
        for h in range(4):
            for j3 in range(3):
                load_w_into(k, ws[j3][0], ws[j3][1], "w_in", l, 0, D, OFF_B + j3 * 512 + h * 128, 128)
            proj_rot(k, rb, hT, t_hT, ws[0][0], ws[0][1], tabs, qT, t_q)
            proj_rot(k, rb, hT, t_hT, ws[1][0], ws[1][1], tabs, kT, t_k)
            wv, t_wv = ws[2]
            for b0 in range(0, 32, 4):
                p_, t_p = k.ps()
                for bb in range(4):
                    blk = b0 + bb
                    st = HOFF + 128 * blk
                    for kc in range(8):
                        k.mm(p_[:, bb * 128:(bb + 1) * 128], hT[:, kc, st:st + 128], wv[:, kc, :], kc == 0, kc == 7,
                             [t_wv] + t_hT, [t_p])
                k.cp("act", vs[:, b0:b0 + 4, :], p_[:].rearrange("p (b c) -> p b c", b=4), [t_p], [t_v])
            items = [(G, j) for G in range(8) for j in range(4 * G + 4)]
            LA = 1

            def geo(G, j):
                jj = max(j - 4 * G, 0)
                qoff = 128 * jj
                return qoff, 512 - qoff, 512 * G + qoff

            def st1(i):
                G, j = items[i]
                qoff, nq, q0 = geo(G, j)
                pp = []
                for m in range(2):
                    hb = 64 * m
                    p_, t_p = k.ps()
                    k.mm(p_[:, 0:nq], kT[hb:hb + 64, 128 * j:128 * j + 128], qT[hb:hb + 64, q0:q0 + nq], True, True,
                         [t_k, t_q], [t_p])
                    pp.append((p_, t_p))
                for m in range(2):
                    p_, t_p = pp[m]
                    E, t_E = Es[(i % 3) * 2 + m]
                    k.act(E[:, 0:nq], p_[:, 0:nq], AF.Exp, [t_p], [t_E], scale=0.125)
                    if j >= 4 * G:
                        k.tt("dve", E[:, 0:nq], E[:, 0:nq], maskD[:, 0:nq], ALU.mult, [t_E, k.t_const], [t_E])

            def st2(i):
                G, j = items[i]
                qoff, nq, q0 = geo(G, j)
                nj = 4 * G + 4
                for m in range(2):
                    E, t_E = Es[(i % 3) * 2 + m]
                    pn, t_pn = accb[2 * m]
                    pd, t_pd = accb[2 * m + 1]
                    k.mm(pn[:, qoff:512], vs[:, j, :], E[:, 0:nq], j == 0, j == nj - 1, [t_v, t_E], [t_pn])
                    k.mm(pd[:, qoff:512], ones_b[:], E[:, 0:nq], j == 0, j == nj - 1, [t_E, k.t_const], [t_pd])
                if j == nj - 1:
                    for m in range(2):
                        pn, t_pn = accb[2 * m]
                        pd, t_pd = accb[2 * m + 1]
                        k.recip(rr[:], pd[:], [t_pd, t_o], [t_o])
                        k.tt("dve", (o1 if m == 0 else o2)[:], pn[:], rr[:], ALU.mult, [t_pn, t_o], [t_o])
                    fin(G)

            def fin(G):
                k.stt(o1[:], o2[:], sc_[:, 5:6], o1[:], ALU.mult, ALU.add, [t_o, t_sc], [t_o])
                k.tt("pool", sq[:], o1[:], o1[:], ALU.mult, [t_o], [t_o])
                pm_, t_pm = k.ps()
                k.mm(pm_[:], ones_b[:], sq[:], True, True, [t_o, k.t_const], [t_pm])
                k.ts("dve", rr[:], pm_[:], 1.0 / 128, EPS, ALU.mult, ALU.add, [t_pm, t_o], [t_o])
                k.act(rr[:], rr[:], AF.Sqrt, [t_o], [t_o])
                k.recip(rr[:], rr[:], [t_o], [t_o])
                yb, t_yb = ybs[G % 2]
                k.stt(yb[:], o1[:], gB[:, 0:1], rr[:], ALU.mult, ALU.mult, [t_o, t_gB], [t_yb])
                S.dma("sp", ctx["yT"][256 + h * 128:256 + (h + 1) * 128, G * 512:(G + 1) * 512], yb[:], reads=[t_yb],
                      writes=[ctx["t_yT"][2 + h][G]])

            for i in range(min(LA, len(items))):
                st1(i)
            for i in range(len(items)):
                if i + LA < len(items):
                    st1(i + LA)
                st2(i)
        S.barrier()
    k.ps_f, k.ps_ft, k.ps_i = saved


def phase_C(ctx, s, l, hT, t_hT):
    k = ctx["k"]
    S = k.S
    cs = k.cs
    with contextlib.ExitStack() as es:
        wc, t_wc = load_w(k, es, "w_in", l, 0, D, OFF_C, 1024, "wc")
        lb = k.sb(es, [128, 256], F32, "lb")
        omlb = k.sb(es, [128, 256], F32, "omlb")
        t_lb = Tr()
        if l == 0:
            k.memset(lb[:], 0.0, [t_lb])
        else:
            S.dma("sp", lb[:], k.vf["hgrn_lb_logits"][1].partition_broadcast(128), writes=[t_lb])
            S.dma("sp", omlb[:], k.vf["hgrn_lb_logits"][0].partition_broadcast(128), writes=[t_lb])
            k.tt("dve", lb[:], lb[:], omlb[:], ALU.subtract, [t_lb], [t_lb])
            k.act(lb[:], lb[:], AF.Sigmoid, [t_lb], [t_lb])
        k.ts("dve", omlb[:], lb[:], -1.0, 1.0, ALU.mult, ALU.add, [t_lb], [t_lb])
        gn4 = k.sb(es, [128, 4, 64], F32, "gn4")
        t_gn = Tr()
        for h in range(4):
            S.dma("sp", gn4[:, h, :], k.vf["hgrn_norm_g"][l].partition_broadcast(128), writes=[t_gn])
        S32 = k.sb(es, [128, 2, 64], F32, "S32")
        t_S = Tr()
        k.memset(S32[:], 0.0, [t_S])
        Sbfs = [(k.sb(es, [128, 2, 64], BF16, "Sbf"), Tr()) for _ in range(2)]
        k.memset(Sbfs[0][0][:], 0.0, [Sbfs[0][1]])

        def mk(shape, dt, n, nm):
            return [(k.sb(es, shape, dt, nm), Tr()) for _ in range(n)]
        qs_, sf_, lf_, kk_, sg_ = (mk([128, 256], F32, 2, nm) for nm in ("cqs", "csf", "clf", "ckk", "csg"))
        ib_, Qp_, Kp_ = (mk([128, 256], BF16, 2, nm) for nm in ("cib", "cQp", "cKp"))
        eb_, enb_ = (mk([128, 256], F32, 2, nm) for nm in ("ceb", "cenb"))
        colv_ = mk([128, 2, 3], F32, 2, "ccolv")
        dd_ = mk([128, 2], F32, 2, "cdd")
        QT_, QTt_, KT_ = (mk([128, 2, 128], BF16, 2, nm) for nm in ("cQT", "cQTt", "cKT"))
        attE_, attO_ = (mk([128, 2, 128], BF16, 2, nm) for nm in ("cattE", "cattO"))
        QTh_ = mk([128, 2, 128], BF16, 2, "cQTh")
        for b_ in range(2):
            k.memset(QTh_[b_][0][:], 0.0, [QTh_[b_][1]])
        o_ = mk([128, 256], F32, 2, "co")
        sq_ = mk([128, 256], F32, 2, "csq")
        st_ = mk([128, 12], F32, 2, "cst")
        tmpU_ = mk([128, 2, 64], F32, 2, "ctu")
        y_ = mk([128, 256], BF16, 2, "cy")
        ygs = mk([128, 2, 512], BF16, 2, "cyg")
        tri2 = cs["c_tri2"]
        for tt_ in range(NT):
            b = tt_ % 2
            hs = slice(HOFF + tt_ * 128, HOFF + (tt_ + 1) * 128)
            p0, t_p0 = k.ps()
            p1, t_p1 = k.ps()
            for kc in range(8):
                k.mm(p0[:], hT[:, kc, hs], wc[:, kc, 0:512], kc == 0, kc == 7, [t_wc, t_hT[tt_]], [t_p0])
            for kc in range(8):
                k.mm(p1[:], hT[:, kc, hs], wc[:, kc, 512:1024], kc == 0, kc == 7, [t_wc, t_hT[tt_]], [t_p1])
            qs, t_qs = qs_[b]
            sf, t_sf = sf_[b]
            lf, t_lf = lf_[b]
            kk, t_kk = kk_[b]
            sg, t_sg = sg_[b]
            ib, t_ib = ib_[b]
            k.act(qs[:], p0[:, 0:256], AF.Silu, [t_p0], [t_qs])
            k.act(sf[:], p0[:, 256:512], AF.Sigmoid, [t_p0], [t_sf])
            k.cp("act", ib[:], p1[:, 0:256], [t_p1], [t_ib])
            k.act(sg[:], p1[:, 256:512], AF.Silu, [t_p1], [t_sg])
            k.tt("dve", sf[:], sf[:], omlb[:], ALU.mult, [t_sf, t_lb], [t_sf])
            k.tt("dve", sf[:], sf[:], lb[:], ALU.add, [t_sf, t_lb], [t_sf])
            k.act(lf[:], sf[:], AF.Ln, [t_sf], [t_lf])
            k.ts("pool", kk[:], sf[:], -1.0, 1.0, ALU.mult, ALU.add, [t_sf], [t_kk])
            pb_, t_pb_ = k.ps()
            k.mm(pb_[:, 0:256], cs["c_triM"][:], lf[:], True, True, [t_lf, k.t_const], [t_pb_])
            pc, t_pc = k.ps()
            for hp in range(2):
                k.mm(pc[:, hp * 2:hp * 2 + 2], lf[:, hp * 128:(hp + 1) * 128], cs["c_sel"][:], True, True,
                     [t_lf, k.t_const], [t_pc])
            colv, t_cv = colv_[b]
            dd, t_dd = dd_[b]
            pcv = pc[:, 0:4].rearrange("p (a c) -> p a c", a=2)
            k.act(colv[:, :, 0:2], pcv, AF.Exp, [t_pc], [t_cv])
            k.cp("act", st_[b][0][:, 0:2], pcv[:, :, 0], [t_pc], [st_[b][1]])
            k.tt("dve", dd[:], pcv[:, :, 1], st_[b][0][:, 0:2], ALU.subtract, [t_pc, st_[b][1]], [t_dd])
            k.act(colv[:, :, 2], dd[:], AF.Exp, [t_dd], [t_cv])
            eb, t_eb = eb_[b]
            enb, t_enb = enb_[b]
            k.act(eb[:], pb_[:, 0:256], AF.Exp, [t_pb_], [t_eb])
            k.act(enb[:], pb_[:, 0:256], AF.Exp, [t_pb_], [t_enb], scale=-1.0)
            Qp, t_Qp = Qp_[b]
            Kp, t_Kp = Kp_[b]
            k.tt("dve", Qp[:], qs[:], eb[:], ALU.mult, [t_qs, t_eb], [t_Qp])
            k.tt("pool", Kp[:], kk[:], enb[:], ALU.mult, [t_kk, t_enb], [t_Kp])
            pt, t_pt = k.psb()
            for hp in range(2):
                k.tp(pt[:, hp * 128:(hp + 1) * 128], Qp[:, hp * 128:(hp + 1) * 128], [t_Qp], [t_pt])
                k.tp(pt[:, 256 + hp * 128:256 + (hp + 1) * 128], Kp[:, hp * 128:(hp + 1) * 128], [t_Kp], [t_pt])
            QT, t_QT = QT_[b]
            QTt, t_QTt = QTt_[b]
            KT, t_KT = KT_[b]
            k.cp("act", QT[:], pt[:, 0:256].rearrange("p (a c) -> p a c", a=2), [t_pt], [t_QT])
            k.cp("act", KT[:], pt[:, 256:512].rearrange("p (a c) -> p a c", a=2), [t_pt], [t_KT])
            QTh, t_QTh = QTh_[b]
            k.cp("pool", QTh[:, :, 64:128], QT[:, :, 64:128], [t_QT], [t_QTh])
            for hp in range(2):
                k.ts("dve", QTt[:, hp, :], pt[:, hp * 128:(hp + 1) * 128], colv[:, hp, 0:1], None, ALU.mult, ALU.bypass,
                     [t_pt, t_cv], [t_QTt])
            paE, t_paE = k.ps()
            paO, t_paO = k.ps()
            for h in range(4):
                hp, par = h // 2, h % 2
                hb = 64 * par
                pa, t_pa = (paE, t_paE) if par == 0 else (paO, t_paO)
                k.mm(pa[0:64, hp * 128:(hp + 1) * 128], KT[hb:hb + 64, hp, 0:64], QT[hb:hb + 64, hp, :], True, True,
                     [t_KT, t_QT], [t_pa])
                k.mm(pa[64:128, hp * 128:(hp + 1) * 128], KT[hb:hb + 64, hp, 64:128], QTh[hb:hb + 64, hp, :], True, True,
                     [t_KT, t_QTh], [t_pa])
            attE, t_aE = attE_[b]
            attO, t_aO = attO_[b]
            k.tt("dve", attE[:].rearrange("p a c -> p (a c)"), paE[:, 0:256], tri2[:], ALU.mult, [t_paE, k.t_const], [t_aE])
            k.tt("dve", attO[:].rearrange("p a c -> p (a c)"), paO[:, 0:256], tri2[:], ALU.mult, [t_paO, k.t_const], [t_aO])
            po, t_po = k.ps()
            for h in range(4):
                hp, par = h // 2, h % 2
                att, t_att = (attE, t_aE) if par == 0 else (attO, t_aO)
                k.mm(po[:, h * 64:(h + 1) * 64], att[:, hp, :], ib[:, h * 64:(h + 1) * 64], True, True, [t_att, t_ib], [t_po])
            Sbf, t_Sbf = Sbfs[b]
            Sbn, t_Sbn = Sbfs[1 - b]
            piE, t_piE = k.ps()
            piO, t_piO = k.ps()
            for h in range(4):
                hp, par = h // 2, h % 2
                hb = 64 * par
                pi, t_pi = (piE, t_piE) if par == 0 else (piO, t_piO)
                k.mm(pi[:, hp * 64:(hp + 1) * 64], QTt[hb:hb + 64, hp, :], Sbf[hb:hb + 64, hp, :], True, True,
                     [t_QTt, t_Sbf], [t_pi])
            o, t_o = o_[b]
            k.cp("act", o[:], po[:, 0:256], [t_po], [t_o])
            ov = o[:].rearrange("p (a b c) -> p a b c", a=2, b=2)
            k.tt("dve", ov[:, :, 0, :], ov[:, :, 0, :], piE[:, 0:128].rearrange("p (a c) -> p a c", a=2), ALU.add,
                 [t_o, t_piE], [t_o])
            k.tt("dve", ov[:, :, 1, :], ov[:, :, 1, :], piO[:, 0:128].rearrange("p (a c) -> p a c", a=2), ALU.add,
                 [t_o, t_piO], [t_o])
            pu, t_pu = k.ps()
            for h in range(4):
                hp, par = h // 2, h % 2
                hb = 64 * par
                k.mm(pu[hb:hb + 64, hp * 64:(hp + 1) * 64], Kp[:, h * 64:(h + 1) * 64], ib[:, h * 64:(h + 1) * 64], True, True,
                     [t_Kp, t_ib], [t_pu])
            tu, t_tu = tmpU_[b]
            for hp in range(2):
                k.ts("dve", tu[:, hp, :], pu[:, hp * 64:(hp + 1) * 64], colv[:, hp, 2:3], None, ALU.mult, ALU.bypass,
                     [t_pu, t_cv], [t_tu])
                k.stt(S32[:, hp, :], S32[:, hp, :], colv[:, hp, 1:2], tu[:, hp, :], ALU.mult, ALU.add, [t_S, t_cv, t_tu], [t_S])
            k.cp("pool", Sbn[:], S32[:], [t_S], [t_Sbn])
            sq, t_sq = sq_[b]
            st, t_st = st_[b]
            y, t_y = y_[b]
            k.tt("pool", sq[:], o[:], o[:], ALU.mult, [t_o], [t_sq])
            S.op("dve", lambda e, st=st, sq=sq: e.reduce_sum(out=st[:, 0:4], in_=sq[:].rearrange("p (a c) -> p a c", a=4),
                                                            axis=AX.X), reads=[t_sq], writes=[t_st])
            k.ts("dve", st[:, 4:8], st[:, 0:4], 1.0 / 64, EPS, ALU.mult, ALU.add, [t_st], [t_st])
            k.act(st[:, 4:8], st[:, 4:8], AF.Sqrt, [t_st], [t_st])
            k.recip(st[:, 8:12], st[:, 4:8], [t_st], [t_st])
            k.tt("pool", sq[:], sg[:], gn4[:].rearrange("p a c -> p (a c)"), ALU.mult, [t_sg, t_gn, t_sq], [t_sq])
            for h in range(4):
                k.stt(y[:, h * 64:(h + 1) * 64], o[:, h * 64:(h + 1) * 64], st[:, 8 + h:9 + h], sq[:, h * 64:(h + 1) * 64],
                      ALU.mult, ALU.mult, [t_o, t_st, t_sq], [t_y])
            pt2, t_pt2 = k.psb()
            for hp in range(2):
                k.tp(pt2[:, hp * 128:(hp + 1) * 128], y[:, hp * 128:(hp + 1) * 128], [t_y], [t_pt2])
            gq, q = tt_ // 4, tt_ % 4
            yg, t_yg = ygs[gq % 2]
            k.cp("act", yg[:, :, q * 128:(q + 1) * 128], pt2[:, 0:256].rearrange("p (a c) -> p a c", a=2), [t_pt2], [t_yg])
            if q == 3:
                for hp in range(2):
                    S.dma("sp", ctx["yT"][768 + hp * 128:768 + (hp + 1) * 128, gq * 512:(gq + 1) * 512], yg[:, hp, :],
                          reads=[t_yg], writes=[ctx["t_yT"][6 + hp][gq]])
        S.barrier()


def phase_D(ctx, s, l, hT, t_hT, t_h0):
    k = ctx["k"]
    S = k.S
    cs = k.cs
    vfirst, t_vf = ctx["vfirst"], ctx["t_vf"][s]
    with contextlib.ExitStack() as es:
        wa = k.sb(es, [128, 8, 1024], BF16, "wda")
        wb_ = k.sb(es, [128, 8, 1024], BF16, "wdb")
        t_wa = Tr()
        with contextlib.ExitStack() as e2:
            wd, t_wd = load_w(k, e2, "w_in", l, 0, D, OFF_D, 1024, "wd")
            mu, t_mu = load_bc(k, e2, k.vf["rwkv_mu"][l], 1024, "mu")
            omu = k.sb(e2, [128, 1024], F32, "omu")
            k.ts("dve", omu[:], mu[:], -1.0, 1.0, ALU.mult, ALU.add, [t_mu], [t_mu])
            for kc in range(8):
                k.tt("dve", wb_[:, kc, :], wd[:, kc, :], mu[:], ALU.mult, [t_wd, t_mu], [t_wa])
                k.tt("pool", wa[:, kc, :], wd[:, kc, :], omu[:], ALU.mult, [t_wd, t_mu], [t_wa])
            S.barrier()
        t_bc = Tr()

        def bc(src, n=256, nm="dbc"):
            t = k.sb(es, [128, n], F32, nm)
            S.dma("sp", t[:], src.partition_broadcast(128), writes=[t_bc])
            return t
        w0 = bc(k.vf["rwkv_w0"][l])
        a0 = bc(k.vf["rwkv_a0"][l])
        kkb = bc(k.vf["rwkv_k_k"][l])
        kab = bc(k.vf["rwkv_k_a"][l])
        lng = bc(k.vf["rwkv_lnx_g"][l])
        lnb = bc(k.vf["rwkv_lnx_b"][l])
        rkb = bc(k.vf["rwkv_r_k"][l].rearrange("a b -> (a b)"))
        omka = k.sb(es, [128, 256], F32, "omka")
        k.ts("dve", omka[:], kab[:], -1.0, 1.0, ALU.mult, ALU.add, [t_bc], [t_bc])
        w2a2 = k.sb(es, [128, 256], BF16, "w2a2")
        g2 = k.sb(es, [128, 256], BF16, "g2")
        t_lw = Tr()
        S.dma("sp", w2a2[0:64, :], k.wb["rwkv_w2"][l], reads=k.t_w["rwkv_w2"], writes=[t_lw])
        S.dma("sp", w2a2[64:128, :], k.wb["rwkv_a2"][l], reads=k.t_w["rwkv_a2"], writes=[t_lw])
        S.dma("sp", g2[:], k.wb["rwkv_g2"][l], reads=k.t_w["rwkv_g2"], writes=[t_lw])
        if l > 0:
            v0b = bc(k.vf["rwkv_v0"][l - 1])
            v1 = k.sb(es, [128, 2, 32], BF16, "v1")
            v2 = k.sb(es, [32, 256], BF16, "v2")
            S.dma("sp", v1[:], k.wb["rwkv_v1"][l - 1].rearrange("(kc p) n -> p kc n", p=128), reads=k.t_w["rwkv_v1"],
                  writes=[t_lw])
            S.dma("sp", v2[:], k.wb["rwkv_v2"][l - 1], reads=k.t_w["rwkv_v2"], writes=[t_lw])
        H32 = k.sb(es, [128, 2, 64], F32, "H32")
        t_H = Tr()
        k.memset(H32[:], 0.0, [t_H])
        Hbfs = [(k.sb(es, [128, 2, 64], BF16, "Hbf"), Tr()) for _ in range(2)]
        k.memset(Hbfs[0][0][:], 0.0, [Hbfs[0][1]])

        def f32(nm, n=256):
            return k.sb(es, [128, n], F32, nm), Tr()

        def b16(nm, shape=(128, 256)):
            return k.sb(es, list(shape), BF16, nm), Tr()
        r32, t_r = f32("r32")
        k32, t_k = f32("k32")
        v32, t_v = f32("v32")
        li, t_li = b16("li")
        liT, t_liT = b16("liT", (128, 2, 128))
        lw, t_lwv = f32("lw")
        a32, t_a = f32("a32")
        g32, t_g = f32("g32")
        tmp, t_tmp = f32("tmp")
        tmp2, t_tmp2 = f32("tmp2")
        kk0, t_kk = f32("kk0")
        kp, t_kp = f32("kp")
        bv, t_bv = f32("bv")
        st, t_st = f32("st", 24)
        e1, t_e1 = f32("e1")
        e2_, t_e2 = f32("e2")
        e4, t_e4 = f32("e4")
        e5, t_e5 = f32("e5")
        ewl, t_ewl = f32("ewl")
        colv, t_cv = k.sb(es, [128, 2, 2], F32, "dcolv"), Tr()
        Rp, t_Rp = b16("Rp")
        Ap, t_Ap = b16("Ap")
        Bp, t_Bp = b16("Bp")
        Kp, t_Kp = b16("Kp")
        At, t_At = b16("At")
        Bc, t_Bc = b16("Bc")
        Kc, t_Kc = b16("Kc")
        Vb, t_Vb = b16("Vb")
        ART, t_ART = b16("ART", (128, 2, 2, 128))
        BT, t_BT = b16("BT", (128, 2, 128))
        KT, t_KT = b16("KT", (128, 2, 128))
        RTt, t_RTt = b16("RTt", (128, 2, 128))
        RhT, t_RhT = b16("RhT", (128, 2, 128))
        LM = [b16("LM", (128, 512)) for _ in range(4)]
        Lp = [[b16("Lp", (128, 2, 128)) for _ in range(7)] for _ in range(4)]
        Xs = [[b16("X", (128, 128)) for _ in range(2)] for _ in range(4)]
        Ysb, t_Y = f32("Ysb")
        GpT, t_GpT = b16("GpT", (128, 2, 64))
        Zsb, t_Z = k.sb(es, [128, 2, 64], F32, "Zsb"), Tr()
        ybf, t_ybf = b16("ybf")
        ygs = [b16("dyg", (128, 2, 512)) for _ in range(2)]
        if l > 0:
            vT, t_vT = b16("vT", (128, 2, 128))
            u1, t_u1 = b16("u1", (32, 128))
            vfb, t_vfb = f32("vfb")
        mSI2, mLow = cs["c_maskSI2"], cs["c_maskLow"]

        for tt_ in range(NT):
            hs = slice(HOFF + tt_ * 128, HOFF + (tt_ + 1) * 128)
            hs1 = slice(HOFF + tt_ * 128 - 1, HOFF + (tt_ + 1) * 128 - 1)
            rdh = [t_wa, t_hT[tt_], t_h0] + ([t_hT[tt_ - 1]] if tt_ > 0 else [])
            pA, t_pA = k.ps()
            pB, t_pB = k.ps()
            for p_, t_p, c0 in ((pA, t_pA, 0), (pB, t_pB, 512)):
                for kc in range(8):
                    k.mm(p_[:], hT[:, kc, hs], wa[:, kc, c0:c0 + 512], kc == 0, False, rdh, [t_p])
                for kc in range(8):
                    k.mm(p_[:], hT[:, kc, hs1], wb_[:, kc, c0:c0 + 512], False, kc == 7, rdh, [t_p])
            k.cp("act", r32[:], pA[:, 0:256], [t_pA], [t_r])
            k.cp("act", k32[:], pA[:, 256:512], [t_pA], [t_k])
            k.cp("act", v32[:], pB[:, 0:256], [t_pB], [t_v])
            k.act(li[:, 0:64], pB[:, 256:320], AF.Tanh, [t_pB], [t_li])
            k.cp("act", li[:, 64:128], pB[:, 320:384], [t_pB], [t_li])
            k.act(li[:, 128:256], pB[:, 384:512], AF.Sigmoid, [t_pB], [t_li])
            pt, t_pt = k.psb()
            for j in range(2):
                k.tp(pt[:, j * 128:(j + 1) * 128], li[:, j * 128:(j + 1) * 128], [t_li], [t_pt])
            k.cp("dve", liT[:], pt[:, 0:256].rearrange("p (a c) -> p a c", a=2), [t_pt], [t_liT])
            pw, t_pw = k.ps()
            pa_, t_pa = k.ps()
            pg, t_pg = k.ps()
            k.mm(pw[:, 0:256], liT[0:64, 0, :], w2a2[0:64, :], True, True, [t_liT, t_lw], [t_pw])
            k.mm(pa_[:, 0:256], liT[64:128, 0, :], w2a2[64:128, :], True, True, [t_liT, t_lw], [t_pa])
            k.mm(pg[:, 0:256], liT[:, 1, :], g2[:], True, True, [t_liT, t_lw], [t_pg])
            k.tt("dve", lw[:], pw[:, 0:256], w0[:], ALU.add, [t_pw, t_bc], [t_lwv])
            k.act(lw[:], lw[:], AF.Sigmoid, [t_lwv], [t_lwv])
            k.ts("dve", lw[:], lw[:], -0.6065306597126334, None, ALU.mult, ALU.bypass, [t_lwv], [t_lwv])
            k.tt("dve", a32[:], pa_[:, 0:256], a0[:], ALU.add, [t_pa, t_bc], [t_a])
            k.act(a32[:], a32[:], AF.Sigmoid, [t_a], [t_a])
            k.cp("act", g32[:], pg[:, 0:256], [t_pg], [t_g])
            if l == 0:
                S.dma("sp", vfirst[s, tt_ * 128:(tt_ + 1) * 128, :], v32[:], reads=[t_v], writes=[t_vf[tt_]])
            else:
                S.dma("sp", vfb[:], vfirst[s, tt_ * 128:(tt_ + 1) * 128, :], reads=[t_vf[tt_]], writes=[t_vfb])
                k.cp("act", Vb[:], v32[:], [t_v], [t_Vb])
                pt, t_pt = k.psb()
                for j in range(2):
                    k.tp(pt[:, j * 128:(j + 1) * 128], Vb[:, j * 128:(j + 1) * 128], [t_Vb], [t_pt])
                k.cp("dve", vT[:], pt[:, 0:256].rearrange("p (a c) -> p a c", a=2), [t_pt], [t_vT])
                p1, t_p1 = k.ps()
                for j in range(2):
                    k.mm(p1[0:32, 0:128], v1[:, j, :], vT[:, j, :], j == 0, j == 1, [t_vT, t_lw], [t_p1])
                k.cp("act", u1[:], p1[0:32, 0:128], [t_p1], [t_u1])
                p2, t_p2 = k.ps()
                k.mm(p2[:, 0:256], u1[:], v2[:], True, True, [t_u1, t_lw], [t_p2])
                k.tt("dve", tmp[:], p2[:, 0:256], v0b[:], ALU.add, [t_p2, t_bc], [t_tmp])
                k.act(tmp[:], tmp[:], AF.Sigmoid, [t_tmp], [t_tmp])
                k.tt("dve", vfb[:], vfb[:], v32[:], ALU.subtract, [t_vfb, t_v], [t_vfb])
                k.tt("dve", vfb[:], vfb[:], tmp[:], ALU.mult, [t_vfb, t_tmp], [t_vfb])
                k.tt("dve", v32[:], v32[:], vfb[:], ALU.add, [t_v, t_vfb], [t_v])
            k.cp("act", Vb[:], v32[:], [t_v], [t_Vb])
            k.tt("dve", kk0[:], k32[:], kkb[:], ALU.mult, [t_k, t_bc], [t_kk])
            k.tt("pool", tmp[:], kk0[:], kk0[:], ALU.mult, [t_kk], [t_tmp])
            S.op("dve", lambda e: e.reduce_sum(out=st[:, 0:4], in_=tmp[:].rearrange("p (a c) -> p a c", a=4), axis=AX.X),
                 reads=[t_tmp], writes=[t_st])
            k.act(st[:, 0:4], st[:, 0:4], AF.Sqrt, [t_st], [t_st])
            k.ts("dve", st[:, 0:4], st[:, 0:4], 1e-12, None, ALU.max, ALU.bypass, [t_st], [t_st])
            k.recip(st[:, 4:8], st[:, 0:4], [t_st], [t_st])
            for h in range(4):
                k.ts("dve", kk0[:, h * 64:(h + 1) * 64], kk0[:, h * 64:(h + 1) * 64], st[:, 4 + h:5 + h], None, ALU.mult,
                     ALU.bypass, [t_kk, t_st], [t_kk])
            k.tt("dve", tmp2[:], a32[:], kab[:], ALU.mult, [t_a, t_bc], [t_tmp2])
            k.tt("pool", tmp2[:], tmp2[:], omka[:], ALU.add, [t_tmp2, t_bc], [t_tmp2])
            k.tt("dve", kp[:], k32[:], tmp2[:], ALU.mult, [t_k, t_tmp2], [t_kp])
            k.tt("pool", bv[:], kk0[:], a32[:], ALU.mult, [t_kk, t_a], [t_bv])
            pCm, t_pCm = k.ps()
            pCt, t_pCt = k.ps()
            pCr, t_pCr = k.ps()
            k.mm(pCm[:, 0:256], cs["c_triM"][:], lw[:], True, True, [t_lwv, k.t_const], [t_pCm])
            k.mm(pCt[:, 0:256], cs["c_tri"][:], lw[:], True, True, [t_lwv, k.t_const], [t_pCt])
            k.mm(pCr[:, 0:256], cs["c_triR"][:], lw[:], True, True, [t_lwv, k.t_const], [t_pCr])
            pc, t_pc = k.ps()
            for hp in range(2):
                k.mm(pc[:, hp * 2:hp * 2 + 2], lw[:, hp * 128:(hp + 1) * 128], cs["c_sel"][:], True, True,
                     [t_lwv, k.t_const], [t_pc])
            k.act(colv[:], pc[:, 0:4].rearrange("p (a c) -> p a c", a=2), AF.Exp, [t_pc], [t_cv])
            k.act(e1[:], pCm[:, 0:256], AF.Exp, [t_pCm], [t_e1])
            k.act(e2_[:], pCm[:, 0:256], AF.Exp, [t_pCm], [t_e2], scale=-1.0)
            k.act(e4[:], pCt[:, 0:256], AF.Exp, [t_pCt], [t_e4])
            k.act(e5[:], pCr[:, 0:256], AF.Exp, [t_pCr], [t_e5])
            k.act(ewl[:], lw[:], AF.Exp, [t_lwv], [t_ewl], scale=-1.0)
            k.tt("dve", Rp[:], r32[:], e1[:], ALU.mult, [t_r, t_e1], [t_Rp])
            k.tt("pool", tmp[:], kk0[:], ewl[:], ALU.mult, [t_kk, t_ewl, t_st], [t_tmp])
            k.stt(Ap[:], tmp[:], -1.0, e1[:], ALU.mult, ALU.mult, [t_tmp, t_e1], [t_Ap])
            k.stt(At[:], tmp[:], -1.0, e4[:], ALU.mult, ALU.mult, [t_tmp, t_e4], [t_At])
            k.tt("pool", Bp[:], bv[:], e2_[:], ALU.mult, [t_bv, t_e2], [t_Bp])
            k.tt("dve", Kp[:], kp[:], e2_[:], ALU.mult, [t_kp, t_e2], [t_Kp])
            k.tt("pool", Bc[:], bv[:], e5[:], ALU.mult, [t_bv, t_e5], [t_Bc])
            k.tt("dve", Kc[:], kp[:], e5[:], ALU.mult, [t_kp, t_e5], [t_Kc])
            pt, t_pt = k.psb()
            for j, src in enumerate((Ap, Rp, Bp, Kp)):
                for hp in range(2):
                    k.tp(pt[:, (j * 2 + hp) * 128:(j * 2 + hp + 1) * 128], src[:, hp * 128:(hp + 1) * 128],
                         [t_Ap, t_Rp, t_Bp, t_Kp], [t_pt])
            ptv = pt[:].rearrange("p (j a c) -> p j a c", j=4, a=2)
            for j in range(2):
                k.cp("act", ART[:, :, j, :], ptv[:, j, :, :], [t_pt], [t_ART])
            k.cp("dve", BT[:], ptv[:, 2, :, :], [t_pt], [t_BT])
            k.cp("dve", KT[:], ptv[:, 3, :, :], [t_pt], [t_KT])
            for hp in range(2):
                k.ts("dve", RTt[:, hp, :], ptv[:, 1, hp, :], colv[:, hp, 0:1], None, ALU.mult, ALU.bypass, [t_pt, t_cv],
                     [t_RTt])
            for h in range(4):
                hp, par = h // 2, h % 2
                hb = 64 * par
                LMh, t_LM = LM[h]
                pl, t_pl = k.ps()
                rhs_ar = ART[hb:hb + 64, hp, :, :].rearrange("p a c -> p (a c)")
                k.mm(pl[:, 0:256], BT[hb:hb + 64, hp, :], rhs_ar, True, True, [t_BT, t_ART], [t_pl])
                k.mm(pl[:, 256:512], KT[hb:hb + 64, hp, :], rhs_ar, True, True, [t_KT, t_ART], [t_pl])
                k.tt("dve", LMh[:], pl[:], mSI2[:], ALU.mult, [t_pl, k.t_const], [t_LM])
                L0, t_L0 = Lp[h][0]
                p0, t_p0 = k.ps()
                k.mm(p0[:, 0:128], ART[hb:hb + 64, hp, 0, :], BT[hb:hb + 64, hp, :], True, True, [t_ART, t_BT], [t_p0])
                k.tt("dve", L0[:, 0, :], p0[:, 0:128], mLow[:], ALU.mult, [t_p0, k.t_const], [t_L0])
                k.cp("pool", L0[:, 1, :], LMh[:, 0:128], [t_LM], [t_L0])
            for h in range(4):
                LMh, t_LM = LM[h]
                X0, t_X0 = Xs[h][0]
                px, t_px = k.ps()
                k.mm(px[:, 0:64], LMh[:, 256:384], Vb[:, h * 64:(h + 1) * 64], True, True, [t_LM, t_Vb], [t_px])
                k.cp("act", X0[:, 64:128], px[:, 0:64], [t_px], [t_X0])
                k.cp("pool", X0[:, 0:64], At[:, h * 64:(h + 1) * 64], [t_At], [t_X0])
            for j in range(7):
                if j < 6:
                    for h in range(4):
                        Lj, t_Lj = Lp[h][j]
                        Ln, t_Ln = Lp[h][j + 1]
                        p_, t_p = k.ps()
                        k.mm(p_[:, 0:128], Lj[:, 1, :], Lj[:, 0, :], True, True, [t_Lj], [t_p])
                        k.mm(p_[:, 128:256], Lj[:, 0, :], Lj[:, 1, :], True, True, [t_Lj], [t_p])
                        k.cp("act", Ln[:].rearrange("p a c -> p (a c)"), p_[:, 0:256], [t_p], [t_Ln])
                for h in range(4):
                    Xc, t_Xc = Xs[h][j % 2]
                    Xn, t_Xn = Xs[h][(j + 1) % 2]
                    Lj, t_Lj = Lp[h][j]
                    px, t_px = k.ps()
                    k.mm(px[:, 0:128], Lj[:, 1, :], Xc[:], True, True, [t_Lj, t_Xc], [t_px])
                    k.tt("dve", Xn[:], px[:, 0:128], Xc[:], ALU.add, [t_px, t_Xc], [t_Xn])
            pR, t_pR = k.ps()
            pY, t_pY = k.ps()
            pG, t_pG = k.ps()
            pZ, t_pZ = k.ps()
            for h in range(4):
                hp, par = h // 2, h % 2
                hb = 64 * par
                LMh, t_LM = LM[h]
                X7, t_X7 = Xs[h][1]
                hc = slice(h * 64, (h + 1) * 64)
                k.mm(pR[hb:hb + 64, hp * 128:(hp + 1) * 128], X7[:, 0:64], LMh[:, 128:256], True, True, [t_X7, t_LM], [t_pR])
                k.mm(pY[:, hc], LMh[:, 128:256], X7[:, 64:128], True, False, [t_X7, t_LM], [t_pY])
                k.mm(pY[:, hc], LMh[:, 384:512], Vb[:, hc], False, True, [t_Vb, t_LM], [t_pY])
                k.mm(pG[hb:hb + 64, hp * 64:(hp + 1) * 64], X7[:, 0:64], Bc[:, hc], True, True, [t_X7, t_Bc], [t_pG])
                k.mm(pZ[hb:hb + 64, hp * 64:(hp + 1) * 64], Bc[:, hc], X7[:, 64:128], True, False, [t_X7, t_Bc], [t_pZ])
                k.mm(pZ[hb:hb + 64, hp * 64:(hp + 1) * 64], Kc[:, hc], Vb[:, hc], False, True, [t_Kc, t_Vb], [t_pZ])
            k.tt("dve", RhT[:].rearrange("p a c -> p (a c)"), pR[:, 0:256], RTt[:].rearrange("p a c -> p (a c)"), ALU.add,
                 [t_pR, t_RTt], [t_RhT])
            k.cp("act", Ysb[:], pY[:, 0:256], [t_pY], [t_Y])
            k.cp("act", GpT[:].rearrange("p a c -> p (a c)"), pG[:, 0:128], [t_pG], [t_GpT])
            k.cp("act", Zsb[:].rearrange("p a c -> p (a c)"), pZ[:, 0:128], [t_pZ], [t_Z])
            Hbf, t_Hbf = Hbfs[tt_ % 2]
            Hbn, t_Hbn = Hbfs[1 - tt_ % 2]
            piE, t_piE = k.ps()
            piO, t_piO = k.ps()
            phE, t_phE = k.ps()
            phO, t_phO = k.ps()
            for h in range(4):
                hp, par = h // 2, h % 2
                hb = 64 * par
                pi, t_pi = (piE, t_piE) if par == 0 else (piO, t_piO)
                ph, t_ph = (phE, t_phE) if par == 0 else (phO, t_phO)
                k.mm(pi[:, hp * 64:(hp + 1) * 64], RhT[hb:hb + 64, hp, :], Hbf[hb:hb + 64, hp, :], True, True,
                     [t_RhT, t_Hbf], [t_pi])
                k.mm(ph[hb:hb + 64, hp * 64:(hp + 1) * 64], GpT[hb:hb + 64, hp, :], Hbf[hb:hb + 64, hp, :], True, True,
                     [t_GpT, t_Hbf], [t_ph])
            Yv = Ysb[:].rearrange("p (a b c) -> p a b c", a=2, b=2)
            k.tt("dve", Yv[:, :, 0, :], Yv[:, :, 0, :], piE[:, 0:128].rearrange("p (a c) -> p a c", a=2), ALU.add,
                 [t_Y, t_piE], [t_Y])
            k.tt("dve", Yv[:, :, 1, :], Yv[:, :, 1, :], piO[:, 0:128].rearrange("p (a c) -> p a c", a=2), ALU.add,
                 [t_Y, t_piO], [t_Y])
            for hp in range(2):
                k.stt(H32[:, hp, :], H32[:, hp, :], colv[:, hp, 1:2], Zsb[:, hp, :], ALU.mult, ALU.add, [t_H, t_cv, t_Z], [t_H])
            H2 = H32[:].rearrange("p a c -> p (a c)")
            k.tt("dve", H2[0:64, :], H2[0:64, :], phE[0:64, 0:128], ALU.add, [t_H, t_phE], [t_H])
            k.tt("dve", H2[64:128, :], H2[64:128, :], phO[64:128, 0:128], ALU.add, [t_H, t_phO], [t_H])
            k.cp("pool", Hbn[:], H32[:], [t_H], [t_Hbn])
            S.op("dve", lambda e: e.reduce_sum(out=st[:, 8:12], in_=Ysb[:].rearrange("p (a c) -> p a c", a=4), axis=AX.X),
                 reads=[t_Y], writes=[t_st])
            k.ts("dve", st[:, 8:12], st[:, 8:12], 1.0 / 64, None, ALU.mult, ALU.bypass, [t_st], [t_st])
            for h in range(4):
                k.ts("dve", Ysb[:, h * 64:(h + 1) * 64], Ysb[:, h * 64:(h + 1) * 64], st[:, 8 + h:9 + h], None, ALU.subtract,
                     ALU.bypass, [t_Y, t_st], [t_Y])
            k.tt("pool", tmp[:], Ysb[:], Ysb[:], ALU.mult, [t_Y], [t_tmp])
            S.op("dve", lambda e: e.reduce_sum(out=st[:, 12:16], in_=tmp[:].rearrange("p (a c) -> p a c", a=4), axis=AX.X),
                 reads=[t_tmp], writes=[t_st])
            k.ts("dve", st[:, 12:16], st[:, 12:16], 1.0 / 64, 64e-5, ALU.mult, ALU.add, [t_st], [t_st])
            k.act(st[:, 12:16], st[:, 12:16], AF.Sqrt, [t_st], [t_st])
            k.recip(st[:, 16:20], st[:, 12:16], [t_st], [t_st])
            for h in range(4):
                k.stt(Ysb[:, h * 64:(h + 1) * 64], Ysb[:, h * 64:(h + 1) * 64], st[:, 16 + h:17 + h], lng[:, h * 64:(h + 1) * 64],
                      ALU.mult, ALU.mult, [t_Y, t_st, t_bc], [t_Y])
            k.tt("pool", Ysb[:], Ysb[:], lnb[:], ALU.add, [t_Y, t_bc], [t_Y])
            k.tt("dve", tmp2[:], r32[:], kp[:], ALU.mult, [t_r, t_kp], [t_tmp2])
            k.tt("pool", tmp2[:], tmp2[:], rkb[:], ALU.mult, [t_tmp2, t_bc], [t_tmp2])
            S.op("dve", lambda e: e.reduce_sum(out=st[:, 20:24], in_=tmp2[:].rearrange("p (a c) -> p a c", a=4), axis=AX.X),
                 reads=[t_tmp2], writes=[t_st])
            for h in range(4):
                k.stt(Ysb[:, h * 64:(h + 1) * 64], v32[:, h * 64:(h + 1) * 64], st[:, 20 + h:21 + h], Ysb[:, h * 64:(h + 1) * 64],
                      ALU.mult, ALU.add, [t_Y, t_st, t_v], [t_Y])
            k.tt("dve", ybf[:], Ysb[:], g32[:], ALU.mult, [t_Y, t_g], [t_ybf])
            pt2, t_pt2 = k.psb()
            for hp in range(2):
                k.tp(pt2[:, hp * 128:(hp + 1) * 128], ybf[:, hp * 128:(hp + 1) * 128], [t_ybf], [t_pt2])
            gq, q = tt_ // 4, tt_ % 4
            yg, t_yg = ygs[gq % 2]
            k.cp("act", yg[:, :, q * 128:(q + 1) * 128], pt2[:, 0:256].rearrange("p (a c) -> p a c", a=2), [t_pt2], [t_yg])
            if q == 3:
                for hp in range(2):
                    S.dma("sp", ctx["yT"][1024 + hp * 128:1024 + (hp + 1) * 128, gq * 512:(gq + 1) * 512], yg[:, hp, :],
                          reads=[t_yg], writes=[ctx["t_yT"][8 + hp][gq]])
        S.barrier()


_NC_CACHE = {}


def kernel(**inputs):
    cfg = {}
    key = "main"
    if key not in _NC_CACHE:
        _NC_CACHE[key] = build(cfg)
    nc = _NC_CACHE[key]
    consts = host_consts()
    in_maps = []
    for c in range(8):
        m = {"x": np.ascontiguousarray(inputs["x"][2 * c:2 * c + 2]),
             "mem": np.ascontiguousarray(inputs["mem"][2 * c:2 * c + 2]),
             "positions": np.ascontiguousarray(inputs["positions"][2 * c:2 * c + 2]).astype(np.int32)}
        for n in WNAMES + VNAMES:
            m[n] = np.ascontiguousarray(inputs[n], dtype=np.float32)
        m.update(consts)
        in_maps.append(m)
    res = run_bass_kernel_spmd(nc, in_maps, core_ids=list(range(8)))
    return np.concatenate([r["out"] for r in res.results], axis=0).astype(np.float32)
```
